# Optimizing a Trainium2 kernel written in Bass

```python
import jax, jax.numpy as jnp
from jax import lax
import numpy as np

D_MODEL = 1024
BATCH = 16
SEQ = 2048
DEPTH = 1
DEC_BATCH = 8
DEC_SEQ = 2048
PAST_LEN = 128

GRID_W = 64
N_MEM = 256
EPS = 1e-6
GLA_HEADS = 4
GLA_DK = D_MODEL // 2 // GLA_HEADS
GLA_DV = D_MODEL // GLA_HEADS
GLA_QK = GLA_HEADS * GLA_DK
GLA_V = GLA_HEADS * GLA_DV
GLA_RANK = 16
GLA_TAU = 16.0
GLA_CHUNK = 64
ATT_HEADS = 8
ATT_KV_HEADS = 2
ATT_HD = 128
ATT_Q = ATT_HEADS * ATT_HD
ATT_KV = ATT_KV_HEADS * ATT_HD
ROPE_THETA = 10000.0
Q_BLOCK = 128
X_HEADS = 4
X_HD = D_MODEL // X_HEADS
D_FF = ((8 * D_MODEL + 3 * 256 - 1) // (3 * 256)) * 256
IN_SPLITS = (GLA_QK, GLA_QK, GLA_V, GLA_V, GLA_RANK, GLA_RANK, ATT_Q, ATT_KV, ATT_KV, D_MODEL, D_MODEL)
D_IN = sum(IN_SPLITS)

kernel_name = "hybrid_gla_gqa_axial_encoder"


def rmsnorm(x, w):
    xf = x.astype(jnp.float32)
    y = xf * lax.rsqrt(jnp.mean(xf * xf, axis=-1, keepdims=True) + EPS) * w.astype(jnp.float32)
    return y.astype(x.dtype)


def gla_chunked(q, k, v, g, strict):
    B, H, T, dk = q.shape
    dv = v.shape[-1]
    C = GLA_CHUNK
    N = T // C
    q = q.reshape(B, H, N, C, dk)
    k = k.reshape(B, H, N, C, dk)
    v = v.reshape(B, H, N, C, dv)
    b = jnp.cumsum(g.reshape(B, H, N, C, dk), axis=3)
    b_last = b[:, :, :, -1:, :]
    qg = q * jnp.exp(b)
    kg = k * jnp.exp(-b)
    k_end = k * jnp.exp(b_last - b)
    mask = jnp.tril(jnp.ones((C, C), dtype=bool), k=-1 if strict else 0)
    a = jnp.where(mask, jnp.einsum('bhnid,bhnjd->bhnij', qg, kg), 0.0)
    o_intra = jnp.einsum('bhnij,bhnje->bhnie', a, v)

    def step(S, inp):
        qg_n, kend_n, v_n, dec_n = inp
        o_n = jnp.einsum('bhid,bhde->bhie', qg_n, S)
        S = dec_n[..., None] * S + jnp.einsum('bhjd,bhje->bhde', kend_n, v_n)
        return S, o_n

    xs = (jnp.moveaxis(qg, 2, 0), jnp.moveaxis(k_end, 2, 0), jnp.moveaxis(v, 2, 0),
          jnp.moveaxis(jnp.exp(b_last[:, :, :, 0, :]), 2, 0))
    S0 = jnp.zeros((B, H, dk, dv), q.dtype)
    _, o_inter = lax.scan(step, S0, xs)
    o = o_intra + jnp.moveaxis(o_inter, 0, 2)
    return o.reshape(B, H, T, dv)


def gla_branch(q, k, v, r, zf, zb, wa_f, ba_f, wa_b, ba_b, gnorm):
    B, T, _ = q.shape
    f32 = jnp.float32

    def heads(t, d):
        return t.astype(f32).reshape(B, T, GLA_HEADS, d).transpose(0, 2, 1, 3)

    qh = heads(q, GLA_DK) * (GLA_DK ** -0.5)
    kh = heads(k, GLA_DK)
    vh = heads(v, GLA_DV)
    gf = heads(jax.nn.log_sigmoid(zf.astype(f32) @ wa_f.astype(f32) + ba_f.astype(f32)) / GLA_TAU, GLA_DK)
    gb = heads(jax.nn.log_sigmoid(zb.astype(f32) @ wa_b.astype(f32) + ba_b.astype(f32)) / GLA_TAU, GLA_DK)
    flip = lambda t: jnp.flip(t, axis=2)
    o_fwd = gla_chunked(qh, kh, vh, gf, False)
    o_bwd = flip(gla_chunked(flip(qh), flip(kh), flip(vh), flip(gb), True))
    o = rmsnorm(o_fwd + o_bwd, gnorm)
    o = o.transpose(0, 2, 1, 3).reshape(B, T, GLA_V)
    return (o * jax.nn.silu(r.astype(f32))).astype(q.dtype)


def axial_rope_tables(T):
    rows = T // GRID_W
    row = jnp.broadcast_to(jnp.arange(rows)[:, None], (rows, GRID_W)).reshape(-1).astype(jnp.float32)
    col = jnp.broadcast_to(jnp.arange(GRID_W)[None, :], (rows, GRID_W)).reshape(-1).astype(jnp.float32)
    half = ATT_HD // 2
    inv_freq = ROPE_THETA ** (-jnp.arange(0, half, 2, dtype=jnp.float32) / half)
    ang_r = row[:, None] * inv_freq[None, :]
    ang_c = col[:, None] * inv_freq[None, :]
    return jnp.cos(ang_r), jnp.sin(ang_r), jnp.cos(ang_c), jnp.sin(ang_c)


def rotate(x, cos, sin):
    shape = (cos.shape[0],) + (1,) * (x.ndim - 3) + (cos.shape[-1],)
    cos = cos.reshape(shape)
    sin = sin.reshape(shape)
    x1, x2 = jnp.split(x, 2, axis=-1)
    return jnp.concatenate([x1 * cos - x2 * sin, x2 * cos + x1 * sin], axis=-1)


def apply_axial_rope(x, tables):
    cr, sr, cc, sc = tables
    xr, xc = jnp.split(x, 2, axis=-1)
    return jnp.concatenate([rotate(xr, cr, sr), rotate(xc, cc, sc)], axis=-1)


def gqa_branch(q, k, v, q_norm, k_norm):
    B, T, _ = q.shape
    G = ATT_HEADS // ATT_KV_HEADS
    f32 = jnp.float32
    qh = rmsnorm(q.astype(f32).reshape(B, T, ATT_KV_HEADS, G, ATT_HD), q_norm)
    kh = rmsnorm(k.astype(f32).reshape(B, T, ATT_KV_HEADS, ATT_HD), k_norm)
    vh = v.astype(f32).reshape(B, T, ATT_KV_HEADS, ATT_HD)
    tables = axial_rope_tables(T)
    qh = apply_axial_rope(qh, tables) * (ATT_HD ** -0.5)
    kh = apply_axial_rope(kh, tables)
    qb = qh.reshape(B, T // Q_BLOCK, Q_BLOCK, ATT_KV_HEADS, G, ATT_HD).transpose(1, 0, 2, 3, 4, 5)

    def block(qi):
        s = jnp.einsum('bqkgd,bskd->bkgqs', qi, kh)
        p = jax.nn.softmax(s, axis=-1)
        return jnp.einsum('bkgqs,bskd->bqkgd', p, vh)

    o = lax.map(block, qb)
    o = o.transpose(1, 0, 2, 3, 4, 5).reshape(B, T, ATT_Q)
    return o.astype(q.dtype)


def cross_attn(h, m, wq, wkv, wo):
    B, T, _ = h.shape
    M = m.shape[1]
    f32 = jnp.float32
    q = (h @ wq).astype(f32).reshape(B, T, X_HEADS, X_HD) * (X_HD ** -0.5)
    k, v = jnp.split((m @ wkv).astype(f32), 2, axis=-1)
    k = k.reshape(B, M, X_HEADS, X_HD)
    v = v.reshape(B, M, X_HEADS, X_HD)
    p = jax.nn.softmax(jnp.einsum('bthd,bmhd->bhtm', q, k), axis=-1)
    o = jnp.einsum('bhtm,bmhd->bthd', p, v).reshape(B, T, D_MODEL)
    return o.astype(h.dtype) @ wo


def encoder_layer(x, mem, ln_mix_pre, w_in, gla_wa_f, gla_ba_f, gla_wa_b, gla_ba_b, gla_norm,
                  att_q_norm, att_k_norm, w_branch_gla, w_branch_att, w_out, ln_mix_post,
                  ln_x_pre, ln_mem, x_wq, x_wkv, x_wo, ln_x_post,
                  ln_ffn_pre, ffn_wi, ffn_wo, ln_ffn_post):
    h = rmsnorm(x, ln_mix_pre)
    proj = h @ w_in
    split_idx = np.cumsum(IN_SPLITS)[:-1].tolist()
    gq, gk, gv, gr, zf, zb, aq, ak, av, gate_a, gate_b = jnp.split(proj, split_idx, axis=-1)
    out_a = gla_branch(gq, gk, gv, gr, zf, zb, gla_wa_f, gla_ba_f, gla_wa_b, gla_ba_b, gla_norm) @ w_branch_gla
    out_b = gqa_branch(aq, ak, av, att_q_norm, att_k_norm) @ w_branch_att
    mixed = jax.nn.sigmoid(gate_a) * out_a + jax.nn.sigmoid(gate_b) * out_b
    x = x + rmsnorm(mixed @ w_out, ln_mix_post)
    h = rmsnorm(x, ln_x_pre)
    m = rmsnorm(mem, ln_mem)
    x = x + rmsnorm(cross_attn(h, m, x_wq, x_wkv, x_wo), ln_x_post)
    h = rmsnorm(x, ln_ffn_pre)
    g, u = jnp.split(h @ ffn_wi, 2, axis=-1)
    x = x + rmsnorm((jax.nn.silu(g) * u) @ ffn_wo, ln_ffn_post)
    return x


def setup_inputs(seed: int = 0) -> dict:
    key = jax.random.key(seed)
    ks = iter(jax.random.split(key, 40))
    f32 = jnp.float32

    def w(shape, fan_in):
        return jax.random.normal(next(ks), (DEPTH,) + shape, f32) * (fan_in ** -0.5)

    def gain(n):
        return 1.0 + 0.05 * jax.random.normal(next(ks), (DEPTH, n), f32)

    def bias(n):
        return 0.01 * jax.random.normal(next(ks), (DEPTH, n), f32)

    return {
        "x_prompt": jax.random.normal(next(ks), (BATCH, SEQ, D_MODEL), f32),
        "x_sample": jax.random.normal(next(ks), (DEC_BATCH, DEC_SEQ, D_MODEL), f32),
        "mem_prompt": jax.random.normal(next(ks), (BATCH, N_MEM, D_MODEL), f32),
        "mem_sample": jax.random.normal(next(ks), (DEC_BATCH, N_MEM, D_MODEL), f32),
        "ln_mix_pre": gain(D_MODEL),
        "w_in": w((D_MODEL, D_IN), D_MODEL),
        "gla_wa_f": w((GLA_RANK, GLA_QK), GLA_RANK),
        "gla_ba_f": bias(GLA_QK),
        "gla_wa_b": w((GLA_RANK, GLA_QK), GLA_RANK),
        "gla_ba_b": bias(GLA_QK),
        "gla_norm": gain(GLA_DV),
        "att_q_norm": gain(ATT_HD),
        "att_k_norm": gain(ATT_HD),
        "w_branch_gla": w((GLA_V, D_MODEL), GLA_V),
        "w_branch_att": w((ATT_Q, D_MODEL), ATT_Q),
        "w_out": w((D_MODEL, D_MODEL), D_MODEL),
        "ln_mix_post": gain(D_MODEL),
        "ln_x_pre": gain(D_MODEL),
        "ln_mem": gain(D_MODEL),
        "x_wq": w((D_MODEL, D_MODEL), D_MODEL),
        "x_wkv": w((D_MODEL, 2 * D_MODEL), D_MODEL),
        "x_wo": w((D_MODEL, D_MODEL), D_MODEL),
        "ln_x_post": gain(D_MODEL),
        "ln_ffn_pre": gain(D_MODEL),
        "ffn_wi": w((D_MODEL, 2 * D_FF), D_MODEL),
        "ffn_wo": w((D_FF, D_MODEL), D_FF),
        "ln_ffn_post": gain(D_MODEL),
    }


def reference(x_prompt, x_sample, mem_prompt, mem_sample, ln_mix_pre, w_in, gla_wa_f, gla_ba_f,
              gla_wa_b, gla_ba_b, gla_norm, att_q_norm, att_k_norm, w_branch_gla, w_branch_att,
              w_out, ln_mix_post, ln_x_pre, ln_mem, x_wq, x_wkv, x_wo, ln_x_post,
              ln_ffn_pre, ffn_wi, ffn_wo, ln_ffn_post):
    params = (ln_mix_pre, w_in, gla_wa_f, gla_ba_f, gla_wa_b, gla_ba_b, gla_norm,
              att_q_norm, att_k_norm, w_branch_gla, w_branch_att, w_out, ln_mix_post,
              ln_x_pre, ln_mem, x_wq, x_wkv, x_wo, ln_x_post,
              ln_ffn_pre, ffn_wi, ffn_wo, ln_ffn_post)
    y_prompt = x_prompt
    y_sample = x_sample
    for l in range(DEPTH):
        lp = [p[l] for p in params]
        y_prompt = encoder_layer(y_prompt, mem_prompt, *lp)
        y_sample = encoder_layer(y_sample, mem_sample, *lp)
    return (y_prompt, y_sample)
```

```python
import numpy as np
import concourse.bass as bass
import concourse.mybir as mybir
from concourse.bass_utils import run_bass_kernel_spmd

F32 = mybir.dt.float32
BF16 = mybir.dt.bfloat16
AF = mybir.ActivationFunctionType
ALU = mybir.AluOpType

T = 2048
D = 1024
NMEM = 256
EPS = 1e-6
DFF = 2816
NCORES = 8
SEQ_PER_CORE = 3

ENGS = ("pe", "act", "dve", "pool", "sp")
SAME_ENGINE_FULL_SYNC = False


class Reg:
    __slots__ = ("w", "r")

    def __init__(self):
        self.w = None
        self.r = {}


def regs(n):
    return [Reg() for _ in range(n)]


class Slot:
    def __init__(self, nc, name):
        self.sem = nc.alloc_semaphore(name)
        self.n = 0


class Prog:
    def __init__(self, nc):
        self.nc = nc
        self.ops = []
        self.by_eng = {e: [] for e in ENGS}
        self.sems = {e: nc.alloc_semaphore("sem_" + e) for e in ENGS}
        self.out_slots = []
        self.fence_deps = set()
        self.fence_pending = set()
        self.dma_since = []

    def fence(self):
        deps = set(self.dma_since)
        for e in ENGS:
            for oid in reversed(self.by_eng[e]):
                if self.ops[oid][3] is None:
                    deps.add(oid)
                    break
        self.fence_deps = deps
        self.fence_pending = set(ENGS)
        self.dma_since = []

    def op(self, eng, fn, reads=(), writes=(), slot=None):
        oid = len(self.ops)
        deps = set()
        is_dma = slot is not None
        if eng in self.fence_pending:
            self.fence_pending.discard(eng)
            for d in self.fence_deps:
                if self.ops[d][3] is None and self.ops[d][0] == eng and not is_dma:
                    continue
                deps.add(d)
        if is_dma:
            self.dma_since.append(oid)

        def add(pid, kind):
            peng, _, _, pslot, _ = self.ops[pid]
            if pslot is None and not is_dma and peng == eng:
                if eng == "pe" or (kind != "raw" and not SAME_ENGINE_FULL_SYNC):
                    return
            deps.add(pid)

        for r in reads:
            if r.w is not None:
                add(r.w, "raw")
        for w in writes:
            if w.w is not None:
                add(w.w, "waw")
            for pid in w.r.values():
                add(pid, "war")
        key = ("dma", oid) if is_dma else eng
        for r in reads:
            r.r[key] = oid
        for w in writes:
            w.w = oid
            w.r = {}
        val = None
        if is_dma:
            slot.n += 1
            val = 16 * slot.n
        self.ops.append((eng, fn, deps, slot, val))
        self.by_eng[eng].append(oid)
        return oid

    def emit(self):
        nc = self.nc
        ops = self.ops
        marked = set()
        for (_, _, deps, _, _) in ops:
            for d in deps:
                if ops[d][3] is None:
                    marked.add(d)
        tok = {}
        for e in ENGS:
            cnt = 0
            for oid in self.by_eng[e]:
                eng, fn, deps, slot, val = ops[oid]
                if slot is not None:
                    tok[oid] = (slot.sem, val)
                elif oid in marked:
                    cnt += 1
                    tok[oid] = (self.sems[e], cnt)
        self.nmarked = len(marked)
        final_waits = [(s.sem, 16 * s.n) for s in self.out_slots if s.n > 0]
        handles = {"pe": "tensor", "act": "scalar", "dve": "vector", "pool": "gpsimd", "sp": "sync"}

        def run_engine(e, engine):
            waited = {}
            for oid in self.by_eng[e]:
                eng, fn, deps, slot, val = ops[oid]
                needs = {}
                for d in deps:
                    sem, v = tok[d]
                    k = sem.num
                    if waited.get(k, 0) < v and needs.get(k, (None, 0))[1] < v:
                        needs[k] = (sem, v)
                for k, (sem, v) in needs.items():
                    engine.wait_ge(sem, v)
                    waited[k] = v
                inst = fn(engine)
                if slot is not None:
                    inst.then_inc(slot.sem, 16)
                elif oid in marked:
                    inst.then_inc(self.sems[e], 1)
            if e == "sp":
                for sem, v in final_waits:
                    engine.wait_ge(sem, v)

        with nc.Block() as block:
            for e in ENGS:
                def mk(e=e):
                    def body(engine):
                        run_engine(e, engine)
                    return body
                getattr(block, handles[e])(mk())


class Arena:
    def __init__(self, nc, nbytes):
        self.t = nc.alloc_sbuf_tensor("arena", [128, nbytes // 2], BF16)
        self.nbytes = nbytes
        self.off = 0
        self.peak = 0

    def alloc(self, shape, dtype, parts=128):
        esz = 4 if dtype == F32 else 2
        n = int(np.prod(shape))
        nb = (n * esz + 31) // 32 * 32
        o = self.off
        self.off += nb
        self.peak = max(self.peak, self.off)
        assert self.off <= self.nbytes, f"arena overflow {self.off} > {self.nbytes}"
        v = self.t[0:parts, o // 2:(o + n * esz) // 2]
        if dtype == F32:
            v = v.bitcast(F32)
        if len(shape) == 2:
            v = v.rearrange("p (a b) -> p a b", b=shape[1])
        elif len(shape) == 3:
            v = v.rearrange("p (a b c) -> p a b c", b=shape[1], c=shape[2])
        return v

    def alloc_at(self, off, shape, dtype, parts=128):
        save = self.off
        self.off = off
        v = self.alloc(shape, dtype, parts)
        end = self.off
        self.off = save
        return v, end

    def mark(self):
        return self.off

    def reset(self, m):
        self.off = m


def _seg(W, groups):
    K = W.shape[0]
    KC = K // 128
    out = []
    for cols in groups:
        t = W[:, cols].reshape(KC, 128, len(cols)).transpose(1, 0, 2)
        out.append(np.ascontiguousarray(t).reshape(128, KC * len(cols)))
    return np.stack(out, 0)


def _ar(a, b):
    return np.arange(a, b)


SEG_ORDER = ["z", "gla0", "gla1", "wba", "ga", "akv", "xwkv", "aq", "wbb", "gb", "wo", "xwq", "xwo", "wi", "wo2"]
SEG_SHAPE = {
    "z": (1, 8 * 32), "akv": (1, 8 * 512), "gla0": (1, 8 * 1536), "gla1": (1, 8 * 1536),
    "ga": (2, 4096), "wba": (2, 4096), "aq": (2, 4096), "gb": (2, 4096), "wbb": (2, 4096),
    "wo": (2, 4096), "xwkv": (4, 4096), "xwq": (2, 4096), "xwo": (2, 4096),
    "wi": (11, 4096), "wo2": (22, 1024),
}


def seg_offsets():
    off = {}
    o = 0
    for s in SEG_ORDER:
        ng, L = SEG_SHAPE[s]
        off[s] = o
        o += ng * 128 * L
    return off, o


def build_wall(inp):
    w_in = inp["w_in"][0]
    segs = {}
    for p in range(2):
        h0, h1 = 2 * p, 2 * p + 1
        cols = np.concatenate([
            _ar(h0 * 128, h0 * 128 + 128), _ar(h1 * 128, h1 * 128 + 128),
            _ar(512 + h0 * 128, 512 + h0 * 128 + 128), _ar(512 + h1 * 128, 512 + h1 * 128 + 128),
            _ar(1024 + h0 * 256, 1024 + h0 * 256 + 256), _ar(1024 + h1 * 256, 1024 + h1 * 256 + 256),
            _ar(2048 + h0 * 256, 2048 + h0 * 256 + 256), _ar(2048 + h1 * 256, 2048 + h1 * 256 + 256)])
        segs["gla%d" % p] = _seg(w_in, [cols])
    segs["z"] = _seg(w_in, [_ar(3072, 3104)])
    segs["akv"] = _seg(w_in, [_ar(4128, 4640)])
    segs["aq"] = _seg(w_in, [_ar(3104, 3616), _ar(3616, 4128)])
    segs["ga"] = _seg(w_in, [_ar(4640, 5152), _ar(5152, 5664)])
    segs["gb"] = _seg(w_in, [_ar(5664, 6176), _ar(6176, 6688)])
    two = [_ar(0, 512), _ar(512, 1024)]
    segs["wba"] = _seg(inp["w_branch_gla"][0], two)
    segs["wbb"] = _seg(inp["w_branch_att"][0], two)
    segs["wo"] = _seg(inp["w_out"][0], two)
    segs["xwq"] = _seg(inp["x_wq"][0], two)
    segs["xwo"] = _seg(inp["x_wo"][0], two)
    segs["xwkv"] = _seg(inp["x_wkv"][0], [_ar(i * 512, (i + 1) * 512) for i in range(4)])
    wi = inp["ffn_wi"][0]
    segs["wi"] = _seg(wi, [np.concatenate([_ar(256 * i, 256 * i + 256), _ar(DFF + 256 * i, DFF + 256 * i + 256)])
                           for i in range(11)])
    w2 = inp["ffn_wo"][0]
    pcs = []
    for half in range(2):
        for i in range(11):
            blk = w2[256 * i:256 * i + 256, half * 512:(half + 1) * 512]
            pcs.append(np.ascontiguousarray(blk.reshape(2, 128, 512).transpose(1, 0, 2)).reshape(128, 1024))
    segs["wo2"] = np.stack(pcs, 0)
    off, tot = seg_offsets()
    wall = np.empty(tot, np.float32)
    for s in SEG_ORDER:
        ng, L = SEG_SHAPE[s]
        a = segs[s]
        assert a.shape == (ng, 128, L), (s, a.shape)
        wall[off[s]:off[s] + a.size] = a.reshape(-1)
    return wall.reshape(-1, 2048)


K_ID = 0
K_UF = 128
K_WF = 256
K_UB = 384
K_WB = 512
K_MASK = 640
K_ROT = 1152
K_COS = 1280
K_SIN = 1376
NK = 1472


def build_kpack():
    k = np.zeros((128, NK), np.float32)
    j = np.arange(128)[:, None]
    i = np.arange(128)[None, :]
    k[:, K_ID:K_ID + 128] = (j == i)
    c = -1.0 / 16.0
    k[:, K_UF:K_UF + 128] = c * (j <= i)
    k[:, K_WF:K_WF + 128] = c * (j > i)
    k[:, K_UB:K_UB + 128] = c * (j >= i)
    k[:, K_WB:K_WB + 128] = c * (j < i)
    mf = (j <= i).astype(np.float32)
    mb = (j > i).astype(np.float32)
    k[:, K_MASK:K_MASK + 512] = np.concatenate([mf, mf, mb, mb], 1)
    R = np.zeros((128, 128), np.float32)
    for m in range(128):
        if (m % 64) < 32:
            R[m + 32, m] = -1.0
        else:
            R[m - 32, m] = 1.0
    k[:, K_ROT:K_ROT + 128] = R
    inv = (10000.0 ** (-np.arange(0, 64, 2, dtype=np.float32) / np.float32(64))).astype(np.float32)
    for d in range(128):
        f = inv[d % 32]
        if d < 64:
            ang = (np.arange(32, dtype=np.float32) * f).astype(np.float32)
            k[d, K_COS:K_COS + 32] = np.cos(ang)
            k[d, K_SIN:K_SIN + 32] = np.sin(ang)
        else:
            ang = (np.arange(64, dtype=np.float32) * f).astype(np.float32)
            k[d, K_COS + 32:K_COS + 96] = np.cos(ang)
            k[d, K_SIN + 32:K_SIN + 96] = np.sin(ang)
    return k


C_MIXPRE, C_XPRE, C_FFNPRE, C_MEM, C_QN, C_KN = 0, 8, 16, 24, 32, 33
NCP = 34


def build_cpack(inp):
    c = np.zeros((128, NCP), np.float32)
    c[:, C_MIXPRE:C_MIXPRE + 8] = inp["ln_mix_pre"][0].reshape(8, 128).T
    c[:, C_XPRE:C_XPRE + 8] = inp["ln_x_pre"][0].reshape(8, 128).T
    c[:, C_FFNPRE:C_FFNPRE + 8] = inp["ln_ffn_pre"][0].reshape(8, 128).T
    c[:, C_MEM:C_MEM + 8] = inp["ln_mem"][0].reshape(8, 128).T
    c[:, C_QN] = inp["att_q_norm"][0]
    c[:, C_KN] = inp["att_k_norm"][0]
    return c


class Builder:
    def __init__(self, nseq=SEQ_PER_CORE, upto="all", dbg=None):
        self.nseq = nseq
        self.upto = upto
        self.dbg = dbg or {}
        nc = bass.Bass("TRN2", target_bir_lowering=False)
        self.nc = nc
        self.P = Prog(nc)
        self.soff, self.wtot = seg_offsets()
        self.x = nc.dram_tensor("x", [nseq, T, D], F32, kind="ExternalInput").ap()
        self.mem = nc.dram_tensor("mem", [nseq, NMEM, D], F32, kind="ExternalInput").ap()
        self.wall = nc.dram_tensor("wall", [self.wtot // 2048, 2048], F32, kind="ExternalInput").ap()
        self.kpack = nc.dram_tensor("kpack", [128, NK], F32, kind="ExternalInput").ap()
        self.cpack = nc.dram_tensor("cpack", [128, NCP], F32, kind="ExternalInput").ap()
        self.gpost = nc.dram_tensor("gpost", [3, 128, D], F32, kind="ExternalInput").ap()
        self.gnorm2 = nc.dram_tensor("gnorm2", [128, 512], F32, kind="ExternalInput").ap()
        self.waug = nc.dram_tensor("waug", [17, 1024], F32, kind="ExternalInput").ap()
        self.y = nc.dram_tensor("y", [nseq, T, D], F32, kind="ExternalOutput").ap()
        self.wbf = nc.dram_tensor("wbf", [self.wtot // 2048, 2048], BF16).ap()
        self.wbf_flat = self.wbf.rearrange("r c -> (r c)")
        self.wbf_reg = {s: Reg() for s in SEG_ORDER}
        self.dbg_out = {}
        for name, shape in self.dbg.items():
            self.dbg_out[name] = nc.dram_tensor("dbg_" + name, list(shape), F32, kind="ExternalOutput").ap()
        self.A = Arena(nc, 207 * 1024)
        self.PS = [nc.alloc_psum_tensor("ps%d" % i, [128, 1024], F32).ap() if False else
                   nc.alloc_psum_tensor("ps%d" % i, [128, 1024], F32) for i in range(4)]
        self.psr = regs(8)

    def bank(self, b, lo=0, hi=512):
        return self.PS[b // 2][:, (b % 2) * 512 + lo:(b % 2) * 512 + hi]

    def mm(self, out, lhsT, rhs, start, stop, reads, writes):
        self.P.op("pe", lambda e: e.matmul(out, lhsT, rhs, start=start, stop=stop), reads, writes)

    def tr(self, out, in_, reads, writes):
        ident = self.K[:, K_ID:K_ID + 128]
        self.P.op("pe", lambda e: e.transpose(out, in_, ident), list(reads) + [self.Kr], writes)

    def act(self, out, in_, func, reads, writes, scale=1.0, bias=0.0, accum=None):
        def fn(e):
            kw = {}
            if accum is not None:
                kw["accum_out"] = accum
            return e.activation(out, in_, func, bias=bias, scale=scale, **kw)
        self.P.op("act", fn, reads, writes)

    def tt(self, eng, out, in0, in1, op, reads, writes):
        self.P.op(eng, lambda e: e.tensor_tensor(out, in0, in1, op), reads, writes)

    def ts(self, eng, out, in0, s1, op0, reads, writes, s2=None, op1=None):
        if op1 is None:
            self.P.op(eng, lambda e: e.tensor_scalar(out, in0, s1, None, op0), reads, writes)
        else:
            self.P.op(eng, lambda e: e.tensor_scalar(out, in0, s1, s2, op0, op1), reads, writes)

    def stt(self, out, in0, scalar, in1, op0, op1, reads, writes):
        self.P.op("dve", lambda e: e.scalar_tensor_tensor(out, in0, scalar, in1, op0, op1), reads, writes)

    def cp(self, eng, out, in_, reads, writes):
        if eng == "act":
            self.P.op("act", lambda e: e.activation(out, in_, AF.Copy), reads, writes)
        else:
            self.P.op(eng, lambda e: e.tensor_copy(out, in_), reads, writes)

    def dma(self, q, out, in_, slot, reads, writes):
        self.P.op(q, lambda e: e.dma_start(out, in_), reads, writes, slot=slot)

    def newslot(self, name, output=False):
        s = Slot(self.nc, name)
        if output:
            self.P.out_slots.append(s)
        return s

    def wsrc(self, seg, g):
        ng, L = SEG_SHAPE[seg]
        o = self.soff[seg] + g * 128 * L
        return self.wbf_flat[o:o + 128 * L].rearrange("(p l) -> p l", p=128), L

    def init_consts(self):
        A, nc = self.A, self.nc
        self.K = A.alloc([NK], F32)
        self.Kr = Reg()
        self.C = A.alloc([NCP], F32)
        self.Cr = Reg()
        self.G2 = A.alloc([512], F32)
        self.G2r = Reg()
        self.gp = A.alloc([D], F32)
        self.gpr = Reg()
        self.gp_slot = self.newslot("gp")
        self.waug_f = A.alloc_at(A.nbytes - 56 * 1024, [1024], F32, parts=17)[0]
        self.waug_b = A.alloc([1024], BF16, parts=17)
        self.waugr = Reg()
        self.ones = A.alloc([128], BF16)
        self.onesr = Reg()
        self.cosB = A.alloc([512], F32)
        self.sinB = A.alloc([512], F32)
        self.csr = Reg()
        s = self.newslot("c0")
        self.dma("sp", self.K, self.kpack, s, [], [self.Kr])
        s = self.newslot("c1")
        self.dma("sp", self.C, self.cpack, s, [], [self.Cr])
        s = self.newslot("c2")
        self.dma("sp", self.G2, self.gnorm2, s, [], [self.G2r])
        s = self.newslot("c3")
        r0 = Reg()
        self.dma("sp", self.waug_f, self.waug, s, [], [r0])
        self.cp("dve", self.waug_b, self.waug_f, [r0], [self.waugr])
        self.P.op("pool", lambda e: e.memset(self.ones, 1.0), [], [self.onesr])
        self.epsv = A.alloc([8], F32)
        self.epsr = Reg()
        self.P.op("pool", lambda e: e.memset(self.epsv[:, 0:1], EPS), [], [self.epsr])
        self.P.op("pool", lambda e: e.memset(self.epsv[:, 1:2], 1.0), [], [self.epsr])
        self.conv_done = set()
        self.issue_conv(["z", "gla0"])
        for tab, kc in ((self.cosB, K_COS), (self.sinB, K_SIN)):
            src = self.K[64:128, kc + 32:kc + 96].unsqueeze(1).broadcast_to([64, 8, 64])
            dst = tab[64:128, :].rearrange("p (r c) -> p r c", c=64)
            self.P.op("pool", (lambda e, dst=dst, src=src: e.tensor_copy(dst, src)), [self.Kr], [self.csr])
        self.nring = 4
        self.ring = [A.alloc([4096], BF16) for _ in range(self.nring)]
        self.ring_r = regs(self.nring)
        self.ring_s = [self.newslot("ring%d" % i) for i in range(self.nring)]
        self.make_sched()

    def issue_conv(self, names):
        for sname in names:
            if sname in self.conv_done:
                continue
            self.conv_done.add(sname)
            ng, L = SEG_SHAPE[sname]
            r_lo = self.soff[sname] // 2048
            r_hi = (self.soff[sname] + ng * 128 * L) // 2048
            s = self.newslot("cv_" + sname)
            src = self.wall[r_lo:r_hi, :]
            dst = self.wbf[r_lo:r_hi, :]
            self.dma("pool", dst, src, s, [], [self.wbf_reg[sname]])

    def make_sched(self):
        L = []
        for s in range(self.nseq):
            L.append(("z", 0))
            L += [("wba", 0), ("ga", 0), ("wba", 1), ("ga", 1)]
            L += [("akv", 0)] + [("xwkv", i) for i in range(4)]
            for tb in range(4):
                L += [("aq", 0), ("aq", 1), ("wbb", 0), ("gb", 0), ("wbb", 1), ("gb", 1), ("wo", 0), ("wo", 1),
                      ("xwq", 0), ("xwq", 1), ("xwo", 0), ("xwo", 1)] + [("wi", i) for i in range(11)]
        self.wsched = L
        self.wptr = 0
        self.wissued = 0

    def wissue_upto(self, n):
        n = min(n, len(self.wsched))
        while self.wissued < n:
            k = self.wissued
            seg, g = self.wsched[k]
            i = k % self.nring
            src, L = self.wsrc(seg, g)
            assert seg in self.conv_done, seg
            self.dma("sp", self.ring[i][:, 0:L], src, self.ring_s[i], [self.wbf_reg[seg]], [self.ring_r[i]])
            self.wissued += 1

    def wget(self, seg, g, prefetch_only=False):
        if prefetch_only:
            return None
        if self.upto != "all":
            while self.wsched[self.wptr] != (seg, g):
                self.wptr += 1
            self.wissued = max(self.wissued, self.wptr)
        k = self.wptr
        assert self.wsched[k] == (seg, g), (k, self.wsched[k], seg, g)
        self.wissue_upto(k + self.nring - 1)
        self.wptr += 1
        i = k % self.nring
        return self.ring[i], self.ring_r[i]

    def rope_top(self, tb):
        for tab, kc in ((self.cosB, K_COS), (self.sinB, K_SIN)):
            src = self.K[0:64, kc + tb * 8:kc + tb * 8 + 8].unsqueeze(2).broadcast_to([64, 8, 64])
            dst = tab[0:64, :].rearrange("p (r c) -> p r c", c=64)
            self.P.op("pool", (lambda e, dst=dst, src=src: e.tensor_copy(dst, src)), [self.Kr], [self.csr])

    def norm_tile(self, t, xb=None, xbr=None):
        sc = self.sc
        if xb is None:
            xb, xbr = self.xb, self.xbr
        ss, nsr, xn, xnr = sc["ss"], sc["nsr"], sc["xn"], sc["xnr"]
        self.act(sc["junk"], xb[:, t, :], AF.Square, [xbr[t]], [sc["junkr"], nsr[t][0]], accum=ss[:, t:t + 1])
        self.act(ss[:, 4 + t:5 + t], ss[:, t:t + 1], AF.Ln, [nsr[t][0], self.epsr], [nsr[t][1]], scale=1.0 / D,
                 bias=self.epsv[:, 0:1])
        self.act(ss[:, 8 + t:9 + t], ss[:, 4 + t:5 + t], AF.Exp, [nsr[t][1]], [nsr[t][2]], scale=-0.5)
        self.ts("dve", xn[:, t, :], xb[:, t, :], ss[:, 8 + t:9 + t], ALU.mult, [xbr[t], nsr[t][2]], [xnr[t]])

    def norm_fin(self, nt, gcol, dst, dst_regs, banks):
        xn, xnr = self.sc["xn"], self.sc["xnr"]
        for c in range(8):
            b = banks[c % len(banks)]
            for t in range(nt):
                self.tr(self.bank(b, t * 128, (t + 1) * 128), xn[:, t, c * 128:(c + 1) * 128], [xnr[t]], [self.psr[b]])
            g = self.C[:, gcol + c:gcol + c + 1]
            if c % 2 == 0:
                self.P.op("act", (lambda e, o=dst(c), i=self.bank(b, 0, nt * 128), g=g:
                                  e.activation(o, i, AF.Copy, scale=g)), [self.psr[b], self.Cr], [dst_regs[c]])
            else:
                self.ts("dve", dst(c), self.bank(b, 0, nt * 128), g, ALU.mult, [self.psr[b], self.Cr], [dst_regs[c]])

    def norm_T(self, xb, xbr, nt, gcol, dst, dst_regs, scratch, banks):
        for t in range(nt):
            self.norm_tile(t)
        self.norm_fin(nt, gcol, dst, dst_regs, banks)

    def eps_ap(self):
        return self.epsv[:, 0:1]

    def pbank(self, b, lo=0, hi=512, p0=0, p1=128):
        return self.PS[b // 2][p0:p1, (b % 2) * 512 + lo:(b % 2) * 512 + hi]

    def alloc_xb(self, nt=4, junk=None):
        A = self.A
        self.xb = A.alloc([nt, D], F32)
        self.xbr = regs(nt)
        if junk is None:
            junk = (A.alloc([D], BF16), Reg())
        self.sc = {"junk": junk[0], "junkr": junk[1], "ss": self.ssP, "nsr": [regs(3) for _ in range(4)],
                   "pnr": [regs(3) for _ in range(4)],
                   "xn": A.alloc([nt, D], F32), "xnr": regs(nt)}

    def load_xb(self, src_rows, nt=4):
        for t in range(nt):
            self.dma("sp", self.xb[:, t, :], src_rows[t * 128:(t + 1) * 128, :], self.xb_slot[t], [], [self.xbr[t]])

    def dbg_store(self, name, src_ap, src_regs, f32tmp=None):
        if name not in self.dbg_out:
            return
        dst = self.dbg_out[name]
        s = self.newslot("dbg_" + name, output=True)
        if f32tmp is not None:
            r = Reg()
            self.cp("dve", f32tmp, src_ap, src_regs, [r])
            self.dma("sp", dst, f32tmp, s, [r], [])
        else:
            self.dma("sp", dst, src_ap, s, src_regs, [])

    def alloc_nr(self):
        A = self.A
        self.nr = {"sqb": A.alloc([512], BF16), "tmpA": A.alloc([512], F32), "knf": A.alloc([512], F32),
                   "t2": A.alloc([512], F32), "r": regs(4), "knf2": A.alloc([512], F32), "r_kn2": Reg()}

    def normrope_pipe(self, items, pbanks, bs, br):
        nr = self.nr
        sqb, tmpA, t2 = nr["sqb"], nr["tmpA"], nr["t2"]
        knfs = [nr["knf"], nr["knf2"]]
        r_sq, r_tmp, r_kn0, r_t2 = nr["r"]
        r_kns = [r_kn0, nr["r_kn2"]]
        psr = self.psr
        n = len(items)
        nb = len(pbanks)
        assert nb >= 3

        def stA(i):
            items[i][0](pbanks[i % nb])

        def stB(i):
            b = pbanks[i % nb]
            src, src_reg = self.bank(b), psr[b]
            gcol = items[i][1]
            knf, r_kn = knfs[i % 2], r_kns[i % 2]
            self.act(sqb, src, AF.Square, [src_reg], [r_sq])
            self.mm(self.bank(bs), self.ones, sqb, True, True, [self.onesr, r_sq], [psr[bs]])
            self.act(tmpA, self.bank(bs), AF.Ln, [psr[bs], self.epsr], [r_tmp], scale=1.0 / 128, bias=self.epsv[:, 0:1])
            self.act(tmpA, tmpA, AF.Exp, [r_tmp], [r_tmp], scale=-0.5)
            self.stt(knf, src, self.C[:, gcol:gcol + 1], tmpA, ALU.mult, ALU.mult, [src_reg, self.Cr, r_tmp], [r_kn])

        def stC(i):
            _, gcol, out, out_regs, pre = items[i]
            knf, r_kn = knfs[i % 2], r_kns[i % 2]
            if pre is not None:
                pre()
            self.mm(self.bank(br), self.K[:, K_ROT:K_ROT + 128], knf, True, True, [self.Kr, r_kn], [psr[br]])
            self.tt("dve", t2, self.bank(br), self.sinB, ALU.mult, [psr[br], self.csr], [r_t2])
            self.tt("pool", knf, knf, self.cosB, ALU.mult, [r_kn, self.csr], [r_kn])
            self.tt("pool", out, knf, t2, ALU.add, [r_kn, r_t2], out_regs)

        for step in range(n + 2):
            if step < n:
                stA(step)
            if 0 <= step - 1 < n:
                stB(step - 1)
            if 0 <= step - 2 < n:
                stC(step - 2)

    def postnorm(self, k, t, ssr_i):
        sc = self.sc
        ss = sc["ss"]
        r0, r1, r2 = sc["pnr"][t]
        u = self.PS[k][:, :]
        pr = [self.psr[2 * k], self.psr[2 * k + 1]]
        self.act(sc["junk"], u, AF.Square, pr, [sc["junkr"], r0], accum=ss[:, 16 + t:17 + t])
        self.act(ss[:, 20 + t:21 + t], ss[:, 16 + t:17 + t], AF.Ln, [r0, self.epsr], [r1], scale=1.0 / D,
                 bias=self.epsv[:, 0:1])
        self.act(ss[:, 24 + t:25 + t], ss[:, 20 + t:21 + t], AF.Exp, [r1], [r2], scale=-0.5)
        xn, xnr = sc["xn"], sc["xnr"]
        self.stt(xn[:, t, :], u, ss[:, 24 + t:25 + t], self.gp, ALU.mult, ALU.mult, pr + [r2, self.gpr], [xnr[t]])
        self.tt("dve", self.xb[:, t, :], self.xb[:, t, :], xn[:, t, :], ALU.add, [self.xbr[t], xnr[t]], [self.xbr[t]])

    def load_gp(self, idx):
        self.dma("sp", self.gp, self.gpost[idx], self.gp_slot, [], [self.gpr])

    def phaseA(self, s):
        for tb in range(4):
            self.load_xb(self.x[s, tb * 512:(tb + 1) * 512, :])
            self.norm_T(self.xb, self.xbr, 4, C_MIXPRE,
                        lambda c, tb=tb: self.hT[:, c, tb * 512:(tb + 1) * 512],
                        [self.hTr[c][tb] for c in range(8)], self.sc, [0, 1])

    def phaseGLA(self, s):
        A, P, psr, bank, pbank = self.A, self.P, self.psr, self.bank, self.pbank
        hT, hTr = self.hT, self.hTr
        Kc = self.K
        zT = [A.alloc([T], BF16, parts=17) for _ in range(2)]
        zTr = [regs(4), regs(4)]
        for d in range(2):
            P.op("pool", (lambda e, z=zT[d]: e.memset(z, 1.0)), [], zTr[d])
        wz, wzr = self.wget("z", 0)
        wz3 = wz[:, 0:256].rearrange("p (c n) -> p c n", n=32)
        for tb in range(4):
            blk = slice(tb * 512, (tb + 1) * 512)
            for d in range(2):
                b = d
                for c in range(8):
                    self.mm(pbank(b, 0, 512, 0, 16), wz3[:, c, d * 16:(d + 1) * 16], hT[:, c, blk], c == 0, c == 7,
                            [wzr, hTr[c][tb]], [psr[b]])
                self.cp("dve", zT[d][0:16, blk], pbank(b, 0, 512, 0, 16), [psr[b]], [zTr[d][tb]])
        wgla = A.alloc([8 * 1536], BF16)
        wglar = Reg()
        wg3 = wgla.rearrange("p (c n) -> p c n", n=1536)
        Sb = A.alloc([16, 512], BF16)
        Sbr = regs(16)
        S32 = A.alloc([512], F32)
        S32r = regs(2)
        Sfbf = A.alloc([512], BF16)
        Sfbfr = Reg()

        def dbl(shape, dt):
            return [A.alloc(shape, dt) for _ in range(2)], regs(2)
        def sgl(shape, dt):
            a, r = A.alloc(shape, dt), Reg()
            return [a, a], [r, r]
        sp, spr = sgl([512], F32)
        E1, E1r = dbl([512], F32)
        E2, E2r = sgl([512], F32)
        E3, E3r = sgl([256], F32)
        vst = A.alloc([16, 512], BF16)
        vstr = regs(16)
        qgf, qgfr = dbl([256], BF16)
        qgb, qgbr = dbl([256], BF16)
        kgf, kgfr = dbl([256], BF16)
        kgb, kgbr = dbl([256], BF16)
        kend, kendr = dbl([256], BF16)
        gr, grr = dbl([512], F32)
        og, ogr = dbl([512], F32)
        decb, decbr = dbl([8], F32)
        AT = A.alloc([512], BF16)
        ATr = Reg()
        ss2 = A.alloc([8], F32)
        ss2r = regs(3)
        junk = A.alloc([256], BF16)
        junkr = Reg()
        one_ap = self.epsv[:, 1:2]
        SC = float(128.0 ** -0.5)
        for p in range(2):
            src, L = self.wsrc("gla%d" % p, 0)
            self.dma("sp", wgla, src, self.wgla_slot, [self.wbf_reg["gla%d" % p]], [wglar])
            self.issue_conv(["aq", "wbb", "gb", "wo", "xwq", "xwo"] if p == 0 else ["wi", "wo2"])
            fo = p * 256
            bo = 512 + p * 256

            def b_s1(tt):
                q = tt % 2
                tk = slice(tt * 128, (tt + 1) * 128)
                tb = tt // 4
                hr = lambda c: hTr[c][tb]
                self.mm(bank(0, 0, 256), zT[1][0:17, tk], self.waug_b[0:17, bo:bo + 256], True, True,
                        [zTr[1][tb], self.waugr], [psr[0]])
                self.act(sp[q][:, 0:256], bank(0, 0, 256), AF.Exp, [psr[0]], [spr[q]], scale=-1.0)
                self.act(sp[q][:, 0:256], sp[q][:, 0:256], AF.Ln, [spr[q], self.epsr], [spr[q]], bias=one_ap)
                for c in range(8):
                    self.mm(bank(3), hT[:, c, tk], wg3[:, c, 512:1024], c == 0, c == 7, [hr(c), wglar], [psr[3]])
                for c in range(8):
                    self.mm(bank(2, 0, 256), hT[:, c, tk], wg3[:, c, 256:512], c == 0, c == 7, [hr(c), wglar], [psr[2]])
                for h in range(2):
                    self.mm(bank(1, h * 128, (h + 1) * 128), sp[q][:, h * 128:(h + 1) * 128], Kc[:, K_UB:K_UB + 128],
                            True, True, [spr[q], self.Kr], [psr[1]])
                self.mm(bank(1, 256, 512), Kc[:, K_WB:K_WB + 128], sp[q][:, 0:256], True, True, [spr[q], self.Kr], [psr[1]])
                self.cp("act", vst[:, tt, :], bank(3), [psr[3]], [vstr[tt]])
                self.act(decb[q][:, 0:2], bank(1, 0, 256).rearrange("p (h i) -> p h i", i=128)[:, :, 0], AF.Exp,
                         [psr[1]], [decbr[q]])
                self.act(E3[q], bank(1, 256, 512), AF.Exp, [psr[1]], [E3r[q]])
                self.tt("dve", kend[q], bank(2, 0, 256), E3[q], ALU.mult, [psr[2], E3r[q]], [kendr[q]])

            def b_s2(tt):
                q = tt % 2
                for h in range(2):
                    self.mm(bank(7, h * 256, (h + 1) * 256), kend[q][:, h * 128:(h + 1) * 128],
                            vst[:, tt, h * 256:(h + 1) * 256], True, True, [kendr[q], vstr[tt]], [psr[7]])
                for h in range(2):
                    hs = slice(h * 256, (h + 1) * 256)
                    self.stt(S32[:, hs], S32[:, hs], decb[q][:, h:h + 1], bank(7, h * 256, (h + 1) * 256), ALU.mult, ALU.add,
                             [S32r[h], decbr[q], psr[7]], [S32r[h]])
                if tt > 0:
                    self.cp("pool", Sb[:, tt - 1, :], S32, S32r, [Sbr[tt - 1]])

            P.op("pool", (lambda e: e.memset(S32, 0.0)), [], S32r)
            b_s1(15)
            for tt in range(15, -1, -1):
                if tt > 0:
                    b_s1(tt - 1)
                b_s2(tt)

            def f_s1(tt, mid_hook=None, pending=None):
                q = tt % 2
                tk = slice(tt * 128, (tt + 1) * 128)
                tb = tt // 4
                hr = lambda c: hTr[c][tb]
                self.mm(bank(0, 0, 256), zT[0][0:17, tk], self.waug_b[0:17, fo:fo + 256], True, True,
                        [zTr[0][tb], self.waugr], [psr[0]])
                self.mm(bank(0, 256, 512), zT[1][0:17, tk], self.waug_b[0:17, bo:bo + 256], True, True,
                        [zTr[1][tb], self.waugr], [psr[0]])
                self.act(sp[q], bank(0), AF.Exp, [psr[0]], [spr[q]], scale=-1.0)
                self.act(sp[q], sp[q], AF.Ln, [spr[q], self.epsr], [spr[q]], bias=one_ap)
                if pending is not None:
                    pending()
                for c in range(8):
                    self.mm(bank(2, 256, 512), hT[:, c, tk], wg3[:, c, 256:512], c == 0, c == 7, [hr(c), wglar], [psr[2]])
                if mid_hook is not None:
                    mid_hook()
                for d in range(2):
                    ku = K_UF if d == 0 else K_UB
                    for h in range(2):
                        o = d * 256 + h * 128
                        self.mm(bank(1, o, o + 128), sp[q][:, o:o + 128], Kc[:, ku:ku + 128], True, True,
                                [spr[q], self.Kr], [psr[1]])
                self.mm(bank(2, 0, 256), Kc[:, K_WF:K_WF + 128], sp[q][:, 0:256], True, True, [spr[q], self.Kr], [psr[2]])
                self.act(E1[q], bank(1), AF.Exp, [psr[1]], [E1r[q]])
                self.act(E2[q], bank(1), AF.Exp, [psr[1]], [E2r[q]], scale=-1.0)
                self.act(E3[q], bank(2, 0, 256), AF.Exp, [psr[2]], [E3r[q]])
                for j in range(4):
                    for c in range(8):
                        self.mm(bank(0, j * 128, (j + 1) * 128), wg3[:, c, j * 128:(j + 1) * 128], hT[:, c, tk],
                                c == 0, c == 7, [hr(c), wglar], [psr[0]])
                self.tt("dve", kend[q], bank(2, 256, 512), E3[q], ALU.mult, [psr[2], E3r[q]], [kendr[q]])
                self.stt(qgf[q], bank(0, 0, 256), SC, E1[q][:, 0:256], ALU.mult, ALU.mult, [psr[0], E1r[q]], [qgfr[q]])
                self.stt(qgb[q], bank(0, 0, 256), SC, E1[q][:, 256:512], ALU.mult, ALU.mult, [psr[0], E1r[q]], [qgbr[q]])
                self.tt("dve", kgf[q], bank(0, 256, 512), E2[q][:, 0:256], ALU.mult, [psr[0], E2r[q]], [kgfr[q]])
                self.tt("dve", kgb[q], bank(0, 256, 512), E2[q][:, 256:512], ALU.mult, [psr[0], E2r[q]], [kgbr[q]])

            def f_R(tt):
                tk = slice(tt * 128, (tt + 1) * 128)
                tb = tt // 4
                for c in range(8):
                    self.mm(bank(4), hT[:, c, tk], wg3[:, c, 1024:1536], c == 0, c == 7, [hTr[c][tb], wglar], [psr[4]])

            def f_gr(tt):
                q = tt % 2
                self.act(gr[q], bank(4), AF.Exp, [psr[4]], [grr[q]], scale=-1.0)
                self.act(gr[q], gr[q], AF.Ln, [grr[q], self.epsr], [grr[q]], bias=one_ap)
                self.act(gr[q], gr[q], AF.Exp, [grr[q]], [grr[q]], scale=-1.0)
                self.tt("dve", gr[q], bank(4), gr[q], ALU.mult, [psr[4], grr[q]], [grr[q]])
                self.tt("pool", gr[q], gr[q], self.G2, ALU.mult, [grr[q], self.G2r], [grr[q]])

            def f_s2a(tt):
                q = tt % 2
                kg = (kgf[q], kgb[q])
                qg = (qgf[q], qgb[q])
                kgr = (kgfr[q], kgbr[q])
                qgr = (qgfr[q], qgbr[q])
                for d in range(2):
                    for h in range(2):
                        o = (d * 2 + h) * 128
                        self.mm(bank(5, o, o + 128), kg[d][:, h * 128:(h + 1) * 128], qg[d][:, h * 128:(h + 1) * 128],
                                True, True, [kgr[d], qgr[d]], [psr[5]])
                self.tt("dve", AT, bank(5), Kc[:, K_MASK:K_MASK + 512], ALU.mult, [psr[5], self.Kr], [ATr])

            def f_s2(tt):
                q = tt % 2
                for h in range(2):
                    self.mm(bank(7, h * 256, (h + 1) * 256), kend[q][:, h * 128:(h + 1) * 128],
                            vst[:, tt, h * 256:(h + 1) * 256], True, True, [kendr[q], vstr[tt]], [psr[7]])
                for h in range(2):
                    vh = vst[:, tt, h * 256:(h + 1) * 256]
                    seq = []
                    if tt > 0:
                        seq.append((qgf[q][:, h * 128:(h + 1) * 128], Sfbf[:, h * 256:(h + 1) * 256], [qgfr[q], Sfbfr]))
                    if tt < 15:
                        seq.append((qgb[q][:, h * 128:(h + 1) * 128], Sb[:, tt, h * 256:(h + 1) * 256], [qgbr[q], Sbr[tt]]))
                    seq += [(AT[:, h * 128:(h + 1) * 128], vh, [ATr, vstr[tt]]),
                            (AT[:, (2 + h) * 128:(3 + h) * 128], vh, [ATr, vstr[tt]])]
                    for i, (l, r, rd) in enumerate(seq):
                        self.mm(bank(6, h * 256, (h + 1) * 256), l, r, i == 0, i == len(seq) - 1, rd, [psr[6]])
                for h in range(2):
                    hs = slice(h * 256, (h + 1) * 256)
                    self.stt(S32[:, hs], S32[:, hs], E1[q][:, h * 128 + 127:h * 128 + 128], bank(7, h * 256, (h + 1) * 256),
                             ALU.mult, ALU.add, [S32r[h], E1r[q], psr[7]], [S32r[h]])
                self.cp("pool", Sfbf, S32, S32r, [Sfbfr])
                for h in range(2):
                    self.act(junk[:, 0:256], bank(6, h * 256, (h + 1) * 256), AF.Square, [psr[6]], [junkr, ss2r[0]],
                             accum=ss2[:, h:h + 1])
                self.act(ss2[:, 2:4], ss2[:, 0:2], AF.Ln, [ss2r[0], self.epsr], [ss2r[1]], scale=1.0 / 256,
                         bias=self.epsv[:, 0:1])
                self.act(ss2[:, 4:6], ss2[:, 2:4], AF.Exp, [ss2r[1]], [ss2r[2]], scale=-0.5)
                for h in range(2):
                    hs = slice(h * 256, (h + 1) * 256)
                    self.stt(og[q][:, hs], bank(6, h * 256, (h + 1) * 256), ss2[:, 4 + h:5 + h], gr[q][:, hs],
                             ALU.mult, ALU.mult, [psr[6], ss2r[2], grr[q]], [ogr[q]])

            def f_s3(tt):
                q = tt % 2
                tk = slice(tt * 128, (tt + 1) * 128)
                for e4 in range(4):
                    self.tr(bank(7, e4 * 128, (e4 + 1) * 128), og[q][:, e4 * 128:(e4 + 1) * 128], [ogr[q]], [psr[7]])
                self.cp("act", self.glaT[:, p * 4:(p + 1) * 4, tk], bank(7).rearrange("p (e t) -> p e t", t=128),
                        [psr[7]], [self.glaTr[c][tt] for c in range(p * 4, p * 4 + 4)])

            P.op("pool", (lambda e: e.memset(S32, 0.0)), [], S32r)
            f_s1(0)
            f_R(0)
            f_gr(0)
            for tt in range(16):
                f_s2a(tt)
                if tt + 1 < 16:
                    f_s1(tt + 1, (lambda tt=tt: f_s3(tt - 1)) if tt > 0 else None,
                         (lambda tt=tt: f_gr(tt)) if tt > 0 else None)
                    f_s2(tt)
                    f_R(tt + 1)
                else:
                    f_gr(tt)
                    f_s2(tt)
                    f_s3(tt - 1)
            f_s3(15)

    def phaseS4(self, s):
        A, psr, bank = self.A, self.psr, self.bank
        sig = [A.alloc([512], F32) for _ in range(2)]
        sigr = regs(2)
        k = 0
        for grp in range(2):
            wb, wbr = self.wget("wba", grp)
            wg, wgr = self.wget("ga", grp)
            wb3 = wb.rearrange("p (c n) -> p c n", n=512)
            wg3 = wg.rearrange("p (c n) -> p c n", n=512)
            for tb in range(4):
                blk = slice(tb * 512, (tb + 1) * 512)
                for fl in range(4):
                    fc = grp * 4 + fl
                    b0, b1 = (0, 1) if k % 2 == 0 else (2, 3)
                    for c in range(8):
                        self.mm(bank(b0), wb3[:, c, fl * 128:(fl + 1) * 128], self.glaT[:, c, blk], c == 0, c == 7,
                                [wbr] + self.glaTr[c][tb * 4:(tb + 1) * 4], [psr[b0]])
                    for c in range(8):
                        self.mm(bank(b1), wg3[:, c, fl * 128:(fl + 1) * 128], self.hT[:, c, blk], c == 0, c == 7,
                                [wgr, self.hTr[c][tb]], [psr[b1]])
                    self.act(sig[k % 2], bank(b1), AF.Sigmoid, [psr[b1]], [sigr[k % 2]])
                    self.tt("dve", self.maT[:, fc, blk], bank(b0), sig[k % 2], ALU.mult, [psr[b0], sigr[k % 2]],
                            [self.maTr[fc][tb]])
                    k += 1

    def phaseS2(self, s):
        A, psr, bank = self.A, self.psr, self.bank
        hT, hTr = self.hT, self.hTr
        self.alloc_nr()
        wkv, wkvr = self.wget("akv", 0)
        w3 = wkv.rearrange("p (c n) -> p c n", n=512)
        items = []
        for tb in range(4):
            blk = slice(tb * 512, (tb + 1) * 512)
            for g in range(2):
                def proj(b, g=g, blk=blk, tb=tb):
                    for c in range(8):
                        self.mm(bank(b), w3[:, c, g * 128:(g + 1) * 128], hT[:, c, blk], c == 0, c == 7,
                                [wkvr, hTr[c][tb]], [psr[b]])
                pre = (lambda tb=tb: self.rope_top(tb)) if g == 0 else None
                items.append((proj, C_KN, self.kT[:, g, blk], [self.kTr[g][tb]], pre))
        self.normrope_pipe(items, [2, 3, 4], 5, 6)
        for tb in range(4):
            for t in range(4):
                tile = tb * 4 + t
                for c in range(8):
                    self.mm(bank(7, 0, 256), hT[:, c, tile * 128:(tile + 1) * 128], w3[:, c, 256:512], c == 0, c == 7,
                            [wkvr, hTr[c][tb]], [psr[7]])
                self.cp("act", self.vatt[:, tile, :], bank(7, 0, 256), [psr[7]], [self.vattr[tile]])
        self.alloc_xb(2, junk=(self.nr["t2"].bitcast(BF16), self.nr["r"][3]))
        mT = A.alloc([8, 256], BF16)
        mTr = regs(8)
        self.load_xb(self.mem[s], 2)
        self.norm_T(self.xb, self.xbr, 2, C_MEM, lambda c: mT[:, c, :], mTr, self.sc, [0, 1])
        for grp in range(2):
            w, wr = self.wget("xwkv", grp)
            w3 = w.rearrange("p (c n) -> p c n", n=512)
            for fl in range(4):
                kc = grp * 4 + fl
                b = 2 + (kc % 2)
                for c in range(8):
                    self.mm(bank(b, 0, 256), w3[:, c, fl * 128:(fl + 1) * 128], mT[:, c, :], c == 0, c == 7,
                            [wr, mTr[c]], [psr[b]])
                self.cp("act" if kc % 2 == 0 else "dve", self.kmT[:, kc, :], bank(b, 0, 256), [psr[b]], [self.kmTr[kc]])
        for grp in range(2):
            w, wr = self.wget("xwkv", 2 + grp)
            w3 = w.rearrange("p (c n) -> p c n", n=512)
            for mt in range(2):
                b = 2 + mt
                for c in range(8):
                    self.mm(bank(b), mT[:, c, mt * 128:(mt + 1) * 128], w3[:, c, :], c == 0, c == 7, [wr, mTr[c]], [psr[b]])
                self.cp("act" if mt == 0 else "dve", self.vm[:, mt, grp * 512:(grp + 1) * 512], bank(b), [psr[b]],
                        [self.vmr[mt]])

    def alloc_S5(self):
        A = self.A
        self.sg = [A.alloc([512], F32) for _ in range(2)]
        self.sgr = regs(2)
        self.alloc_xb(4, junk=(self.sg[1].bitcast(BF16), self.sgr[1]))
        self.xb2 = [(self.xb, self.xbr), (A.alloc([4, D], F32), regs(4))]
        self.bufA = A.alloc([8, 512], BF16)
        self.bufB = A.alloc([8, 512], BF16)
        self.bufAr, self.bufBr = regs(8), regs(8)
        self.actT = A.alloc([22, 512], BF16)
        r = regs(22)
        for j in (13, 15, 17, 19):
            r[j + 1] = r[j]
        self.actTr = r
        self.bufC = self.actT[:, 0:8, :]
        self.bufCr = r[0:8]
        self.PT = [self.actT[:, 8 + i, :] for i in range(4)]
        self.PTr = [r[8 + i] for i in range(4)]

        def f32v(j):
            return self.actT[:, j:j + 2, :].rearrange("p a b -> p (a b)").bitcast(F32)
        self.nr = {"sqb": self.actT[:, 12, :], "tmpA": f32v(13), "knf": f32v(15), "t2": f32v(17),
                   "r": [r[12], r[13], r[15], r[17]], "knf2": f32v(19), "r_kn2": r[19]}
        self.rden = self.nr["tmpA"]
        self.rdenr = self.nr["r"][1]
        self.woring = [A.alloc([1024], BF16) for _ in range(3)]
        self.woring_r = regs(3)

    def wo2issue_upto(self, n):
        n = min(n, (self.cur_seq + 1) * 4 * 22)
        while self.wo2_issued < n:
            k = self.wo2_issued
            i = k % 3
            src, L = self.wsrc("wo2", k % 22)
            self.dma("sp", self.woring[i], src, self.woring_s[i], [self.wbf_reg["wo2"]], [self.woring_r[i]])
            self.wo2_issued += 1

    def wo2get(self, g):
        k = self.wo2_i
        assert k % 22 == g
        self.wo2issue_upto(k + 2)
        self.wo2_i += 1
        i = k % 3
        return self.woring[i].rearrange("p (k n) -> p k n", n=512), self.woring_r[i]

    def proj_res(self, srcs, wseg, gain_idx, next_norm=True):
        psr, bank = self.psr, self.bank
        self.load_gp(gain_idx)
        w0, w0r = self.wget(wseg, 0)
        w1, w1r = self.wget(wseg, 1)
        ws = [(w0.rearrange("p (c n) -> p c n", n=512), w0r), (w1.rearrange("p (c n) -> p c n", n=512), w1r)]
        for t in range(4):
            k = self.pn_k
            self.pn_k = (k + 1) % 2
            for half in range(2):
                w3, wr = ws[half]
                n = len(srcs) * 8
                i = 0
                for (apf, rf) in srcs:
                    for c in range(8):
                        self.mm(bank(2 * k + half), apf(c, t), w3[:, c, :], i == 0, i == n - 1, [wr] + rf(c, t),
                                [psr[2 * k + half]])
                        i += 1
            self.postnorm(k, t, 0)
            if next_norm:
                self.norm_tile(t)

    def phaseS5(self, s, tb):
        psr, bank = self.psr, self.bank
        pb = tb % 2
        bufs = [(self.bufA, self.bufAr), (self.bufB, self.bufBr)]
        bufA, bufAr = bufs[pb]
        bufB, bufBr = bufs[1 - pb]
        bufC, bufCr = self.bufC, self.bufCr
        self.xb, self.xbr = self.xb2[pb]
        blk = slice(tb * 512, (tb + 1) * 512)
        if tb == 0:
            self.load_xb(self.x[s, blk, :])
            for t in range(4):
                self.norm_tile(t)
            self.norm_fin(4, C_MIXPRE, lambda c: bufA[:, c, :], bufAr, [0, 1])
        self.rope_top(tb)
        items = []
        wq = [self.wget("aq", grp) for grp in range(2)]
        for head in range(8):
            w, wr = wq[head // 4]
            w3 = w.rearrange("p (c n) -> p c n", n=512)
            hl = head % 4

            def proj(b, w3=w3, wr=wr, hl=hl):
                for c in range(8):
                    self.mm(bank(b), w3[:, c, hl * 128:(hl + 1) * 128], bufA[:, c, :], c == 0, c == 7,
                            [wr, bufAr[c]], [psr[b]])
            items.append((proj, C_QN, bufB[:, head, :], [bufBr[head]], None))
        self.normrope_pipe(items, [2, 5, 6], 3, 4)
        SCL = float(128.0 ** -0.5)
        sbanks = [2, 3, 0, 1]
        LA = 3
        ptl = list(zip(self.PT, self.PTr)) + [(self.actT[:, 15, :], self.actTr[15]), (self.actT[:, 17, :], self.actTr[17])]
        NP = len(ptl)
        seq = [(head, st) for head in range(8) for st in range(16)]

        aT, ar = self.actT, self.actTr
        S1 = [(aT[:, 12, :], ar[12]), (aT[:, 21, :], ar[21]), (aT[:, 19, :], ar[19])]
        S2 = [(aT[:, 15, :], ar[15]), (aT[:, 17, :], ar[17])]

        def score(j):
            head, st = seq[j]
            g = head // 4
            sb = sbanks[j % 4]
            self.mm(bank(sb), self.kT[:, g, st * 128:(st + 1) * 128], bufB[:, head, :], True, True,
                    [self.kTr[g][st // 4], bufBr[head]], [psr[sb]])
            self.act(ptl[j % NP][0], bank(sb), AF.Exp, [psr[sb]], [ptl[j % NP][1]], scale=SCL)
            if st % 2 == 1:
                s1, s1r = S1[(j // 2) % 3]
                self.tt("dve", s1, ptl[(j - 1) % NP][0], ptl[j % NP][0], ALU.add,
                        [ptl[(j - 1) % NP][1], ptl[j % NP][1]], [s1r])

        for j0 in range(LA):
            score(j0)
        for j in range(len(seq)):
            if j + LA < len(seq):
                score(j + LA)
            head, st = seq[j]
            g = head // 4
            ob = 4 + 2 * (head % 2)
            db = ob + 1
            pt, ptr = ptl[j % NP]
            self.mm(bank(ob), self.vatt[:, st, g * 128:(g + 1) * 128], pt, st == 0, st == 15,
                    [self.vattr[st], ptr], [psr[ob]])
            if st % 2 == 1:
                s1, s1r = S1[(j // 2) % 3]
                self.mm(bank(db), self.ones, s1, st == 1, st == 15, [self.onesr, s1r], [psr[db]])
            if st == 15:
                self.act(self.rden, bank(db), AF.Ln, [psr[db]], [self.rdenr])
                self.act(self.rden, self.rden, AF.Exp, [self.rdenr], [self.rdenr], scale=-1.0)
                self.tt("dve", bufC[:, head, :], bank(ob), self.rden, ALU.mult, [psr[ob], self.rdenr], [bufCr[head]])
        k = 0
        for grp in range(2):
            wb, wbr = self.wget("wbb", grp)
            wg, wgr = self.wget("gb", grp)
            wb3 = wb.rearrange("p (c n) -> p c n", n=512)
            wg3 = wg.rearrange("p (c n) -> p c n", n=512)
            for fl in range(4):
                fc = grp * 4 + fl
                b0, b1 = (0, 1) if k % 2 == 0 else (2, 3)
                for c in range(8):
                    self.mm(bank(b0), wb3[:, c, fl * 128:(fl + 1) * 128], bufC[:, c, :], c == 0, c == 7,
                            [wbr, bufCr[c]], [psr[b0]])
                for c in range(8):
                    self.mm(bank(b1), wg3[:, c, fl * 128:(fl + 1) * 128], bufA[:, c, :], c == 0, c == 7,
                            [wgr, bufAr[c]], [psr[b1]])
                self.act(self.sg[k % 2], bank(b1), AF.Sigmoid, [psr[b1]], [self.sgr[k % 2]])
                self.tt("dve", self.sg[k % 2], bank(b0), self.sg[k % 2], ALU.mult, [psr[b0], self.sgr[k % 2]], [self.sgr[k % 2]])
                self.tt("pool", bufB[:, fc, :], self.sg[k % 2], self.maT[:, fc, blk], ALU.add,
                        [self.sgr[k % 2], self.maTr[fc][tb]], [bufBr[fc]])
                k += 1
        srcs = [(lambda c, t: bufB[:, c, t * 128:(t + 1) * 128], lambda c, t: [bufBr[c]])]
        self.proj_res(srcs, "wo", 0)
        if self.upto == "x1":
            return
        self.norm_fin(4, C_XPRE, lambda c: bufA[:, c, :], bufAr, [0, 1])
        for grp in range(2):
            w, wr = self.wget("xwq", grp)
            w3 = w.rearrange("p (c n) -> p c n", n=512)
            for fl in range(4):
                fc = grp * 4 + fl
                b = 4 + (fc % 2)
                for c in range(8):
                    self.mm(bank(b), w3[:, c, fl * 128:(fl + 1) * 128], bufA[:, c, :], c == 0, c == 7,
                            [wr, bufAr[c]], [psr[b]])
                self.cp("act" if fc % 2 == 0 else "dve", bufC[:, fc, :], bank(b), [psr[b]], [bufCr[fc]])
        for head in range(4):
            for mt in range(2):
                sb = 6 + mt
                for dc in range(2):
                    self.mm(bank(sb), self.kmT[:, head * 2 + dc, mt * 128:(mt + 1) * 128], bufC[:, head * 2 + dc, :],
                            dc == 0, dc == 1, [self.kmTr[head * 2 + dc], bufCr[head * 2 + dc]], [psr[sb]])
                pi = (head % 2) * 2 + mt
                self.act(self.PT[pi], bank(sb), AF.Exp, [psr[sb]], [self.PTr[pi]], scale=1.0 / 16.0)
            base_b = 0 if head % 2 == 0 else 3
            for dc in range(2):
                for mt in range(2):
                    pi = (head % 2) * 2 + mt
                    self.mm(bank(base_b + dc), self.vm[:, mt, head * 256 + dc * 128:head * 256 + (dc + 1) * 128],
                            self.PT[pi], mt == 0, mt == 1, [self.vmr[mt], self.PTr[pi]], [psr[base_b + dc]])
            db = base_b + 2
            for mt in range(2):
                pi = (head % 2) * 2 + mt
                self.mm(bank(db), self.ones, self.PT[pi], mt == 0, mt == 1, [self.onesr, self.PTr[pi]], [psr[db]])
            self.act(self.rden, bank(db), AF.Ln, [psr[db]], [self.rdenr])
            self.act(self.rden, self.rden, AF.Exp, [self.rdenr], [self.rdenr], scale=-1.0)
            for dc in range(2):
                self.tt("dve", bufB[:, head * 2 + dc, :], bank(base_b + dc), self.rden, ALU.mult,
                        [psr[base_b + dc], self.rdenr], [bufBr[head * 2 + dc]])
        nxb, nxbr = self.xb2[1 - pb]
        if tb < 3 and self.upto == "all":
            nblk = self.x[s, (tb + 1) * 512:(tb + 2) * 512, :]
            for t in range(4):
                self.dma("sp", nxb[:, t, :], nblk[t * 128:(t + 1) * 128, :], self.xb_slot[t], [], [nxbr[t]])
        srcs = [(lambda c, t: bufB[:, c, t * 128:(t + 1) * 128], lambda c, t: [bufBr[c]])]
        self.proj_res(srcs, "xwo", 1)
        if self.upto == "x2":
            return
        self.norm_fin(4, C_FFNPRE, lambda c: bufA[:, c, :], bufAr, [0, 1])
        k = 0
        prefetch = tb < 3 and self.upto == "all"
        for i in range(11):
            w, wr = self.wget("wi", i)
            w3 = w.rearrange("p (c n) -> p c n", n=512)
            if prefetch and i == 1:
                for t in range(4):
                    self.norm_tile(t, nxb, nxbr)
            if prefetch and i == 5:
                self.norm_fin(4, C_MIXPRE, lambda c: bufB[:, c, :], bufBr, [0, 1])
            for j in range(2):
                bg, bu = (0, 1) if k % 2 == 0 else (2, 3)
                for c in range(8):
                    self.mm(bank(bg), w3[:, c, j * 128:(j + 1) * 128], bufA[:, c, :], c == 0, c == 7, [wr, bufAr[c]], [psr[bg]])
                for c in range(8):
                    self.mm(bank(bu), w3[:, c, 256 + j * 128:256 + (j + 1) * 128], bufA[:, c, :], c == 0, c == 7,
                            [wr, bufAr[c]], [psr[bu]])
                self.act(self.sg[k % 2], bank(bg), AF.Silu, [psr[bg]], [self.sgr[k % 2]])
                self.tt("dve", self.actT[:, 2 * i + j, :], bank(bu), self.sg[k % 2], ALU.mult, [psr[bu], self.sgr[k % 2]],
                        [self.actTr[2 * i + j]])
                k += 1
        self.load_gp(2)
        for hf in range(2):
            for i in range(11):
                w3, wr = self.wo2get(hf * 11 + i)
                for t in range(4):
                    for kk in range(2):
                        self.mm(bank(2 * t + hf), self.actT[:, 2 * i + kk, t * 128:(t + 1) * 128], w3[:, kk, :],
                                i == 0 and kk == 0, i == 10 and kk == 1, [wr, self.actTr[2 * i + kk]], [psr[2 * t + hf]])
        for t in range(4):
            self.postnorm(t, t, 0)
            self.dma("sp", self.y[s, tb * 512 + t * 128:tb * 512 + (t + 1) * 128, :], self.xb[:, t, :], self.y_slot[t],
                     [self.xbr[t]], [])

    def build(self):
        self.init_consts()
        A, P = self.A, self.P
        self.ssP = A.alloc([32], F32)
        self.xb_slot = [self.newslot("xb%d" % i) for i in range(4)]
        self.y_slot = [self.newslot("ystore%d" % i, output=True) for i in range(4)]
        self.wgla_slot = self.newslot("wgla")
        self.woring_s = [self.newslot("wo2r%d" % i) for i in range(4)]
        self.wo2_i = 0
        self.wo2_issued = 0
        self.pn_k = 0
        self.base = A.mark()
        o = A.nbytes - 56 * 1024
        self.qoff = o
        self.maT, o = A.alloc_at(o, [8, T], BF16)
        self.kT, o = A.alloc_at(o, [2, T], BF16)
        self.vatt, o = A.alloc_at(o, [16, 256], BF16)
        self.kmT, o = A.alloc_at(o, [8, 256], BF16)
        self.vm, o = A.alloc_at(o, [2, 1024], BF16)
        assert o <= A.nbytes
        up = self.upto
        for s in range(self.nseq):
            self.cur_seq = s
            self.maTr = [regs(4) for _ in range(8)]
            self.kTr = [regs(4) for _ in range(2)]
            self.vattr = regs(16)
            self.kmTr = regs(8)
            self.vmr = regs(2)
            if s > 0:
                P.fence()
            A.reset(self.base)
            self.hT = A.alloc([8, T], BF16)
            self.hTr = [regs(4) for _ in range(8)]
            m1 = A.mark()
            self.alloc_xb(4)
            self.phaseA(s)
            self.issue_conv(["gla1", "wba", "ga", "akv", "xwkv"])
            if up == "A":
                tmp = A.alloc([8, T], F32)
                self.dbg_store("hT", self.hT, [r for rr in self.hTr for r in rr], tmp)
                break
            P.fence()
            A.reset(m1)
            self.glaT = A.alloc([8, T], BF16)
            self.glaTr = [regs(16) for _ in range(8)]
            m2 = A.mark()
            self.phaseGLA(s)
            P.fence()
            A.reset(m2)
            if up == "GLA":
                tmp = A.alloc([8, T], F32)
                self.dbg_store("glaT", self.glaT, [r for rr in self.glaTr for r in rr], tmp)
                break
            self.phaseS4(s)
            self.phaseS2(s)
            assert A.off <= self.qoff, (A.off, self.qoff)
            if up == "S2":
                tmp = A.alloc_at(self.base, [8, T], F32)[0]
                P.fence()
                self.dbg_store("maT", self.maT, [r for rr in self.maTr for r in rr], tmp)
                tmp2 = A.alloc_at(self.base + 65536, [2, T], F32)[0]
                self.dbg_store("kT", self.kT, [r for rr in self.kTr for r in rr], tmp2)
                tmp3 = A.alloc_at(self.base + 65536 + 16384, [16, 256], F32)[0]
                self.dbg_store("vatt", self.vatt, self.vattr, tmp3)
                tmp4 = A.alloc_at(self.base + 65536 + 32768, [8, 256], F32)[0]
                self.dbg_store("kmT", self.kmT, self.kmTr, tmp4)
                tmp5 = A.alloc_at(self.base + 65536 + 32768 + 8192, [2, 1024], F32)[0]
                self.dbg_store("vm", self.vm, self.vmr, tmp5)
                break
            P.fence()
            A.reset(self.base)
            self.alloc_S5()
            assert A.off <= self.qoff, (A.off, self.qoff)
            for tb in range(4):
                self.phaseS5(s, tb)
                if up in ("x1", "x2"):
                    for t in range(4):
                        self.dma("sp", self.y[s, tb * 512 + t * 128:tb * 512 + (t + 1) * 128, :], self.xb[:, t, :],
                                 self.y_slot[t], [self.xbr[t]], [])
        P.emit()
        return self.nc


def make_core_inputs(inp, xs, ms, shared=None):
    if shared is None:
        shared = make_shared(inp)
    d = dict(shared)
    d["x"] = np.ascontiguousarray(xs, dtype=np.float32)
    d["mem"] = np.ascontiguousarray(ms, dtype=np.float32)
    return d


def make_shared(inp):
    gpost = np.stack([np.broadcast_to(inp[k][0][None, :], (128, D)) for k in ("ln_mix_post", "ln_x_post", "ln_ffn_post")], 0)
    gn = inp["gla_norm"][0]
    gnorm2 = np.broadcast_to(np.concatenate([gn, gn])[None, :], (128, 512))
    waug = np.zeros((17, 1024), np.float32)
    waug[:16, :512] = inp["gla_wa_f"][0]
    waug[:16, 512:] = inp["gla_wa_b"][0]
    waug[16, :512] = inp["gla_ba_f"][0]
    waug[16, 512:] = inp["gla_ba_b"][0]
    return {
        "wall": build_wall(inp),
        "kpack": build_kpack(),
        "cpack": build_cpack(inp),
        "gpost": np.ascontiguousarray(gpost, dtype=np.float32),
        "gnorm2": np.ascontiguousarray(gnorm2, dtype=np.float32),
        "waug": waug,
    }


_CACHE = {}


def kernel(**inputs):
    inp = {k: np.asarray(v) for k, v in inputs.items()}
    xs = np.concatenate([inp["x_prompt"], inp["x_sample"]], 0)
    ms = np.concatenate([inp["mem_prompt"], inp["mem_sample"]], 0)
    nb = xs.shape[0]
    assert nb == NCORES * SEQ_PER_CORE
    shared = make_shared(inp)
    if "nc" not in _CACHE:
        _CACHE["nc"] = Builder(nseq=SEQ_PER_CORE, upto="all").build()
    nc = _CACHE["nc"]
    in_maps = []
    for c in range(NCORES):
        sl = slice(c * SEQ_PER_CORE, (c + 1) * SEQ_PER_CORE)
        in_maps.append(make_core_inputs(inp, xs[sl], ms[sl], shared))
    res = run_bass_kernel_spmd(nc, in_maps, core_ids=list(range(NCORES)))
    y = np.concatenate([np.asarray(r["y"]) for r in res.results], 0).astype(np.float32)
    nprompt = inp["x_prompt"].shape[0]
    return (np.ascontiguousarray(y[:nprompt]), np.ascontiguousarray(y[nprompt:]))
```

```python
import numpy as np
import concourse.bass as bass
import concourse.mybir as mybir
from concourse.bass_utils import run_bass_kernel_spmd

F32 = mybir.dt.float32
BF16 = mybir.dt.bfloat16
AF = mybir.ActivationFunctionType
ALU = mybir.AluOpType

T = 2048
D = 1024
NMEM = 256
EPS = 1e-6
DFF = 2816
NCORES = 8
SEQ_PER_CORE = 3

ENGS = ("pe", "act", "dve", "pool", "sp")
SAME_ENGINE_FULL_SYNC = False


class Reg:
    __slots__ = ("w", "r")

    def __init__(self):
        self.w = None
        self.r = {}


def regs(n):
    return [Reg() for _ in range(n)]


class Slot:
    def __init__(self, nc, name):
        self.sem = nc.alloc_semaphore(name)
        self.n = 0


class Prog:
    def __init__(self, nc):
        self.nc = nc
        self.ops = []
        self.by_eng = {e: [] for e in ENGS}
        self.sems = {e: nc.alloc_semaphore("sem_" + e) for e in ENGS}
        self.out_slots = []
        self.fence_deps = set()
        self.fence_pending = set()
        self.dma_since = []

    def fence(self):
        deps = set(self.dma_since)
        for e in ENGS:
            for oid in reversed(self.by_eng[e]):
                if self.ops[oid][3] is None:
                    deps.add(oid)
                    break
        self.fence_deps = deps
        self.fence_pending = set(ENGS)
        self.dma_since = []

    def op(self, eng, fn, reads=(), writes=(), slot=None):
        oid = len(self.ops)
        deps = set()
        is_dma = slot is not None
        if eng in self.fence_pending:
            self.fence_pending.discard(eng)
            for d in self.fence_deps:
                if self.ops[d][3] is None and self.ops[d][0] == eng and not is_dma:
                    continue
                deps.add(d)
        if is_dma:
            self.dma_since.append(oid)

        def add(pid, kind):
            peng, _, _, pslot, _ = self.ops[pid]
            if pslot is None and not is_dma and peng == eng:
                if eng == "pe" or (kind != "raw" and not SAME_ENGINE_FULL_SYNC):
                    return
            deps.add(pid)

        for r in reads:
            if r.w is not None:
                add(r.w, "raw")
        for w in writes:
            if w.w is not None:
                add(w.w, "waw")
            for pid in w.r.values():
                add(pid, "war")
        key = ("dma", oid) if is_dma else eng
        for r in reads:
            r.r[key] = oid
        for w in writes:
            w.w = oid
            w.r = {}
        val = None
        if is_dma:
            slot.n += 1
            val = 16 * slot.n
        self.ops.append((eng, fn, deps, slot, val))
        self.by_eng[eng].append(oid)
        return oid

    def emit(self):
        nc = self.nc
        ops = self.ops
        marked = set()
        for (_, _, deps, _, _) in ops:
            for d in deps:
                if ops[d][3] is None:
                    marked.add(d)
        tok = {}
        for e in ENGS:
            cnt = 0
            for oid in self.by_eng[e]:
                eng, fn, deps, slot, val = ops[oid]
                if slot is not None:
                    tok[oid] = (slot.sem, val)
                elif oid in marked:
                    cnt += 1
                    tok[oid] = (self.sems[e], cnt)
        self.nmarked = len(marked)
        final_waits = [(s.sem, 16 * s.n) for s in self.out_slots if s.n > 0]
        handles = {"pe": "tensor", "act": "scalar", "dve": "vector", "pool": "gpsimd", "sp": "sync"}

        def run_engine(e, engine):
            waited = {}
            for oid in self.by_eng[e]:
                eng, fn, deps, slot, val = ops[oid]
                needs = {}
                for d in deps:
                    sem, v = tok[d]
                    k = sem.num
                    if waited.get(k, 0) < v and needs.get(k, (None, 0))[1] < v:
                        needs[k] = (sem, v)
                for k, (sem, v) in needs.items():
                    engine.wait_ge(sem, v)
                    waited[k] = v
                inst = fn(engine)
                if slot is not None:
                    inst.then_inc(slot.sem, 16)
                elif oid in marked:
                    inst.then_inc(self.sems[e], 1)
            if e == "sp":
                for sem, v in final_waits:
                    engine.wait_ge(sem, v)

        with nc.Block() as block:
            for e in ENGS:
                def mk(e=e):
                    def body(engine):
                        run_engine(e, engine)
                    return body
                getattr(block, handles[e])(mk())


class Arena:
    def __init__(self, nc, nbytes):
        self.t = nc.alloc_sbuf_tensor("arena", [128, nbytes // 2], BF16)
        self.nbytes = nbytes
        self.off = 0
        self.peak = 0

    def alloc(self, shape, dtype, parts=128):
        esz = 4 if dtype == F32 else 2
        n = int(np.prod(shape))
        nb = (n * esz + 31) // 32 * 32
        o = self.off
        self.off += nb
        self.peak = max(self.peak, self.off)
        assert self.off <= self.nbytes, f"arena overflow {self.off} > {self.nbytes}"
        v = self.t[0:parts, o // 2:(o + n * esz) // 2]
        if dtype == F32:
            v = v.bitcast(F32)
        if len(shape) == 2:
            v = v.rearrange("p (a b) -> p a b", b=shape[1])
        elif len(shape) == 3:
            v = v.rearrange("p (a b c) -> p a b c", b=shape[1], c=shape[2])
        return v

    def alloc_at(self, off, shape, dtype, parts=128):
        save = self.off
        self.off = off
        v = self.alloc(shape, dtype, parts)
        end = self.off
        self.off = save
        return v, end

    def mark(self):
        return self.off

    def reset(self, m):
        self.off = m


def _seg(W, groups):
    K = W.shape[0]
    KC = K // 128
    out = []
    for cols in groups:
        t = W[:, cols].reshape(KC, 128, len(cols)).transpose(1, 0, 2)
        out.append(np.ascontiguousarray(t).reshape(128, KC * len(cols)))
    return np.stack(out, 0)


def _ar(a, b):
    return np.arange(a, b)


SEG_ORDER = ["z", "gla0", "gla1", "wba", "ga", "akv", "xwkv", "aq", "wbb", "gb", "wo", "xwq", "xwo", "wi", "wo2"]
SEG_SHAPE = {
    "z": (1, 8 * 32), "akv": (1, 8 * 512), "gla0": (1, 8 * 1536), "gla1": (1, 8 * 1536),
    "ga": (2, 4096), "wba": (2, 4096), "aq": (2, 4096), "gb": (2, 4096), "wbb": (2, 4096),
    "wo": (2, 4096), "xwkv": (4, 4096), "xwq": (2, 4096), "xwo": (2, 4096),
    "wi": (11, 4096), "wo2": (22, 1024),
}


def seg_offsets():
    off = {}
    o = 0
    for s in SEG_ORDER:
        ng, L = SEG_SHAPE[s]
        off[s] = o
        o += ng * 128 * L
    return off, o


def build_wall(inp):
    w_in = inp["w_in"][0]
    segs = {}
    for p in range(2):
        h0, h1 = 2 * p, 2 * p + 1
        cols = np.concatenate([
            _ar(h0 * 128, h0 * 128 + 128), _ar(h1 * 128, h1 * 128 + 128),
            _ar(512 + h0 * 128, 512 + h0 * 128 + 128), _ar(512 + h1 * 128, 512 + h1 * 128 + 128),
            _ar(1024 + h0 * 256, 1024 + h0 * 256 + 256), _ar(1024 + h1 * 256, 1024 + h1 * 256 + 256),
            _ar(2048 + h0 * 256, 2048 + h0 * 256 + 256), _ar(2048 + h1 * 256, 2048 + h1 * 256 + 256)])
        segs["gla%d" % p] = _seg(w_in, [cols])
    segs["z"] = _seg(w_in, [_ar(3072, 3104)])
    segs["akv"] = _seg(w_in, [_ar(4128, 4640)])
    segs["aq"] = _seg(w_in, [_ar(3104, 3616), _ar(3616, 4128)])
    segs["ga"] = _seg(w_in, [_ar(4640, 5152), _ar(5152, 5664)])
    segs["gb"] = _seg(w_in, [_ar(5664, 6176), _ar(6176, 6688)])
    two = [_ar(0, 512), _ar(512, 1024)]
    segs["wba"] = _seg(inp["w_branch_gla"][0], two)
    segs["wbb"] = _seg(inp["w_branch_att"][0], two)
    segs["wo"] = _seg(inp["w_out"][0], two)
    segs["xwq"] = _seg(inp["x_wq"][0], two)
    segs["xwo"] = _seg(inp["x_wo"][0], two)
    segs["xwkv"] = _seg(inp["x_wkv"][0], [_ar(i * 512, (i + 1) * 512) for i in range(4)])
    wi = inp["ffn_wi"][0]
    segs["wi"] = _seg(wi, [np.concatenate([_ar(256 * i, 256 * i + 256), _ar(DFF + 256 * i, DFF + 256 * i + 256)])
                           for i in range(11)])
    w2 = inp["ffn_wo"][0]
    pcs = []
    for half in range(2):
        for i in range(11):
            blk = w2[256 * i:256 * i + 256, half * 512:(half + 1) * 512]
            pcs.append(np.ascontiguousarray(blk.reshape(2, 128, 512).transpose(1, 0, 2)).reshape(128, 1024))
    segs["wo2"] = np.stack(pcs, 0)
    off, tot = seg_offsets()
    wall = np.empty(tot, np.float32)
    for s in SEG_ORDER:
        ng, L = SEG_SHAPE[s]
        a = segs[s]
        assert a.shape == (ng, 128, L), (s, a.shape)
        wall[off[s]:off[s] + a.size] = a.reshape(-1)
    return wall.reshape(-1, 2048)


K_ID = 0
K_UF = 128
K_WF = 256
K_UB = 384
K_WB = 512
K_MASK = 640
K_ROT = 1152
K_COS = 1280
K_SIN = 1376
NK = 1472


def build_kpack():
    k = np.zeros((128, NK), np.float32)
    j = np.arange(128)[:, None]
    i = np.arange(128)[None, :]
    k[:, K_ID:K_ID + 128] = (j == i)
    c = -1.0 / 16.0
    k[:, K_UF:K_UF + 128] = c * (j <= i)
    k[:, K_WF:K_WF + 128] = c * (j > i)
    k[:, K_UB:K_UB + 128] = c * (j >= i)
    k[:, K_WB:K_WB + 128] = c * (j < i)
    mf = (j <= i).astype(np.float32)
    mb = (j > i).astype(np.float32)
    k[:, K_MASK:K_MASK + 512] = np.concatenate([mf, mf, mb, mb], 1)
    R = np.zeros((128, 128), np.float32)
    for m in range(128):
        if (m % 64) < 32:
            R[m + 32, m] = -1.0
        else:
            R[m - 32, m] = 1.0
    k[:, K_ROT:K_ROT + 128] = R
    inv = (10000.0 ** (-np.arange(0, 64, 2, dtype=np.float32) / np.float32(64))).astype(np.float32)
    for d in range(128):
        f = inv[d % 32]
        if d < 64:
            ang = (np.arange(32, dtype=np.float32) * f).astype(np.float32)
            k[d, K_COS:K_COS + 32] = np.cos(ang)
            k[d, K_SIN:K_SIN + 32] = np.sin(ang)
        else:
            ang = (np.arange(64, dtype=np.float32) * f).astype(np.float32)
            k[d, K_COS + 32:K_COS + 96] = np.cos(ang)
            k[d, K_SIN + 32:K_SIN + 96] = np.sin(ang)
    return k


C_MIXPRE, C_XPRE, C_FFNPRE, C_MEM, C_QN, C_KN = 0, 8, 16, 24, 32, 33
NCP = 34


def build_cpack(inp):
    c = np.zeros((128, NCP), np.float32)
    c[:, C_MIXPRE:C_MIXPRE + 8] = inp["ln_mix_pre"][0].reshape(8, 128).T
    c[:, C_XPRE:C_XPRE + 8] = inp["ln_x_pre"][0].reshape(8, 128).T
    c[:, C_FFNPRE:C_FFNPRE + 8] = inp["ln_ffn_pre"][0].reshape(8, 128).T
    c[:, C_MEM:C_MEM + 8] = inp["ln_mem"][0].reshape(8, 128).T
    c[:, C_QN] = inp["att_q_norm"][0]
    c[:, C_KN] = inp["att_k_norm"][0]
    return c


class Builder:
    def __init__(self, nseq=SEQ_PER_CORE, upto="all", dbg=None):
        self.nseq = nseq
        self.upto = upto
        self.dbg = dbg or {}
        nc = bass.Bass("TRN2", target_bir_lowering=False)
        self.nc = nc
        self.P = Prog(nc)
        self.soff, self.wtot = seg_offsets()
        self.x = nc.dram_tensor("x", [nseq, T, D], F32, kind="ExternalInput").ap()
        self.mem = nc.dram_tensor("mem", [nseq, NMEM, D], F32, kind="ExternalInput").ap()
        self.wall = nc.dram_tensor("wall", [self.wtot // 2048, 2048], F32, kind="ExternalInput").ap()
        self.kpack = nc.dram_tensor("kpack", [128, NK], F32, kind="ExternalInput").ap()
        self.cpack = nc.dram_tensor("cpack", [128, NCP], F32, kind="ExternalInput").ap()
        self.gpost = nc.dram_tensor("gpost", [3, 128, D], F32, kind="ExternalInput").ap()
        self.gnorm2 = nc.dram_tensor("gnorm2", [128, 512], F32, kind="ExternalInput").ap()
        self.waug = nc.dram_tensor("waug", [17, 1024], F32, kind="ExternalInput").ap()
        self.y = nc.dram_tensor("y", [nseq, T, D], F32, kind="ExternalOutput").ap()
        self.wbf = nc.dram_tensor("wbf", [self.wtot // 2048, 2048], BF16).ap()
        self.wbf_flat = self.wbf.rearrange("r c -> (r c)")
        self.wbf_reg = {s: Reg() for s in SEG_ORDER}
        self.dbg_out = {}
        for name, shape in self.dbg.items():
            self.dbg_out[name] = nc.dram_tensor("dbg_" + name, list(shape), F32, kind="ExternalOutput").ap()
        self.A = Arena(nc, 207 * 1024)
        self.PS = [nc.alloc_psum_tensor("ps%d" % i, [128, 1024], F32).ap() if False else
                   nc.alloc_psum_tensor("ps%d" % i, [128, 1024], F32) for i in range(4)]
        self.psr = regs(8)

    def bank(self, b, lo=0, hi=512):
        return self.PS[b // 2][:, (b % 2) * 512 + lo:(b % 2) * 512 + hi]

    def mm(self, out, lhsT, rhs, start, stop, reads, writes):
        self.P.op("pe", lambda e: e.matmul(out, lhsT, rhs, start=start, stop=stop), reads, writes)

    def tr(self, out, in_, reads, writes):
        ident = self.K[:, K_ID:K_ID + 128]
        self.P.op("pe", lambda e: e.transpose(out, in_, ident), list(reads) + [self.Kr], writes)

    def act(self, out, in_, func, reads, writes, scale=1.0, bias=0.0, accum=None):
        def fn(e):
            kw = {}
            if accum is not None:
                kw["accum_out"] = accum
            return e.activation(out, in_, func, bias=bias, scale=scale, **kw)
        self.P.op("act", fn, reads, writes)

    def tt(self, eng, out, in0, in1, op, reads, writes):
        self.P.op(eng, lambda e: e.tensor_tensor(out, in0, in1, op), reads, writes)

    def ts(self, eng, out, in0, s1, op0, reads, writes, s2=None, op1=None):
        if op1 is None:
            self.P.op(eng, lambda e: e.tensor_scalar(out, in0, s1, None, op0), reads, writes)
        else:
            self.P.op(eng, lambda e: e.tensor_scalar(out, in0, s1, s2, op0, op1), reads, writes)

    def stt(self, out, in0, scalar, in1, op0, op1, reads, writes):
        self.P.op("dve", lambda e: e.scalar_tensor_tensor(out, in0, scalar, in1, op0, op1), reads, writes)

    def cp(self, eng, out, in_, reads, writes):
        if eng == "act":
            self.P.op("act", lambda e: e.activation(out, in_, AF.Copy), reads, writes)
        else:
            self.P.op(eng, lambda e: e.tensor_copy(out, in_), reads, writes)

    def dma(self, q, out, in_, slot, reads, writes):
        self.P.op(q, lambda e: e.dma_start(out, in_), reads, writes, slot=slot)

    def newslot(self, name, output=False):
        s = Slot(self.nc, name)
        if output:
            self.P.out_slots.append(s)
        return s

    def wsrc(self, seg, g):
        ng, L = SEG_SHAPE[seg]
        o = self.soff[seg] + g * 128 * L
        return self.wbf_flat[o:o + 128 * L].rearrange("(p l) -> p l", p=128), L

    def init_consts(self):
        A, nc = self.A, self.nc
        self.K = A.alloc([NK], F32)
        self.Kr = Reg()
        self.C = A.alloc([NCP], F32)
        self.Cr = Reg()
        self.G2 = A.alloc([512], F32)
        self.G2r = Reg()
        self.gp = A.alloc([D], F32)
        self.gpr = Reg()
        self.gp_slot = self.newslot("gp")
        self.waug_f = A.alloc_at(A.nbytes - 56 * 1024, [1024], F32, parts=17)[0]
        self.waug_b = A.alloc([1024], BF16, parts=17)
        self.waugr = Reg()
        self.ones = A.alloc([128], BF16)
        self.onesr = Reg()
        self.cosB = A.alloc([512], F32)
        self.sinB = A.alloc([512], F32)
        self.csr = Reg()
        s = self.newslot("c0")
        self.dma("sp", self.K, self.kpack, s, [], [self.Kr])
        s = self.newslot("c1")
        self.dma("sp", self.C, self.cpack, s, [], [self.Cr])
        s = self.newslot("c2")
        self.dma("sp", self.G2, self.gnorm2, s, [], [self.G2r])
        s = self.newslot("c3")
        r0 = Reg()
        self.dma("sp", self.waug_f, self.waug, s, [], [r0])
        self.cp("dve", self.waug_b, self.waug_f, [r0], [self.waugr])
        self.P.op("pool", lambda e: e.memset(self.ones, 1.0), [], [self.onesr])
        self.epsv = A.alloc([8], F32)
        self.epsr = Reg()
        self.P.op("pool", lambda e: e.memset(self.epsv[:, 0:1], EPS), [], [self.epsr])
        self.P.op("pool", lambda e: e.memset(self.epsv[:, 1:2], 1.0), [], [self.epsr])
        self.conv_done = set()
        self.issue_conv(["z", "gla0"])
        for tab, kc in ((self.cosB, K_COS), (self.sinB, K_SIN)):
            src = self.K[64:128, kc + 32:kc + 96].unsqueeze(1).broadcast_to([64, 8, 64])
            dst = tab[64:128, :].rearrange("p (r c) -> p r c", c=64)
            self.P.op("pool", (lambda e, dst=dst, src=src: e.tensor_copy(dst, src)), [self.Kr], [self.csr])
        self.nring = 4
        self.ring = [A.alloc([4096], BF16) for _ in range(self.nring)]
        self.ring_r = regs(self.nring)
        self.ring_s = [self.newslot("ring%d" % i) for i in range(self.nring)]
        self.make_sched()

    def issue_conv(self, names):
        for sname in names:
            if sname in self.conv_done:
                continue
            self.conv_done.add(sname)
            ng, L = SEG_SHAPE[sname]
            r_lo = self.soff[sname] // 2048
            r_hi = (self.soff[sname] + ng * 128 * L) // 2048
            s = self.newslot("cv_" + sname)
            src = self.wall[r_lo:r_hi, :]
            dst = self.wbf[r_lo:r_hi, :]
            self.dma("pool", dst, src, s, [], [self.wbf_reg[sname]])

    def make_sched(self):
        L = []
        for s in range(self.nseq):
            L.append(("z", 0))
            L += [("wba", 0), ("ga", 0), ("wba", 1), ("ga", 1)]
            L += [("akv", 0)] + [("xwkv", i) for i in range(4)]
            for tb in range(4):
                L += [("aq", 0), ("aq", 1), ("wbb", 0), ("gb", 0), ("wbb", 1), ("gb", 1), ("wo", 0), ("wo", 1),
                      ("xwq", 0), ("xwq", 1), ("xwo", 0), ("xwo", 1)] + [("wi", i) for i in range(11)]
        self.wsched = L
        self.wptr = 0
        self.wissued = 0

    def wissue_upto(self, n):
        n = min(n, len(self.wsched))
        while self.wissued < n:
            k = self.wissued
            seg, g = self.wsched[k]
            i = k % self.nring
            src, L = self.wsrc(seg, g)
            assert seg in self.conv_done, seg
            self.dma("sp", self.ring[i][:, 0:L], src, self.ring_s[i], [self.wbf_reg[seg]], [self.ring_r[i]])
            self.wissued += 1

    def wget(self, seg, g, prefetch_only=False):
        if prefetch_only:
            return None
        if self.upto != "all":
            while self.wsched[self.wptr] != (seg, g):
                self.wptr += 1
            self.wissued = max(self.wissued, self.wptr)
        k = self.wptr
        assert self.wsched[k] == (seg, g), (k, self.wsched[k], seg, g)
        self.wissue_upto(k + self.nring - 1)
        self.wptr += 1
        i = k % self.nring
        return self.ring[i], self.ring_r[i]

    def rope_top(self, tb):
        for tab, kc in ((self.cosB, K_COS), (self.sinB, K_SIN)):
            src = self.K[0:64, kc + tb * 8:kc + tb * 8 + 8].unsqueeze(2).broadcast_to([64, 8, 64])
            dst = tab[0:64, :].rearrange("p (r c) -> p r c", c=64)
            self.P.op("pool", (lambda e, dst=dst, src=src: e.tensor_copy(dst, src)), [self.Kr], [self.csr])

    def norm_tile(self, t, xb=None, xbr=None):
        sc = self.sc
        if xb is None:
            xb, xbr = self.xb, self.xbr
        ss, nsr, xn, xnr = sc["ss"], sc["nsr"], sc["xn"], sc["xnr"]
        self.act(sc["junk"], xb[:, t, :], AF.Square, [xbr[t]], [sc["junkr"], nsr[t][0]], accum=ss[:, t:t + 1])
        self.act(ss[:, 4 + t:5 + t], ss[:, t:t + 1], AF.Ln, [nsr[t][0], self.epsr], [nsr[t][1]], scale=1.0 / D,
                 bias=self.epsv[:, 0:1])
        self.act(ss[:, 8 + t:9 + t], ss[:, 4 + t:5 + t], AF.Exp, [nsr[t][1]], [nsr[t][2]], scale=-0.5)
        self.ts("dve", xn[:, t, :], xb[:, t, :], ss[:, 8 + t:9 + t], ALU.mult, [xbr[t], nsr[t][2]], [xnr[t]])

    def norm_fin(self, nt, gcol, dst, dst_regs, banks):
        xn, xnr = self.sc["xn"], self.sc["xnr"]
        for c in range(8):
            b = banks[c % len(banks)]
            for t in range(nt):
                self.tr(self.bank(b, t * 128, (t + 1) * 128), xn[:, t, c * 128:(c + 1) * 128], [xnr[t]], [self.psr[b]])
            g = self.C[:, gcol + c:gcol + c + 1]
            if c % 2 == 0:
                self.P.op("act", (lambda e, o=dst(c), i=self.bank(b, 0, nt * 128), g=g:
                                  e.activation(o, i, AF.Copy, scale=g)), [self.psr[b], self.Cr], [dst_regs[c]])
            else:
                self.ts("dve", dst(c), self.bank(b, 0, nt * 128), g, ALU.mult, [self.psr[b], self.Cr], [dst_regs[c]])

    def norm_T(self, xb, xbr, nt, gcol, dst, dst_regs, scratch, banks):
        for t in range(nt):
            self.norm_tile(t)
        self.norm_fin(nt, gcol, dst, dst_regs, banks)

    def eps_ap(self):
        return self.epsv[:, 0:1]

    def pbank(self, b, lo=0, hi=512, p0=0, p1=128):
        return self.PS[b // 2][p0:p1, (b % 2) * 512 + lo:(b % 2) * 512 + hi]

    def alloc_xb(self, nt=4, junk=None):
        A = self.A
        self.xb = A.alloc([nt, D], F32)
        self.xbr = regs(nt)
        if junk is None:
            junk = (A.alloc([D], BF16), Reg())
        self.sc = {"junk": junk[0], "junkr": junk[1], "ss": self.ssP, "nsr": [regs(3) for _ in range(4)],
                   "pnr": [regs(3) for _ in range(4)],
                   "xn": A.alloc([nt, D], F32), "xnr": regs(nt)}

    def load_xb(self, src_rows, nt=4):
        for t in range(nt):
            self.dma("sp", self.xb[:, t, :], src_rows[t * 128:(t + 1) * 128, :], self.xb_slot[t], [], [self.xbr[t]])

    def dbg_store(self, name, src_ap, src_regs, f32tmp=None):
        if name not in self.dbg_out:
            return
        dst = self.dbg_out[name]
        s = self.newslot("dbg_" + name, output=True)
        if f32tmp is not None:
            r = Reg()
            self.cp("dve", f32tmp, src_ap, src_regs, [r])
            self.dma("sp", dst, f32tmp, s, [r], [])
        else:
            self.dma("sp", dst, src_ap, s, src_regs, [])

    def alloc_nr(self):
        A = self.A
        self.nr = {"sqb": A.alloc([512], BF16), "tmpA": A.alloc([512], F32), "knf": A.alloc([512], F32),
                   "t2": A.alloc([512], F32), "r": regs(4), "knf2": A.alloc([512], F32), "r_kn2": Reg()}

    def normrope_pipe(self, items, pbanks, bs, br):
        nr = self.nr
        sqb, tmpA, t2 = nr["sqb"], nr["tmpA"], nr["t2"]
        knfs = [nr["knf"], nr["knf2"]]
        r_sq, r_tmp, r_kn0, r_t2 = nr["r"]
        r_kns = [r_kn0, nr["r_kn2"]]
        psr = self.psr
        n = len(items)
        nb = len(pbanks)
        assert nb >= 3

        def stA(i):
            items[i][0](pbanks[i % nb])

        def stB(i):
            b = pbanks[i % nb]
            src, src_reg = self.bank(b), psr[b]
            gcol = items[i][1]
            knf, r_kn = knfs[i % 2], r_kns[i % 2]
            self.act(sqb, src, AF.Square, [src_reg], [r_sq])
            self.mm(self.bank(bs), self.ones, sqb, True, True, [self.onesr, r_sq], [psr[bs]])
            self.act(tmpA, self.bank(bs), AF.Ln, [psr[bs], self.epsr], [r_tmp], scale=1.0 / 128, bias=self.epsv[:, 0:1])
            self.act(tmpA, tmpA, AF.Exp, [r_tmp], [r_tmp], scale=-0.5)
            self.stt(knf, src, self.C[:, gcol:gcol + 1], tmpA, ALU.mult, ALU.mult, [src_reg, self.Cr, r_tmp], [r_kn])

        def stC(i):
            _, gcol, out, out_regs, pre = items[i]
            knf, r_kn = knfs[i % 2], r_kns[i % 2]
            if pre is not None:
                pre()
            self.mm(self.bank(br), self.K[:, K_ROT:K_ROT + 128], knf, True, True, [self.Kr, r_kn], [psr[br]])
            self.tt("dve", t2, self.bank(br), self.sinB, ALU.mult, [psr[br], self.csr], [r_t2])
            self.tt("pool", knf, knf, self.cosB, ALU.mult, [r_kn, self.csr], [r_kn])
            self.tt("pool", out, knf, t2, ALU.add, [r_kn, r_t2], out_regs)

        for step in range(n + 2):
            if step < n:
                stA(step)
            if 0 <= step - 1 < n:
                stB(step - 1)
            if 0 <= step - 2 < n:
                stC(step - 2)

    def postnorm(self, k, t, ssr_i):
        sc = self.sc
        ss = sc["ss"]
        r0, r1, r2 = sc["pnr"][t]
        u = self.PS[k][:, :]
        pr = [self.psr[2 * k], self.psr[2 * k + 1]]
        self.act(sc["junk"], u, AF.Square, pr, [sc["junkr"], r0], accum=ss[:, 16 + t:17 + t])
        self.act(ss[:, 20 + t:21 + t], ss[:, 16 + t:17 + t], AF.Ln, [r0, self.epsr], [r1], scale=1.0 / D,
                 bias=self.epsv[:, 0:1])
        self.act(ss[:, 24 + t:25 + t], ss[:, 20 + t:21 + t], AF.Exp, [r1], [r2], scale=-0.5)
        xn, xnr = sc["xn"], sc["xnr"]
        self.stt(xn[:, t, :], u, ss[:, 24 + t:25 + t], self.gp, ALU.mult, ALU.mult, pr + [r2, self.gpr], [xnr[t]])
        self.tt("dve", self.xb[:, t, :], self.xb[:, t, :], xn[:, t, :], ALU.add, [self.xbr[t], xnr[t]], [self.xbr[t]])

    def load_gp(self, idx):
        self.dma("sp", self.gp, self.gpost[idx], self.gp_slot, [], [self.gpr])

    def phaseA(self, s):
        for tb in range(4):
            self.load_xb(self.x[s, tb * 512:(tb + 1) * 512, :])
            self.norm_T(self.xb, self.xbr, 4, C_MIXPRE,
                        lambda c, tb=tb: self.hT[:, c, tb * 512:(tb + 1) * 512],
                        [self.hTr[c][tb] for c in range(8)], self.sc, [0, 1])

    def phaseGLA(self, s):
        A, P, psr, bank, pbank = self.A, self.P, self.psr, self.bank, self.pbank
        hT, hTr = self.hT, self.hTr
        Kc = self.K
        zT = [A.alloc([T], BF16, parts=17) for _ in range(2)]
        zTr = [regs(4), regs(4)]
        for d in range(2):
            P.op("pool", (lambda e, z=zT[d]: e.memset(z, 1.0)), [], zTr[d])
        wz, wzr = self.wget("z", 0)
        wz3 = wz[:, 0:256].rearrange("p (c n) -> p c n", n=32)
        for tb in range(4):
            blk = slice(tb * 512, (tb + 1) * 512)
            for d in range(2):
                b = d
                for c in range(8):
                    self.mm(pbank(b, 0, 512, 0, 16), wz3[:, c, d * 16:(d + 1) * 16], hT[:, c, blk], c == 0, c == 7,
                            [wzr, hTr[c][tb]], [psr[b]])
                self.cp("dve", zT[d][0:16, blk], pbank(b, 0, 512, 0, 16), [psr[b]], [zTr[d][tb]])
        wgla = A.alloc([8 * 1536], BF16)
        wglar = Reg()
        wg3 = wgla.rearrange("p (c n) -> p c n", n=1536)
        Sb = A.alloc([16, 512], BF16)
        Sbr = regs(16)
        S32 = A.alloc([512], F32)
        S32r = regs(2)
        Sfbf = A.alloc([512], BF16)
        Sfbfr = Reg()

        def dbl(shape, dt):
            return [A.alloc(shape, dt) for _ in range(2)], regs(2)
        def sgl(shape, dt):
            a, r = A.alloc(shape, dt), Reg()
            return [a, a], [r, r]
        sp, spr = sgl([512], F32)
        E1, E1r = dbl([512], F32)
        E2, E2r = sgl([512], F32)
        E3, E3r = sgl([256], F32)
        vst = A.alloc([16, 512], BF16)
        vstr = regs(16)
        qgf, qgfr = dbl([256], BF16)
        qgb, qgbr = dbl([256], BF16)
        kgf, kgfr = dbl([256], BF16)
        kgb, kgbr = dbl([256], BF16)
        kend, kendr = dbl([256], BF16)
        gr, grr = dbl([512], F32)
        og, ogr = dbl([512], F32)
        decb, decbr = dbl([8], F32)
        AT = A.alloc([512], BF16)
        ATr = Reg()
        ss2 = A.alloc([8], F32)
        ss2r = regs(3)
        junk = A.alloc([256], BF16)
        junkr = Reg()
        one_ap = self.epsv[:, 1:2]
        SC = float(128.0 ** -0.5)
        for p in range(2):
            src, L = self.wsrc("gla%d" % p, 0)
            self.dma("sp", wgla, src, self.wgla_slot, [self.wbf_reg["gla%d" % p]], [wglar])
            self.issue_conv(["aq", "wbb", "gb", "wo", "xwq", "xwo"] if p == 0 else ["wi", "wo2"])
            fo = p * 256
            bo = 512 + p * 256

            def b_s1(tt):
                q = tt % 2
                tk = slice(tt * 128, (tt + 1) * 128)
                tb = tt // 4
                hr = lambda c: hTr[c][tb]
                self.mm(bank(0, 0, 256), zT[1][0:17, tk], self.waug_b[0:17, bo:bo + 256], True, True,
                        [zTr[1][tb], self.waugr], [psr[0]])
                self.act(sp[q][:, 0:256], bank(0, 0, 256), AF.Exp, [psr[0]], [spr[q]], scale=-1.0)
                self.act(sp[q][:, 0:256], sp[q][:, 0:256], AF.Ln, [spr[q], self.epsr], [spr[q]], bias=one_ap)
                for c in range(8):
                    self.mm(bank(3), hT[:, c, tk], wg3[:, c, 512:1024], c == 0, c == 7, [hr(c), wglar], [psr[3]])
                for c in range(8):
                    self.mm(bank(2, 0, 256), hT[:, c, tk], wg3[:, c, 256:512], c == 0, c == 7, [hr(c), wglar], [psr[2]])
                for h in range(2):
                    self.mm(bank(1, h * 128, (h + 1) * 128), sp[q][:, h * 128:(h + 1) * 128], Kc[:, K_UB:K_UB + 128],
                            True, True, [spr[q], self.Kr], [psr[1]])
                self.mm(bank(1, 256, 512), Kc[:, K_WB:K_WB + 128], sp[q][:, 0:256], True, True, [spr[q], self.Kr], [psr[1]])
                self.cp("act", vst[:, tt, :], bank(3), [psr[3]], [vstr[tt]])
                self.act(decb[q][:, 0:2], bank(1, 0, 256).rearrange("p (h i) -> p h i", i=128)[:, :, 0], AF.Exp,
                         [psr[1]], [decbr[q]])
                self.act(E3[q], bank(1, 256, 512), AF.Exp, [psr[1]], [E3r[q]])
                self.tt("dve", kend[q], bank(2, 0, 256), E3[q], ALU.mult, [psr[2], E3r[q]], [kendr[q]])

            def b_s2(tt):
                q = tt % 2
                for h in range(2):
                    self.mm(bank(7, h * 256, (h + 1) * 256), kend[q][:, h * 128:(h + 1) * 128],
                            vst[:, tt, h * 256:(h + 1) * 256], True, True, [kendr[q], vstr[tt]], [psr[7]])
                for h in range(2):
                    hs = slice(h * 256, (h + 1) * 256)
                    self.stt(S32[:, hs], S32[:, hs], decb[q][:, h:h + 1], bank(7, h * 256, (h + 1) * 256), ALU.mult, ALU.add,
                             [S32r[h], decbr[q], psr[7]], [S32r[h]])
                if tt > 0:
                    self.cp("pool", Sb[:, tt - 1, :], S32, S32r, [Sbr[tt - 1]])

            P.op("pool", (lambda e: e.memset(S32, 0.0)), [], S32r)
            b_s1(15)
            for tt in range(15, -1, -1):
                if tt > 0:
                    b_s1(tt - 1)
                b_s2(tt)

            def f_s1(tt, mid_hook=None, pending=None):
                q = tt % 2
                tk = slice(tt * 128, (tt + 1) * 128)
                tb = tt // 4
                hr = lambda c: hTr[c][tb]
                self.mm(bank(0, 0, 256), zT[0][0:17, tk], self.waug_b[0:17, fo:fo + 256], True, True,
                        [zTr[0][tb], self.waugr], [psr[0]])
                self.mm(bank(0, 256, 512), zT[1][0:17, tk], self.waug_b[0:17, bo:bo + 256], True, True,
                        [zTr[1][tb], self.waugr], [psr[0]])
                self.act(sp[q], bank(0), AF.Exp, [psr[0]], [spr[q]], scale=-1.0)
                self.act(sp[q], sp[q], AF.Ln, [spr[q], self.epsr], [spr[q]], bias=one_ap)
                if pending is not None:
                    pending()
                for c in range(8):
                    self.mm(bank(2, 256, 512), hT[:, c, tk], wg3[:, c, 256:512], c == 0, c == 7, [hr(c), wglar], [psr[2]])
                if mid_hook is not None:
                    mid_hook()
                for d in range(2):
                    ku = K_UF if d == 0 else K_UB
                    for h in range(2):
                        o = d * 256 + h * 128
                        self.mm(bank(1, o, o + 128), sp[q][:, o:o + 128], Kc[:, ku:ku + 128], True, True,
                                [spr[q], self.Kr], [psr[1]])
                self.mm(bank(2, 0, 256), Kc[:, K_WF:K_WF + 128], sp[q][:, 0:256], True, True, [spr[q], self.Kr], [psr[2]])
                self.act(E1[q], bank(1), AF.Exp, [psr[1]], [E1r[q]])
                self.act(E2[q], bank(1), AF.Exp, [psr[1]], [E2r[q]], scale=-1.0)
                self.act(E3[q], bank(2, 0, 256), AF.Exp, [psr[2]], [E3r[q]])
                for j in range(4):
                    for c in range(8):
                        self.mm(bank(0, j * 128, (j + 1) * 128), wg3[:, c, j * 128:(j + 1) * 128], hT[:, c, tk],
                                c == 0, c == 7, [hr(c), wglar], [psr[0]])
                self.tt("dve", kend[q], bank(2, 256, 512), E3[q], ALU.mult, [psr[2], E3r[q]], [kendr[q]])
                self.stt(qgf[q], bank(0, 0, 256), SC, E1[q][:, 0:256], ALU.mult, ALU.mult, [psr[0], E1r[q]], [qgfr[q]])
                self.stt(qgb[q], bank(0, 0, 256), SC, E1[q][:, 256:512], ALU.mult, ALU.mult, [psr[0], E1r[q]], [qgbr[q]])
                self.tt("dve", kgf[q], bank(0, 256, 512), E2[q][:, 0:256], ALU.mult, [psr[0], E2r[q]], [kgfr[q]])
                self.tt("dve", kgb[q], bank(0, 256, 512), E2[q][:, 256:512], ALU.mult, [psr[0], E2r[q]], [kgbr[q]])

            def f_R(tt):
                tk = slice(tt * 128, (tt + 1) * 128)
                tb = tt // 4
                for c in range(8):
                    self.mm(bank(4), hT[:, c, tk], wg3[:, c, 1024:1536], c == 0, c == 7, [hTr[c][tb], wglar], [psr[4]])

            def f_gr(tt):
                q = tt % 2
                self.act(gr[q], bank(4), AF.Exp, [psr[4]], [grr[q]], scale=-1.0)
                self.act(gr[q], gr[q], AF.Ln, [grr[q], self.epsr], [grr[q]], bias=one_ap)
                self.act(gr[q], gr[q], AF.Exp, [grr[q]], [grr[q]], scale=-1.0)
                self.tt("dve", gr[q], bank(4), gr[q], ALU.mult, [psr[4], grr[q]], [grr[q]])
                self.tt("pool", gr[q], gr[q], self.G2, ALU.mult, [grr[q], self.G2r], [grr[q]])

            def f_s2a(tt):
                q = tt % 2
                kg = (kgf[q], kgb[q])
                qg = (qgf[q], qgb[q])
                kgr = (kgfr[q], kgbr[q])
                qgr = (qgfr[q], qgbr[q])
                for d in range(2):
                    for h in range(2):
                        o = (d * 2 + h) * 128
                        self.mm(bank(5, o, o + 128), kg[d][:, h * 128:(h + 1) * 128], qg[d][:, h * 128:(h + 1) * 128],
                                True, True, [kgr[d], qgr[d]], [psr[5]])
                self.tt("dve", AT, bank(5), Kc[:, K_MASK:K_MASK + 512], ALU.mult, [psr[5], self.Kr], [ATr])

            def f_s2(tt):
                q = tt % 2
                for h in range(2):
                    self.mm(bank(7, h * 256, (h + 1) * 256), kend[q][:, h * 128:(h + 1) * 128],
                            vst[:, tt, h * 256:(h + 1) * 256], True, True, [kendr[q], vstr[tt]], [psr[7]])
                for h in range(2):
                    vh = vst[:, tt, h * 256:(h + 1) * 256]
                    seq = []
                    if tt > 0:
                        seq.append((qgf[q][:, h * 128:(h + 1) * 128], Sfbf[:, h * 256:(h + 1) * 256], [qgfr[q], Sfbfr]))
                    if tt < 15:
                        seq.append((qgb[q][:, h * 128:(h + 1) * 128], Sb[:, tt, h * 256:(h + 1) * 256], [qgbr[q], Sbr[tt]]))
                    seq += [(AT[:, h * 128:(h + 1) * 128], vh, [ATr, vstr[tt]]),
                            (AT[:, (2 + h) * 128:(3 + h) * 128], vh, [ATr, vstr[tt]])]
                    for i, (l, r, rd) in enumerate(seq):
                        self.mm(bank(6, h * 256, (h + 1) * 256), l, r, i == 0, i == len(seq) - 1, rd, [psr[6]])
                for h in range(2):
                    hs = slice(h * 256, (h + 1) * 256)
                    self.stt(S32[:, hs], S32[:, hs], E1[q][:, h * 128 + 127:h * 128 + 128], bank(7, h * 256, (h + 1) * 256),
                             ALU.mult, ALU.add, [S32r[h], E1r[q], psr[7]], [S32r[h]])
                self.cp("pool", Sfbf, S32, S32r, [Sfbfr])
                for h in range(2):
                    self.act(junk[:, 0:256], bank(6, h * 256, (h + 1) * 256), AF.Square, [psr[6]], [junkr, ss2r[0]],
                             accum=ss2[:, h:h + 1])
                self.act(ss2[:, 2:4], ss2[:, 0:2], AF.Ln, [ss2r[0], self.epsr], [ss2r[1]], scale=1.0 / 256,
                         bias=self.epsv[:, 0:1])
                self.act(ss2[:, 4:6], ss2[:, 2:4], AF.Exp, [ss2r[1]], [ss2r[2]], scale=-0.5)
                for h in range(2):
                    hs = slice(h * 256, (h + 1) * 256)
                    self.stt(og[q][:, hs], bank(6, h * 256, (h + 1) * 256), ss2[:, 4 + h:5 + h], gr[q][:, hs],
                             ALU.mult, ALU.mult, [psr[6], ss2r[2], grr[q]], [ogr[q]])

            def f_s3(tt):
                q = tt % 2
                tk = slice(tt * 128, (tt + 1) * 128)
                for e4 in range(4):
                    self.tr(bank(7, e4 * 128, (e4 + 1) * 128), og[q][:, e4 * 128:(e4 + 1) * 128], [ogr[q]], [psr[7]])
                self.cp("act", self.glaT[:, p * 4:(p + 1) * 4, tk], bank(7).rearrange("p (e t) -> p e t", t=128),
                        [psr[7]], [self.glaTr[c][tt] for c in range(p * 4, p * 4 + 4)])

            P.op("pool", (lambda e: e.memset(S32, 0.0)), [], S32r)
            f_s1(0)
            f_R(0)
            f_gr(0)
            for tt in range(16):
                f_s2a(tt)
                if tt + 1 < 16:
                    f_s1(tt + 1, (lambda tt=tt: f_s3(tt - 1)) if tt > 0 else None,
                         (lambda tt=tt: f_gr(tt)) if tt > 0 else None)
                    f_s2(tt)
                    f_R(tt + 1)
                else:
                    f_gr(tt)
                    f_s2(tt)
                    f_s3(tt - 1)
            f_s3(15)

    def phaseS4(self, s):
        A, psr, bank = self.A, self.psr, self.bank
        sig = [A.alloc([512], F32) for _ in range(2)]
        sigr = regs(2)
        k = 0
        for grp in range(2):
            wb, wbr = self.wget("wba", grp)
            wg, wgr = self.wget("ga", grp)
            wb3 = wb.rearrange("p (c n) -> p c n", n=512)
            wg3 = wg.rearrange("p (c n) -> p c n", n=512)
            for tb in range(4):
                blk = slice(tb * 512, (tb + 1) * 512)
                for fl in range(4):
                    fc = grp * 4 + fl
                    b0, b1 = (0, 1) if k % 2 == 0 else (2, 3)
                    for c in range(8):
                        self.mm(bank(b0), wb3[:, c, fl * 128:(fl + 1) * 128], self.glaT[:, c, blk], c == 0, c == 7,
                                [wbr] + self.glaTr[c][tb * 4:(tb + 1) * 4], [psr[b0]])
                    for c in range(8):
                        self.mm(bank(b1), wg3[:, c, fl * 128:(fl + 1) * 128], self.hT[:, c, blk], c == 0, c == 7,
                                [wgr, self.hTr[c][tb]], [psr[b1]])
                    self.act(sig[k % 2], bank(b1), AF.Sigmoid, [psr[b1]], [sigr[k % 2]])
                    self.tt("dve", self.maT[:, fc, blk], bank(b0), sig[k % 2], ALU.mult, [psr[b0], sigr[k % 2]],
                            [self.maTr[fc][tb]])
                    k += 1

    def phaseS2(self, s):
        A, psr, bank = self.A, self.psr, self.bank
        hT, hTr = self.hT, self.hTr
        self.alloc_nr()
        wkv, wkvr = self.wget("akv", 0)
        w3 = wkv.rearrange("p (c n) -> p c n", n=512)
        items = []
        for tb in range(4):
            blk = slice(tb * 512, (tb + 1) * 512)
            for g in range(2):
                def proj(b, g=g, blk=blk, tb=tb):
                    for c in range(8):
                        self.mm(bank(b), w3[:, c, g * 128:(g + 1) * 128], hT[:, c, blk], c == 0, c == 7,
                                [wkvr, hTr[c][tb]], [psr[b]])
                pre = (lambda tb=tb: self.rope_top(tb)) if g == 0 else None
                items.append((proj, C_KN, self.kT[:, g, blk], [self.kTr[g][tb]], pre))
        self.normrope_pipe(items, [2, 3, 4], 5, 6)
        for tb in range(4):
            for t in range(4):
                tile = tb * 4 + t
                for c in range(8):
                    self.mm(bank(7, 0, 256), hT[:, c, tile * 128:(tile + 1) * 128], w3[:, c, 256:512], c == 0, c == 7,
                            [wkvr, hTr[c][tb]], [psr[7]])
                self.cp("act", self.vatt[:, tile, :], bank(7, 0, 256), [psr[7]], [self.vattr[tile]])
        self.alloc_xb(2, junk=(self.nr["t2"].bitcast(BF16), self.nr["r"][3]))
        mT = A.alloc([8, 256], BF16)
        mTr = regs(8)
        self.load_xb(self.mem[s], 2)
        self.norm_T(self.xb, self.xbr, 2, C_MEM, lambda c: mT[:, c, :], mTr, self.sc, [0, 1])
        for grp in range(2):
            w, wr = self.wget("xwkv", grp)
            w3 = w.rearrange("p (c n) -> p c n", n=512)
            for fl in range(4):
                kc = grp * 4 + fl
                b = 2 + (kc % 2)
                for c in range(8):
                    self.mm(bank(b, 0, 256), w3[:, c, fl * 128:(fl + 1) * 128], mT[:, c, :], c == 0, c == 7,
                            [wr, mTr[c]], [psr[b]])
                self.cp("act" if kc % 2 == 0 else "dve", self.kmT[:, kc, :], bank(b, 0, 256), [psr[b]], [self.kmTr[kc]])
        for grp in range(2):
            w, wr = self.wget("xwkv", 2 + grp)
            w3 = w.rearrange("p (c n) -> p c n", n=512)
            for mt in range(2):
                b = 2 + mt
                for c in range(8):
                    self.mm(bank(b), mT[:, c, mt * 128:(mt + 1) * 128], w3[:, c, :], c == 0, c == 7, [wr, mTr[c]], [psr[b]])
                self.cp("act" if mt == 0 else "dve", self.vm[:, mt, grp * 512:(grp + 1) * 512], bank(b), [psr[b]],
                        [self.vmr[mt]])

    def alloc_S5(self):
        A = self.A
        self.sg = [A.alloc([512], F32) for _ in range(2)]
        self.sgr = regs(2)
        self.alloc_xb(4, junk=(self.sg[1].bitcast(BF16), self.sgr[1]))
        self.xb2 = [(self.xb, self.xbr), (A.alloc([4, D], F32), regs(4))]
        self.bufA = A.alloc([8, 512], BF16)
        self.bufB = A.alloc([8, 512], BF16)
        self.bufAr, self.bufBr = regs(8), regs(8)
        self.actT = A.alloc([22, 512], BF16)
        r = regs(22)
        for j in (13, 15, 17, 19):
            r[j + 1] = r[j]
        self.actTr = r
        self.bufC = self.actT[:, 0:8, :]
        self.bufCr = r[0:8]
        self.PT = [self.actT[:, 8 + i, :] for i in range(4)]
        self.PTr = [r[8 + i] for i in range(4)]

        def f32v(j):
            return self.actT[:, j:j + 2, :].rearrange("p a b -> p (a b)").bitcast(F32)
        self.nr = {"sqb": self.actT[:, 12, :], "tmpA": f32v(13), "knf": f32v(15), "t2": f32v(17),
                   "r": [r[12], r[13], r[15], r[17]], "knf2": f32v(19), "r_kn2": r[19]}
        self.rden = self.nr["tmpA"]
        self.rdenr = self.nr["r"][1]
        self.woring = [A.alloc([1024], BF16) for _ in range(3)]
        self.woring_r = regs(3)

    def wo2issue_upto(self, n):
        n = min(n, (self.cur_seq + 1) * 4 * 22)
        while self.wo2_issued < n:
            k = self.wo2_issued
            i = k % 3
            src, L = self.wsrc("wo2", k % 22)
            self.dma("sp", self.woring[i], src, self.woring_s[i], [self.wbf_reg["wo2"]], [self.woring_r[i]])
            self.wo2_issued += 1

    def wo2get(self, g):
        k = self.wo2_i
        assert k % 22 == g
        self.wo2issue_upto(k + 2)
        self.wo2_i += 1
        i = k % 3
        return self.woring[i].rearrange("p (k n) -> p k n", n=512), self.woring_r[i]

    def proj_res(self, srcs, wseg, gain_idx, next_norm=True):
        psr, bank = self.psr, self.bank
        self.load_gp(gain_idx)
        w0, w0r = self.wget(wseg, 0)
        w1, w1r = self.wget(wseg, 1)
        ws = [(w0.rearrange("p (c n) -> p c n", n=512), w0r), (w1.rearrange("p (c n) -> p c n", n=512), w1r)]
        for t in range(4):
            k = self.pn_k
            self.pn_k = (k + 1) % 2
            for half in range(2):
                w3, wr = ws[half]
                n = len(srcs) * 8
                i = 0
                for (apf, rf) in srcs:
                    for c in range(8):
                        self.mm(bank(2 * k + half), apf(c, t), w3[:, c, :], i == 0, i == n - 1, [wr] + rf(c, t),
                                [psr[2 * k + half]])
                        i += 1
            self.postnorm(k, t, 0)
            if next_norm and t >= 1:
                self.norm_tile(t - 1)
        if next_norm:
            self.norm_tile(3)

    def phaseS5(self, s, tb):
        psr, bank = self.psr, self.bank
        pb = tb % 2
        bufs = [(self.bufA, self.bufAr), (self.bufB, self.bufBr)]
        bufA, bufAr = bufs[pb]
        bufB, bufBr = bufs[1 - pb]
        bufC, bufCr = self.bufC, self.bufCr
        self.xb, self.xbr = self.xb2[pb]
        blk = slice(tb * 512, (tb + 1) * 512)
        if tb == 0:
            self.load_xb(self.x[s, blk, :])
            for t in range(4):
                self.norm_tile(t)
            self.norm_fin(4, C_MIXPRE, lambda c: bufA[:, c, :], bufAr, [0, 1])
        self.rope_top(tb)
        items = []
        wq = [self.wget("aq", grp) for grp in range(2)]
        for head in range(8):
            w, wr = wq[head // 4]
            w3 = w.rearrange("p (c n) -> p c n", n=512)
            hl = head % 4

            def proj(b, w3=w3, wr=wr, hl=hl):
                for c in range(8):
                    self.mm(bank(b), w3[:, c, hl * 128:(hl + 1) * 128], bufA[:, c, :], c == 0, c == 7,
                            [wr, bufAr[c]], [psr[b]])
            items.append((proj, C_QN, bufB[:, head, :], [bufBr[head]], None))
        self.normrope_pipe(items, [2, 5, 6], 3, 4)
        SCL = float(128.0 ** -0.5)
        sbanks = [2, 3, 0, 1]
        LA = 3
        ptl = list(zip(self.PT, self.PTr)) + [(self.actT[:, 15, :], self.actTr[15]), (self.actT[:, 17, :], self.actTr[17])]
        NP = len(ptl)
        seq = [(head, st) for head in range(8) for st in range(16)]

        aT, ar = self.actT, self.actTr
        S1 = [(aT[:, 12, :], ar[12]), (aT[:, 21, :], ar[21]), (aT[:, 19, :], ar[19])]
        S2 = [(aT[:, 15, :], ar[15]), (aT[:, 17, :], ar[17])]

        def score(j):
            head, st = seq[j]
            g = head // 4
            sb = sbanks[j % 4]
            self.mm(bank(sb), self.kT[:, g, st * 128:(st + 1) * 128], bufB[:, head, :], True, True,
                    [self.kTr[g][st // 4], bufBr[head]], [psr[sb]])
            self.act(ptl[j % NP][0], bank(sb), AF.Exp, [psr[sb]], [ptl[j % NP][1]], scale=SCL)
            if st % 2 == 1:
                s1, s1r = S1[(j // 2) % 3]
                self.tt("dve", s1, ptl[(j - 1) % NP][0], ptl[j % NP][0], ALU.add,
                        [ptl[(j - 1) % NP][1], ptl[j % NP][1]], [s1r])

        for j0 in range(LA):
            score(j0)
        for j in range(len(seq)):
            if j + LA < len(seq):
                score(j + LA)
            head, st = seq[j]
            g = head // 4
            ob = 4 + 2 * (head % 2)
            db = ob + 1
            pt, ptr = ptl[j % NP]
            self.mm(bank(ob), self.vatt[:, st, g * 128:(g + 1) * 128], pt, st == 0, st == 15,
                    [self.vattr[st], ptr], [psr[ob]])
            if st % 2 == 1:
                s1, s1r = S1[(j // 2) % 3]
                self.mm(bank(db), self.ones, s1, st == 1, st == 15, [self.onesr, s1r], [psr[db]])
            if st == 15:
                self.act(self.rden, bank(db), AF.Ln, [psr[db]], [self.rdenr])
                self.act(self.rden, self.rden, AF.Exp, [self.rdenr], [self.rdenr], scale=-1.0)
                self.tt("dve", bufC[:, head, :], bank(ob), self.rden, ALU.mult, [psr[ob], self.rdenr], [bufCr[head]])
        k = 0
        for grp in range(2):
            wb, wbr = self.wget("wbb", grp)
            wg, wgr = self.wget("gb", grp)
            wb3 = wb.rearrange("p (c n) -> p c n", n=512)
            wg3 = wg.rearrange("p (c n) -> p c n", n=512)
            for fl in range(4):
                fc = grp * 4 + fl
                b0, b1 = (0, 1) if k % 2 == 0 else (2, 3)
                for c in range(8):
                    self.mm(bank(b0), wb3[:, c, fl * 128:(fl + 1) * 128], bufC[:, c, :], c == 0, c == 7,
                            [wbr, bufCr[c]], [psr[b0]])
                for c in range(8):
                    self.mm(bank(b1), wg3[:, c, fl * 128:(fl + 1) * 128], bufA[:, c, :], c == 0, c == 7,
                            [wgr, bufAr[c]], [psr[b1]])
                self.act(self.sg[k % 2], bank(b1), AF.Sigmoid, [psr[b1]], [self.sgr[k % 2]])
                self.tt("dve", self.sg[k % 2], bank(b0), self.sg[k % 2], ALU.mult, [psr[b0], self.sgr[k % 2]], [self.sgr[k % 2]])
                self.tt("pool", bufB[:, fc, :], self.sg[k % 2], self.maT[:, fc, blk], ALU.add,
                        [self.sgr[k % 2], self.maTr[fc][tb]], [bufBr[fc]])
                k += 1
        srcs = [(lambda c, t: bufB[:, c, t * 128:(t + 1) * 128], lambda c, t: [bufBr[c]])]
        self.proj_res(srcs, "wo", 0)
        if self.upto == "x1":
            return
        self.norm_fin(4, C_XPRE, lambda c: bufA[:, c, :], bufAr, [0, 1])
        for grp in range(2):
            w, wr = self.wget("xwq", grp)
            w3 = w.rearrange("p (c n) -> p c n", n=512)
            for fl in range(4):
                fc = grp * 4 + fl
                b = 4 + (fc % 2)
                for c in range(8):
                    self.mm(bank(b), w3[:, c, fl * 128:(fl + 1) * 128], bufA[:, c, :], c == 0, c == 7,
                            [wr, bufAr[c]], [psr[b]])
                self.cp("act" if fc % 2 == 0 else "dve", bufC[:, fc, :], bank(b), [psr[b]], [bufCr[fc]])
        for head in range(4):
            for mt in range(2):
                sb = 6 + mt
                for dc in range(2):
                    self.mm(bank(sb), self.kmT[:, head * 2 + dc, mt * 128:(mt + 1) * 128], bufC[:, head * 2 + dc, :],
                            dc == 0, dc == 1, [self.kmTr[head * 2 + dc], bufCr[head * 2 + dc]], [psr[sb]])
                pi = (head % 2) * 2 + mt
                self.act(self.PT[pi], bank(sb), AF.Exp, [psr[sb]], [self.PTr[pi]], scale=1.0 / 16.0)
            base_b = 0 if head % 2 == 0 else 3
            for dc in range(2):
                for mt in range(2):
                    pi = (head % 2) * 2 + mt
                    self.mm(bank(base_b + dc), self.vm[:, mt, head * 256 + dc * 128:head * 256 + (dc + 1) * 128],
                            self.PT[pi], mt == 0, mt == 1, [self.vmr[mt], self.PTr[pi]], [psr[base_b + dc]])
            db = base_b + 2
            for mt in range(2):
                pi = (head % 2) * 2 + mt
                self.mm(bank(db), self.ones, self.PT[pi], mt == 0, mt == 1, [self.onesr, self.PTr[pi]], [psr[db]])
            self.act(self.rden, bank(db), AF.Ln, [psr[db]], [self.rdenr])
            self.act(self.rden, self.rden, AF.Exp, [self.rdenr], [self.rdenr], scale=-1.0)
            for dc in range(2):
                self.tt("dve", bufB[:, head * 2 + dc, :], bank(base_b + dc), self.rden, ALU.mult,
                        [psr[base_b + dc], self.rdenr], [bufBr[head * 2 + dc]])
        nxb, nxbr = self.xb2[1 - pb]
        if tb < 3 and self.upto == "all":
            nblk = self.x[s, (tb + 1) * 512:(tb + 2) * 512, :]
            for t in range(4):
                self.dma("sp", nxb[:, t, :], nblk[t * 128:(t + 1) * 128, :], self.xb_slot[t], [], [nxbr[t]])
        srcs = [(lambda c, t: bufB[:, c, t * 128:(t + 1) * 128], lambda c, t: [bufBr[c]])]
        self.proj_res(srcs, "xwo", 1)
        if self.upto == "x2":
            return
        self.norm_fin(4, C_FFNPRE, lambda c: bufA[:, c, :], bufAr, [0, 1])
        k = 0
        prefetch = tb < 3 and self.upto == "all"
        for i in range(11):
            w, wr = self.wget("wi", i)
            w3 = w.rearrange("p (c n) -> p c n", n=512)
            if prefetch and i == 1:
                for t in range(4):
                    self.norm_tile(t, nxb, nxbr)
            if prefetch and i == 5:
                self.norm_fin(4, C_MIXPRE, lambda c: bufB[:, c, :], bufBr, [0, 1])
            for j in range(2):
                bg, bu = (0, 1) if k % 2 == 0 else (2, 3)
                for c in range(8):
                    self.mm(bank(bg), w3[:, c, j * 128:(j + 1) * 128], bufA[:, c, :], c == 0, c == 7, [wr, bufAr[c]], [psr[bg]])
                for c in range(8):
                    self.mm(bank(bu), w3[:, c, 256 + j * 128:256 + (j + 1) * 128], bufA[:, c, :], c == 0, c == 7,
                            [wr, bufAr[c]], [psr[bu]])
                self.act(self.sg[k % 2], bank(bg), AF.Silu, [psr[bg]], [self.sgr[k % 2]])
                self.tt("dve", self.actT[:, 2 * i + j, :], bank(bu), self.sg[k % 2], ALU.mult, [psr[bu], self.sgr[k % 2]],
                        [self.actTr[2 * i + j]])
                k += 1
        self.load_gp(2)
        for hf in range(2):
            for i in range(11):
                w3, wr = self.wo2get(hf * 11 + i)
                for t in range(4):
                    for kk in range(2):
                        self.mm(bank(2 * t + hf), self.actT[:, 2 * i + kk, t * 128:(t + 1) * 128], w3[:, kk, :],
                                i == 0 and kk == 0, i == 10 and kk == 1, [wr, self.actTr[2 * i + kk]], [psr[2 * t + hf]])
        for t in range(4):
            self.postnorm(t, t, 0)
            self.dma("sp", self.y[s, tb * 512 + t * 128:tb * 512 + (t + 1) * 128, :], self.xb[:, t, :], self.y_slot[t],
                     [self.xbr[t]], [])

    def build(self):
        self.init_consts()
        A, P = self.A, self.P
        self.ssP = A.alloc([32], F32)
        self.xb_slot = [self.newslot("xb%d" % i) for i in range(4)]
        self.y_slot = [self.newslot("ystore%d" % i, output=True) for i in range(4)]
        self.wgla_slot = self.newslot("wgla")
        self.woring_s = [self.newslot("wo2r%d" % i) for i in range(4)]
        self.wo2_i = 0
        self.wo2_issued = 0
        self.pn_k = 0
        self.base = A.mark()
        o = A.nbytes - 56 * 1024
        self.qoff = o
        self.maT, o = A.alloc_at(o, [8, T], BF16)
        self.kT, o = A.alloc_at(o, [2, T], BF16)
        self.vatt, o = A.alloc_at(o, [16, 256], BF16)
        self.kmT, o = A.alloc_at(o, [8, 256], BF16)
        self.vm, o = A.alloc_at(o, [2, 1024], BF16)
        assert o <= A.nbytes
        up = self.upto
        for s in range(self.nseq):
            self.cur_seq = s
            self.maTr = [regs(4) for _ in range(8)]
            self.kTr = [regs(4) for _ in range(2)]
            self.vattr = regs(16)
            self.kmTr = regs(8)
            self.vmr = regs(2)
            if s > 0:
                P.fence()
            A.reset(self.base)
            self.hT = A.alloc([8, T], BF16)
            self.hTr = [regs(4) for _ in range(8)]
            m1 = A.mark()
            self.alloc_xb(4)
            self.phaseA(s)
            self.issue_conv(["gla1", "wba", "ga", "akv", "xwkv"])
            if up == "A":
                tmp = A.alloc([8, T], F32)
                self.dbg_store("hT", self.hT, [r for rr in self.hTr for r in rr], tmp)
                break
            P.fence()
            A.reset(m1)
            self.glaT = A.alloc([8, T], BF16)
            self.glaTr = [regs(16) for _ in range(8)]
            m2 = A.mark()
            self.phaseGLA(s)
            P.fence()
            A.reset(m2)
            if up == "GLA":
                tmp = A.alloc([8, T], F32)
                self.dbg_store("glaT", self.glaT, [r for rr in self.glaTr for r in rr], tmp)
                break
            self.phaseS4(s)
            self.phaseS2(s)
            assert A.off <= self.qoff, (A.off, self.qoff)
            if up == "S2":
                tmp = A.alloc_at(self.base, [8, T], F32)[0]
                P.fence()
                self.dbg_store("maT", self.maT, [r for rr in self.maTr for r in rr], tmp)
                tmp2 = A.alloc_at(self.base + 65536, [2, T], F32)[0]
                self.dbg_store("kT", self.kT, [r for rr in self.kTr for r in rr], tmp2)
                tmp3 = A.alloc_at(self.base + 65536 + 16384, [16, 256], F32)[0]
                self.dbg_store("vatt", self.vatt, self.vattr, tmp3)
                tmp4 = A.alloc_at(self.base + 65536 + 32768, [8, 256], F32)[0]
                self.dbg_store("kmT", self.kmT, self.kmTr, tmp4)
                tmp5 = A.alloc_at(self.base + 65536 + 32768 + 8192, [2, 1024], F32)[0]
                self.dbg_store("vm", self.vm, self.vmr, tmp5)
                break
            P.fence()
            A.reset(self.base)
            self.alloc_S5()
            assert A.off <= self.qoff, (A.off, self.qoff)
            for tb in range(4):
                self.phaseS5(s, tb)
                if up in ("x1", "x2"):
                    for t in range(4):
                        self.dma("sp", self.y[s, tb * 512 + t * 128:tb * 512 + (t + 1) * 128, :], self.xb[:, t, :],
                                 self.y_slot[t], [self.xbr[t]], [])
        P.emit()
        return self.nc


def make_core_inputs(inp, xs, ms, shared=None):
    if shared is None:
        shared = make_shared(inp)
    d = dict(shared)
    d["x"] = np.ascontiguousarray(xs, dtype=np.float32)
    d["mem"] = np.ascontiguousarray(ms, dtype=np.float32)
    return d


def make_shared(inp):
    gpost = np.stack([np.broadcast_to(inp[k][0][None, :], (128, D)) for k in ("ln_mix_post", "ln_x_post", "ln_ffn_post")], 0)
    gn = inp["gla_norm"][0]
    gnorm2 = np.broadcast_to(np.concatenate([gn, gn])[None, :], (128, 512))
    waug = np.zeros((17, 1024), np.float32)
    waug[:16, :512] = inp["gla_wa_f"][0]
    waug[:16, 512:] = inp["gla_wa_b"][0]
    waug[16, :512] = inp["gla_ba_f"][0]
    waug[16, 512:] = inp["gla_ba_b"][0]
    return {
        "wall": build_wall(inp),
        "kpack": build_kpack(),
        "cpack": build_cpack(inp),
        "gpost": np.ascontiguousarray(gpost, dtype=np.float32),
        "gnorm2": np.ascontiguousarray(gnorm2, dtype=np.float32),
        "waug": waug,
    }


_CACHE = {}


def kernel(**inputs):
    inp = {k: np.asarray(v) for k, v in inputs.items()}
    xs = np.concatenate([inp["x_prompt"], inp["x_sample"]], 0)
    ms = np.concatenate([inp["mem_prompt"], inp["mem_sample"]], 0)
    nb = xs.shape[0]
    assert nb == NCORES * SEQ_PER_CORE
    shared = make_shared(inp)
    if "nc" not in _CACHE:
        _CACHE["nc"] = Builder(nseq=SEQ_PER_CORE, upto="all").build()
    nc = _CACHE["nc"]
    in_maps = []
    for c in range(NCORES):
        sl = slice(c * SEQ_PER_CORE, (c + 1) * SEQ_PER_CORE)
        in_maps.append(make_core_inputs(inp, xs[sl], ms[sl], shared))
    res = run_bass_kernel_spmd(nc, in_maps, core_ids=list(range(NCORES)))
    y = np.concatenate([np.asarray(r["y"]) for r in res.results], 0).astype(np.float32)
    nprompt = inp["x_prompt"].shape[0]
    return (np.ascontiguousarray(y[:nprompt]), np.ascontiguousarray(y[nprompt:]))
```

```python
import numpy as np
import concourse.bass as bass
import concourse.mybir as mybir
from concourse.bass_utils import run_bass_kernel_spmd

F32 = mybir.dt.float32
BF16 = mybir.dt.bfloat16
AF = mybir.ActivationFunctionType
ALU = mybir.AluOpType

T = 2048
D = 1024
NMEM = 256
EPS = 1e-6
DFF = 2816
NCORES = 8
SEQ_PER_CORE = 3

ENGS = ("pe", "act", "dve", "pool", "sp")
SAME_ENGINE_FULL_SYNC = True


class Reg:
    __slots__ = ("w", "r")

    def __init__(self):
        self.w = None
        self.r = {}


def regs(n):
    return [Reg() for _ in range(n)]


class Slot:
    def __init__(self, nc, name):
        self.sem = nc.alloc_semaphore(name)
        self.n = 0


class Prog:
    def __init__(self, nc):
        self.nc = nc
        self.ops = []
        self.by_eng = {e: [] for e in ENGS}
        self.sems = {e: nc.alloc_semaphore("sem_" + e) for e in ENGS}
        self.out_slots = []
        self.fence_deps = set()
        self.fence_pending = set()
        self.dma_since = []

    def fence(self):
        deps = set(self.dma_since)
        for e in ENGS:
            for oid in reversed(self.by_eng[e]):
                if self.ops[oid][3] is None:
                    deps.add(oid)
                    break
        self.fence_deps = deps
        self.fence_pending = set(ENGS)
        self.dma_since = []

    def op(self, eng, fn, reads=(), writes=(), slot=None):
        oid = len(self.ops)
        deps = set()
        is_dma = slot is not None
        if eng in self.fence_pending:
            self.fence_pending.discard(eng)
            for d in self.fence_deps:
                if self.ops[d][3] is None and self.ops[d][0] == eng and not is_dma:
                    continue
                deps.add(d)
        if is_dma:
            self.dma_since.append(oid)

        def add(pid, kind):
            peng, _, _, pslot, _ = self.ops[pid]
            if pslot is None and not is_dma and peng == eng:
                if eng == "pe" or (kind != "raw" and not SAME_ENGINE_FULL_SYNC):
                    return
            deps.add(pid)

        for r in reads:
            if r.w is not None:
                add(r.w, "raw")
        for w in writes:
            if w.w is not None:
                add(w.w, "waw")
            for pid in w.r.values():
                add(pid, "war")
        key = ("dma", oid) if is_dma else eng
        for r in reads:
            r.r[key] = oid
        for w in writes:
            w.w = oid
            w.r = {}
        val = None
        if is_dma:
            slot.n += 1
            val = 16 * slot.n
        self.ops.append((eng, fn, deps, slot, val))
        self.by_eng[eng].append(oid)
        return oid

    def emit(self):
        nc = self.nc
        ops = self.ops
        marked = set()
        for (_, _, deps, _, _) in ops:
            for d in deps:
                if ops[d][3] is None:
                    marked.add(d)
        tok = {}
        for e in ENGS:
            cnt = 0
            for oid in self.by_eng[e]:
                eng, fn, deps, slot, val = ops[oid]
                if slot is not None:
                    tok[oid] = (slot.sem, val)
                elif oid in marked:
                    cnt += 1
                    tok[oid] = (self.sems[e], cnt)
        self.nmarked = len(marked)
        final_waits = [(s.sem, 16 * s.n) for s in self.out_slots if s.n > 0]
        handles = {"pe": "tensor", "act": "scalar", "dve": "vector", "pool": "gpsimd", "sp": "sync"}

        def run_engine(e, engine):
            waited = {}
            for oid in self.by_eng[e]:
                eng, fn, deps, slot, val = ops[oid]
                needs = {}
                for d in deps:
                    sem, v = tok[d]
                    k = sem.num
                    if waited.get(k, 0) < v and needs.get(k, (None, 0))[1] < v:
                        needs[k] = (sem, v)
                for k, (sem, v) in needs.items():
                    engine.wait_ge(sem, v)
                    waited[k] = v
                inst = fn(engine)
                if slot is not None:
                    inst.then_inc(slot.sem, 16)
                elif oid in marked:
                    inst.then_inc(self.sems[e], 1)
            if e == "sp":
                for sem, v in final_waits:
                    engine.wait_ge(sem, v)

        with nc.Block() as block:
            for e in ENGS:
                def mk(e=e):
                    def body(engine):
                        run_engine(e, engine)
                    return body
                getattr(block, handles[e])(mk())


class Arena:
    def __init__(self, nc, nbytes):
        self.t = nc.alloc_sbuf_tensor("arena", [128, nbytes // 2], BF16)
        self.nbytes = nbytes
        self.off = 0
        self.peak = 0

    def alloc(self, shape, dtype, parts=128):
        esz = 4 if dtype == F32 else 2
        n = int(np.prod(shape))
        nb = (n * esz + 31) // 32 * 32
        o = self.off
        self.off += nb
        self.peak = max(self.peak, self.off)
        assert self.off <= self.nbytes, f"arena overflow {self.off} > {self.nbytes}"
        v = self.t[0:parts, o // 2:(o + n * esz) // 2]
        if dtype == F32:
            v = v.bitcast(F32)
        if len(shape) == 2:
            v = v.rearrange("p (a b) -> p a b", b=shape[1])
        elif len(shape) == 3:
            v = v.rearrange("p (a b c) -> p a b c", b=shape[1], c=shape[2])
        return v

    def alloc_at(self, off, shape, dtype, parts=128):
        save = self.off
        self.off = off
        v = self.alloc(shape, dtype, parts)
        end = self.off
        self.off = save
        return v, end

    def mark(self):
        return self.off

    def reset(self, m):
        self.off = m


def _seg(W, groups):
    K = W.shape[0]
    KC = K // 128
    out = []
    for cols in groups:
        t = W[:, cols].reshape(KC, 128, len(cols)).transpose(1, 0, 2)
        out.append(np.ascontiguousarray(t).reshape(128, KC * len(cols)))
    return np.stack(out, 0)


def _ar(a, b):
    return np.arange(a, b)


SEG_ORDER = ["z", "gla0", "gla1", "wba", "ga", "akv", "xwkv", "aq", "wbb", "gb", "wo", "xwq", "xwo", "wi", "wo2"]
SEG_SHAPE = {
    "z": (1, 8 * 32), "akv": (1, 8 * 512), "gla0": (1, 8 * 1536), "gla1": (1, 8 * 1536),
    "ga": (2, 4096), "wba": (2, 4096), "aq": (2, 4096), "gb": (2, 4096), "wbb": (2, 4096),
    "wo": (2, 4096), "xwkv": (4, 4096), "xwq": (2, 4096), "xwo": (2, 4096),
    "wi": (11, 4096), "wo2": (22, 1024),
}


def seg_offsets():
    off = {}
    o = 0
    for s in SEG_ORDER:
        ng, L = SEG_SHAPE[s]
        off[s] = o
        o += ng * 128 * L
    return off, o


def build_wall(inp):
    w_in = inp["w_in"][0]
    segs = {}
    for p in range(2):
        h0, h1 = 2 * p, 2 * p + 1
        cols = np.concatenate([
            _ar(h0 * 128, h0 * 128 + 128), _ar(h1 * 128, h1 * 128 + 128),
            _ar(512 + h0 * 128, 512 + h0 * 128 + 128), _ar(512 + h1 * 128, 512 + h1 * 128 + 128),
            _ar(1024 + h0 * 256, 1024 + h0 * 256 + 256), _ar(1024 + h1 * 256, 1024 + h1 * 256 + 256),
            _ar(2048 + h0 * 256, 2048 + h0 * 256 + 256), _ar(2048 + h1 * 256, 2048 + h1 * 256 + 256)])
        segs["gla%d" % p] = _seg(w_in, [cols])
    segs["z"] = _seg(w_in, [_ar(3072, 3104)])
    segs["akv"] = _seg(w_in, [_ar(4128, 4640)])
    segs["aq"] = _seg(w_in, [_ar(3104, 3616), _ar(3616, 4128)])
    segs["ga"] = _seg(w_in, [_ar(4640, 5152), _ar(5152, 5664)])
    segs["gb"] = _seg(w_in, [_ar(5664, 6176), _ar(6176, 6688)])
    two = [_ar(0, 512), _ar(512, 1024)]
    segs["wba"] = _seg(inp["w_branch_gla"][0], two)
    segs["wbb"] = _seg(inp["w_branch_att"][0], two)
    segs["wo"] = _seg(inp["w_out"][0], two)
    segs["xwq"] = _seg(inp["x_wq"][0], two)
    segs["xwo"] = _seg(inp["x_wo"][0], two)
    segs["xwkv"] = _seg(inp["x_wkv"][0], [_ar(i * 512, (i + 1) * 512) for i in range(4)])
    wi = inp["ffn_wi"][0]
    segs["wi"] = _seg(wi, [np.concatenate([_ar(256 * i, 256 * i + 256), _ar(DFF + 256 * i, DFF + 256 * i + 256)])
                           for i in range(11)])
    w2 = inp["ffn_wo"][0]
    pcs = []
    for half in range(2):
        for i in range(11):
            blk = w2[256 * i:256 * i + 256, half * 512:(half + 1) * 512]
            pcs.append(np.ascontiguousarray(blk.reshape(2, 128, 512).transpose(1, 0, 2)).reshape(128, 1024))
    segs["wo2"] = np.stack(pcs, 0)
    off, tot = seg_offsets()
    wall = np.empty(tot, np.float32)
    for s in SEG_ORDER:
        ng, L = SEG_SHAPE[s]
        a = segs[s]
        assert a.shape == (ng, 128, L), (s, a.shape)
        wall[off[s]:off[s] + a.size] = a.reshape(-1)
    return wall.reshape(-1, 2048)


K_ID = 0
K_UF = 128
K_WF = 256
K_UB = 384
K_WB = 512
K_MASK = 640
K_ROT = 1152
K_COS = 1280
K_SIN = 1376
NK = 1472


def build_kpack():
    k = np.zeros((128, NK), np.float32)
    j = np.arange(128)[:, None]
    i = np.arange(128)[None, :]
    k[:, K_ID:K_ID + 128] = (j == i)
    c = -1.0 / 16.0
    k[:, K_UF:K_UF + 128] = c * (j <= i)
    k[:, K_WF:K_WF + 128] = c * (j > i)
    k[:, K_UB:K_UB + 128] = c * (j >= i)
    k[:, K_WB:K_WB + 128] = c * (j < i)
    mf = (j <= i).astype(np.float32)
    mb = (j > i).astype(np.float32)
    k[:, K_MASK:K_MASK + 512] = np.concatenate([mf, mf, mb, mb], 1)
    R = np.zeros((128, 128), np.float32)
    for m in range(128):
        if (m % 64) < 32:
            R[m + 32, m] = -1.0
        else:
            R[m - 32, m] = 1.0
    k[:, K_ROT:K_ROT + 128] = R
    inv = (10000.0 ** (-np.arange(0, 64, 2, dtype=np.float32) / np.float32(64))).astype(np.float32)
    for d in range(128):
        f = inv[d % 32]
        if d < 64:
            ang = (np.arange(32, dtype=np.float32) * f).astype(np.float32)
            k[d, K_COS:K_COS + 32] = np.cos(ang)
            k[d, K_SIN:K_SIN + 32] = np.sin(ang)
        else:
            ang = (np.arange(64, dtype=np.float32) * f).astype(np.float32)
            k[d, K_COS + 32:K_COS + 96] = np.cos(ang)
            k[d, K_SIN + 32:K_SIN + 96] = np.sin(ang)
    return k


C_MIXPRE, C_XPRE, C_FFNPRE, C_MEM, C_QN, C_KN = 0, 8, 16, 24, 32, 33
NCP = 34


def build_cpack(inp):
    c = np.zeros((128, NCP), np.float32)
    c[:, C_MIXPRE:C_MIXPRE + 8] = inp["ln_mix_pre"][0].reshape(8, 128).T
    c[:, C_XPRE:C_XPRE + 8] = inp["ln_x_pre"][0].reshape(8, 128).T
    c[:, C_FFNPRE:C_FFNPRE + 8] = inp["ln_ffn_pre"][0].reshape(8, 128).T
    c[:, C_MEM:C_MEM + 8] = inp["ln_mem"][0].reshape(8, 128).T
    c[:, C_QN] = inp["att_q_norm"][0]
    c[:, C_KN] = inp["att_k_norm"][0]
    return c


class Builder:
    def __init__(self, nseq=SEQ_PER_CORE, upto="all", dbg=None):
        self.nseq = nseq
        self.upto = upto
        self.dbg = dbg or {}
        nc = bass.Bass("TRN2", target_bir_lowering=False)
        self.nc = nc
        self.P = Prog(nc)
        self.soff, self.wtot = seg_offsets()
        self.x = nc.dram_tensor("x", [nseq, T, D], F32, kind="ExternalInput").ap()
        self.mem = nc.dram_tensor("mem", [nseq, NMEM, D], F32, kind="ExternalInput").ap()
        self.wall = nc.dram_tensor("wall", [self.wtot // 2048, 2048], F32, kind="ExternalInput").ap()
        self.kpack = nc.dram_tensor("kpack", [128, NK], F32, kind="ExternalInput").ap()
        self.cpack = nc.dram_tensor("cpack", [128, NCP], F32, kind="ExternalInput").ap()
        self.gpost = nc.dram_tensor("gpost", [3, 128, D], F32, kind="ExternalInput").ap()
        self.gnorm2 = nc.dram_tensor("gnorm2", [128, 512], F32, kind="ExternalInput").ap()
        self.waug = nc.dram_tensor("waug", [17, 1024], F32, kind="ExternalInput").ap()
        self.y = nc.dram_tensor("y", [nseq, T, D], F32, kind="ExternalOutput").ap()
        self.wbf = nc.dram_tensor("wbf", [self.wtot // 2048, 2048], BF16).ap()
        self.wbf_flat = self.wbf.rearrange("r c -> (r c)")
        self.wbf_reg = {s: Reg() for s in SEG_ORDER}
        self.dbg_out = {}
        for name, shape in self.dbg.items():
            self.dbg_out[name] = nc.dram_tensor("dbg_" + name, list(shape), F32, kind="ExternalOutput").ap()
        self.A = Arena(nc, 207 * 1024)
        self.PS = [nc.alloc_psum_tensor("ps%d" % i, [128, 1024], F32).ap() if False else
                   nc.alloc_psum_tensor("ps%d" % i, [128, 1024], F32) for i in range(4)]
        self.psr = regs(8)

    def bank(self, b, lo=0, hi=512):
        return self.PS[b // 2][:, (b % 2) * 512 + lo:(b % 2) * 512 + hi]

    def mm(self, out, lhsT, rhs, start, stop, reads, writes):
        self.P.op("pe", lambda e: e.matmul(out, lhsT, rhs, start=start, stop=stop), reads, writes)

    def tr(self, out, in_, reads, writes):
        ident = self.K[:, K_ID:K_ID + 128]
        self.P.op("pe", lambda e: e.transpose(out, in_, ident), list(reads) + [self.Kr], writes)

    def act(self, out, in_, func, reads, writes, scale=1.0, bias=0.0, accum=None):
        def fn(e):
            kw = {}
            if accum is not None:
                kw["accum_out"] = accum
            return e.activation(out, in_, func, bias=bias, scale=scale, **kw)
        self.P.op("act", fn, reads, writes)

    def tt(self, eng, out, in0, in1, op, reads, writes):
        self.P.op(eng, lambda e: e.tensor_tensor(out, in0, in1, op), reads, writes)

    def ts(self, eng, out, in0, s1, op0, reads, writes, s2=None, op1=None):
        if op1 is None:
            self.P.op(eng, lambda e: e.tensor_scalar(out, in0, s1, None, op0), reads, writes)
        else:
            self.P.op(eng, lambda e: e.tensor_scalar(out, in0, s1, s2, op0, op1), reads, writes)

    def stt(self, out, in0, scalar, in1, op0, op1, reads, writes):
        self.P.op("dve", lambda e: e.scalar_tensor_tensor(out, in0, scalar, in1, op0, op1), reads, writes)

    def cp(self, eng, out, in_, reads, writes):
        if eng == "act":
            self.P.op("act", lambda e: e.activation(out, in_, AF.Copy), reads, writes)
        else:
            self.P.op(eng, lambda e: e.tensor_copy(out, in_), reads, writes)

    def dma(self, q, out, in_, slot, reads, writes):
        self.P.op(q, lambda e: e.dma_start(out, in_), reads, writes, slot=slot)

    def newslot(self, name, output=False):
        s = Slot(self.nc, name)
        if output:
            self.P.out_slots.append(s)
        return s

    def wsrc(self, seg, g):
        ng, L = SEG_SHAPE[seg]
        o = self.soff[seg] + g * 128 * L
        return self.wbf_flat[o:o + 128 * L].rearrange("(p l) -> p l", p=128), L

    def init_consts(self):
        A, nc = self.A, self.nc
        self.K = A.alloc([NK], F32)
        self.Kr = Reg()
        self.C = A.alloc([NCP], F32)
        self.Cr = Reg()
        self.G2 = A.alloc([512], F32)
        self.G2r = Reg()
        self.gp = A.alloc([D], F32)
        self.gpr = Reg()
        self.gp_slot = self.newslot("gp")
        self.waug_f = A.alloc_at(A.nbytes - 56 * 1024, [1024], F32, parts=17)[0]
        self.waug_b = A.alloc([1024], BF16, parts=17)
        self.waugr = Reg()
        self.ones = A.alloc([128], BF16)
        self.onesr = Reg()
        self.cosB = A.alloc([512], F32)
        self.sinB = A.alloc([512], F32)
        self.csr = Reg()
        s = self.newslot("c0")
        self.dma("sp", self.K, self.kpack, s, [], [self.Kr])
        s = self.newslot("c1")
        self.dma("sp", self.C, self.cpack, s, [], [self.Cr])
        s = self.newslot("c2")
        self.dma("sp", self.G2, self.gnorm2, s, [], [self.G2r])
        s = self.newslot("c3")
        r0 = Reg()
        self.dma("sp", self.waug_f, self.waug, s, [], [r0])
        self.cp("dve", self.waug_b, self.waug_f, [r0], [self.waugr])
        self.P.op("pool", lambda e: e.memset(self.ones, 1.0), [], [self.onesr])
        self.epsv = A.alloc([8], F32)
        self.epsr = Reg()
        self.P.op("pool", lambda e: e.memset(self.epsv[:, 0:1], EPS), [], [self.epsr])
        self.P.op("pool", lambda e: e.memset(self.epsv[:, 1:2], 1.0), [], [self.epsr])
        self.conv_done = set()
        for tab, kc in ((self.cosB, K_COS), (self.sinB, K_SIN)):
            src = self.K[64:128, kc + 32:kc + 96].unsqueeze(1).broadcast_to([64, 8, 64])
            dst = tab[64:128, :].rearrange("p (r c) -> p r c", c=64)
            self.P.op("pool", (lambda e, dst=dst, src=src: e.tensor_copy(dst, src)), [self.Kr], [self.csr])
        self.nring = 4
        self.ring = [A.alloc([4096], BF16) for _ in range(self.nring)]
        self.ring_r = regs(self.nring)
        self.ring_s = [self.newslot("ring%d" % i) for i in range(self.nring)]
        self.make_sched()

    def issue_conv(self, names, after=()):
        for sname in names:
            if sname in self.conv_done:
                continue
            self.conv_done.add(sname)
            ng, L = SEG_SHAPE[sname]
            r_lo = self.soff[sname] // 2048
            r_hi = (self.soff[sname] + ng * 128 * L) // 2048
            s = self.newslot("cv_" + sname)
            src = self.wall[r_lo:r_hi, :]
            dst = self.wbf[r_lo:r_hi, :]
            self.dma("pool", dst, src, s, list(after), [self.wbf_reg[sname]])

    def make_sched(self):
        L = []
        for s in range(self.nseq):
            L.append(("z", 0))
            L += [("wba", 0), ("ga", 0), ("wba", 1), ("ga", 1)]
            L += [("akv", 0)] + [("xwkv", i) for i in range(4)]
            for tb in range(4):
                L += [("aq", 0), ("aq", 1), ("wbb", 0), ("gb", 0), ("wbb", 1), ("gb", 1), ("wo", 0), ("wo", 1),
                      ("xwq", 0), ("xwq", 1), ("xwo", 0), ("xwo", 1)] + [("wi", i) for i in range(11)]
        self.wsched = L
        self.wptr = 0
        self.wissued = 0

    def wissue_upto(self, n):
        n = min(n, len(self.wsched))
        while self.wissued < n:
            k = self.wissued
            seg, g = self.wsched[k]
            i = k % self.nring
            src, L = self.wsrc(seg, g)
            assert seg in self.conv_done, seg
            self.dma("sp", self.ring[i][:, 0:L], src, self.ring_s[i], [self.wbf_reg[seg]], [self.ring_r[i]])
            self.wissued += 1

    def wget(self, seg, g, prefetch_only=False):
        if prefetch_only:
            return None
        if self.upto != "all":
            while self.wsched[self.wptr] != (seg, g):
                self.wptr += 1
            self.wissued = max(self.wissued, self.wptr)
        k = self.wptr
        assert self.wsched[k] == (seg, g), (k, self.wsched[k], seg, g)
        self.wissue_upto(k + self.nring - 1)
        self.wptr += 1
        i = k % self.nring
        return self.ring[i], self.ring_r[i]

    def rope_top(self, tb):
        for tab, kc in ((self.cosB, K_COS), (self.sinB, K_SIN)):
            src = self.K[0:64, kc + tb * 8:kc + tb * 8 + 8].unsqueeze(2).broadcast_to([64, 8, 64])
            dst = tab[0:64, :].rearrange("p (r c) -> p r c", c=64)
            self.P.op("pool", (lambda e, dst=dst, src=src: e.tensor_copy(dst, src)), [self.Kr], [self.csr])

    def norm_tile(self, t, xb=None, xbr=None):
        sc = self.sc
        if xb is None:
            xb, xbr = self.xb, self.xbr
        ss, nsr, xn, xnr = sc["ss"], sc["nsr"], sc["xn"], sc["xnr"]
        self.act(sc["junk"], xb[:, t, :], AF.Square, [xbr[t]], [sc["junkr"], nsr[t][0]], accum=ss[:, t:t + 1])
        self.act(ss[:, 4 + t:5 + t], ss[:, t:t + 1], AF.Ln, [nsr[t][0], self.epsr], [nsr[t][1]], scale=1.0 / D,
                 bias=self.epsv[:, 0:1])
        self.act(ss[:, 8 + t:9 + t], ss[:, 4 + t:5 + t], AF.Exp, [nsr[t][1]], [nsr[t][2]], scale=-0.5)
        self.ts("dve", xn[:, t, :], xb[:, t, :], ss[:, 8 + t:9 + t], ALU.mult, [xbr[t], nsr[t][2]], [xnr[t]])

    def norm_fin(self, nt, gcol, dst, dst_regs, banks):
        xn, xnr = self.sc["xn"], self.sc["xnr"]
        for c in range(8):
            b = banks[c % len(banks)]
            for t in range(nt):
                self.tr(self.bank(b, t * 128, (t + 1) * 128), xn[:, t, c * 128:(c + 1) * 128], [xnr[t]], [self.psr[b]])
            g = self.C[:, gcol + c:gcol + c + 1]
            if c % 2 == 0:
                self.P.op("act", (lambda e, o=dst(c), i=self.bank(b, 0, nt * 128), g=g:
                                  e.activation(o, i, AF.Copy, scale=g)), [self.psr[b], self.Cr], [dst_regs[c]])
            else:
                self.ts("dve", dst(c), self.bank(b, 0, nt * 128), g, ALU.mult, [self.psr[b], self.Cr], [dst_regs[c]])

    def norm_T(self, xb, xbr, nt, gcol, dst, dst_regs, scratch, banks):
        for t in range(nt):
            self.norm_tile(t)
        self.norm_fin(nt, gcol, dst, dst_regs, banks)

    def eps_ap(self):
        return self.epsv[:, 0:1]

    def pbank(self, b, lo=0, hi=512, p0=0, p1=128):
        return self.PS[b // 2][p0:p1, (b % 2) * 512 + lo:(b % 2) * 512 + hi]

    def alloc_xb(self, nt=4, junk=None):
        A = self.A
        self.xb = A.alloc([nt, D], F32)
        self.xbr = regs(nt)
        if junk is None:
            junk = (A.alloc([D], BF16), Reg())
        self.sc = {"junk": junk[0], "junkr": junk[1], "ss": self.ssP, "nsr": [regs(3) for _ in range(4)],
                   "pnr": [regs(3) for _ in range(4)],
                   "xn": A.alloc([nt, D], F32), "xnr": regs(nt)}

    def load_xb(self, src_rows, nt=4, extra_w=()):
        for t in range(nt):
            self.dma("sp", self.xb[:, t, :], src_rows[t * 128:(t + 1) * 128, :], self.xb_slot[t], [],
                     [self.xbr[t]] + list(extra_w))

    def dbg_store(self, name, src_ap, src_regs, f32tmp=None):
        if name not in self.dbg_out:
            return
        dst = self.dbg_out[name]
        s = self.newslot("dbg_" + name, output=True)
        if f32tmp is not None:
            r = Reg()
            self.cp("dve", f32tmp, src_ap, src_regs, [r])
            self.dma("sp", dst, f32tmp, s, [r], [])
        else:
            self.dma("sp", dst, src_ap, s, src_regs, [])

    def alloc_nr(self):
        A = self.A
        self.nr = {"sqb": A.alloc([512], BF16), "tmpA": A.alloc([512], F32), "knf": A.alloc([512], F32),
                   "t2": A.alloc([512], F32), "r": regs(4), "knf2": A.alloc([512], F32), "r_kn2": Reg()}

    def normrope_pipe(self, items, pbanks, bs, br):
        nr = self.nr
        sqb, tmpA, t2 = nr["sqb"], nr["tmpA"], nr["t2"]
        knfs = [nr["knf"], nr["knf2"]]
        r_sq, r_tmp, r_kn0, r_t2 = nr["r"]
        r_kns = [r_kn0, nr["r_kn2"]]
        psr = self.psr
        n = len(items)
        nb = len(pbanks)
        assert nb >= 3

        def stA(i):
            items[i][0](pbanks[i % nb])

        def stB(i):
            b = pbanks[i % nb]
            src, src_reg = self.bank(b), psr[b]
            gcol = items[i][1]
            knf, r_kn = knfs[i % 2], r_kns[i % 2]
            self.act(sqb, src, AF.Square, [src_reg], [r_sq])
            self.mm(self.bank(bs), self.ones, sqb, True, True, [self.onesr, r_sq], [psr[bs]])
            self.act(tmpA, self.bank(bs), AF.Ln, [psr[bs], self.epsr], [r_tmp], scale=1.0 / 128, bias=self.epsv[:, 0:1])
            self.act(tmpA, tmpA, AF.Exp, [r_tmp], [r_tmp], scale=-0.5)
            self.stt(knf, src, self.C[:, gcol:gcol + 1], tmpA, ALU.mult, ALU.mult, [src_reg, self.Cr, r_tmp], [r_kn])

        def stC(i):
            _, gcol, out, out_regs, pre = items[i]
            knf, r_kn = knfs[i % 2], r_kns[i % 2]
            if pre is not None:
                pre()
            self.mm(self.bank(br), self.K[:, K_ROT:K_ROT + 128], knf, True, True, [self.Kr, r_kn], [psr[br]])
            self.tt("dve", t2, self.bank(br), self.sinB, ALU.mult, [psr[br], self.csr], [r_t2])
            self.tt("pool", knf, knf, self.cosB, ALU.mult, [r_kn, self.csr], [r_kn])
            self.tt("pool", out, knf, t2, ALU.add, [r_kn, r_t2], out_regs)

        for step in range(n + 2):
            if step < n:
                stA(step)
            if 0 <= step - 1 < n:
                stB(step - 1)
            if 0 <= step - 2 < n:
                stC(step - 2)

    def postnorm(self, k, t, ssr_i):
        sc = self.sc
        ss = sc["ss"]
        r0, r1, r2 = sc["pnr"][t]
        u = self.PS[k][:, :]
        pr = [self.psr[2 * k], self.psr[2 * k + 1]]
        self.act(sc["junk"], u, AF.Square, pr, [sc["junkr"], r0], accum=ss[:, 16 + t:17 + t])
        self.act(ss[:, 20 + t:21 + t], ss[:, 16 + t:17 + t], AF.Ln, [r0, self.epsr], [r1], scale=1.0 / D,
                 bias=self.epsv[:, 0:1])
        self.act(ss[:, 24 + t:25 + t], ss[:, 20 + t:21 + t], AF.Exp, [r1], [r2], scale=-0.5)
        xn, xnr = sc["xn"], sc["xnr"]
        self.stt(xn[:, t, :], u, ss[:, 24 + t:25 + t], self.gp, ALU.mult, ALU.mult, pr + [r2, self.gpr], [xnr[t]])
        self.tt("dve", self.xb[:, t, :], self.xb[:, t, :], xn[:, t, :], ALU.add, [self.xbr[t], xnr[t]], [self.xbr[t]])

    def load_gp(self, idx):
        self.dma("sp", self.gp, self.gpost[idx], self.gp_slot, [], [self.gpr])

    def phaseA(self, s):
        for tb in range(4):
            if s == 0 and tb == 0:
                start = Reg()
                self.load_xb(self.x[s, 0:512, :], extra_w=[start])
                self.issue_conv(["z", "gla0"], after=[start])
            else:
                self.load_xb(self.x[s, tb * 512:(tb + 1) * 512, :])
            self.norm_T(self.xb, self.xbr, 4, C_MIXPRE,
                        lambda c, tb=tb: self.hT[:, c, tb * 512:(tb + 1) * 512],
                        [self.hTr[c][tb] for c in range(8)], self.sc, [0, 1])

    def phaseGLA(self, s):
        A, P, psr, bank, pbank = self.A, self.P, self.psr, self.bank, self.pbank
        hT, hTr = self.hT, self.hTr
        Kc = self.K
        zT = [A.alloc([T], BF16, parts=17) for _ in range(2)]
        zTr = [regs(4), regs(4)]
        for d in range(2):
            P.op("pool", (lambda e, z=zT[d]: e.memset(z, 1.0)), [], zTr[d])
        wz, wzr = self.wget("z", 0)
        wz3 = wz[:, 0:256].rearrange("p (c n) -> p c n", n=32)
        for tb in range(4):
            blk = slice(tb * 512, (tb + 1) * 512)
            for d in range(2):
                b = d
                for c in range(8):
                    self.mm(pbank(b, 0, 512, 0, 16), wz3[:, c, d * 16:(d + 1) * 16], hT[:, c, blk], c == 0, c == 7,
                            [wzr, hTr[c][tb]], [psr[b]])
                self.cp("dve", zT[d][0:16, blk], pbank(b, 0, 512, 0, 16), [psr[b]], [zTr[d][tb]])
        wgla = A.alloc([8 * 1536], BF16)
        wglar = Reg()
        wg3 = wgla.rearrange("p (c n) -> p c n", n=1536)
        Sb = A.alloc([16, 512], BF16)
        Sbr = regs(16)
        S32 = A.alloc([512], F32)
        S32r = regs(2)
        Sfbf = A.alloc([512], BF16)
        Sfbfr = Reg()

        def dbl(shape, dt):
            return [A.alloc(shape, dt) for _ in range(2)], regs(2)
        def sgl(shape, dt):
            a, r = A.alloc(shape, dt), Reg()
            return [a, a], [r, r]
        sp, spr = sgl([512], F32)
        E1, E1r = dbl([512], F32)
        E2, E2r = sgl([512], F32)
        E3, E3r = sgl([256], F32)
        vst = A.alloc([16, 512], BF16)
        vstr = regs(16)
        qgf, qgfr = dbl([256], BF16)
        qgb, qgbr = dbl([256], BF16)
        kgf, kgfr = dbl([256], BF16)
        kgb, kgbr = dbl([256], BF16)
        kend, kendr = dbl([256], BF16)
        gr, grr = dbl([512], F32)
        og, ogr = dbl([512], F32)
        decb, decbr = dbl([8], F32)
        AT = A.alloc([512], BF16)
        ATr = Reg()
        ss2 = A.alloc([8], F32)
        ss2r = regs(3)
        junk = A.alloc([256], BF16)
        junkr = Reg()
        one_ap = self.epsv[:, 1:2]
        SC = float(128.0 ** -0.5)
        for p in range(2):
            src, L = self.wsrc("gla%d" % p, 0)
            self.dma("sp", wgla, src, self.wgla_slot, [self.wbf_reg["gla%d" % p]], [wglar])
            self.issue_conv(["aq", "wbb", "gb", "wo", "xwq", "xwo"] if p == 0 else ["wi", "wo2"])
            fo = p * 256
            bo = 512 + p * 256

            def b_s1(tt):
                q = tt % 2
                tk = slice(tt * 128, (tt + 1) * 128)
                tb = tt // 4
                hr = lambda c: hTr[c][tb]
                self.mm(bank(0, 0, 256), zT[1][0:17, tk], self.waug_b[0:17, bo:bo + 256], True, True,
                        [zTr[1][tb], self.waugr], [psr[0]])
                self.act(sp[q][:, 0:256], bank(0, 0, 256), AF.Exp, [psr[0]], [spr[q]], scale=-1.0)
                self.act(sp[q][:, 0:256], sp[q][:, 0:256], AF.Ln, [spr[q], self.epsr], [spr[q]], bias=one_ap)
                for c in range(8):
                    self.mm(bank(3), hT[:, c, tk], wg3[:, c, 512:1024], c == 0, c == 7, [hr(c), wglar], [psr[3]])
                for c in range(8):
                    self.mm(bank(2, 0, 256), hT[:, c, tk], wg3[:, c, 256:512], c == 0, c == 7, [hr(c), wglar], [psr[2]])
                for h in range(2):
                    self.mm(bank(1, h * 128, (h + 1) * 128), sp[q][:, h * 128:(h + 1) * 128], Kc[:, K_UB:K_UB + 128],
                            True, True, [spr[q], self.Kr], [psr[1]])
                self.mm(bank(1, 256, 512), Kc[:, K_WB:K_WB + 128], sp[q][:, 0:256], True, True, [spr[q], self.Kr], [psr[1]])
                self.cp("act", vst[:, tt, :], bank(3), [psr[3]], [vstr[tt]])
                self.act(decb[q][:, 0:2], bank(1, 0, 256).rearrange("p (h i) -> p h i", i=128)[:, :, 0], AF.Exp,
                         [psr[1]], [decbr[q]])
                self.act(E3[q], bank(1, 256, 512), AF.Exp, [psr[1]], [E3r[q]])
                self.tt("dve", kend[q], bank(2, 0, 256), E3[q], ALU.mult, [psr[2], E3r[q]], [kendr[q]])

            def b_s2(tt):
                q = tt % 2
                for h in range(2):
                    self.mm(bank(7, h * 256, (h + 1) * 256), kend[q][:, h * 128:(h + 1) * 128],
                            vst[:, tt, h * 256:(h + 1) * 256], True, True, [kendr[q], vstr[tt]], [psr[7]])
                for h in range(2):
                    hs = slice(h * 256, (h + 1) * 256)
                    self.stt(S32[:, hs], S32[:, hs], decb[q][:, h:h + 1], bank(7, h * 256, (h + 1) * 256), ALU.mult, ALU.add,
                             [S32r[h], decbr[q], psr[7]], [S32r[h]])
                if tt > 0:
                    self.cp("pool", Sb[:, tt - 1, :], S32, S32r, [Sbr[tt - 1]])

            P.op("pool", (lambda e: e.memset(S32, 0.0)), [], S32r)
            b_s1(15)
            for tt in range(15, -1, -1):
                if tt > 0:
                    b_s1(tt - 1)
                b_s2(tt)

            def f_s1(tt, mid_hook=None, pending=None):
                q = tt % 2
                tk = slice(tt * 128, (tt + 1) * 128)
                tb = tt // 4
                hr = lambda c: hTr[c][tb]
                self.mm(bank(0, 0, 256), zT[0][0:17, tk], self.waug_b[0:17, fo:fo + 256], True, True,
                        [zTr[0][tb], self.waugr], [psr[0]])
                self.mm(bank(0, 256, 512), zT[1][0:17, tk], self.waug_b[0:17, bo:bo + 256], True, True,
                        [zTr[1][tb], self.waugr], [psr[0]])
                self.act(sp[q], bank(0), AF.Exp, [psr[0]], [spr[q]], scale=-1.0)
                self.act(sp[q], sp[q], AF.Ln, [spr[q], self.epsr], [spr[q]], bias=one_ap)
                if pending is not None:
                    pending()
                for c in range(8):
                    self.mm(bank(2, 256, 512), hT[:, c, tk], wg3[:, c, 256:512], c == 0, c == 7, [hr(c), wglar], [psr[2]])
                if mid_hook is not None:
                    mid_hook()
                for d in range(2):
                    ku = K_UF if d == 0 else K_UB
                    for h in range(2):
                        o = d * 256 + h * 128
                        self.mm(bank(1, o, o + 128), sp[q][:, o:o + 128], Kc[:, ku:ku + 128], True, True,
                                [spr[q], self.Kr], [psr[1]])
                self.mm(bank(2, 0, 256), Kc[:, K_WF:K_WF + 128], sp[q][:, 0:256], True, True, [spr[q], self.Kr], [psr[2]])
                self.act(E1[q], bank(1), AF.Exp, [psr[1]], [E1r[q]])
                self.act(E2[q], bank(1), AF.Exp, [psr[1]], [E2r[q]], scale=-1.0)
                self.act(E3[q], bank(2, 0, 256), AF.Exp, [psr[2]], [E3r[q]])
                for j in range(4):
                    for c in range(8):
                        self.mm(bank(0, j * 128, (j + 1) * 128), wg3[:, c, j * 128:(j + 1) * 128], hT[:, c, tk],
                                c == 0, c == 7, [hr(c), wglar], [psr[0]])
                self.tt("dve", kend[q], bank(2, 256, 512), E3[q], ALU.mult, [psr[2], E3r[q]], [kendr[q]])
                self.stt(qgf[q], bank(0, 0, 256), SC, E1[q][:, 0:256], ALU.mult, ALU.mult, [psr[0], E1r[q]], [qgfr[q]])
                self.stt(qgb[q], bank(0, 0, 256), SC, E1[q][:, 256:512], ALU.mult, ALU.mult, [psr[0], E1r[q]], [qgbr[q]])
                self.tt("dve", kgf[q], bank(0, 256, 512), E2[q][:, 0:256], ALU.mult, [psr[0], E2r[q]], [kgfr[q]])
                self.tt("dve", kgb[q], bank(0, 256, 512), E2[q][:, 256:512], ALU.mult, [psr[0], E2r[q]], [kgbr[q]])

            def f_R(tt):
                tk = slice(tt * 128, (tt + 1) * 128)
                tb = tt // 4
                for c in range(8):
                    self.mm(bank(4), hT[:, c, tk], wg3[:, c, 1024:1536], c == 0, c == 7, [hTr[c][tb], wglar], [psr[4]])

            def f_gr(tt):
                q = tt % 2
                self.act(gr[q], bank(4), AF.Exp, [psr[4]], [grr[q]], scale=-1.0)
                self.act(gr[q], gr[q], AF.Ln, [grr[q], self.epsr], [grr[q]], bias=one_ap)
                self.act(gr[q], gr[q], AF.Exp, [grr[q]], [grr[q]], scale=-1.0)
                self.tt("dve", gr[q], bank(4), gr[q], ALU.mult, [psr[4], grr[q]], [grr[q]])
                self.tt("pool", gr[q], gr[q], self.G2, ALU.mult, [grr[q], self.G2r], [grr[q]])

            def f_s2a(tt):
                q = tt % 2
                kg = (kgf[q], kgb[q])
                qg = (qgf[q], qgb[q])
                kgr = (kgfr[q], kgbr[q])
                qgr = (qgfr[q], qgbr[q])
                for d in range(2):
                    for h in range(2):
                        o = (d * 2 + h) * 128
                        self.mm(bank(5, o, o + 128), kg[d][:, h * 128:(h + 1) * 128], qg[d][:, h * 128:(h + 1) * 128],
                                True, True, [kgr[d], qgr[d]], [psr[5]])
                self.tt("dve", AT, bank(5), Kc[:, K_MASK:K_MASK + 512], ALU.mult, [psr[5], self.Kr], [ATr])

            def f_s2(tt):
                q = tt % 2
                for h in range(2):
                    self.mm(bank(7, h * 256, (h + 1) * 256), kend[q][:, h * 128:(h + 1) * 128],
                            vst[:, tt, h * 256:(h + 1) * 256], True, True, [kendr[q], vstr[tt]], [psr[7]])
                for h in range(2):
                    vh = vst[:, tt, h * 256:(h + 1) * 256]
                    seq = []
                    if tt > 0:
                        seq.append((qgf[q][:, h * 128:(h + 1) * 128], Sfbf[:, h * 256:(h + 1) * 256], [qgfr[q], Sfbfr]))
                    if tt < 15:
                        seq.append((qgb[q][:, h * 128:(h + 1) * 128], Sb[:, tt, h * 256:(h + 1) * 256], [qgbr[q], Sbr[tt]]))
                    seq += [(AT[:, h * 128:(h + 1) * 128], vh, [ATr, vstr[tt]]),
                            (AT[:, (2 + h) * 128:(3 + h) * 128], vh, [ATr, vstr[tt]])]
                    for i, (l, r, rd) in enumerate(seq):
                        self.mm(bank(6, h * 256, (h + 1) * 256), l, r, i == 0, i == len(seq) - 1, rd, [psr[6]])
                for h in range(2):
                    hs = slice(h * 256, (h + 1) * 256)
                    self.stt(S32[:, hs], S32[:, hs], E1[q][:, h * 128 + 127:h * 128 + 128], bank(7, h * 256, (h + 1) * 256),
                             ALU.mult, ALU.add, [S32r[h], E1r[q], psr[7]], [S32r[h]])
                self.cp("pool", Sfbf, S32, S32r, [Sfbfr])
                for h in range(2):
                    self.act(junk[:, 0:256], bank(6, h * 256, (h + 1) * 256), AF.Square, [psr[6]], [junkr, ss2r[0]],
                             accum=ss2[:, h:h + 1])
                self.act(ss2[:, 2:4], ss2[:, 0:2], AF.Ln, [ss2r[0], self.epsr], [ss2r[1]], scale=1.0 / 256,
                         bias=self.epsv[:, 0:1])
                self.act(ss2[:, 4:6], ss2[:, 2:4], AF.Exp, [ss2r[1]], [ss2r[2]], scale=-0.5)
                for h in range(2):
                    hs = slice(h * 256, (h + 1) * 256)
                    self.stt(og[q][:, hs], bank(6, h * 256, (h + 1) * 256), ss2[:, 4 + h:5 + h], gr[q][:, hs],
                             ALU.mult, ALU.mult, [psr[6], ss2r[2], grr[q]], [ogr[q]])

            def f_s3(tt):
                q = tt % 2
                tk = slice(tt * 128, (tt + 1) * 128)
                for e4 in range(4):
                    self.tr(bank(7, e4 * 128, (e4 + 1) * 128), og[q][:, e4 * 128:(e4 + 1) * 128], [ogr[q]], [psr[7]])
                self.cp("act", self.glaT[:, p * 4:(p + 1) * 4, tk], bank(7).rearrange("p (e t) -> p e t", t=128),
                        [psr[7]], [self.glaTr[c][tt] for c in range(p * 4, p * 4 + 4)])

            P.op("pool", (lambda e: e.memset(S32, 0.0)), [], S32r)
            f_s1(0)
            f_R(0)
            f_gr(0)
            for tt in range(16):
                f_s2a(tt)
                if tt + 1 < 16:
                    f_s1(tt + 1, (lambda tt=tt: f_s3(tt - 1)) if tt > 0 else None,
                         (lambda tt=tt: f_gr(tt)) if tt > 0 else None)
                    f_s2(tt)
                    f_R(tt + 1)
                else:
                    f_gr(tt)
                    f_s2(tt)
                    f_s3(tt - 1)
            f_s3(15)

    def phaseS4(self, s):
        A, psr, bank = self.A, self.psr, self.bank
        sig = [A.alloc([512], F32) for _ in range(2)]
        sigr = regs(2)
        k = 0
        for grp in range(2):
            wb, wbr = self.wget("wba", grp)
            wg, wgr = self.wget("ga", grp)
            wb3 = wb.rearrange("p (c n) -> p c n", n=512)
            wg3 = wg.rearrange("p (c n) -> p c n", n=512)
            for tb in range(4):
                blk = slice(tb * 512, (tb + 1) * 512)
                for fl in range(4):
                    fc = grp * 4 + fl
                    b0, b1 = (0, 1) if k % 2 == 0 else (2, 3)
                    for c in range(8):
                        self.mm(bank(b0), wb3[:, c, fl * 128:(fl + 1) * 128], self.glaT[:, c, blk], c == 0, c == 7,
                                [wbr] + self.glaTr[c][tb * 4:(tb + 1) * 4], [psr[b0]])
                    for c in range(8):
                        self.mm(bank(b1), wg3[:, c, fl * 128:(fl + 1) * 128], self.hT[:, c, blk], c == 0, c == 7,
                                [wgr, self.hTr[c][tb]], [psr[b1]])
                    self.act(sig[k % 2], bank(b1), AF.Sigmoid, [psr[b1]], [sigr[k % 2]])
                    self.tt("dve", self.maT[:, fc, blk], bank(b0), sig[k % 2], ALU.mult, [psr[b0], sigr[k % 2]],
                            [self.maTr[fc][tb]])
                    k += 1

    def phaseS2(self, s):
        A, psr, bank = self.A, self.psr, self.bank
        hT, hTr = self.hT, self.hTr
        self.alloc_nr()
        wkv, wkvr = self.wget("akv", 0)
        w3 = wkv.rearrange("p (c n) -> p c n", n=512)
        items = []
        for tb in range(4):
            blk = slice(tb * 512, (tb + 1) * 512)
            for g in range(2):
                def proj(b, g=g, blk=blk, tb=tb):
                    for c in range(8):
                        self.mm(bank(b), w3[:, c, g * 128:(g + 1) * 128], hT[:, c, blk], c == 0, c == 7,
                                [wkvr, hTr[c][tb]], [psr[b]])
                pre = (lambda tb=tb: self.rope_top(tb)) if g == 0 else None
                items.append((proj, C_KN, self.kT[:, g, blk], [self.kTr[g][tb]], pre))
        self.normrope_pipe(items, [2, 3, 4], 5, 6)
        for tb in range(4):
            for t in range(4):
                tile = tb * 4 + t
                for c in range(8):
                    self.mm(bank(7, 0, 256), hT[:, c, tile * 128:(tile + 1) * 128], w3[:, c, 256:512], c == 0, c == 7,
                            [wkvr, hTr[c][tb]], [psr[7]])
                self.cp("act", self.vatt[:, tile, :], bank(7, 0, 256), [psr[7]], [self.vattr[tile]])
        self.alloc_xb(2, junk=(self.nr["t2"].bitcast(BF16), self.nr["r"][3]))
        mT = A.alloc([8, 256], BF16)
        mTr = regs(8)
        self.load_xb(self.mem[s], 2)
        self.norm_T(self.xb, self.xbr, 2, C_MEM, lambda c: mT[:, c, :], mTr, self.sc, [0, 1])
        for grp in range(2):
            w, wr = self.wget("xwkv", grp)
            w3 = w.rearrange("p (c n) -> p c n", n=512)
            for fl in range(4):
                kc = grp * 4 + fl
                b = 2 + (kc % 2)
                for c in range(8):
                    self.mm(bank(b, 0, 256), w3[:, c, fl * 128:(fl + 1) * 128], mT[:, c, :], c == 0, c == 7,
                            [wr, mTr[c]], [psr[b]])
                self.cp("act" if kc % 2 == 0 else "dve", self.kmT[:, kc, :], bank(b, 0, 256), [psr[b]], [self.kmTr[kc]])
        for grp in range(2):
            w, wr = self.wget("xwkv", 2 + grp)
            w3 = w.rearrange("p (c n) -> p c n", n=512)
            for mt in range(2):
                b = 2 + mt
                for c in range(8):
                    self.mm(bank(b), mT[:, c, mt * 128:(mt + 1) * 128], w3[:, c, :], c == 0, c == 7, [wr, mTr[c]], [psr[b]])
                self.cp("act" if mt == 0 else "dve", self.vm[:, mt, grp * 512:(grp + 1) * 512], bank(b), [psr[b]],
                        [self.vmr[mt]])

    def alloc_S5(self):
        A = self.A
        self.sg = [A.alloc([512], F32) for _ in range(2)]
        self.sgr = regs(2)
        self.alloc_xb(4, junk=(self.sg[1].bitcast(BF16), self.sgr[1]))
        self.xb2 = [(self.xb, self.xbr), (A.alloc([4, D], F32), regs(4))]
        self.bufA = A.alloc([8, 512], BF16)
        self.bufB = A.alloc([8, 512], BF16)
        self.bufAr, self.bufBr = regs(8), regs(8)
        self.actT = A.alloc([22, 512], BF16)
        r = regs(22)
        for j in (13, 15, 17, 19):
            r[j + 1] = r[j]
        self.actTr = r
        self.bufC = self.actT[:, 0:8, :]
        self.bufCr = r[0:8]
        self.PT = [self.actT[:, 8 + i, :] for i in range(4)]
        self.PTr = [r[8 + i] for i in range(4)]

        def f32v(j):
            return self.actT[:, j:j + 2, :].rearrange("p a b -> p (a b)").bitcast(F32)
        self.nr = {"sqb": self.actT[:, 12, :], "tmpA": f32v(13), "knf": f32v(15), "t2": f32v(17),
                   "r": [r[12], r[13], r[15], r[17]], "knf2": f32v(19), "r_kn2": r[19]}
        self.rden = self.nr["tmpA"]
        self.rdenr = self.nr["r"][1]
        self.woring = [A.alloc([1024], BF16) for _ in range(3)]
        self.woring_r = regs(3)

    def wo2issue_upto(self, n):
        n = min(n, (self.cur_seq + 1) * 4 * 22)
        while self.wo2_issued < n:
            k = self.wo2_issued
            i = k % 3
            src, L = self.wsrc("wo2", k % 22)
            self.dma("sp", self.woring[i], src, self.woring_s[i], [self.wbf_reg["wo2"]], [self.woring_r[i]])
            self.wo2_issued += 1

    def wo2get(self, g):
        k = self.wo2_i
        assert k % 22 == g
        self.wo2issue_upto(k + 2)
        self.wo2_i += 1
        i = k % 3
        return self.woring[i].rearrange("p (k n) -> p k n", n=512), self.woring_r[i]

    def proj_res(self, srcs, wseg, gain_idx, next_norm=True):
        psr, bank = self.psr, self.bank
        self.load_gp(gain_idx)
        w0, w0r = self.wget(wseg, 0)
        w1, w1r = self.wget(wseg, 1)
        ws = [(w0.rearrange("p (c n) -> p c n", n=512), w0r), (w1.rearrange("p (c n) -> p c n", n=512), w1r)]
        for t in range(4):
            k = self.pn_k
            self.pn_k = (k + 1) % 2
            for half in range(2):
                w3, wr = ws[half]
                n = len(srcs) * 8
                i = 0
                for (apf, rf) in srcs:
                    for c in range(8):
                        self.mm(bank(2 * k + half), apf(c, t), w3[:, c, :], i == 0, i == n - 1, [wr] + rf(c, t),
                                [psr[2 * k + half]])
                        i += 1
            self.postnorm(k, t, 0)
            if next_norm and t >= 1:
                self.norm_tile(t - 1)
        if next_norm:
            self.norm_tile(3)

    def phaseS5(self, s, tb):
        psr, bank = self.psr, self.bank
        pb = tb % 2
        bufs = [(self.bufA, self.bufAr), (self.bufB, self.bufBr)]
        bufA, bufAr = bufs[pb]
        bufB, bufBr = bufs[1 - pb]
        bufC, bufCr = self.bufC, self.bufCr
        self.xb, self.xbr = self.xb2[pb]
        blk = slice(tb * 512, (tb + 1) * 512)
        if tb == 0:
            self.load_xb(self.x[s, blk, :])
            for t in range(4):
                self.norm_tile(t)
            self.norm_fin(4, C_MIXPRE, lambda c: bufA[:, c, :], bufAr, [0, 1])
        self.rope_top(tb)
        items = []
        wq = [self.wget("aq", grp) for grp in range(2)]
        for head in range(8):
            w, wr = wq[head // 4]
            w3 = w.rearrange("p (c n) -> p c n", n=512)
            hl = head % 4

            def proj(b, w3=w3, wr=wr, hl=hl):
                for c in range(8):
                    self.mm(bank(b), w3[:, c, hl * 128:(hl + 1) * 128], bufA[:, c, :], c == 0, c == 7,
                            [wr, bufAr[c]], [psr[b]])
            items.append((proj, C_QN, bufB[:, head, :], [bufBr[head]], None))
        self.normrope_pipe(items, [2, 5, 6], 3, 4)
        SCL = float(128.0 ** -0.5)
        sbanks = [2, 3, 0, 1]
        LA = 3
        ptl = list(zip(self.PT, self.PTr)) + [(self.actT[:, 15, :], self.actTr[15]), (self.actT[:, 17, :], self.actTr[17])]
        NP = len(ptl)
        seq = [(head, st) for head in range(8) for st in range(16)]

        aT, ar = self.actT, self.actTr
        S1 = [(aT[:, 12, :], ar[12]), (aT[:, 21, :], ar[21]), (aT[:, 19, :], ar[19])]
        S2 = [(aT[:, 15, :], ar[15]), (aT[:, 17, :], ar[17])]

        def score(j):
            head, st = seq[j]
            g = head // 4
            sb = sbanks[j % 4]
            self.mm(bank(sb), self.kT[:, g, st * 128:(st + 1) * 128], bufB[:, head, :], True, True,
                    [self.kTr[g][st // 4], bufBr[head]], [psr[sb]])
            self.act(ptl[j % NP][0], bank(sb), AF.Exp, [psr[sb]], [ptl[j % NP][1]], scale=SCL)
            if st % 2 == 1:
                s1, s1r = S1[(j // 2) % 3]
                self.tt("dve", s1, ptl[(j - 1) % NP][0], ptl[j % NP][0], ALU.add,
                        [ptl[(j - 1) % NP][1], ptl[j % NP][1]], [s1r])

        for j0 in range(LA):
            score(j0)
        for j in range(len(seq)):
            if j + LA < len(seq):
                score(j + LA)
            head, st = seq[j]
            g = head // 4
            ob = 4 + 2 * (head % 2)
            db = ob + 1
            pt, ptr = ptl[j % NP]
            self.mm(bank(ob), self.vatt[:, st, g * 128:(g + 1) * 128], pt, st == 0, st == 15,
                    [self.vattr[st], ptr], [psr[ob]])
            if st % 2 == 1:
                s1, s1r = S1[(j // 2) % 3]
                self.mm(bank(db), self.ones, s1, st == 1, st == 15, [self.onesr, s1r], [psr[db]])
            if st == 15:
                self.act(self.rden, bank(db), AF.Ln, [psr[db]], [self.rdenr])
                self.act(self.rden, self.rden, AF.Exp, [self.rdenr], [self.rdenr], scale=-1.0)
                self.tt("dve", bufC[:, head, :], bank(ob), self.rden, ALU.mult, [psr[ob], self.rdenr], [bufCr[head]])
        k = 0
        for grp in range(2):
            wb, wbr = self.wget("wbb", grp)
            wg, wgr = self.wget("gb", grp)
            wb3 = wb.rearrange("p (c n) -> p c n", n=512)
            wg3 = wg.rearrange("p (c n) -> p c n", n=512)
            for fl in range(4):
                fc = grp * 4 + fl
                b0, b1 = (0, 1) if k % 2 == 0 else (2, 3)
                for c in range(8):
                    self.mm(bank(b0), wb3[:, c, fl * 128:(fl + 1) * 128], bufC[:, c, :], c == 0, c == 7,
                            [wbr, bufCr[c]], [psr[b0]])
                for c in range(8):
                    self.mm(bank(b1), wg3[:, c, fl * 128:(fl + 1) * 128], bufA[:, c, :], c == 0, c == 7,
                            [wgr, bufAr[c]], [psr[b1]])
                self.act(self.sg[k % 2], bank(b1), AF.Sigmoid, [psr[b1]], [self.sgr[k % 2]])
                self.tt("dve", self.sg[k % 2], bank(b0), self.sg[k % 2], ALU.mult, [psr[b0], self.sgr[k % 2]], [self.sgr[k % 2]])
                self.tt("pool", bufB[:, fc, :], self.sg[k % 2], self.maT[:, fc, blk], ALU.add,
                        [self.sgr[k % 2], self.maTr[fc][tb]], [bufBr[fc]])
                k += 1
        srcs = [(lambda c, t: bufB[:, c, t * 128:(t + 1) * 128], lambda c, t: [bufBr[c]])]
        self.proj_res(srcs, "wo", 0)
        if self.upto == "x1":
            return
        self.norm_fin(4, C_XPRE, lambda c: bufA[:, c, :], bufAr, [0, 1])
        for grp in range(2):
            w, wr = self.wget("xwq", grp)
            w3 = w.rearrange("p (c n) -> p c n", n=512)
            for fl in range(4):
                fc = grp * 4 + fl
                b = 4 + (fc % 2)
                for c in range(8):
                    self.mm(bank(b), w3[:, c, fl * 128:(fl + 1) * 128], bufA[:, c, :], c == 0, c == 7,
                            [wr, bufAr[c]], [psr[b]])
                self.cp("act" if fc % 2 == 0 else "dve", bufC[:, fc, :], bank(b), [psr[b]], [bufCr[fc]])
        for head in range(4):
            for mt in range(2):
                sb = 6 + mt
                for dc in range(2):
                    self.mm(bank(sb), self.kmT[:, head * 2 + dc, mt * 128:(mt + 1) * 128], bufC[:, head * 2 + dc, :],
                            dc == 0, dc == 1, [self.kmTr[head * 2 + dc], bufCr[head * 2 + dc]], [psr[sb]])
                pi = (head % 2) * 2 + mt
                self.act(self.PT[pi], bank(sb), AF.Exp, [psr[sb]], [self.PTr[pi]], scale=1.0 / 16.0)
            base_b = 0 if head % 2 == 0 else 3
            for dc in range(2):
                for mt in range(2):
                    pi = (head % 2) * 2 + mt
                    self.mm(bank(base_b + dc), self.vm[:, mt, head * 256 + dc * 128:head * 256 + (dc + 1) * 128],
                            self.PT[pi], mt == 0, mt == 1, [self.vmr[mt], self.PTr[pi]], [psr[base_b + dc]])
            db = base_b + 2
            for mt in range(2):
                pi = (head % 2) * 2 + mt
                self.mm(bank(db), self.ones, self.PT[pi], mt == 0, mt == 1, [self.onesr, self.PTr[pi]], [psr[db]])
            self.act(self.rden, bank(db), AF.Ln, [psr[db]], [self.rdenr])
            self.act(self.rden, self.rden, AF.Exp, [self.rdenr], [self.rdenr], scale=-1.0)
            for dc in range(2):
                self.tt("dve", bufB[:, head * 2 + dc, :], bank(base_b + dc), self.rden, ALU.mult,
                        [psr[base_b + dc], self.rdenr], [bufBr[head * 2 + dc]])
        nxb, nxbr = self.xb2[1 - pb]
        if tb < 3 and self.upto == "all":
            nblk = self.x[s, (tb + 1) * 512:(tb + 2) * 512, :]
            for t in range(4):
                self.dma("sp", nxb[:, t, :], nblk[t * 128:(t + 1) * 128, :], self.xb_slot[t], [], [nxbr[t]])
        srcs = [(lambda c, t: bufB[:, c, t * 128:(t + 1) * 128], lambda c, t: [bufBr[c]])]
        self.proj_res(srcs, "xwo", 1)
        if self.upto == "x2":
            return
        self.norm_fin(4, C_FFNPRE, lambda c: bufA[:, c, :], bufAr, [0, 1])
        k = 0
        prefetch = tb < 3 and self.upto == "all"
        for i in range(11):
            w, wr = self.wget("wi", i)
            w3 = w.rearrange("p (c n) -> p c n", n=512)
            if prefetch and i == 1:
                for t in range(4):
                    self.norm_tile(t, nxb, nxbr)
            if prefetch and i == 5:
                self.norm_fin(4, C_MIXPRE, lambda c: bufB[:, c, :], bufBr, [0, 1])
            for j in range(2):
                bg, bu = (0, 1) if k % 2 == 0 else (2, 3)
                for c in range(8):
                    self.mm(bank(bg), w3[:, c, j * 128:(j + 1) * 128], bufA[:, c, :], c == 0, c == 7, [wr, bufAr[c]], [psr[bg]])
                for c in range(8):
                    self.mm(bank(bu), w3[:, c, 256 + j * 128:256 + (j + 1) * 128], bufA[:, c, :], c == 0, c == 7,
                            [wr, bufAr[c]], [psr[bu]])
                self.act(self.sg[k % 2], bank(bg), AF.Silu, [psr[bg]], [self.sgr[k % 2]])
                self.tt("dve", self.actT[:, 2 * i + j, :], bank(bu), self.sg[k % 2], ALU.mult, [psr[bu], self.sgr[k % 2]],
                        [self.actTr[2 * i + j]])
                k += 1
        self.load_gp(2)
        for hf in range(2):
            for i in range(11):
                w3, wr = self.wo2get(hf * 11 + i)
                for t in range(4):
                    for kk in range(2):
                        self.mm(bank(2 * t + hf), self.actT[:, 2 * i + kk, t * 128:(t + 1) * 128], w3[:, kk, :],
                                i == 0 and kk == 0, i == 10 and kk == 1, [wr, self.actTr[2 * i + kk]], [psr[2 * t + hf]])
        for t in range(4):
            self.postnorm(t, t, 0)
            self.dma("sp", self.y[s, tb * 512 + t * 128:tb * 512 + (t + 1) * 128, :], self.xb[:, t, :], self.y_slot[t],
                     [self.xbr[t]], [])

    def build(self):
        self.init_consts()
        A, P = self.A, self.P
        self.ssP = A.alloc([32], F32)
        self.xb_slot = [self.newslot("xb%d" % i) for i in range(4)]
        self.y_slot = [self.newslot("ystore%d" % i, output=True) for i in range(4)]
        self.wgla_slot = self.newslot("wgla")
        self.woring_s = [self.newslot("wo2r%d" % i) for i in range(4)]
        self.wo2_i = 0
        self.wo2_issued = 0
        self.pn_k = 0
        self.base = A.mark()
        o = A.nbytes - 56 * 1024
        self.qoff = o
        self.maT, o = A.alloc_at(o, [8, T], BF16)
        self.kT, o = A.alloc_at(o, [2, T], BF16)
        self.vatt, o = A.alloc_at(o, [16, 256], BF16)
        self.kmT, o = A.alloc_at(o, [8, 256], BF16)
        self.vm, o = A.alloc_at(o, [2, 1024], BF16)
        assert o <= A.nbytes
        up = self.upto
        for s in range(self.nseq):
            self.cur_seq = s
            self.maTr = [regs(4) for _ in range(8)]
            self.kTr = [regs(4) for _ in range(2)]
            self.vattr = regs(16)
            self.kmTr = regs(8)
            self.vmr = regs(2)
            if s > 0:
                P.fence()
            A.reset(self.base)
            self.hT = A.alloc([8, T], BF16)
            self.hTr = [regs(4) for _ in range(8)]
            m1 = A.mark()
            self.alloc_xb(4)
            self.phaseA(s)
            self.issue_conv(["gla1", "wba", "ga", "akv", "xwkv"])
            if up == "A":
                tmp = A.alloc([8, T], F32)
                self.dbg_store("hT", self.hT, [r for rr in self.hTr for r in rr], tmp)
                break
            P.fence()
            A.reset(m1)
            self.glaT = A.alloc([8, T], BF16)
            self.glaTr = [regs(16) for _ in range(8)]
            m2 = A.mark()
            self.phaseGLA(s)
            P.fence()
            A.reset(m2)
            if up == "GLA":
                tmp = A.alloc([8, T], F32)
                self.dbg_store("glaT", self.glaT, [r for rr in self.glaTr for r in rr], tmp)
                break
            self.phaseS4(s)
            self.phaseS2(s)
            assert A.off <= self.qoff, (A.off, self.qoff)
            if up == "S2":
                tmp = A.alloc_at(self.base, [8, T], F32)[0]
                P.fence()
                self.dbg_store("maT", self.maT, [r for rr in self.maTr for r in rr], tmp)
                tmp2 = A.alloc_at(self.base + 65536, [2, T], F32)[0]
                self.dbg_store("kT", self.kT, [r for rr in self.kTr for r in rr], tmp2)
                tmp3 = A.alloc_at(self.base + 65536 + 16384, [16, 256], F32)[0]
                self.dbg_store("vatt", self.vatt, self.vattr, tmp3)
                tmp4 = A.alloc_at(self.base + 65536 + 32768, [8, 256], F32)[0]
                self.dbg_store("kmT", self.kmT, self.kmTr, tmp4)
                tmp5 = A.alloc_at(self.base + 65536 + 32768 + 8192, [2, 1024], F32)[0]
                self.dbg_store("vm", self.vm, self.vmr, tmp5)
                break
            P.fence()
            A.reset(self.base)
            self.alloc_S5()
            assert A.off <= self.qoff, (A.off, self.qoff)
            for tb in range(4):
                self.phaseS5(s, tb)
                if up in ("x1", "x2"):
                    for t in range(4):
                        self.dma("sp", self.y[s, tb * 512 + t * 128:tb * 512 + (t + 1) * 128, :], self.xb[:, t, :],
                                 self.y_slot[t], [self.xbr[t]], [])
        P.emit()
        return self.nc


def make_core_inputs(inp, xs, ms, shared=None):
    if shared is None:
        shared = make_shared(inp)
    d = dict(shared)
    d["x"] = np.ascontiguousarray(xs, dtype=np.float32)
    d["mem"] = np.ascontiguousarray(ms, dtype=np.float32)
    return d


def make_shared(inp):
    gpost = np.stack([np.broadcast_to(inp[k][0][None, :], (128, D)) for k in ("ln_mix_post", "ln_x_post", "ln_ffn_post")], 0)
    gn = inp["gla_norm"][0]
    gnorm2 = np.broadcast_to(np.concatenate([gn, gn])[None, :], (128, 512))
    waug = np.zeros((17, 1024), np.float32)
    waug[:16, :512] = inp["gla_wa_f"][0]
    waug[:16, 512:] = inp["gla_wa_b"][0]
    waug[16, :512] = inp["gla_ba_f"][0]
    waug[16, 512:] = inp["gla_ba_b"][0]
    return {
        "wall": build_wall(inp),
        "kpack": build_kpack(),
        "cpack": build_cpack(inp),
        "gpost": np.ascontiguousarray(gpost, dtype=np.float32),
        "gnorm2": np.ascontiguousarray(gnorm2, dtype=np.float32),
        "waug": waug,
    }


_CACHE = {}


def kernel(**inputs):
    inp = {k: np.asarray(v) for k, v in inputs.items()}
    xs = np.concatenate([inp["x_prompt"], inp["x_sample"]], 0)
    ms = np.concatenate([inp["mem_prompt"], inp["mem_sample"]], 0)
    nb = xs.shape[0]
    assert nb == NCORES * SEQ_PER_CORE
    shared = make_shared(inp)
    if "nc" not in _CACHE:
        _CACHE["nc"] = Builder(nseq=SEQ_PER_CORE, upto="all").build()
    nc = _CACHE["nc"]
    in_maps = []
    for c in range(NCORES):
        sl = slice(c * SEQ_PER_CORE, (c + 1) * SEQ_PER_CORE)
        in_maps.append(make_core_inputs(inp, xs[sl], ms[sl], shared))
    res = run_bass_kernel_spmd(nc, in_maps, core_ids=list(range(NCORES)))
    y = np.concatenate([np.asarray(r["y"]) for r in res.results], 0).astype(np.float32)
    nprompt = inp["x_prompt"].shape[0]
    return (np.ascontiguousarray(y[:nprompt]), np.ascontiguousarray(y[nprompt:]))
```

```python
import numpy as np
import concourse.bass as bass
import concourse.mybir as mybir
from concourse.bass_utils import run_bass_kernel_spmd

F32 = mybir.dt.float32
BF16 = mybir.dt.bfloat16
AF = mybir.ActivationFunctionType
ALU = mybir.AluOpType

T = 2048
D = 1024
NMEM = 256
EPS = 1e-6
DFF = 2816
NCORES = 8
SEQ_PER_CORE = 3

ENGS = ("pe", "act", "dve", "pool", "sp")
SAME_ENGINE_FULL_SYNC = True


class Reg:
    __slots__ = ("w", "r")

    def __init__(self):
        self.w = None
        self.r = {}


def regs(n):
    return [Reg() for _ in range(n)]


class Slot:
    def __init__(self, nc, name):
        self.sem = nc.alloc_semaphore(name)
        self.n = 0


class Prog:
    def __init__(self, nc):
        self.nc = nc
        self.ops = []
        self.by_eng = {e: [] for e in ENGS}
        self.sems = {e: nc.alloc_semaphore("sem_" + e) for e in ENGS}
        self.out_slots = []
        self.fence_deps = set()
        self.fence_pending = set()
        self.dma_since = []

    def fence(self):
        deps = set(self.dma_since)
        for e in ENGS:
            for oid in reversed(self.by_eng[e]):
                if self.ops[oid][3] is None:
                    deps.add(oid)
                    break
        self.fence_deps = deps
        self.fence_pending = set(ENGS)
        self.dma_since = []

    def op(self, eng, fn, reads=(), writes=(), slot=None):
        oid = len(self.ops)
        deps = set()
        is_dma = slot is not None
        if eng in self.fence_pending:
            self.fence_pending.discard(eng)
            for d in self.fence_deps:
                if self.ops[d][3] is None and self.ops[d][0] == eng and not is_dma:
                    continue
                deps.add(d)
        if is_dma:
            self.dma_since.append(oid)

        def add(pid, kind):
            peng, _, _, pslot, _ = self.ops[pid]
            if pslot is None and not is_dma and peng == eng:
                if eng == "pe" or (kind != "raw" and not SAME_ENGINE_FULL_SYNC):
                    return
            deps.add(pid)

        for r in reads:
            if r.w is not None:
                add(r.w, "raw")
        for w in writes:
            if w.w is not None:
                add(w.w, "waw")
            for pid in w.r.values():
                add(pid, "war")
        key = ("dma", oid) if is_dma else eng
        for r in reads:
            r.r[key] = oid
        for w in writes:
            w.w = oid
            w.r = {}
        val = None
        if is_dma:
            slot.n += 1
            val = 16 * slot.n
        self.ops.append((eng, fn, deps, slot, val))
        self.by_eng[eng].append(oid)
        return oid

    def emit(self):
        nc = self.nc
        ops = self.ops
        marked = set()
        for (_, _, deps, _, _) in ops:
            for d in deps:
                if ops[d][3] is None:
                    marked.add(d)
        tok = {}
        for e in ENGS:
            cnt = 0
            for oid in self.by_eng[e]:
                eng, fn, deps, slot, val = ops[oid]
                if slot is not None:
                    tok[oid] = (slot.sem, val)
                elif oid in marked:
                    cnt += 1
                    tok[oid] = (self.sems[e], cnt)
        self.nmarked = len(marked)
        final_waits = [(s.sem, 16 * s.n) for s in self.out_slots if s.n > 0]
        handles = {"pe": "tensor", "act": "scalar", "dve": "vector", "pool": "gpsimd", "sp": "sync"}

        def run_engine(e, engine):
            waited = {}
            for oid in self.by_eng[e]:
                eng, fn, deps, slot, val = ops[oid]
                needs = {}
                for d in deps:
                    sem, v = tok[d]
                    k = sem.num
                    if waited.get(k, 0) < v and needs.get(k, (None, 0))[1] < v:
                        needs[k] = (sem, v)
                for k, (sem, v) in needs.items():
                    engine.wait_ge(sem, v)
                    waited[k] = v
                inst = fn(engine)
                if slot is not None:
                    inst.then_inc(slot.sem, 16)
                elif oid in marked:
                    inst.then_inc(self.sems[e], 1)
            if e == "sp":
                for sem, v in final_waits:
                    engine.wait_ge(sem, v)

        with nc.Block() as block:
            for e in ENGS:
                def mk(e=e):
                    def body(engine):
                        run_engine(e, engine)
                    return body
                getattr(block, handles[e])(mk())


class Arena:
    def __init__(self, nc, nbytes):
        self.t = nc.alloc_sbuf_tensor("arena", [128, nbytes // 2], BF16)
        self.nbytes = nbytes
        self.off = 0
        self.peak = 0

    def alloc(self, shape, dtype, parts=128):
        esz = 4 if dtype == F32 else 2
        n = int(np.prod(shape))
        nb = (n * esz + 31) // 32 * 32
        o = self.off
        self.off += nb
        self.peak = max(self.peak, self.off)
        assert self.off <= self.nbytes, f"arena overflow {self.off} > {self.nbytes}"
        v = self.t[0:parts, o // 2:(o + n * esz) // 2]
        if dtype == F32:
            v = v.bitcast(F32)
        if len(shape) == 2:
            v = v.rearrange("p (a b) -> p a b", b=shape[1])
        elif len(shape) == 3:
            v = v.rearrange("p (a b c) -> p a b c", b=shape[1], c=shape[2])
        return v

    def alloc_at(self, off, shape, dtype, parts=128):
        save = self.off
        self.off = off
        v = self.alloc(shape, dtype, parts)
        end = self.off
        self.off = save
        return v, end

    def mark(self):
        return self.off

    def reset(self, m):
        self.off = m


def _seg(W, groups):
    K = W.shape[0]
    KC = K // 128
    out = []
    for cols in groups:
        t = W[:, cols].reshape(KC, 128, len(cols)).transpose(1, 0, 2)
        out.append(np.ascontiguousarray(t).reshape(128, KC * len(cols)))
    return np.stack(out, 0)


def _ar(a, b):
    return np.arange(a, b)


SEG_ORDER = ["z", "gla0", "gla1", "wba", "ga", "akv", "xwkv", "aq", "wbb", "gb", "wo", "xwq", "xwo", "wi", "wo2"]
SEG_SHAPE = {
    "z": (1, 8 * 32), "akv": (1, 8 * 512), "gla0": (1, 8 * 1536), "gla1": (1, 8 * 1536),
    "ga": (2, 4096), "wba": (2, 4096), "aq": (2, 4096), "gb": (2, 4096), "wbb": (2, 4096),
    "wo": (2, 4096), "xwkv": (4, 4096), "xwq": (2, 4096), "xwo": (2, 4096),
    "wi": (11, 4096), "wo2": (22, 1024),
}


def seg_offsets():
    off = {}
    o = 0
    for s in SEG_ORDER:
        ng, L = SEG_SHAPE[s]
        off[s] = o
        o += ng * 128 * L
    return off, o


def build_wall(inp):
    w_in = inp["w_in"][0]
    segs = {}
    for p in range(2):
        h0, h1 = 2 * p, 2 * p + 1
        cols = np.concatenate([
            _ar(h0 * 128, h0 * 128 + 128), _ar(h1 * 128, h1 * 128 + 128),
            _ar(512 + h0 * 128, 512 + h0 * 128 + 128), _ar(512 + h1 * 128, 512 + h1 * 128 + 128),
            _ar(1024 + h0 * 256, 1024 + h0 * 256 + 256), _ar(1024 + h1 * 256, 1024 + h1 * 256 + 256),
            _ar(2048 + h0 * 256, 2048 + h0 * 256 + 256), _ar(2048 + h1 * 256, 2048 + h1 * 256 + 256)])
        segs["gla%d" % p] = _seg(w_in, [cols])
    segs["z"] = _seg(w_in, [_ar(3072, 3104)])
    segs["akv"] = _seg(w_in, [_ar(4128, 4640)])
    segs["aq"] = _seg(w_in, [_ar(3104, 3616), _ar(3616, 4128)])
    segs["ga"] = _seg(w_in, [_ar(4640, 5152), _ar(5152, 5664)])
    segs["gb"] = _seg(w_in, [_ar(5664, 6176), _ar(6176, 6688)])
    two = [_ar(0, 512), _ar(512, 1024)]
    segs["wba"] = _seg(inp["w_branch_gla"][0], two)
    segs["wbb"] = _seg(inp["w_branch_att"][0], two)
    segs["wo"] = _seg(inp["w_out"][0], two)
    segs["xwq"] = _seg(inp["x_wq"][0], two)
    segs["xwo"] = _seg(inp["x_wo"][0], two)
    segs["xwkv"] = _seg(inp["x_wkv"][0], [_ar(i * 512, (i + 1) * 512) for i in range(4)])
    wi = inp["ffn_wi"][0]
    segs["wi"] = _seg(wi, [np.concatenate([_ar(256 * i, 256 * i + 256), _ar(DFF + 256 * i, DFF + 256 * i + 256)])
                           for i in range(11)])
    w2 = inp["ffn_wo"][0]
    pcs = []
    for half in range(2):
        for i in range(11):
            blk = w2[256 * i:256 * i + 256, half * 512:(half + 1) * 512]
            pcs.append(np.ascontiguousarray(blk.reshape(2, 128, 512).transpose(1, 0, 2)).reshape(128, 1024))
    segs["wo2"] = np.stack(pcs, 0)
    off, tot = seg_offsets()
    wall = np.empty(tot, np.float32)
    for s in SEG_ORDER:
        ng, L = SEG_SHAPE[s]
        a = segs[s]
        assert a.shape == (ng, 128, L), (s, a.shape)
        wall[off[s]:off[s] + a.size] = a.reshape(-1)
    return wall.reshape(-1, 2048)


K_ID = 0
K_UF = 128
K_WF = 256
K_UB = 384
K_WB = 512
K_MASK = 640
K_ROT = 1152
K_COS = 1280
K_SIN = 1376
NK = 1472


def build_kpack():
    k = np.zeros((128, NK), np.float32)
    j = np.arange(128)[:, None]
    i = np.arange(128)[None, :]
    k[:, K_ID:K_ID + 128] = (j == i)
    c = -1.0 / 16.0
    k[:, K_UF:K_UF + 128] = c * (j <= i)
    k[:, K_WF:K_WF + 128] = c * (j > i)
    k[:, K_UB:K_UB + 128] = c * (j >= i)
    k[:, K_WB:K_WB + 128] = c * (j < i)
    mf = (j <= i).astype(np.float32)
    mb = (j > i).astype(np.float32)
    k[:, K_MASK:K_MASK + 512] = np.concatenate([mf, mf, mb, mb], 1)
    R = np.zeros((128, 128), np.float32)
    for m in range(128):
        if (m % 64) < 32:
            R[m + 32, m] = -1.0
        else:
            R[m - 32, m] = 1.0
    k[:, K_ROT:K_ROT + 128] = R
    inv = (10000.0 ** (-np.arange(0, 64, 2, dtype=np.float32) / np.float32(64))).astype(np.float32)
    for d in range(128):
        f = inv[d % 32]
        if d < 64:
            ang = (np.arange(32, dtype=np.float32) * f).astype(np.float32)
            k[d, K_COS:K_COS + 32] = np.cos(ang)
            k[d, K_SIN:K_SIN + 32] = np.sin(ang)
        else:
            ang = (np.arange(64, dtype=np.float32) * f).astype(np.float32)
            k[d, K_COS + 32:K_COS + 96] = np.cos(ang)
            k[d, K_SIN + 32:K_SIN + 96] = np.sin(ang)
    return k


C_MIXPRE, C_XPRE, C_FFNPRE, C_MEM, C_QN, C_KN = 0, 8, 16, 24, 32, 33
NCP = 34


def build_cpack(inp):
    c = np.zeros((128, NCP), np.float32)
    c[:, C_MIXPRE:C_MIXPRE + 8] = inp["ln_mix_pre"][0].reshape(8, 128).T
    c[:, C_XPRE:C_XPRE + 8] = inp["ln_x_pre"][0].reshape(8, 128).T
    c[:, C_FFNPRE:C_FFNPRE + 8] = inp["ln_ffn_pre"][0].reshape(8, 128).T
    c[:, C_MEM:C_MEM + 8] = inp["ln_mem"][0].reshape(8, 128).T
    c[:, C_QN] = inp["att_q_norm"][0]
    c[:, C_KN] = inp["att_k_norm"][0]
    return c


class Builder:
    def __init__(self, nseq=SEQ_PER_CORE, upto="all", dbg=None):
        self.nseq = nseq
        self.upto = upto
        self.dbg = dbg or {}
        nc = bass.Bass("TRN2", target_bir_lowering=False)
        self.nc = nc
        self.P = Prog(nc)
        self.soff, self.wtot = seg_offsets()
        self.x = nc.dram_tensor("x", [nseq, T, D], F32, kind="ExternalInput").ap()
        self.mem = nc.dram_tensor("mem", [nseq, NMEM, D], F32, kind="ExternalInput").ap()
        self.wall = nc.dram_tensor("wall", [self.wtot // 2048, 2048], F32, kind="ExternalInput").ap()
        self.kpack = nc.dram_tensor("kpack", [128, NK], F32, kind="ExternalInput").ap()
        self.cpack = nc.dram_tensor("cpack", [128, NCP], F32, kind="ExternalInput").ap()
        self.gpost = nc.dram_tensor("gpost", [3, 128, D], F32, kind="ExternalInput").ap()
        self.gnorm2 = nc.dram_tensor("gnorm2", [128, 512], F32, kind="ExternalInput").ap()
        self.waug = nc.dram_tensor("waug", [17, 1024], F32, kind="ExternalInput").ap()
        self.y = nc.dram_tensor("y", [nseq, T, D], F32, kind="ExternalOutput").ap()
        self.wbf = nc.dram_tensor("wbf", [self.wtot // 2048, 2048], BF16).ap()
        self.wbf_flat = self.wbf.rearrange("r c -> (r c)")
        self.wbf_reg = {s: Reg() for s in SEG_ORDER}
        self.dbg_out = {}
        for name, shape in self.dbg.items():
            self.dbg_out[name] = nc.dram_tensor("dbg_" + name, list(shape), F32, kind="ExternalOutput").ap()
        self.A = Arena(nc, 207 * 1024)
        self.PS = [nc.alloc_psum_tensor("ps%d" % i, [128, 1024], F32).ap() if False else
                   nc.alloc_psum_tensor("ps%d" % i, [128, 1024], F32) for i in range(4)]
        self.psr = regs(8)

    def bank(self, b, lo=0, hi=512):
        return self.PS[b // 2][:, (b % 2) * 512 + lo:(b % 2) * 512 + hi]

    def mm(self, out, lhsT, rhs, start, stop, reads, writes):
        self.P.op("pe", lambda e: e.matmul(out, lhsT, rhs, start=start, stop=stop), reads, writes)

    def tr(self, out, in_, reads, writes):
        ident = self.K[:, K_ID:K_ID + 128]
        self.P.op("pe", lambda e: e.transpose(out, in_, ident), list(reads) + [self.Kr], writes)

    def act(self, out, in_, func, reads, writes, scale=1.0, bias=0.0, accum=None):
        def fn(e):
            kw = {}
            if accum is not None:
                kw["accum_out"] = accum
            return e.activation(out, in_, func, bias=bias, scale=scale, **kw)
        self.P.op("act", fn, reads, writes)

    def tt(self, eng, out, in0, in1, op, reads, writes):
        self.P.op(eng, lambda e: e.tensor_tensor(out, in0, in1, op), reads, writes)

    def ts(self, eng, out, in0, s1, op0, reads, writes, s2=None, op1=None):
        if op1 is None:
            self.P.op(eng, lambda e: e.tensor_scalar(out, in0, s1, None, op0), reads, writes)
        else:
            self.P.op(eng, lambda e: e.tensor_scalar(out, in0, s1, s2, op0, op1), reads, writes)

    def stt(self, out, in0, scalar, in1, op0, op1, reads, writes):
        self.P.op("dve", lambda e: e.scalar_tensor_tensor(out, in0, scalar, in1, op0, op1), reads, writes)

    def cp(self, eng, out, in_, reads, writes):
        if eng == "act":
            self.P.op("act", lambda e: e.activation(out, in_, AF.Copy), reads, writes)
        else:
            self.P.op(eng, lambda e: e.tensor_copy(out, in_), reads, writes)

    def dma(self, q, out, in_, slot, reads, writes):
        self.P.op(q, lambda e: e.dma_start(out, in_), reads, writes, slot=slot)

    def newslot(self, name, output=False):
        s = Slot(self.nc, name)
        if output:
            self.P.out_slots.append(s)
        return s

    def wsrc(self, seg, g):
        ng, L = SEG_SHAPE[seg]
        o = self.soff[seg] + g * 128 * L
        return self.wbf_flat[o:o + 128 * L].rearrange("(p l) -> p l", p=128), L

    def init_consts(self):
        A, nc = self.A, self.nc
        self.K = A.alloc([NK], F32)
        self.Kr = Reg()
        self.C = A.alloc([NCP], F32)
        self.Cr = Reg()
        self.G2 = A.alloc([512], F32)
        self.G2r = Reg()
        self.gp = A.alloc([D], F32)
        self.gpr = Reg()
        self.gp_slot = self.newslot("gp")
        self.waug_f = A.alloc_at(A.nbytes - 56 * 1024, [1024], F32, parts=17)[0]
        self.waug_b = A.alloc([1024], BF16, parts=17)
        self.waugr = Reg()
        self.ones = A.alloc([128], BF16)
        self.onesr = Reg()
        self.cosB = A.alloc([512], F32)
        self.sinB = A.alloc([512], F32)
        self.csr = Reg()
        s = self.newslot("c0")
        self.dma("sp", self.K, self.kpack, s, [], [self.Kr])
        self.Kbf = A.alloc([512], BF16)
        self.cp("dve", self.Kbf, self.K[:, K_UF:K_UF + 512], [self.Kr], [self.Kr])
        s = self.newslot("c1")
        self.dma("sp", self.C, self.cpack, s, [], [self.Cr])
        s = self.newslot("c2")
        self.dma("sp", self.G2, self.gnorm2, s, [], [self.G2r])
        s = self.newslot("c3")
        r0 = Reg()
        self.dma("sp", self.waug_f, self.waug, s, [], [r0])
        self.cp("dve", self.waug_b, self.waug_f, [r0], [self.waugr])
        self.P.op("pool", lambda e: e.memset(self.ones, 1.0), [], [self.onesr])
        self.epsv = A.alloc([8], F32)
        self.epsr = Reg()
        self.P.op("pool", lambda e: e.memset(self.epsv[:, 0:1], EPS), [], [self.epsr])
        self.P.op("pool", lambda e: e.memset(self.epsv[:, 1:2], 1.0), [], [self.epsr])
        self.conv_done = set()
        for tab, kc in ((self.cosB, K_COS), (self.sinB, K_SIN)):
            src = self.K[64:128, kc + 32:kc + 96].unsqueeze(1).broadcast_to([64, 8, 64])
            dst = tab[64:128, :].rearrange("p (r c) -> p r c", c=64)
            self.P.op("pool", (lambda e, dst=dst, src=src: e.tensor_copy(dst, src)), [self.Kr], [self.csr])
        self.nring = 4
        self.ring = [A.alloc([4096], BF16) for _ in range(self.nring)]
        self.ring_r = regs(self.nring)
        self.ring_s = [self.newslot("ring%d" % i) for i in range(self.nring)]
        self.make_sched()

    def issue_conv(self, names, after=()):
        for sname in names:
            if sname in self.conv_done:
                continue
            self.conv_done.add(sname)
            ng, L = SEG_SHAPE[sname]
            r_lo = self.soff[sname] // 2048
            r_hi = (self.soff[sname] + ng * 128 * L) // 2048
            s = self.newslot("cv_" + sname)
            src = self.wall[r_lo:r_hi, :]
            dst = self.wbf[r_lo:r_hi, :]
            self.dma("pool", dst, src, s, list(after), [self.wbf_reg[sname]])

    def make_sched(self):
        L = []
        for s in range(self.nseq):
            L.append(("z", 0))
            L += [("wba", 0), ("ga", 0), ("wba", 1), ("ga", 1)]
            L += [("akv", 0)] + [("xwkv", i) for i in range(4)]
            for tb in range(4):
                L += [("aq", 0), ("aq", 1), ("wbb", 0), ("gb", 0), ("wbb", 1), ("gb", 1), ("wo", 0), ("wo", 1),
                      ("xwq", 0), ("xwq", 1), ("xwo", 0), ("xwo", 1)] + [("wi", i) for i in range(11)]
        self.wsched = L
        self.wptr = 0
        self.wissued = 0

    def wissue_upto(self, n):
        n = min(n, len(self.wsched))
        while self.wissued < n:
            k = self.wissued
            seg, g = self.wsched[k]
            i = k % self.nring
            src, L = self.wsrc(seg, g)
            assert seg in self.conv_done, seg
            self.dma("sp", self.ring[i][:, 0:L], src, self.ring_s[i], [self.wbf_reg[seg]], [self.ring_r[i]])
            self.wissued += 1

    def wget(self, seg, g, prefetch_only=False):
        if prefetch_only:
            return None
        if self.upto != "all":
            while self.wsched[self.wptr] != (seg, g):
                self.wptr += 1
            self.wissued = max(self.wissued, self.wptr)
        k = self.wptr
        assert self.wsched[k] == (seg, g), (k, self.wsched[k], seg, g)
        self.wissue_upto(k + self.nring - 1)
        self.wptr += 1
        i = k % self.nring
        return self.ring[i], self.ring_r[i]

    def rope_top(self, tb):
        for tab, kc in ((self.cosB, K_COS), (self.sinB, K_SIN)):
            src = self.K[0:64, kc + tb * 8:kc + tb * 8 + 8].unsqueeze(2).broadcast_to([64, 8, 64])
            dst = tab[0:64, :].rearrange("p (r c) -> p r c", c=64)
            self.P.op("pool", (lambda e, dst=dst, src=src: e.tensor_copy(dst, src)), [self.Kr], [self.csr])

    def norm_tile(self, t, xb=None, xbr=None):
        sc = self.sc
        if xb is None:
            xb, xbr = self.xb, self.xbr
        ss, nsr, xn, xnr = sc["ss"], sc["nsr"], sc["xn"], sc["xnr"]
        self.act(sc["junk"], xb[:, t, :], AF.Square, [xbr[t]], [sc["junkr"], nsr[t][0]], accum=ss[:, t:t + 1])
        self.act(ss[:, 4 + t:5 + t], ss[:, t:t + 1], AF.Ln, [nsr[t][0], self.epsr], [nsr[t][1]], scale=1.0 / D,
                 bias=self.epsv[:, 0:1])
        self.act(ss[:, 8 + t:9 + t], ss[:, 4 + t:5 + t], AF.Exp, [nsr[t][1]], [nsr[t][2]], scale=-0.5)
        self.ts("dve", xn[:, t, :], xb[:, t, :], ss[:, 8 + t:9 + t], ALU.mult, [xbr[t], nsr[t][2]], [xnr[t]])

    def norm_fin(self, nt, gcol, dst, dst_regs, banks):
        xn, xnr = self.sc["xn"], self.sc["xnr"]
        for c in range(8):
            b = banks[c % len(banks)]
            for t in range(nt):
                self.tr(self.bank(b, t * 128, (t + 1) * 128), xn[:, t, c * 128:(c + 1) * 128], [xnr[t]], [self.psr[b]])
            g = self.C[:, gcol + c:gcol + c + 1]
            if c % 2 == 0:
                self.P.op("act", (lambda e, o=dst(c), i=self.bank(b, 0, nt * 128), g=g:
                                  e.activation(o, i, AF.Copy, scale=g)), [self.psr[b], self.Cr], [dst_regs[c]])
            else:
                self.ts("dve", dst(c), self.bank(b, 0, nt * 128), g, ALU.mult, [self.psr[b], self.Cr], [dst_regs[c]])

    def norm_T(self, xb, xbr, nt, gcol, dst, dst_regs, scratch, banks):
        for t in range(nt):
            self.norm_tile(t)
        self.norm_fin(nt, gcol, dst, dst_regs, banks)

    def eps_ap(self):
        return self.epsv[:, 0:1]

    def pbank(self, b, lo=0, hi=512, p0=0, p1=128):
        return self.PS[b // 2][p0:p1, (b % 2) * 512 + lo:(b % 2) * 512 + hi]

    def alloc_xb(self, nt=4, junk=None):
        A = self.A
        self.xb = A.alloc([nt, D], F32)
        self.xbr = regs(nt)
        if junk is None:
            junk = (A.alloc([D], BF16), Reg())
        self.sc = {"junk": junk[0], "junkr": junk[1], "ss": self.ssP, "nsr": [regs(3) for _ in range(4)],
                   "pnr": [regs(3) for _ in range(4)],
                   "xn": A.alloc([nt, D], F32), "xnr": regs(nt)}

    def load_xb(self, src_rows, nt=4, extra_w=(), slots=None):
        slots = slots or self.xb_slot
        for t in range(nt):
            self.dma("sp", self.xb[:, t, :], src_rows[t * 128:(t + 1) * 128, :], slots[t], [],
                     [self.xbr[t]] + list(extra_w))

    def dbg_store(self, name, src_ap, src_regs, f32tmp=None):
        if name not in self.dbg_out:
            return
        dst = self.dbg_out[name]
        s = self.newslot("dbg_" + name, output=True)
        if f32tmp is not None:
            r = Reg()
            self.cp("dve", f32tmp, src_ap, src_regs, [r])
            self.dma("sp", dst, f32tmp, s, [r], [])
        else:
            self.dma("sp", dst, src_ap, s, src_regs, [])

    def alloc_nr(self):
        A = self.A
        self.nr = {"sqb": A.alloc([512], BF16), "tmpA": A.alloc([512], F32), "knf": A.alloc([512], F32),
                   "t2": A.alloc([512], F32), "r": regs(4), "knf2": A.alloc([512], F32), "r_kn2": Reg()}

    def normrope_pipe(self, items, pbanks, bs, br):
        nr = self.nr
        sqb, tmpA, t2 = nr["sqb"], nr["tmpA"], nr["t2"]
        knfs = [nr["knf"], nr["knf2"]]
        r_sq, r_tmp, r_kn0, r_t2 = nr["r"]
        r_kns = [r_kn0, nr["r_kn2"]]
        psr = self.psr
        n = len(items)
        nb = len(pbanks)
        assert nb >= 3

        def stA(i):
            items[i][0](pbanks[i % nb])

        def stB(i):
            b = pbanks[i % nb]
            src, src_reg = self.bank(b), psr[b]
            gcol = items[i][1]
            knf, r_kn = knfs[i % 2], r_kns[i % 2]
            self.act(sqb, src, AF.Square, [src_reg], [r_sq])
            self.mm(self.bank(bs), self.ones, sqb, True, True, [self.onesr, r_sq], [psr[bs]])
            self.act(tmpA, self.bank(bs), AF.Ln, [psr[bs], self.epsr], [r_tmp], scale=1.0 / 128, bias=self.epsv[:, 0:1])
            self.act(tmpA, tmpA, AF.Exp, [r_tmp], [r_tmp], scale=-0.5)
            self.stt(knf, src, self.C[:, gcol:gcol + 1], tmpA, ALU.mult, ALU.mult, [src_reg, self.Cr, r_tmp], [r_kn])

        def stC(i):
            _, gcol, out, out_regs, pre = items[i]
            knf, r_kn = knfs[i % 2], r_kns[i % 2]
            if pre is not None:
                pre()
            self.mm(self.bank(br), self.K[:, K_ROT:K_ROT + 128], knf, True, True, [self.Kr, r_kn], [psr[br]])
            self.tt("dve", t2, self.bank(br), self.sinB, ALU.mult, [psr[br], self.csr], [r_t2])
            self.tt("pool", knf, knf, self.cosB, ALU.mult, [r_kn, self.csr], [r_kn])
            self.tt("pool", out, knf, t2, ALU.add, [r_kn, r_t2], out_regs)

        for step in range(n + 2):
            if step < n:
                stA(step)
            if 0 <= step - 1 < n:
                stB(step - 1)
            if 0 <= step - 2 < n:
                stC(step - 2)

    def postnorm(self, k, t, ssr_i):
        sc = self.sc
        ss = sc["ss"]
        r0, r1, r2 = sc["pnr"][t]
        u = self.PS[k][:, :]
        pr = [self.psr[2 * k], self.psr[2 * k + 1]]
        self.act(sc["junk"], u, AF.Square, pr, [sc["junkr"], r0], accum=ss[:, 16 + t:17 + t])
        self.act(ss[:, 20 + t:21 + t], ss[:, 16 + t:17 + t], AF.Ln, [r0, self.epsr], [r1], scale=1.0 / D,
                 bias=self.epsv[:, 0:1])
        self.act(ss[:, 24 + t:25 + t], ss[:, 20 + t:21 + t], AF.Exp, [r1], [r2], scale=-0.5)
        xn, xnr = sc["xn"], sc["xnr"]
        self.stt(xn[:, t, :], u, ss[:, 24 + t:25 + t], self.gp, ALU.mult, ALU.mult, pr + [r2, self.gpr], [xnr[t]])
        self.tt("dve", self.xb[:, t, :], self.xb[:, t, :], xn[:, t, :], ALU.add, [self.xbr[t], xnr[t]], [self.xbr[t]])

    def load_gp(self, idx):
        self.dma("sp", self.gp, self.gpost[idx], self.gp_slot, [], [self.gpr])

    def phaseA(self, s):
        sets = []
        for i in range(2):
            self.alloc_xb(4)
            if i == 1:
                self.sc["nsr"], self.sc["pnr"] = sets[0][2]["nsr"], sets[0][2]["pnr"]
            sets.append((self.xb, self.xbr, self.sc))

        def load(tb):
            self.xb, self.xbr, self.sc = sets[tb % 2]
            if s == 0 and tb == 0:
                start = Reg()
                self.load_xb(self.x[s, 0:512, :], extra_w=[start], slots=self.xb_slot2[tb % 2])
                self.issue_conv(["z", "gla0"], after=[start])
            else:
                self.load_xb(self.x[s, tb * 512:(tb + 1) * 512, :], slots=self.xb_slot2[tb % 2])

        load(0)
        for tb in range(4):
            if tb + 1 < 4:
                load(tb + 1)
            self.xb, self.xbr, self.sc = sets[tb % 2]
            self.norm_T(self.xb, self.xbr, 4, C_MIXPRE,
                        lambda c, tb=tb: self.hT[:, c, tb * 512:(tb + 1) * 512],
                        [self.hTr[c][tb] for c in range(8)], self.sc, [0, 1] if tb % 2 == 0 else [2, 3])

    def phaseGLA(self, s):
        A, P, psr, bank, pbank = self.A, self.P, self.psr, self.bank, self.pbank
        hT, hTr = self.hT, self.hTr
        Kc = self.K
        Kb = self.Kbf
        zT = [A.alloc([T], BF16, parts=17) for _ in range(2)]
        zTr = [regs(4), regs(4)]
        for d in range(2):
            P.op("pool", (lambda e, z=zT[d]: e.memset(z, 1.0)), [], zTr[d])
        wz, wzr = self.wget("z", 0)
        wz3 = wz[:, 0:256].rearrange("p (c n) -> p c n", n=32)
        for tb in range(4):
            blk = slice(tb * 512, (tb + 1) * 512)
            for d in range(2):
                b = d
                for c in range(8):
                    self.mm(pbank(b, 0, 512, 0, 16), wz3[:, c, d * 16:(d + 1) * 16], hT[:, c, blk], c == 0, c == 7,
                            [wzr, hTr[c][tb]], [psr[b]])
                self.cp("dve", zT[d][0:16, blk], pbank(b, 0, 512, 0, 16), [psr[b]], [zTr[d][tb]])
        wgla = A.alloc([8 * 1536], BF16)
        wglar = Reg()
        wg3 = wgla.rearrange("p (c n) -> p c n", n=1536)
        Sb = A.alloc([16, 512], BF16)
        Sbr = regs(16)
        S32 = A.alloc([512], F32)
        S32r = regs(2)
        Sfbf = A.alloc([512], BF16)
        Sfbfr = Reg()

        def dbl(shape, dt):
            return [A.alloc(shape, dt) for _ in range(2)], regs(2)
        def sgl(shape, dt):
            a, r = A.alloc(shape, dt), Reg()
            return [a, a], [r, r]
        ef, efr = sgl([512], F32)
        sp, spr = sgl([512], BF16)
        E1, E1r = dbl([512], F32)
        E2, E2r = sgl([512], F32)
        E3, E3r = sgl([256], F32)
        vst = A.alloc([16, 512], BF16)
        vstr = regs(16)
        qgf, qgfr = dbl([256], BF16)
        qgb, qgbr = dbl([256], BF16)
        kgf, kgfr = dbl([256], BF16)
        kgb, kgbr = dbl([256], BF16)
        kend, kendr = dbl([256], BF16)
        gr, grr = dbl([512], F32)
        og, ogr = dbl([512], F32)
        decb, decbr = dbl([8], F32)
        AT = A.alloc([512], BF16)
        ATr = Reg()
        ss2 = A.alloc([8], F32)
        ss2r = regs(3)
        junk = A.alloc([256], BF16)
        junkr = Reg()
        one_ap = self.epsv[:, 1:2]
        SC = float(128.0 ** -0.5)
        for p in range(2):
            src, L = self.wsrc("gla%d" % p, 0)
            self.dma("sp", wgla, src, self.wgla_slot, [self.wbf_reg["gla%d" % p]], [wglar])
            self.issue_conv(["aq", "wbb", "gb", "wo", "xwq", "xwo"] if p == 0 else ["wi", "wo2"])
            fo = p * 256
            bo = 512 + p * 256

            def b_s1(tt):
                q = tt % 2
                tk = slice(tt * 128, (tt + 1) * 128)
                tb = tt // 4
                hr = lambda c: hTr[c][tb]
                self.mm(bank(0, 0, 256), zT[1][0:17, tk], self.waug_b[0:17, bo:bo + 256], True, True,
                        [zTr[1][tb], self.waugr], [psr[0]])
                self.act(ef[q][:, 0:256], bank(0, 0, 256), AF.Exp, [psr[0]], [efr[q]], scale=-1.0)
                self.act(sp[q][:, 0:256], ef[q][:, 0:256], AF.Ln, [efr[q], self.epsr], [spr[q]], bias=one_ap)
                for c in range(8):
                    self.mm(bank(3), hT[:, c, tk], wg3[:, c, 512:1024], c == 0, c == 7, [hr(c), wglar], [psr[3]])
                for c in range(8):
                    self.mm(bank(2, 0, 256), hT[:, c, tk], wg3[:, c, 256:512], c == 0, c == 7, [hr(c), wglar], [psr[2]])
                for h in range(2):
                    self.mm(bank(1, h * 128, (h + 1) * 128), sp[q][:, h * 128:(h + 1) * 128], Kb[:, 256:384],
                            True, True, [spr[q], self.Kr], [psr[1]])
                self.mm(bank(1, 256, 512), Kb[:, 384:512], sp[q][:, 0:256], True, True, [spr[q], self.Kr], [psr[1]])
                self.cp("act", vst[:, tt, :], bank(3), [psr[3]], [vstr[tt]])
                self.act(decb[q][:, 0:2], bank(1, 0, 256).rearrange("p (h i) -> p h i", i=128)[:, :, 0], AF.Exp,
                         [psr[1]], [decbr[q]])
                self.act(E3[q], bank(1, 256, 512), AF.Exp, [psr[1]], [E3r[q]])
                self.tt("dve", kend[q], bank(2, 0, 256), E3[q], ALU.mult, [psr[2], E3r[q]], [kendr[q]])

            def b_s2(tt):
                q = tt % 2
                for h in range(2):
                    self.mm(bank(7, h * 256, (h + 1) * 256), kend[q][:, h * 128:(h + 1) * 128],
                            vst[:, tt, h * 256:(h + 1) * 256], True, True, [kendr[q], vstr[tt]], [psr[7]])
                for h in range(2):
                    hs = slice(h * 256, (h + 1) * 256)
                    self.stt(S32[:, hs], S32[:, hs], decb[q][:, h:h + 1], bank(7, h * 256, (h + 1) * 256), ALU.mult, ALU.add,
                             [S32r[h], decbr[q], psr[7]], [S32r[h]])
                if tt > 0:
                    self.cp("pool", Sb[:, tt - 1, :], S32, S32r, [Sbr[tt - 1]])

            P.op("pool", (lambda e: e.memset(S32, 0.0)), [], S32r)
            b_s1(15)
            for tt in range(15, -1, -1):
                if tt > 0:
                    b_s1(tt - 1)
                b_s2(tt)

            def f_s1(tt, mid_hook=None, pending=None):
                q = tt % 2
                tk = slice(tt * 128, (tt + 1) * 128)
                tb = tt // 4
                hr = lambda c: hTr[c][tb]
                self.mm(bank(0, 0, 256), zT[0][0:17, tk], self.waug_b[0:17, fo:fo + 256], True, True,
                        [zTr[0][tb], self.waugr], [psr[0]])
                self.mm(bank(0, 256, 512), zT[1][0:17, tk], self.waug_b[0:17, bo:bo + 256], True, True,
                        [zTr[1][tb], self.waugr], [psr[0]])
                self.act(ef[q], bank(0), AF.Exp, [psr[0]], [efr[q]], scale=-1.0)
                self.act(sp[q], ef[q], AF.Ln, [efr[q], self.epsr], [spr[q]], bias=one_ap)
                if pending is not None:
                    pending()
                for c in range(8):
                    self.mm(bank(2, 256, 512), hT[:, c, tk], wg3[:, c, 256:512], c == 0, c == 7, [hr(c), wglar], [psr[2]])
                if mid_hook is not None:
                    mid_hook()
                for d in range(2):
                    ku = 0 if d == 0 else 256
                    for h in range(2):
                        o = d * 256 + h * 128
                        self.mm(bank(1, o, o + 128), sp[q][:, o:o + 128], Kb[:, ku:ku + 128], True, True,
                                [spr[q], self.Kr], [psr[1]])
                self.mm(bank(2, 0, 256), Kb[:, 128:256], sp[q][:, 0:256], True, True, [spr[q], self.Kr], [psr[2]])
                self.act(E1[q], bank(1), AF.Exp, [psr[1]], [E1r[q]])
                self.act(E2[q], bank(1), AF.Exp, [psr[1]], [E2r[q]], scale=-1.0)
                self.act(E3[q], bank(2, 0, 256), AF.Exp, [psr[2]], [E3r[q]])
                for j in range(4):
                    for c in range(8):
                        self.mm(bank(0, j * 128, (j + 1) * 128), wg3[:, c, j * 128:(j + 1) * 128], hT[:, c, tk],
                                c == 0, c == 7, [hr(c), wglar], [psr[0]])
                self.tt("dve", kend[q], bank(2, 256, 512), E3[q], ALU.mult, [psr[2], E3r[q]], [kendr[q]])
                self.stt(qgf[q], bank(0, 0, 256), SC, E1[q][:, 0:256], ALU.mult, ALU.mult, [psr[0], E1r[q]], [qgfr[q]])
                self.stt(qgb[q], bank(0, 0, 256), SC, E1[q][:, 256:512], ALU.mult, ALU.mult, [psr[0], E1r[q]], [qgbr[q]])
                self.tt("dve", kgf[q], bank(0, 256, 512), E2[q][:, 0:256], ALU.mult, [psr[0], E2r[q]], [kgfr[q]])
                self.tt("dve", kgb[q], bank(0, 256, 512), E2[q][:, 256:512], ALU.mult, [psr[0], E2r[q]], [kgbr[q]])

            def f_R(tt):
                tk = slice(tt * 128, (tt + 1) * 128)
                tb = tt // 4
                for c in range(8):
                    self.mm(bank(4), hT[:, c, tk], wg3[:, c, 1024:1536], c == 0, c == 7, [hTr[c][tb], wglar], [psr[4]])

            def f_gr(tt):
                q = tt % 2
                self.act(gr[q], bank(4), AF.Exp, [psr[4]], [grr[q]], scale=-1.0)
                self.act(gr[q], gr[q], AF.Ln, [grr[q], self.epsr], [grr[q]], bias=one_ap)
                self.act(gr[q], gr[q], AF.Exp, [grr[q]], [grr[q]], scale=-1.0)
                self.tt("dve", gr[q], bank(4), gr[q], ALU.mult, [psr[4], grr[q]], [grr[q]])
                self.tt("pool", gr[q], gr[q], self.G2, ALU.mult, [grr[q], self.G2r], [grr[q]])

            def f_s2a(tt):
                q = tt % 2
                kg = (kgf[q], kgb[q])
                qg = (qgf[q], qgb[q])
                kgr = (kgfr[q], kgbr[q])
                qgr = (qgfr[q], qgbr[q])
                for d in range(2):
                    for h in range(2):
                        o = (d * 2 + h) * 128
                        self.mm(bank(5, o, o + 128), kg[d][:, h * 128:(h + 1) * 128], qg[d][:, h * 128:(h + 1) * 128],
                                True, True, [kgr[d], qgr[d]], [psr[5]])
                self.tt("dve", AT, bank(5), Kc[:, K_MASK:K_MASK + 512], ALU.mult, [psr[5], self.Kr], [ATr])

            def f_s2(tt):
                q = tt % 2
                for h in range(2):
                    self.mm(bank(7, h * 256, (h + 1) * 256), kend[q][:, h * 128:(h + 1) * 128],
                            vst[:, tt, h * 256:(h + 1) * 256], True, True, [kendr[q], vstr[tt]], [psr[7]])
                for h in range(2):
                    vh = vst[:, tt, h * 256:(h + 1) * 256]
                    seq = []
                    if tt > 0:
                        seq.append((qgf[q][:, h * 128:(h + 1) * 128], Sfbf[:, h * 256:(h + 1) * 256], [qgfr[q], Sfbfr]))
                    if tt < 15:
                        seq.append((qgb[q][:, h * 128:(h + 1) * 128], Sb[:, tt, h * 256:(h + 1) * 256], [qgbr[q], Sbr[tt]]))
                    seq += [(AT[:, h * 128:(h + 1) * 128], vh, [ATr, vstr[tt]]),
                            (AT[:, (2 + h) * 128:(3 + h) * 128], vh, [ATr, vstr[tt]])]
                    for i, (l, r, rd) in enumerate(seq):
                        self.mm(bank(6, h * 256, (h + 1) * 256), l, r, i == 0, i == len(seq) - 1, rd, [psr[6]])
                for h in range(2):
                    hs = slice(h * 256, (h + 1) * 256)
                    self.stt(S32[:, hs], S32[:, hs], E1[q][:, h * 128 + 127:h * 128 + 128], bank(7, h * 256, (h + 1) * 256),
                             ALU.mult, ALU.add, [S32r[h], E1r[q], psr[7]], [S32r[h]])
                self.cp("pool", Sfbf, S32, S32r, [Sfbfr])
                for h in range(2):
                    self.act(junk[:, 0:256], bank(6, h * 256, (h + 1) * 256), AF.Square, [psr[6]], [junkr, ss2r[0]],
                             accum=ss2[:, h:h + 1])
                self.act(ss2[:, 2:4], ss2[:, 0:2], AF.Ln, [ss2r[0], self.epsr], [ss2r[1]], scale=1.0 / 256,
                         bias=self.epsv[:, 0:1])
                self.act(ss2[:, 4:6], ss2[:, 2:4], AF.Exp, [ss2r[1]], [ss2r[2]], scale=-0.5)
                for h in range(2):
                    hs = slice(h * 256, (h + 1) * 256)
                    self.stt(og[q][:, hs], bank(6, h * 256, (h + 1) * 256), ss2[:, 4 + h:5 + h], gr[q][:, hs],
                             ALU.mult, ALU.mult, [psr[6], ss2r[2], grr[q]], [ogr[q]])

            def f_s3(tt):
                q = tt % 2
                tk = slice(tt * 128, (tt + 1) * 128)
                for e4 in range(4):
                    self.tr(bank(7, e4 * 128, (e4 + 1) * 128), og[q][:, e4 * 128:(e4 + 1) * 128], [ogr[q]], [psr[7]])
                self.cp("act", self.glaT[:, p * 4:(p + 1) * 4, tk], bank(7).rearrange("p (e t) -> p e t", t=128),
                        [psr[7]], [self.glaTr[c][tt] for c in range(p * 4, p * 4 + 4)])

            P.op("pool", (lambda e: e.memset(S32, 0.0)), [], S32r)
            f_s1(0)
            f_R(0)
            f_gr(0)
            for tt in range(16):
                f_s2a(tt)
                if tt + 1 < 16:
                    f_s1(tt + 1, (lambda tt=tt: f_s3(tt - 1)) if tt > 0 else None,
                         (lambda tt=tt: f_gr(tt)) if tt > 0 else None)
                    f_s2(tt)
                    f_R(tt + 1)
                else:
                    f_gr(tt)
                    f_s2(tt)
                    f_s3(tt - 1)
            f_s3(15)

    def phaseS4(self, s):
        A, psr, bank = self.A, self.psr, self.bank
        sig = [A.alloc([512], F32) for _ in range(2)]
        sigr = regs(2)
        k = 0
        for grp in range(2):
            wb, wbr = self.wget("wba", grp)
            wg, wgr = self.wget("ga", grp)
            wb3 = wb.rearrange("p (c n) -> p c n", n=512)
            wg3 = wg.rearrange("p (c n) -> p c n", n=512)
            for tb in range(4):
                blk = slice(tb * 512, (tb + 1) * 512)
                for fl in range(4):
                    fc = grp * 4 + fl
                    b0, b1 = (0, 1) if k % 2 == 0 else (2, 3)
                    for c in range(8):
                        self.mm(bank(b0), wb3[:, c, fl * 128:(fl + 1) * 128], self.glaT[:, c, blk], c == 0, c == 7,
                                [wbr] + self.glaTr[c][tb * 4:(tb + 1) * 4], [psr[b0]])
                    for c in range(8):
                        self.mm(bank(b1), wg3[:, c, fl * 128:(fl + 1) * 128], self.hT[:, c, blk], c == 0, c == 7,
                                [wgr, self.hTr[c][tb]], [psr[b1]])
                    self.act(sig[k % 2], bank(b1), AF.Sigmoid, [psr[b1]], [sigr[k % 2]])
                    self.tt("dve", self.maT[:, fc, blk], bank(b0), sig[k % 2], ALU.mult, [psr[b0], sigr[k % 2]],
                            [self.maTr[fc][tb]])
                    k += 1

    def phaseS2(self, s):
        A, psr, bank = self.A, self.psr, self.bank
        hT, hTr = self.hT, self.hTr
        self.alloc_nr()
        wkv, wkvr = self.wget("akv", 0)
        w3 = wkv.rearrange("p (c n) -> p c n", n=512)
        items = []
        for tb in range(4):
            blk = slice(tb * 512, (tb + 1) * 512)
            for g in range(2):
                def proj(b, g=g, blk=blk, tb=tb):
                    for c in range(8):
                        self.mm(bank(b), w3[:, c, g * 128:(g + 1) * 128], hT[:, c, blk], c == 0, c == 7,
                                [wkvr, hTr[c][tb]], [psr[b]])
                pre = (lambda tb=tb: self.rope_top(tb)) if g == 0 else None
                items.append((proj, C_KN, self.kT[:, g, blk], [self.kTr[g][tb]], pre))
        self.normrope_pipe(items, [2, 3, 4], 5, 6)
        for tb in range(4):
            for t in range(4):
                tile = tb * 4 + t
                for c in range(8):
                    self.mm(bank(7, 0, 256), hT[:, c, tile * 128:(tile + 1) * 128], w3[:, c, 256:512], c == 0, c == 7,
                            [wkvr, hTr[c][tb]], [psr[7]])
                self.cp("act", self.vatt[:, tile, :], bank(7, 0, 256), [psr[7]], [self.vattr[tile]])
        self.alloc_xb(2, junk=(self.nr["t2"].bitcast(BF16), self.nr["r"][3]))
        mT = A.alloc([8, 256], BF16)
        mTr = regs(8)
        self.load_xb(self.mem[s], 2)
        self.norm_T(self.xb, self.xbr, 2, C_MEM, lambda c: mT[:, c, :], mTr, self.sc, [0, 1])
        for grp in range(2):
            w, wr = self.wget("xwkv", grp)
            w3 = w.rearrange("p (c n) -> p c n", n=512)
            for fl in range(4):
                kc = grp * 4 + fl
                b = 2 + (kc % 2)
                for c in range(8):
                    self.mm(bank(b, 0, 256), w3[:, c, fl * 128:(fl + 1) * 128], mT[:, c, :], c == 0, c == 7,
                            [wr, mTr[c]], [psr[b]])
                self.cp("act" if kc % 2 == 0 else "dve", self.kmT[:, kc, :], bank(b, 0, 256), [psr[b]], [self.kmTr[kc]])
        for grp in range(2):
            w, wr = self.wget("xwkv", 2 + grp)
            w3 = w.rearrange("p (c n) -> p c n", n=512)
            for mt in range(2):
                b = 2 + mt
                for c in range(8):
                    self.mm(bank(b), mT[:, c, mt * 128:(mt + 1) * 128], w3[:, c, :], c == 0, c == 7, [wr, mTr[c]], [psr[b]])
                self.cp("act" if mt == 0 else "dve", self.vm[:, mt, grp * 512:(grp + 1) * 512], bank(b), [psr[b]],
                        [self.vmr[mt]])

    def alloc_S5(self):
        A = self.A
        self.sg = [A.alloc([512], F32) for _ in range(2)]
        self.sgr = regs(2)
        self.alloc_xb(4, junk=(self.sg[1].bitcast(BF16), self.sgr[1]))
        self.xb2 = [(self.xb, self.xbr), (A.alloc([4, D], F32), regs(4))]
        self.bufA = A.alloc([8, 512], BF16)
        self.bufB = A.alloc([8, 512], BF16)
        self.bufAr, self.bufBr = regs(8), regs(8)
        self.actT = A.alloc([22, 512], BF16)
        r = regs(22)
        for j in (13, 15, 17, 19):
            r[j + 1] = r[j]
        self.actTr = r
        self.bufC = self.actT[:, 0:8, :]
        self.bufCr = r[0:8]
        self.PT = [self.actT[:, 8 + i, :] for i in range(4)]
        self.PTr = [r[8 + i] for i in range(4)]

        def f32v(j):
            return self.actT[:, j:j + 2, :].rearrange("p a b -> p (a b)").bitcast(F32)
        self.nr = {"sqb": self.actT[:, 12, :], "tmpA": f32v(13), "knf": f32v(15), "t2": f32v(17),
                   "r": [r[12], r[13], r[15], r[17]], "knf2": f32v(19), "r_kn2": r[19]}
        self.rden = self.nr["tmpA"]
        self.rdenr = self.nr["r"][1]
        self.woring = [A.alloc([1024], BF16) for _ in range(3)]
        self.woring_r = regs(3)

    def wo2issue_upto(self, n):
        n = min(n, (self.cur_seq + 1) * 4 * 22)
        while self.wo2_issued < n:
            k = self.wo2_issued
            i = k % 3
            src, L = self.wsrc("wo2", k % 22)
            self.dma("sp", self.woring[i], src, self.woring_s[i], [self.wbf_reg["wo2"]], [self.woring_r[i]])
            self.wo2_issued += 1

    def wo2get(self, g):
        k = self.wo2_i
        assert k % 22 == g
        self.wo2issue_upto(k + 2)
        self.wo2_i += 1
        i = k % 3
        return self.woring[i].rearrange("p (k n) -> p k n", n=512), self.woring_r[i]

    def proj_res(self, srcs, wseg, gain_idx, next_norm=True):
        psr, bank = self.psr, self.bank
        self.load_gp(gain_idx)
        w0, w0r = self.wget(wseg, 0)
        w1, w1r = self.wget(wseg, 1)
        ws = [(w0.rearrange("p (c n) -> p c n", n=512), w0r), (w1.rearrange("p (c n) -> p c n", n=512), w1r)]
        for t in range(4):
            k = self.pn_k
            self.pn_k = (k + 1) % 2
            for half in range(2):
                w3, wr = ws[half]
                n = len(srcs) * 8
                i = 0
                for (apf, rf) in srcs:
                    for c in range(8):
                        self.mm(bank(2 * k + half), apf(c, t), w3[:, c, :], i == 0, i == n - 1, [wr] + rf(c, t),
                                [psr[2 * k + half]])
                        i += 1
            self.postnorm(k, t, 0)
            if next_norm and t >= 1:
                self.norm_tile(t - 1)
        if next_norm:
            self.norm_tile(3)

    def phaseS5(self, s, tb):
        psr, bank = self.psr, self.bank
        pb = tb % 2
        bufs = [(self.bufA, self.bufAr), (self.bufB, self.bufBr)]
        bufA, bufAr = bufs[pb]
        bufB, bufBr = bufs[1 - pb]
        bufC, bufCr = self.bufC, self.bufCr
        self.xb, self.xbr = self.xb2[pb]
        blk = slice(tb * 512, (tb + 1) * 512)
        if tb == 0:
            self.load_xb(self.x[s, blk, :])
            for t in range(4):
                self.norm_tile(t)
            self.norm_fin(4, C_MIXPRE, lambda c: bufA[:, c, :], bufAr, [0, 1])
        self.rope_top(tb)
        items = []
        wq = [self.wget("aq", grp) for grp in range(2)]
        for head in range(8):
            w, wr = wq[head // 4]
            w3 = w.rearrange("p (c n) -> p c n", n=512)
            hl = head % 4

            def proj(b, w3=w3, wr=wr, hl=hl):
                for c in range(8):
                    self.mm(bank(b), w3[:, c, hl * 128:(hl + 1) * 128], bufA[:, c, :], c == 0, c == 7,
                            [wr, bufAr[c]], [psr[b]])
            items.append((proj, C_QN, bufB[:, head, :], [bufBr[head]], None))
        self.normrope_pipe(items, [2, 5, 6], 3, 4)
        SCL = float(128.0 ** -0.5)
        sbanks = [2, 3, 0, 1]
        LA = 3
        ptl = list(zip(self.PT, self.PTr)) + [(self.actT[:, 15, :], self.actTr[15]), (self.actT[:, 17, :], self.actTr[17])]
        NP = len(ptl)
        seq = [(head, st) for head in range(8) for st in range(16)]

        aT, ar = self.actT, self.actTr
        S1 = [(aT[:, 12, :], ar[12]), (aT[:, 21, :], ar[21]), (aT[:, 19, :], ar[19])]
        S2 = [(aT[:, 15, :], ar[15]), (aT[:, 17, :], ar[17])]

        def score(j):
            head, st = seq[j]
            g = head // 4
            sb = sbanks[j % 4]
            self.mm(bank(sb), self.kT[:, g, st * 128:(st + 1) * 128], bufB[:, head, :], True, True,
                    [self.kTr[g][st // 4], bufBr[head]], [psr[sb]])
            self.act(ptl[j % NP][0], bank(sb), AF.Exp, [psr[sb]], [ptl[j % NP][1]], scale=SCL)
            if st % 2 == 1:
                s1, s1r = S1[(j // 2) % 3]
                self.tt("dve", s1, ptl[(j - 1) % NP][0], ptl[j % NP][0], ALU.add,
                        [ptl[(j - 1) % NP][1], ptl[j % NP][1]], [s1r])

        for j0 in range(LA):
            score(j0)
        for j in range(len(seq)):
            if j + LA < len(seq):
                score(j + LA)
            head, st = seq[j]
            g = head // 4
            ob = 4 + 2 * (head % 2)
            db = ob + 1
            pt, ptr = ptl[j % NP]
            self.mm(bank(ob), self.vatt[:, st, g * 128:(g + 1) * 128], pt, st == 0, st == 15,
                    [self.vattr[st], ptr], [psr[ob]])
            if st % 2 == 1:
                s1, s1r = S1[(j // 2) % 3]
                self.mm(bank(db), self.ones, s1, st == 1, st == 15, [self.onesr, s1r], [psr[db]])
            if st == 15:
                self.act(self.rden, bank(db), AF.Ln, [psr[db]], [self.rdenr])
                self.act(self.rden, self.rden, AF.Exp, [self.rdenr], [self.rdenr], scale=-1.0)
                self.tt("dve", bufC[:, head, :], bank(ob), self.rden, ALU.mult, [psr[ob], self.rdenr], [bufCr[head]])
        k = 0
        for grp in range(2):
            wb, wbr = self.wget("wbb", grp)
            wg, wgr = self.wget("gb", grp)
            wb3 = wb.rearrange("p (c n) -> p c n", n=512)
            wg3 = wg.rearrange("p (c n) -> p c n", n=512)
            for fl in range(4):
                fc = grp * 4 + fl
                b0, b1 = (0, 1) if k % 2 == 0 else (2, 3)
                for c in range(8):
                    self.mm(bank(b0), wb3[:, c, fl * 128:(fl + 1) * 128], bufC[:, c, :], c == 0, c == 7,
                            [wbr, bufCr[c]], [psr[b0]])
                for c in range(8):
                    self.mm(bank(b1), wg3[:, c, fl * 128:(fl + 1) * 128], bufA[:, c, :], c == 0, c == 7,
                            [wgr, bufAr[c]], [psr[b1]])
                self.act(self.sg[k % 2], bank(b1), AF.Sigmoid, [psr[b1]], [self.sgr[k % 2]])
                self.tt("dve", self.sg[k % 2], bank(b0), self.sg[k % 2], ALU.mult, [psr[b0], self.sgr[k % 2]], [self.sgr[k % 2]])
                self.tt("pool", bufB[:, fc, :], self.sg[k % 2], self.maT[:, fc, blk], ALU.add,
                        [self.sgr[k % 2], self.maTr[fc][tb]], [bufBr[fc]])
                k += 1
        srcs = [(lambda c, t: bufB[:, c, t * 128:(t + 1) * 128], lambda c, t: [bufBr[c]])]
        self.proj_res(srcs, "wo", 0)
        if self.upto == "x1":
            return
        self.norm_fin(4, C_XPRE, lambda c: bufA[:, c, :], bufAr, [0, 1])
        for grp in range(2):
            w, wr = self.wget("xwq", grp)
            w3 = w.rearrange("p (c n) -> p c n", n=512)
            for fl in range(4):
                fc = grp * 4 + fl
                b = 4 + (fc % 2)
                for c in range(8):
                    self.mm(bank(b), w3[:, c, fl * 128:(fl + 1) * 128], bufA[:, c, :], c == 0, c == 7,
                            [wr, bufAr[c]], [psr[b]])
                self.cp("act" if fc % 2 == 0 else "dve", bufC[:, fc, :], bank(b), [psr[b]], [bufCr[fc]])
        for head in range(4):
            for mt in range(2):
                sb = 6 + mt
                for dc in range(2):
                    self.mm(bank(sb), self.kmT[:, head * 2 + dc, mt * 128:(mt + 1) * 128], bufC[:, head * 2 + dc, :],
                            dc == 0, dc == 1, [self.kmTr[head * 2 + dc], bufCr[head * 2 + dc]], [psr[sb]])
                pi = (head % 2) * 2 + mt
                self.act(self.PT[pi], bank(sb), AF.Exp, [psr[sb]], [self.PTr[pi]], scale=1.0 / 16.0)
            base_b = 0 if head % 2 == 0 else 3
            for dc in range(2):
                for mt in range(2):
                    pi = (head % 2) * 2 + mt
                    self.mm(bank(base_b + dc), self.vm[:, mt, head * 256 + dc * 128:head * 256 + (dc + 1) * 128],
                            self.PT[pi], mt == 0, mt == 1, [self.vmr[mt], self.PTr[pi]], [psr[base_b + dc]])
            db = base_b + 2
            for mt in range(2):
                pi = (head % 2) * 2 + mt
                self.mm(bank(db), self.ones, self.PT[pi], mt == 0, mt == 1, [self.onesr, self.PTr[pi]], [psr[db]])
            self.act(self.rden, bank(db), AF.Ln, [psr[db]], [self.rdenr])
            self.act(self.rden, self.rden, AF.Exp, [self.rdenr], [self.rdenr], scale=-1.0)
            for dc in range(2):
                self.tt("dve", bufB[:, head * 2 + dc, :], bank(base_b + dc), self.rden, ALU.mult,
                        [psr[base_b + dc], self.rdenr], [bufBr[head * 2 + dc]])
        nxb, nxbr = self.xb2[1 - pb]
        if tb < 3 and self.upto == "all":
            nblk = self.x[s, (tb + 1) * 512:(tb + 2) * 512, :]
            for t in range(4):
                self.dma("sp", nxb[:, t, :], nblk[t * 128:(t + 1) * 128, :], self.xb_slot2[1 - pb][t], [], [nxbr[t]])
        srcs = [(lambda c, t: bufB[:, c, t * 128:(t + 1) * 128], lambda c, t: [bufBr[c]])]
        self.proj_res(srcs, "xwo", 1)
        if self.upto == "x2":
            return
        self.norm_fin(4, C_FFNPRE, lambda c: bufA[:, c, :], bufAr, [0, 1])
        k = 0
        prefetch = tb < 3 and self.upto == "all"
        for i in range(11):
            w, wr = self.wget("wi", i)
            w3 = w.rearrange("p (c n) -> p c n", n=512)
            if prefetch and i == 1:
                for t in range(4):
                    self.norm_tile(t, nxb, nxbr)
            if prefetch and i == 5:
                self.norm_fin(4, C_MIXPRE, lambda c: bufB[:, c, :], bufBr, [0, 1])
            for j in range(2):
                bg, bu = (0, 1) if k % 2 == 0 else (2, 3)
                for c in range(8):
                    self.mm(bank(bg), w3[:, c, j * 128:(j + 1) * 128], bufA[:, c, :], c == 0, c == 7, [wr, bufAr[c]], [psr[bg]])
                for c in range(8):
                    self.mm(bank(bu), w3[:, c, 256 + j * 128:256 + (j + 1) * 128], bufA[:, c, :], c == 0, c == 7,
                            [wr, bufAr[c]], [psr[bu]])
                self.act(self.sg[k % 2], bank(bg), AF.Silu, [psr[bg]], [self.sgr[k % 2]])
                self.tt("dve", self.actT[:, 2 * i + j, :], bank(bu), self.sg[k % 2], ALU.mult, [psr[bu], self.sgr[k % 2]],
                        [self.actTr[2 * i + j]])
                k += 1
        self.load_gp(2)
        for hf in range(2):
            for i in range(11):
                w3, wr = self.wo2get(hf * 11 + i)
                for t in range(4):
                    for kk in range(2):
                        self.mm(bank(2 * t + hf), self.actT[:, 2 * i + kk, t * 128:(t + 1) * 128], w3[:, kk, :],
                                i == 0 and kk == 0, i == 10 and kk == 1, [wr, self.actTr[2 * i + kk]], [psr[2 * t + hf]])
        for t in range(4):
            self.postnorm(t, t, 0)
            self.dma("sp", self.y[s, tb * 512 + t * 128:tb * 512 + (t + 1) * 128, :], self.xb[:, t, :], self.y_slot[t],
                     [self.xbr[t]], [])

    def build(self):
        self.init_consts()
        A, P = self.A, self.P
        self.ssP = A.alloc([32], F32)
        self.xb_slot = [self.newslot("xb%d" % i) for i in range(4)]
        self.xb_slot2 = [self.xb_slot, [self.newslot("xbB%d" % i) for i in range(4)]]
        self.y_slot = [self.newslot("ystore%d" % i, output=True) for i in range(4)]
        self.wgla_slot = self.newslot("wgla")
        self.woring_s = [self.newslot("wo2r%d" % i) for i in range(4)]
        self.wo2_i = 0
        self.wo2_issued = 0
        self.pn_k = 0
        self.base = A.mark()
        o = A.nbytes - 56 * 1024
        self.qoff = o
        self.maT, o = A.alloc_at(o, [8, T], BF16)
        self.kT, o = A.alloc_at(o, [2, T], BF16)
        self.vatt, o = A.alloc_at(o, [16, 256], BF16)
        self.kmT, o = A.alloc_at(o, [8, 256], BF16)
        self.vm, o = A.alloc_at(o, [2, 1024], BF16)
        assert o <= A.nbytes
        up = self.upto
        for s in range(self.nseq):
            self.cur_seq = s
            self.maTr = [regs(4) for _ in range(8)]
            self.kTr = [regs(4) for _ in range(2)]
            self.vattr = regs(16)
            self.kmTr = regs(8)
            self.vmr = regs(2)
            if s > 0:
                P.fence()
            A.reset(self.base)
            self.hT = A.alloc([8, T], BF16)
            self.hTr = [regs(4) for _ in range(8)]
            m1 = A.mark()
            self.phaseA(s)
            self.issue_conv(["gla1", "wba", "ga", "akv", "xwkv"])
            if up == "A":
                tmp = A.alloc([8, T], F32)
                self.dbg_store("hT", self.hT, [r for rr in self.hTr for r in rr], tmp)
                break
            P.fence()
            A.reset(m1)
            self.glaT = A.alloc([8, T], BF16)
            self.glaTr = [regs(16) for _ in range(8)]
            m2 = A.mark()
            self.phaseGLA(s)
            P.fence()
            A.reset(m2)
            if up == "GLA":
                tmp = A.alloc([8, T], F32)
                self.dbg_store("glaT", self.glaT, [r for rr in self.glaTr for r in rr], tmp)
                break
            self.phaseS4(s)
            self.phaseS2(s)
            assert A.off <= self.qoff, (A.off, self.qoff)
            if up == "S2":
                tmp = A.alloc_at(self.base, [8, T], F32)[0]
                P.fence()
                self.dbg_store("maT", self.maT, [r for rr in self.maTr for r in rr], tmp)
                tmp2 = A.alloc_at(self.base + 65536, [2, T], F32)[0]
                self.dbg_store("kT", self.kT, [r for rr in self.kTr for r in rr], tmp2)
                tmp3 = A.alloc_at(self.base + 65536 + 16384, [16, 256], F32)[0]
                self.dbg_store("vatt", self.vatt, self.vattr, tmp3)
                tmp4 = A.alloc_at(self.base + 65536 + 32768, [8, 256], F32)[0]
                self.dbg_store("kmT", self.kmT, self.kmTr, tmp4)
                tmp5 = A.alloc_at(self.base + 65536 + 32768 + 8192, [2, 1024], F32)[0]
                self.dbg_store("vm", self.vm, self.vmr, tmp5)
                break
            P.fence()
            A.reset(self.base)
            self.alloc_S5()
            assert A.off <= self.qoff, (A.off, self.qoff)
            for tb in range(4):
                self.phaseS5(s, tb)
                if up in ("x1", "x2"):
                    for t in range(4):
                        self.dma("sp", self.y[s, tb * 512 + t * 128:tb * 512 + (t + 1) * 128, :], self.xb[:, t, :],
                                 self.y_slot[t], [self.xbr[t]], [])
        P.emit()
        return self.nc


def make_core_inputs(inp, xs, ms, shared=None):
    if shared is None:
        shared = make_shared(inp)
    d = dict(shared)
    d["x"] = np.ascontiguousarray(xs, dtype=np.float32)
    d["mem"] = np.ascontiguousarray(ms, dtype=np.float32)
    return d


def make_shared(inp):
    gpost = np.stack([np.broadcast_to(inp[k][0][None, :], (128, D)) for k in ("ln_mix_post", "ln_x_post", "ln_ffn_post")], 0)
    gn = inp["gla_norm"][0]
    gnorm2 = np.broadcast_to(np.concatenate([gn, gn])[None, :], (128, 512))
    waug = np.zeros((17, 1024), np.float32)
    waug[:16, :512] = inp["gla_wa_f"][0]
    waug[:16, 512:] = inp["gla_wa_b"][0]
    waug[16, :512] = inp["gla_ba_f"][0]
    waug[16, 512:] = inp["gla_ba_b"][0]
    return {
        "wall": build_wall(inp),
        "kpack": build_kpack(),
        "cpack": build_cpack(inp),
        "gpost": np.ascontiguousarray(gpost, dtype=np.float32),
        "gnorm2": np.ascontiguousarray(gnorm2, dtype=np.float32),
        "waug": waug,
    }


_CACHE = {}


def kernel(**inputs):
    inp = {k: np.asarray(v) for k, v in inputs.items()}
    xs = np.concatenate([inp["x_prompt"], inp["x_sample"]], 0)
    ms = np.concatenate([inp["mem_prompt"], inp["mem_sample"]], 0)
    nb = xs.shape[0]
    assert nb == NCORES * SEQ_PER_CORE
    shared = make_shared(inp)
    if "nc" not in _CACHE:
        _CACHE["nc"] = Builder(nseq=SEQ_PER_CORE, upto="all").build()
    nc = _CACHE["nc"]
    in_maps = []
    for c in range(NCORES):
        sl = slice(c * SEQ_PER_CORE, (c + 1) * SEQ_PER_CORE)
        in_maps.append(make_core_inputs(inp, xs[sl], ms[sl], shared))
    res = run_bass_kernel_spmd(nc, in_maps, core_ids=list(range(NCORES)))
    y = np.concatenate([np.asarray(r["y"]) for r in res.results], 0).astype(np.float32)
    nprompt = inp["x_prompt"].shape[0]
    return (np.ascontiguousarray(y[:nprompt]), np.ascontiguousarray(y[nprompt:]))
```

```python
import numpy as np
import concourse.bass as bass
import concourse.mybir as mybir
from concourse.bass_utils import run_bass_kernel_spmd

F32 = mybir.dt.float32
BF16 = mybir.dt.bfloat16
AF = mybir.ActivationFunctionType
ALU = mybir.AluOpType

T = 2048
D = 1024
NMEM = 256
EPS = 1e-6
DFF = 2816
NCORES = 8
SEQ_PER_CORE = 3

ENGS = ("pe", "act", "dve", "pool", "sp")
SAME_ENGINE_FULL_SYNC = True


class Reg:
    __slots__ = ("w", "r")

    def __init__(self):
        self.w = None
        self.r = {}


def regs(n):
    return [Reg() for _ in range(n)]


class Slot:
    def __init__(self, nc, name):
        self.sem = nc.alloc_semaphore(name)
        self.n = 0


class Prog:
    def __init__(self, nc):
        self.nc = nc
        self.ops = []
        self.by_eng = {e: [] for e in ENGS}
        self.sems = {e: nc.alloc_semaphore("sem_" + e) for e in ENGS}
        self.out_slots = []
        self.fence_deps = set()
        self.fence_pending = set()
        self.dma_since = []

    def fence(self):
        deps = set(self.dma_since)
        for e in ENGS:
            for oid in reversed(self.by_eng[e]):
                if self.ops[oid][3] is None:
                    deps.add(oid)
                    break
        self.fence_deps = deps
        self.fence_pending = set(ENGS)
        self.dma_since = []

    def op(self, eng, fn, reads=(), writes=(), slot=None):
        oid = len(self.ops)
        deps = set()
        is_dma = slot is not None
        if eng in self.fence_pending:
            self.fence_pending.discard(eng)
            for d in self.fence_deps:
                if self.ops[d][3] is None and self.ops[d][0] == eng and not is_dma:
                    continue
                deps.add(d)
        if is_dma:
            self.dma_since.append(oid)

        def add(pid, kind):
            peng, _, _, pslot, _ = self.ops[pid]
            if pslot is None and not is_dma and peng == eng:
                if eng == "pe" or (kind != "raw" and not SAME_ENGINE_FULL_SYNC):
                    return
            deps.add(pid)

        for r in reads:
            if r.w is not None:
                add(r.w, "raw")
        for w in writes:
            if w.w is not None:
                add(w.w, "waw")
            for pid in w.r.values():
                add(pid, "war")
        key = ("dma", oid) if is_dma else eng
        for r in reads:
            r.r[key] = oid
        for w in writes:
            w.w = oid
            w.r = {}
        val = None
        if is_dma:
            slot.n += 1
            val = 16 * slot.n
        self.ops.append((eng, fn, deps, slot, val))
        self.by_eng[eng].append(oid)
        return oid

    def emit(self):
        nc = self.nc
        ops = self.ops
        marked = set()
        for (_, _, deps, _, _) in ops:
            for d in deps:
                if ops[d][3] is None:
                    marked.add(d)
        tok = {}
        for e in ENGS:
            cnt = 0
            for oid in self.by_eng[e]:
                eng, fn, deps, slot, val = ops[oid]
                if slot is not None:
                    tok[oid] = (slot.sem, val)
                elif oid in marked:
                    cnt += 1
                    tok[oid] = (self.sems[e], cnt)
        self.nmarked = len(marked)
        final_waits = [(s.sem, 16 * s.n) for s in self.out_slots if s.n > 0]
        handles = {"pe": "tensor", "act": "scalar", "dve": "vector", "pool": "gpsimd", "sp": "sync"}

        def run_engine(e, engine):
            waited = {}
            for oid in self.by_eng[e]:
                eng, fn, deps, slot, val = ops[oid]
                needs = {}
                for d in deps:
                    sem, v = tok[d]
                    k = sem.num
                    if waited.get(k, 0) < v and needs.get(k, (None, 0))[1] < v:
                        needs[k] = (sem, v)
                for k, (sem, v) in needs.items():
                    engine.wait_ge(sem, v)
                    waited[k] = v
                inst = fn(engine)
                if slot is not None:
                    inst.then_inc(slot.sem, 16)
                elif oid in marked:
                    inst.then_inc(self.sems[e], 1)
            if e == "sp":
                for sem, v in final_waits:
                    engine.wait_ge(sem, v)

        with nc.Block() as block:
            for e in ENGS:
                def mk(e=e):
                    def body(engine):
                        run_engine(e, engine)
                    return body
                getattr(block, handles[e])(mk())


class Arena:
    def __init__(self, nc, nbytes):
        self.t = nc.alloc_sbuf_tensor("arena", [128, nbytes // 2], BF16)
        self.nbytes = nbytes
        self.off = 0
        self.peak = 0

    def alloc(self, shape, dtype, parts=128):
        esz = 4 if dtype == F32 else 2
        n = int(np.prod(shape))
        nb = (n * esz + 31) // 32 * 32
        o = self.off
        self.off += nb
        self.peak = max(self.peak, self.off)
        assert self.off <= self.nbytes, f"arena overflow {self.off} > {self.nbytes}"
        v = self.t[0:parts, o // 2:(o + n * esz) // 2]
        if dtype == F32:
            v = v.bitcast(F32)
        if len(shape) == 2:
            v = v.rearrange("p (a b) -> p a b", b=shape[1])
        elif len(shape) == 3:
            v = v.rearrange("p (a b c) -> p a b c", b=shape[1], c=shape[2])
        return v

    def alloc_at(self, off, shape, dtype, parts=128):
        save = self.off
        self.off = off
        v = self.alloc(shape, dtype, parts)
        end = self.off
        self.off = save
        return v, end

    def mark(self):
        return self.off

    def reset(self, m):
        self.off = m


def _seg(W, groups):
    K = W.shape[0]
    KC = K // 128
    out = []
    for cols in groups:
        t = W[:, cols].reshape(KC, 128, len(cols)).transpose(1, 0, 2)
        out.append(np.ascontiguousarray(t).reshape(128, KC * len(cols)))
    return np.stack(out, 0)


def _ar(a, b):
    return np.arange(a, b)


SEG_ORDER = ["z", "gla0", "gla1", "wba", "ga", "akv", "xwkv", "aq", "wbb", "gb", "wo", "xwq", "xwo", "wi", "wo2"]
SEG_SHAPE = {
    "z": (1, 8 * 32), "akv": (1, 8 * 512), "gla0": (1, 8 * 1536), "gla1": (1, 8 * 1536),
    "ga": (2, 4096), "wba": (2, 4096), "aq": (2, 4096), "gb": (2, 4096), "wbb": (2, 4096),
    "wo": (2, 4096), "xwkv": (4, 4096), "xwq": (2, 4096), "xwo": (2, 4096),
    "wi": (11, 4096), "wo2": (22, 1024),
}


def seg_offsets():
    off = {}
    o = 0
    for s in SEG_ORDER:
        ng, L = SEG_SHAPE[s]
        off[s] = o
        o += ng * 128 * L
    return off, o


def build_wall(inp):
    w_in = inp["w_in"][0]
    segs = {}
    for p in range(2):
        h0, h1 = 2 * p, 2 * p + 1
        cols = np.concatenate([
            _ar(h0 * 128, h0 * 128 + 128), _ar(h1 * 128, h1 * 128 + 128),
            _ar(512 + h0 * 128, 512 + h0 * 128 + 128), _ar(512 + h1 * 128, 512 + h1 * 128 + 128),
            _ar(1024 + h0 * 256, 1024 + h0 * 256 + 256), _ar(1024 + h1 * 256, 1024 + h1 * 256 + 256),
            _ar(2048 + h0 * 256, 2048 + h0 * 256 + 256), _ar(2048 + h1 * 256, 2048 + h1 * 256 + 256)])
        segs["gla%d" % p] = _seg(w_in, [cols])
    segs["z"] = _seg(w_in, [_ar(3072, 3104)])
    segs["akv"] = _seg(w_in, [_ar(4128, 4640)])
    segs["aq"] = _seg(w_in, [_ar(3104, 3616), _ar(3616, 4128)])
    segs["ga"] = _seg(w_in, [_ar(4640, 5152), _ar(5152, 5664)])
    segs["gb"] = _seg(w_in, [_ar(5664, 6176), _ar(6176, 6688)])
    two = [_ar(0, 512), _ar(512, 1024)]
    segs["wba"] = _seg(inp["w_branch_gla"][0], two)
    segs["wbb"] = _seg(inp["w_branch_att"][0], two)
    segs["wo"] = _seg(inp["w_out"][0], two)
    segs["xwq"] = _seg(inp["x_wq"][0], two)
    segs["xwo"] = _seg(inp["x_wo"][0], two)
    segs["xwkv"] = _seg(inp["x_wkv"][0], [_ar(i * 512, (i + 1) * 512) for i in range(4)])
    wi = inp["ffn_wi"][0]
    segs["wi"] = _seg(wi, [np.concatenate([_ar(256 * i, 256 * i + 256), _ar(DFF + 256 * i, DFF + 256 * i + 256)])
                           for i in range(11)])
    w2 = inp["ffn_wo"][0]
    pcs = []
    for half in range(2):
        for i in range(11):
            blk = w2[256 * i:256 * i + 256, half * 512:(half + 1) * 512]
            pcs.append(np.ascontiguousarray(blk.reshape(2, 128, 512).transpose(1, 0, 2)).reshape(128, 1024))
    segs["wo2"] = np.stack(pcs, 0)
    off, tot = seg_offsets()
    wall = np.empty(tot, np.float32)
    for s in SEG_ORDER:
        ng, L = SEG_SHAPE[s]
        a = segs[s]
        assert a.shape == (ng, 128, L), (s, a.shape)
        wall[off[s]:off[s] + a.size] = a.reshape(-1)
    return wall.reshape(-1, 2048)


K_ID = 0
K_UF = 128
K_WF = 256
K_UB = 384
K_WB = 512
K_MASK = 640
K_ROT = 1152
K_COS = 1280
K_SIN = 1376
NK = 1472


def build_kpack():
    k = np.zeros((128, NK), np.float32)
    j = np.arange(128)[:, None]
    i = np.arange(128)[None, :]
    k[:, K_ID:K_ID + 128] = (j == i)
    c = -1.0 / 16.0
    k[:, K_UF:K_UF + 128] = c * (j <= i)
    k[:, K_WF:K_WF + 128] = c * (j > i)
    k[:, K_UB:K_UB + 128] = c * (j >= i)
    k[:, K_WB:K_WB + 128] = c * (j < i)
    mf = (j <= i).astype(np.float32)
    mb = (j > i).astype(np.float32)
    k[:, K_MASK:K_MASK + 512] = np.concatenate([mf, mf, mb, mb], 1)
    R = np.zeros((128, 128), np.float32)
    for m in range(128):
        if (m % 64) < 32:
            R[m + 32, m] = -1.0
        else:
            R[m - 32, m] = 1.0
    k[:, K_ROT:K_ROT + 128] = R
    inv = (10000.0 ** (-np.arange(0, 64, 2, dtype=np.float32) / np.float32(64))).astype(np.float32)
    for d in range(128):
        f = inv[d % 32]
        if d < 64:
            ang = (np.arange(32, dtype=np.float32) * f).astype(np.float32)
            k[d, K_COS:K_COS + 32] = np.cos(ang)
            k[d, K_SIN:K_SIN + 32] = np.sin(ang)
        else:
            ang = (np.arange(64, dtype=np.float32) * f).astype(np.float32)
            k[d, K_COS + 32:K_COS + 96] = np.cos(ang)
            k[d, K_SIN + 32:K_SIN + 96] = np.sin(ang)
    return k


C_MIXPRE, C_XPRE, C_FFNPRE, C_MEM, C_QN, C_KN = 0, 8, 16, 24, 32, 33
NCP = 34


def build_cpack(inp):
    c = np.zeros((128, NCP), np.float32)
    c[:, C_MIXPRE:C_MIXPRE + 8] = inp["ln_mix_pre"][0].reshape(8, 128).T
    c[:, C_XPRE:C_XPRE + 8] = inp["ln_x_pre"][0].reshape(8, 128).T
    c[:, C_FFNPRE:C_FFNPRE + 8] = inp["ln_ffn_pre"][0].reshape(8, 128).T
    c[:, C_MEM:C_MEM + 8] = inp["ln_mem"][0].reshape(8, 128).T
    c[:, C_QN] = inp["att_q_norm"][0]
    c[:, C_KN] = inp["att_k_norm"][0]
    return c


class Builder:
    def __init__(self, nseq=SEQ_PER_CORE, upto="all", dbg=None):
        self.nseq = nseq
        self.upto = upto
        self.dbg = dbg or {}
        nc = bass.Bass("TRN2", target_bir_lowering=False)
        self.nc = nc
        self.P = Prog(nc)
        self.soff, self.wtot = seg_offsets()
        self.x = nc.dram_tensor("x", [nseq, T, D], F32, kind="ExternalInput").ap()
        self.mem = nc.dram_tensor("mem", [nseq, NMEM, D], F32, kind="ExternalInput").ap()
        self.wall = nc.dram_tensor("wall", [self.wtot // 2048, 2048], F32, kind="ExternalInput").ap()
        self.kpack = nc.dram_tensor("kpack", [128, NK], F32, kind="ExternalInput").ap()
        self.cpack = nc.dram_tensor("cpack", [128, NCP], F32, kind="ExternalInput").ap()
        self.gpost = nc.dram_tensor("gpost", [3, 128, D], F32, kind="ExternalInput").ap()
        self.gnorm2 = nc.dram_tensor("gnorm2", [128, 512], F32, kind="ExternalInput").ap()
        self.waug = nc.dram_tensor("waug", [17, 1024], F32, kind="ExternalInput").ap()
        self.y = nc.dram_tensor("y", [nseq, T, D], F32, kind="ExternalOutput").ap()
        self.wbf = nc.dram_tensor("wbf", [self.wtot // 2048, 2048], BF16).ap()
        self.wbf_flat = self.wbf.rearrange("r c -> (r c)")
        self.wbf_reg = {s: Reg() for s in SEG_ORDER}
        self.dbg_out = {}
        for name, shape in self.dbg.items():
            self.dbg_out[name] = nc.dram_tensor("dbg_" + name, list(shape), F32, kind="ExternalOutput").ap()
        self.A = Arena(nc, 207 * 1024)
        self.PS = [nc.alloc_psum_tensor("ps%d" % i, [128, 1024], F32).ap() if False else
                   nc.alloc_psum_tensor("ps%d" % i, [128, 1024], F32) for i in range(4)]
        self.psr = regs(8)

    def bank(self, b, lo=0, hi=512):
        return self.PS[b // 2][:, (b % 2) * 512 + lo:(b % 2) * 512 + hi]

    def mm(self, out, lhsT, rhs, start, stop, reads, writes):
        self.P.op("pe", lambda e: e.matmul(out, lhsT, rhs, start=start, stop=stop), reads, writes)

    def tr(self, out, in_, reads, writes):
        ident = self.K[:, K_ID:K_ID + 128]
        self.P.op("pe", lambda e: e.transpose(out, in_, ident), list(reads) + [self.Kr], writes)

    def act(self, out, in_, func, reads, writes, scale=1.0, bias=0.0, accum=None):
        def fn(e):
            kw = {}
            if accum is not None:
                kw["accum_out"] = accum
            return e.activation(out, in_, func, bias=bias, scale=scale, **kw)
        self.P.op("act", fn, reads, writes)

    def tt(self, eng, out, in0, in1, op, reads, writes):
        self.P.op(eng, lambda e: e.tensor_tensor(out, in0, in1, op), reads, writes)

    def ts(self, eng, out, in0, s1, op0, reads, writes, s2=None, op1=None):
        if op1 is None:
            self.P.op(eng, lambda e: e.tensor_scalar(out, in0, s1, None, op0), reads, writes)
        else:
            self.P.op(eng, lambda e: e.tensor_scalar(out, in0, s1, s2, op0, op1), reads, writes)

    def stt(self, out, in0, scalar, in1, op0, op1, reads, writes):
        self.P.op("dve", lambda e: e.scalar_tensor_tensor(out, in0, scalar, in1, op0, op1), reads, writes)

    def cp(self, eng, out, in_, reads, writes):
        if eng == "act":
            self.P.op("act", lambda e: e.activation(out, in_, AF.Copy), reads, writes)
        else:
            self.P.op(eng, lambda e: e.tensor_copy(out, in_), reads, writes)

    def dma(self, q, out, in_, slot, reads, writes):
        self.P.op(q, lambda e: e.dma_start(out, in_), reads, writes, slot=slot)

    def newslot(self, name, output=False):
        s = Slot(self.nc, name)
        if output:
            self.P.out_slots.append(s)
        return s

    def wsrc(self, seg, g):
        ng, L = SEG_SHAPE[seg]
        o = self.soff[seg] + g * 128 * L
        return self.wbf_flat[o:o + 128 * L].rearrange("(p l) -> p l", p=128), L

    def init_consts(self):
        A, nc = self.A, self.nc
        self.K = A.alloc([NK], F32)
        self.Kr = Reg()
        self.C = A.alloc([NCP], F32)
        self.Cr = Reg()
        self.G2 = A.alloc([512], F32)
        self.G2r = Reg()
        self.gp = A.alloc([D], F32)
        self.gpr = Reg()
        self.gp_slot = self.newslot("gp")
        self.waug_f = A.alloc_at(A.nbytes - 56 * 1024, [1024], F32, parts=17)[0]
        self.waug_b = A.alloc([1024], BF16, parts=17)
        self.waugr = Reg()
        self.ones = A.alloc([128], BF16)
        self.onesr = Reg()
        self.cosB = A.alloc([512], F32)
        self.sinB = A.alloc([512], F32)
        self.csr = Reg()
        s = self.newslot("c0")
        self.dma("sp", self.K, self.kpack, s, [], [self.Kr])
        self.Kbf = A.alloc([512], BF16)
        self.cp("dve", self.Kbf, self.K[:, K_UF:K_UF + 512], [self.Kr], [self.Kr])
        self.ident_bf = A.alloc([128], BF16)
        self.cp("dve", self.ident_bf, self.K[:, K_ID:K_ID + 128], [self.Kr], [self.Kr])
        s = self.newslot("c1")
        self.dma("sp", self.C, self.cpack, s, [], [self.Cr])
        s = self.newslot("c2")
        self.dma("sp", self.G2, self.gnorm2, s, [], [self.G2r])
        s = self.newslot("c3")
        r0 = Reg()
        self.dma("sp", self.waug_f, self.waug, s, [], [r0])
        self.cp("dve", self.waug_b, self.waug_f, [r0], [self.waugr])
        self.P.op("pool", lambda e: e.memset(self.ones, 1.0), [], [self.onesr])
        self.epsv = A.alloc([8], F32)
        self.epsr = Reg()
        self.P.op("pool", lambda e: e.memset(self.epsv[:, 0:1], EPS), [], [self.epsr])
        self.P.op("pool", lambda e: e.memset(self.epsv[:, 1:2], 1.0), [], [self.epsr])
        self.conv_done = set()
        for tab, kc in ((self.cosB, K_COS), (self.sinB, K_SIN)):
            src = self.K[64:128, kc + 32:kc + 96].unsqueeze(1).broadcast_to([64, 8, 64])
            dst = tab[64:128, :].rearrange("p (r c) -> p r c", c=64)
            self.P.op("pool", (lambda e, dst=dst, src=src: e.tensor_copy(dst, src)), [self.Kr], [self.csr])
        self.nring = 4
        self.ring = [A.alloc([4096], BF16) for _ in range(self.nring)]
        self.ring_r = regs(self.nring)
        self.ring_s = [self.newslot("ring%d" % i) for i in range(self.nring)]
        self.make_sched()

    def issue_conv(self, names, after=()):
        for sname in names:
            if sname in self.conv_done:
                continue
            self.conv_done.add(sname)
            ng, L = SEG_SHAPE[sname]
            r_lo = self.soff[sname] // 2048
            r_hi = (self.soff[sname] + ng * 128 * L) // 2048
            s = self.newslot("cv_" + sname)
            src = self.wall[r_lo:r_hi, :]
            dst = self.wbf[r_lo:r_hi, :]
            self.dma("pool", dst, src, s, list(after), [self.wbf_reg[sname]])

    def make_sched(self):
        L = []
        for s in range(self.nseq):
            L.append(("z", 0))
            L += [("wba", 0), ("ga", 0), ("wba", 1), ("ga", 1)]
            L += [("akv", 0)] + [("xwkv", i) for i in range(4)]
            for tb in range(4):
                L += [("aq", 0), ("aq", 1), ("wbb", 0), ("gb", 0), ("wbb", 1), ("gb", 1), ("wo", 0), ("wo", 1),
                      ("xwq", 0), ("xwq", 1), ("xwo", 0), ("xwo", 1)] + [("wi", i) for i in range(11)]
        self.wsched = L
        self.wptr = 0
        self.wissued = 0

    def wissue_upto(self, n):
        n = min(n, len(self.wsched))
        while self.wissued < n:
            k = self.wissued
            seg, g = self.wsched[k]
            i = k % self.nring
            src, L = self.wsrc(seg, g)
            assert seg in self.conv_done, seg
            self.dma("sp", self.ring[i][:, 0:L], src, self.ring_s[i], [self.wbf_reg[seg]], [self.ring_r[i]])
            self.wissued += 1

    def wget(self, seg, g, prefetch_only=False):
        if prefetch_only:
            return None
        if self.upto != "all":
            while self.wsched[self.wptr] != (seg, g):
                self.wptr += 1
            self.wissued = max(self.wissued, self.wptr)
        k = self.wptr
        assert self.wsched[k] == (seg, g), (k, self.wsched[k], seg, g)
        self.wissue_upto(k + self.nring - 1)
        self.wptr += 1
        i = k % self.nring
        return self.ring[i], self.ring_r[i]

    def rope_top(self, tb):
        for tab, kc in ((self.cosB, K_COS), (self.sinB, K_SIN)):
            src = self.K[0:64, kc + tb * 8:kc + tb * 8 + 8].unsqueeze(2).broadcast_to([64, 8, 64])
            dst = tab[0:64, :].rearrange("p (r c) -> p r c", c=64)
            self.P.op("pool", (lambda e, dst=dst, src=src: e.tensor_copy(dst, src)), [self.Kr], [self.csr])

    def norm_tile(self, t, xb=None, xbr=None):
        sc = self.sc
        if xb is None:
            xb, xbr = self.xb, self.xbr
        ss, nsr, xn, xnr = sc["ss"], sc["nsr"], sc["xn"], sc["xnr"]
        self.act(sc["junk"], xb[:, t, :], AF.Square, [xbr[t]], [sc["junkr"], nsr[t][0]], accum=ss[:, t:t + 1])
        self.act(ss[:, 4 + t:5 + t], ss[:, t:t + 1], AF.Ln, [nsr[t][0], self.epsr], [nsr[t][1]], scale=1.0 / D,
                 bias=self.epsv[:, 0:1])
        self.act(ss[:, 8 + t:9 + t], ss[:, 4 + t:5 + t], AF.Exp, [nsr[t][1]], [nsr[t][2]], scale=-0.5)
        self.ts("dve", xn[:, t, :], xb[:, t, :], ss[:, 8 + t:9 + t], ALU.mult, [xbr[t], nsr[t][2]], [xnr[t]])

    def bank_bf(self, b, lo, hi):
        v = self.PS[b // 2][:, :].bitcast(BF16)
        o = (b % 2) * 1024
        return v[:, o + lo:o + hi]

    def norm_fin(self, nt, gcol, dst, dst_regs, banks):
        xn, xnr = self.sc["xn"], self.sc["xnr"]
        ident = self.ident_bf
        for c in range(8):
            b = banks[c % len(banks)]
            for t in range(nt):
                o_ap = self.bank_bf(b, t * 128, (t + 1) * 128)
                i_ap = xn[:, t, c * 128:(c + 1) * 128]
                self.P.op("pe", (lambda e, o_ap=o_ap, i_ap=i_ap: e.transpose(o_ap, i_ap, ident)),
                          [xnr[t], self.Kr], [self.psr[b]])
            g = self.C[:, gcol + c:gcol + c + 1]
            src = self.bank_bf(b, 0, nt * 128)
            if c % 2 == 0:
                self.P.op("act", (lambda e, o=dst(c), i=src, g=g:
                                  e.activation(o, i, AF.Copy, scale=g)), [self.psr[b], self.Cr], [dst_regs[c]])
            else:
                self.ts("dve", dst(c), src, g, ALU.mult, [self.psr[b], self.Cr], [dst_regs[c]])

    def norm_T(self, xb, xbr, nt, gcol, dst, dst_regs, scratch, banks):
        for t in range(nt):
            self.norm_tile(t)
        self.norm_fin(nt, gcol, dst, dst_regs, banks)

    def eps_ap(self):
        return self.epsv[:, 0:1]

    def pbank(self, b, lo=0, hi=512, p0=0, p1=128):
        return self.PS[b // 2][p0:p1, (b % 2) * 512 + lo:(b % 2) * 512 + hi]

    def alloc_xb(self, nt=4, junk=None, need_tmp=False):
        A = self.A
        self.xb = A.alloc([nt, D], F32)
        self.xbr = regs(nt)
        if junk is None:
            junk = (A.alloc([D], BF16), Reg())
        self.sc = {"junk": junk[0], "junkr": junk[1], "ss": self.ssP, "nsr": [regs(3) for _ in range(4)],
                   "pnr": [regs(3) for _ in range(4)],
                   "xn": A.alloc([nt, D], BF16), "xnr": regs(nt)}
        if need_tmp:
            self.sc["tmp32"] = [A.alloc([D], F32) for _ in range(2)]
            self.sc["tmp32r"] = regs(2)

    def load_xb(self, src_rows, nt=4, extra_w=(), slots=None):
        slots = slots or self.xb_slot
        for t in range(nt):
            self.dma("sp", self.xb[:, t, :], src_rows[t * 128:(t + 1) * 128, :], slots[t], [],
                     [self.xbr[t]] + list(extra_w))

    def dbg_store(self, name, src_ap, src_regs, f32tmp=None):
        if name not in self.dbg_out:
            return
        dst = self.dbg_out[name]
        s = self.newslot("dbg_" + name, output=True)
        if f32tmp is not None:
            r = Reg()
            self.cp("dve", f32tmp, src_ap, src_regs, [r])
            self.dma("sp", dst, f32tmp, s, [r], [])
        else:
            self.dma("sp", dst, src_ap, s, src_regs, [])

    def alloc_nr(self):
        A = self.A
        self.nr = {"sqb": A.alloc([512], BF16), "tmpA": A.alloc([512], F32), "knf": A.alloc([512], F32),
                   "t2": A.alloc([512], F32), "r": regs(4), "knf2": A.alloc([512], F32), "r_kn2": Reg()}

    def normrope_pipe(self, items, pbanks, bs, br):
        nr = self.nr
        sqb, tmpA, t2 = nr["sqb"], nr["tmpA"], nr["t2"]
        knfs = [nr["knf"], nr["knf2"]]
        r_sq, r_tmp, r_kn0, r_t2 = nr["r"]
        r_kns = [r_kn0, nr["r_kn2"]]
        psr = self.psr
        n = len(items)
        nb = len(pbanks)
        assert nb >= 3

        def stA(i):
            items[i][0](pbanks[i % nb])

        def stB(i):
            b = pbanks[i % nb]
            src, src_reg = self.bank(b), psr[b]
            gcol = items[i][1]
            knf, r_kn = knfs[i % 2], r_kns[i % 2]
            self.act(sqb, src, AF.Square, [src_reg], [r_sq])
            self.mm(self.bank(bs), self.ones, sqb, True, True, [self.onesr, r_sq], [psr[bs]])
            self.act(tmpA, self.bank(bs), AF.Ln, [psr[bs], self.epsr], [r_tmp], scale=1.0 / 128, bias=self.epsv[:, 0:1])
            self.act(tmpA, tmpA, AF.Exp, [r_tmp], [r_tmp], scale=-0.5)
            self.stt(knf, src, self.C[:, gcol:gcol + 1], tmpA, ALU.mult, ALU.mult, [src_reg, self.Cr, r_tmp], [r_kn])

        def stC(i):
            _, gcol, out, out_regs, pre = items[i]
            knf, r_kn = knfs[i % 2], r_kns[i % 2]
            if pre is not None:
                pre()
            self.mm(self.bank(br), self.K[:, K_ROT:K_ROT + 128], knf, True, True, [self.Kr, r_kn], [psr[br]])
            self.tt("dve", t2, self.bank(br), self.sinB, ALU.mult, [psr[br], self.csr], [r_t2])
            self.tt("pool", knf, knf, self.cosB, ALU.mult, [r_kn, self.csr], [r_kn])
            self.tt("pool", out, knf, t2, ALU.add, [r_kn, r_t2], out_regs)

        for step in range(n + 2):
            if step < n:
                stA(step)
            if 0 <= step - 1 < n:
                stB(step - 1)
            if 0 <= step - 2 < n:
                stC(step - 2)

    def postnorm(self, k, t, ssr_i):
        sc = self.sc
        ss = sc["ss"]
        r0, r1, r2 = sc["pnr"][t]
        u = self.PS[k][:, :]
        pr = [self.psr[2 * k], self.psr[2 * k + 1]]
        self.act(sc["junk"], u, AF.Square, pr, [sc["junkr"], r0], accum=ss[:, 16 + t:17 + t])
        self.act(ss[:, 20 + t:21 + t], ss[:, 16 + t:17 + t], AF.Ln, [r0, self.epsr], [r1], scale=1.0 / D,
                 bias=self.epsv[:, 0:1])
        self.act(ss[:, 24 + t:25 + t], ss[:, 20 + t:21 + t], AF.Exp, [r1], [r2], scale=-0.5)
        tmp, tmpr = sc["tmp32"][t % 2], sc["tmp32r"][t % 2]
        self.stt(tmp, u, ss[:, 24 + t:25 + t], self.gp, ALU.mult, ALU.mult, pr + [r2, self.gpr], [tmpr])
        self.tt("dve", self.xb[:, t, :], self.xb[:, t, :], tmp, ALU.add, [self.xbr[t], tmpr], [self.xbr[t]])

    def load_gp(self, idx):
        self.dma("sp", self.gp, self.gpost[idx], self.gp_slot, [], [self.gpr])

    def phaseA(self, s):
        sets = []
        for i in range(2):
            self.alloc_xb(4)
            if i == 1:
                self.sc["nsr"], self.sc["pnr"] = sets[0][2]["nsr"], sets[0][2]["pnr"]
            sets.append((self.xb, self.xbr, self.sc))

        def load(tb):
            self.xb, self.xbr, self.sc = sets[tb % 2]
            if s == 0 and tb == 0:
                start = Reg()
                self.load_xb(self.x[s, 0:512, :], extra_w=[start], slots=self.xb_slot2[tb % 2])
                self.issue_conv(["z", "gla0"], after=[start])
            else:
                self.load_xb(self.x[s, tb * 512:(tb + 1) * 512, :], slots=self.xb_slot2[tb % 2])

        load(0)
        for tb in range(4):
            if tb + 1 < 4:
                load(tb + 1)
            self.xb, self.xbr, self.sc = sets[tb % 2]
            self.norm_T(self.xb, self.xbr, 4, C_MIXPRE,
                        lambda c, tb=tb: self.hT[:, c, tb * 512:(tb + 1) * 512],
                        [self.hTr[c][tb] for c in range(8)], self.sc, [0, 1] if tb % 2 == 0 else [2, 3])

    def phaseGLA(self, s):
        A, P, psr, bank, pbank = self.A, self.P, self.psr, self.bank, self.pbank
        hT, hTr = self.hT, self.hTr
        Kc = self.K
        Kb = self.Kbf
        zT = [A.alloc([T], BF16, parts=17) for _ in range(2)]
        zTr = [regs(4), regs(4)]
        for d in range(2):
            P.op("pool", (lambda e, z=zT[d]: e.memset(z, 1.0)), [], zTr[d])
        wz, wzr = self.wget("z", 0)
        wz3 = wz[:, 0:256].rearrange("p (c n) -> p c n", n=32)
        for tb in range(4):
            blk = slice(tb * 512, (tb + 1) * 512)
            for d in range(2):
                b = d
                for c in range(8):
                    self.mm(pbank(b, 0, 512, 0, 16), wz3[:, c, d * 16:(d + 1) * 16], hT[:, c, blk], c == 0, c == 7,
                            [wzr, hTr[c][tb]], [psr[b]])
                self.cp("dve", zT[d][0:16, blk], pbank(b, 0, 512, 0, 16), [psr[b]], [zTr[d][tb]])
        wgla = A.alloc([8 * 1536], BF16)
        wglar = Reg()
        wg3 = wgla.rearrange("p (c n) -> p c n", n=1536)
        Sb = A.alloc([16, 512], BF16)
        Sbr = regs(16)
        S32 = A.alloc([512], F32)
        S32r = regs(2)
        Sfbf = A.alloc([512], BF16)
        Sfbfr = Reg()

        def dbl(shape, dt):
            return [A.alloc(shape, dt) for _ in range(2)], regs(2)
        def sgl(shape, dt):
            a, r = A.alloc(shape, dt), Reg()
            return [a, a], [r, r]
        ef, efr = sgl([512], F32)
        sp, spr = sgl([512], BF16)
        E1, E1r = dbl([512], F32)
        E2, E2r = sgl([512], F32)
        E3, E3r = sgl([256], F32)
        vst = A.alloc([16, 512], BF16)
        vstr = regs(16)
        qgf, qgfr = dbl([256], BF16)
        qgb, qgbr = dbl([256], BF16)
        kgf, kgfr = dbl([256], BF16)
        kgb, kgbr = dbl([256], BF16)
        kend, kendr = dbl([256], BF16)
        gr, grr = dbl([512], F32)
        og, ogr = dbl([512], F32)
        decb, decbr = dbl([8], F32)
        AT = A.alloc([512], BF16)
        ATr = Reg()
        ss2 = A.alloc([8], F32)
        ss2r = regs(3)
        junk = AT[:, 0:256]
        junkr = ATr
        one_ap = self.epsv[:, 1:2]
        SC = float(128.0 ** -0.5)
        for p in range(2):
            src, L = self.wsrc("gla%d" % p, 0)
            self.dma("sp", wgla, src, self.wgla_slot, [self.wbf_reg["gla%d" % p]], [wglar])
            self.issue_conv(["aq", "wbb", "gb", "wo", "xwq", "xwo"] if p == 0 else ["wi", "wo2"])
            fo = p * 256
            bo = 512 + p * 256

            def b_s1(tt):
                q = tt % 2
                tk = slice(tt * 128, (tt + 1) * 128)
                tb = tt // 4
                hr = lambda c: hTr[c][tb]
                self.mm(bank(0, 0, 256), zT[1][0:17, tk], self.waug_b[0:17, bo:bo + 256], True, True,
                        [zTr[1][tb], self.waugr], [psr[0]])
                self.act(ef[q][:, 0:256], bank(0, 0, 256), AF.Exp, [psr[0]], [efr[q]], scale=-1.0)
                self.act(sp[q][:, 0:256], ef[q][:, 0:256], AF.Ln, [efr[q], self.epsr], [spr[q]], bias=one_ap)
                for c in range(8):
                    self.mm(bank(3), hT[:, c, tk], wg3[:, c, 512:1024], c == 0, c == 7, [hr(c), wglar], [psr[3]])
                for c in range(8):
                    self.mm(bank(2, 0, 256), hT[:, c, tk], wg3[:, c, 256:512], c == 0, c == 7, [hr(c), wglar], [psr[2]])
                for h in range(2):
                    self.mm(bank(1, h * 128, (h + 1) * 128), sp[q][:, h * 128:(h + 1) * 128], Kb[:, 256:384],
                            True, True, [spr[q], self.Kr], [psr[1]])
                self.mm(bank(1, 256, 512), Kb[:, 384:512], sp[q][:, 0:256], True, True, [spr[q], self.Kr], [psr[1]])
                self.cp("act", vst[:, tt, :], bank(3), [psr[3]], [vstr[tt]])
                self.act(decb[q][:, 0:2], bank(1, 0, 256).rearrange("p (h i) -> p h i", i=128)[:, :, 0], AF.Exp,
                         [psr[1]], [decbr[q]])
                self.act(E3[q], bank(1, 256, 512), AF.Exp, [psr[1]], [E3r[q]])
                self.tt("dve", kend[q], bank(2, 0, 256), E3[q], ALU.mult, [psr[2], E3r[q]], [kendr[q]])

            def b_s2(tt):
                q = tt % 2
                for h in range(2):
                    self.mm(bank(7, h * 256, (h + 1) * 256), kend[q][:, h * 128:(h + 1) * 128],
                            vst[:, tt, h * 256:(h + 1) * 256], True, True, [kendr[q], vstr[tt]], [psr[7]])
                for h in range(2):
                    hs = slice(h * 256, (h + 1) * 256)
                    self.stt(S32[:, hs], S32[:, hs], decb[q][:, h:h + 1], bank(7, h * 256, (h + 1) * 256), ALU.mult, ALU.add,
                             [S32r[h], decbr[q], psr[7]], [S32r[h]])
                if tt > 0:
                    self.cp("pool", Sb[:, tt - 1, :], S32, S32r, [Sbr[tt - 1]])

            P.op("pool", (lambda e: e.memset(S32, 0.0)), [], S32r)
            b_s1(15)
            for tt in range(15, -1, -1):
                if tt > 0:
                    b_s1(tt - 1)
                b_s2(tt)

            def f_s1(tt, mid_hook=None, pending=None):
                q = tt % 2
                tk = slice(tt * 128, (tt + 1) * 128)
                tb = tt // 4
                hr = lambda c: hTr[c][tb]
                self.mm(bank(0, 0, 256), zT[0][0:17, tk], self.waug_b[0:17, fo:fo + 256], True, True,
                        [zTr[0][tb], self.waugr], [psr[0]])
                self.mm(bank(0, 256, 512), zT[1][0:17, tk], self.waug_b[0:17, bo:bo + 256], True, True,
                        [zTr[1][tb], self.waugr], [psr[0]])
                self.act(ef[q], bank(0), AF.Exp, [psr[0]], [efr[q]], scale=-1.0)
                self.act(sp[q], ef[q], AF.Ln, [efr[q], self.epsr], [spr[q]], bias=one_ap)
                if pending is not None:
                    pending()
                for c in range(8):
                    self.mm(bank(2, 256, 512), hT[:, c, tk], wg3[:, c, 256:512], c == 0, c == 7, [hr(c), wglar], [psr[2]])
                if mid_hook is not None:
                    mid_hook()
                for d in range(2):
                    ku = 0 if d == 0 else 256
                    for h in range(2):
                        o = d * 256 + h * 128
                        self.mm(bank(1, o, o + 128), sp[q][:, o:o + 128], Kb[:, ku:ku + 128], True, True,
                                [spr[q], self.Kr], [psr[1]])
                self.mm(bank(2, 0, 256), Kb[:, 128:256], sp[q][:, 0:256], True, True, [spr[q], self.Kr], [psr[2]])
                self.act(E1[q], bank(1), AF.Exp, [psr[1]], [E1r[q]])
                self.act(E2[q], bank(1), AF.Exp, [psr[1]], [E2r[q]], scale=-1.0)
                self.act(E3[q], bank(2, 0, 256), AF.Exp, [psr[2]], [E3r[q]])
                for j in range(4):
                    for c in range(8):
                        self.mm(bank(0, j * 128, (j + 1) * 128), wg3[:, c, j * 128:(j + 1) * 128], hT[:, c, tk],
                                c == 0, c == 7, [hr(c), wglar], [psr[0]])
                self.tt("dve", kend[q], bank(2, 256, 512), E3[q], ALU.mult, [psr[2], E3r[q]], [kendr[q]])
                self.stt(qgf[q], bank(0, 0, 256), SC, E1[q][:, 0:256], ALU.mult, ALU.mult, [psr[0], E1r[q]], [qgfr[q]])
                self.stt(qgb[q], bank(0, 0, 256), SC, E1[q][:, 256:512], ALU.mult, ALU.mult, [psr[0], E1r[q]], [qgbr[q]])
                self.tt("dve", kgf[q], bank(0, 256, 512), E2[q][:, 0:256], ALU.mult, [psr[0], E2r[q]], [kgfr[q]])
                self.tt("dve", kgb[q], bank(0, 256, 512), E2[q][:, 256:512], ALU.mult, [psr[0], E2r[q]], [kgbr[q]])

            def f_R(tt):
                tk = slice(tt * 128, (tt + 1) * 128)
                tb = tt // 4
                for c in range(8):
                    self.mm(bank(4), hT[:, c, tk], wg3[:, c, 1024:1536], c == 0, c == 7, [hTr[c][tb], wglar], [psr[4]])

            def f_gr(tt):
                q = tt % 2
                self.act(gr[q], bank(4), AF.Exp, [psr[4]], [grr[q]], scale=-1.0)
                self.act(gr[q], gr[q], AF.Ln, [grr[q], self.epsr], [grr[q]], bias=one_ap)
                self.act(gr[q], gr[q], AF.Exp, [grr[q]], [grr[q]], scale=-1.0)
                self.tt("dve", gr[q], bank(4), gr[q], ALU.mult, [psr[4], grr[q]], [grr[q]])
                self.tt("pool", gr[q], gr[q], self.G2, ALU.mult, [grr[q], self.G2r], [grr[q]])

            def f_s2a(tt):
                q = tt % 2
                kg = (kgf[q], kgb[q])
                qg = (qgf[q], qgb[q])
                kgr = (kgfr[q], kgbr[q])
                qgr = (qgfr[q], qgbr[q])
                for d in range(2):
                    for h in range(2):
                        o = (d * 2 + h) * 128
                        self.mm(bank(5, o, o + 128), kg[d][:, h * 128:(h + 1) * 128], qg[d][:, h * 128:(h + 1) * 128],
                                True, True, [kgr[d], qgr[d]], [psr[5]])
                self.tt("dve", AT, bank(5), Kc[:, K_MASK:K_MASK + 512], ALU.mult, [psr[5], self.Kr], [ATr])

            def f_s2(tt):
                q = tt % 2
                for h in range(2):
                    self.mm(bank(7, h * 256, (h + 1) * 256), kend[q][:, h * 128:(h + 1) * 128],
                            vst[:, tt, h * 256:(h + 1) * 256], True, True, [kendr[q], vstr[tt]], [psr[7]])
                for h in range(2):
                    vh = vst[:, tt, h * 256:(h + 1) * 256]
                    seq = []
                    if tt > 0:
                        seq.append((qgf[q][:, h * 128:(h + 1) * 128], Sfbf[:, h * 256:(h + 1) * 256], [qgfr[q], Sfbfr]))
                    if tt < 15:
                        seq.append((qgb[q][:, h * 128:(h + 1) * 128], Sb[:, tt, h * 256:(h + 1) * 256], [qgbr[q], Sbr[tt]]))
                    seq += [(AT[:, h * 128:(h + 1) * 128], vh, [ATr, vstr[tt]]),
                            (AT[:, (2 + h) * 128:(3 + h) * 128], vh, [ATr, vstr[tt]])]
                    for i, (l, r, rd) in enumerate(seq):
                        self.mm(bank(6, h * 256, (h + 1) * 256), l, r, i == 0, i == len(seq) - 1, rd, [psr[6]])
                for h in range(2):
                    hs = slice(h * 256, (h + 1) * 256)
                    self.stt(S32[:, hs], S32[:, hs], E1[q][:, h * 128 + 127:h * 128 + 128], bank(7, h * 256, (h + 1) * 256),
                             ALU.mult, ALU.add, [S32r[h], E1r[q], psr[7]], [S32r[h]])
                self.cp("pool", Sfbf, S32, S32r, [Sfbfr])
                for h in range(2):
                    self.act(junk[:, 0:256], bank(6, h * 256, (h + 1) * 256), AF.Square, [psr[6]], [junkr, ss2r[0]],
                             accum=ss2[:, h:h + 1])
                self.act(ss2[:, 2:4], ss2[:, 0:2], AF.Ln, [ss2r[0], self.epsr], [ss2r[1]], scale=1.0 / 256,
                         bias=self.epsv[:, 0:1])
                self.act(ss2[:, 4:6], ss2[:, 2:4], AF.Exp, [ss2r[1]], [ss2r[2]], scale=-0.5)
                for h in range(2):
                    hs = slice(h * 256, (h + 1) * 256)
                    self.stt(og[q][:, hs], bank(6, h * 256, (h + 1) * 256), ss2[:, 4 + h:5 + h], gr[q][:, hs],
                             ALU.mult, ALU.mult, [psr[6], ss2r[2], grr[q]], [ogr[q]])

            def f_s3(tt):
                q = tt % 2
                tk = slice(tt * 128, (tt + 1) * 128)
                for e4 in range(4):
                    self.tr(bank(7, e4 * 128, (e4 + 1) * 128), og[q][:, e4 * 128:(e4 + 1) * 128], [ogr[q]], [psr[7]])
                self.cp("act", self.glaT[:, p * 4:(p + 1) * 4, tk], bank(7).rearrange("p (e t) -> p e t", t=128),
                        [psr[7]], [self.glaTr[c][tt] for c in range(p * 4, p * 4 + 4)])

            P.op("pool", (lambda e: e.memset(S32, 0.0)), [], S32r)
            f_s1(0)
            f_R(0)
            f_gr(0)
            for tt in range(16):
                f_s2a(tt)
                if tt + 1 < 16:
                    f_s1(tt + 1, (lambda tt=tt: f_s3(tt - 1)) if tt > 0 else None,
                         (lambda tt=tt: f_gr(tt)) if tt > 0 else None)
                    f_s2(tt)
                    f_R(tt + 1)
                else:
                    f_gr(tt)
                    f_s2(tt)
                    f_s3(tt - 1)
            f_s3(15)

    def phaseS4(self, s):
        A, psr, bank = self.A, self.psr, self.bank
        sig = [A.alloc([512], F32) for _ in range(2)]
        sigr = regs(2)
        k = 0
        for grp in range(2):
            wb, wbr = self.wget("wba", grp)
            wg, wgr = self.wget("ga", grp)
            wb3 = wb.rearrange("p (c n) -> p c n", n=512)
            wg3 = wg.rearrange("p (c n) -> p c n", n=512)
            for tb in range(4):
                blk = slice(tb * 512, (tb + 1) * 512)
                for fl in range(4):
                    fc = grp * 4 + fl
                    b0, b1 = (0, 1) if k % 2 == 0 else (2, 3)
                    for c in range(8):
                        self.mm(bank(b0), wb3[:, c, fl * 128:(fl + 1) * 128], self.glaT[:, c, blk], c == 0, c == 7,
                                [wbr] + self.glaTr[c][tb * 4:(tb + 1) * 4], [psr[b0]])
                    for c in range(8):
                        self.mm(bank(b1), wg3[:, c, fl * 128:(fl + 1) * 128], self.hT[:, c, blk], c == 0, c == 7,
                                [wgr, self.hTr[c][tb]], [psr[b1]])
                    self.act(sig[k % 2], bank(b1), AF.Sigmoid, [psr[b1]], [sigr[k % 2]])
                    self.tt("dve", self.maT[:, fc, blk], bank(b0), sig[k % 2], ALU.mult, [psr[b0], sigr[k % 2]],
                            [self.maTr[fc][tb]])
                    k += 1

    def phaseS2(self, s):
        A, psr, bank = self.A, self.psr, self.bank
        hT, hTr = self.hT, self.hTr
        self.alloc_nr()
        wkv, wkvr = self.wget("akv", 0)
        w3 = wkv.rearrange("p (c n) -> p c n", n=512)
        items = []
        for tb in range(4):
            blk = slice(tb * 512, (tb + 1) * 512)
            for g in range(2):
                def proj(b, g=g, blk=blk, tb=tb):
                    for c in range(8):
                        self.mm(bank(b), w3[:, c, g * 128:(g + 1) * 128], hT[:, c, blk], c == 0, c == 7,
                                [wkvr, hTr[c][tb]], [psr[b]])
                pre = (lambda tb=tb: self.rope_top(tb)) if g == 0 else None
                items.append((proj, C_KN, self.kT[:, g, blk], [self.kTr[g][tb]], pre))
        self.normrope_pipe(items, [2, 3, 4], 5, 6)
        for tb in range(4):
            for t in range(4):
                tile = tb * 4 + t
                for c in range(8):
                    self.mm(bank(7, 0, 256), hT[:, c, tile * 128:(tile + 1) * 128], w3[:, c, 256:512], c == 0, c == 7,
                            [wkvr, hTr[c][tb]], [psr[7]])
                self.cp("act", self.vatt[:, tile, :], bank(7, 0, 256), [psr[7]], [self.vattr[tile]])
        self.alloc_xb(2, junk=(self.nr["t2"].bitcast(BF16), self.nr["r"][3]))
        mT = A.alloc([8, 256], BF16)
        mTr = regs(8)
        self.load_xb(self.mem[s], 2)
        self.norm_T(self.xb, self.xbr, 2, C_MEM, lambda c: mT[:, c, :], mTr, self.sc, [0, 1])
        for grp in range(2):
            w, wr = self.wget("xwkv", grp)
            w3 = w.rearrange("p (c n) -> p c n", n=512)
            for fl in range(4):
                kc = grp * 4 + fl
                b = 2 + (kc % 2)
                for c in range(8):
                    self.mm(bank(b, 0, 256), w3[:, c, fl * 128:(fl + 1) * 128], mT[:, c, :], c == 0, c == 7,
                            [wr, mTr[c]], [psr[b]])
                self.cp("act" if kc % 2 == 0 else "dve", self.kmT[:, kc, :], bank(b, 0, 256), [psr[b]], [self.kmTr[kc]])
        for grp in range(2):
            w, wr = self.wget("xwkv", 2 + grp)
            w3 = w.rearrange("p (c n) -> p c n", n=512)
            for mt in range(2):
                b = 2 + mt
                for c in range(8):
                    self.mm(bank(b), mT[:, c, mt * 128:(mt + 1) * 128], w3[:, c, :], c == 0, c == 7, [wr, mTr[c]], [psr[b]])
                self.cp("act" if mt == 0 else "dve", self.vm[:, mt, grp * 512:(grp + 1) * 512], bank(b), [psr[b]],
                        [self.vmr[mt]])

    def alloc_S5(self):
        A = self.A
        self.sg = [A.alloc([512], F32) for _ in range(2)]
        self.sgr = regs(2)
        self.alloc_xb(4, junk=(self.sg[1].bitcast(BF16), self.sgr[1]), need_tmp=True)
        self.xb2 = [(self.xb, self.xbr), (A.alloc([4, D], F32), regs(4))]
        self.bufA = A.alloc([8, 512], BF16)
        self.bufB = A.alloc([8, 512], BF16)
        self.bufAr, self.bufBr = regs(8), regs(8)
        self.actT = A.alloc([22, 512], BF16)
        r = regs(22)
        for j in (13, 15, 17, 19):
            r[j + 1] = r[j]
        self.actTr = r
        self.bufC = self.actT[:, 0:8, :]
        self.bufCr = r[0:8]
        self.PT = [self.actT[:, 8 + i, :] for i in range(4)]
        self.PTr = [r[8 + i] for i in range(4)]

        def f32v(j):
            return self.actT[:, j:j + 2, :].rearrange("p a b -> p (a b)").bitcast(F32)
        self.nr = {"sqb": self.actT[:, 12, :], "tmpA": f32v(13), "knf": f32v(15), "t2": f32v(17),
                   "r": [r[12], r[13], r[15], r[17]], "knf2": f32v(19), "r_kn2": r[19]}
        self.rden = self.nr["tmpA"]
        self.rdenr = self.nr["r"][1]
        self.woring = [A.alloc([1024], BF16) for _ in range(3)]
        self.woring_r = regs(3)

    def wo2issue_upto(self, n):
        n = min(n, (self.cur_seq + 1) * 4 * 22)
        while self.wo2_issued < n:
            k = self.wo2_issued
            i = k % 3
            src, L = self.wsrc("wo2", k % 22)
            self.dma("sp", self.woring[i], src, self.woring_s[i], [self.wbf_reg["wo2"]], [self.woring_r[i]])
            self.wo2_issued += 1

    def wo2get(self, g):
        k = self.wo2_i
        assert k % 22 == g
        self.wo2issue_upto(k + 2)
        self.wo2_i += 1
        i = k % 3
        return self.woring[i].rearrange("p (k n) -> p k n", n=512), self.woring_r[i]

    def proj_res(self, srcs, wseg, gain_idx, next_norm=True):
        psr, bank = self.psr, self.bank
        self.load_gp(gain_idx)
        w0, w0r = self.wget(wseg, 0)
        w1, w1r = self.wget(wseg, 1)
        ws = [(w0.rearrange("p (c n) -> p c n", n=512), w0r), (w1.rearrange("p (c n) -> p c n", n=512), w1r)]
        for t in range(4):
            k = self.pn_k
            self.pn_k = (k + 1) % 2
            for half in range(2):
                w3, wr = ws[half]
                n = len(srcs) * 8
                i = 0
                for (apf, rf) in srcs:
                    for c in range(8):
                        self.mm(bank(2 * k + half), apf(c, t), w3[:, c, :], i == 0, i == n - 1, [wr] + rf(c, t),
                                [psr[2 * k + half]])
                        i += 1
            self.postnorm(k, t, 0)
            if next_norm and t >= 1:
                self.norm_tile(t - 1)
        if next_norm:
            self.norm_tile(3)

    def phaseS5(self, s, tb):
        psr, bank = self.psr, self.bank
        pb = tb % 2
        bufs = [(self.bufA, self.bufAr), (self.bufB, self.bufBr)]
        bufA, bufAr = bufs[pb]
        bufB, bufBr = bufs[1 - pb]
        bufC, bufCr = self.bufC, self.bufCr
        self.xb, self.xbr = self.xb2[pb]
        blk = slice(tb * 512, (tb + 1) * 512)
        if tb == 0:
            self.load_xb(self.x[s, blk, :])
            for t in range(4):
                self.norm_tile(t)
            self.norm_fin(4, C_MIXPRE, lambda c: bufA[:, c, :], bufAr, [0, 1])
        self.rope_top(tb)
        items = []
        wq = [self.wget("aq", grp) for grp in range(2)]
        for head in range(8):
            w, wr = wq[head // 4]
            w3 = w.rearrange("p (c n) -> p c n", n=512)
            hl = head % 4

            def proj(b, w3=w3, wr=wr, hl=hl):
                for c in range(8):
                    self.mm(bank(b), w3[:, c, hl * 128:(hl + 1) * 128], bufA[:, c, :], c == 0, c == 7,
                            [wr, bufAr[c]], [psr[b]])
            items.append((proj, C_QN, bufB[:, head, :], [bufBr[head]], None))
        self.normrope_pipe(items, [2, 5, 6], 3, 4)
        SCL = float(128.0 ** -0.5)
        sbanks = [2, 3, 0, 1]
        LA = 3
        ptl = list(zip(self.PT, self.PTr)) + [(self.actT[:, 15, :], self.actTr[15]), (self.actT[:, 17, :], self.actTr[17])]
        NP = len(ptl)
        seq = [(head, st) for head in range(8) for st in range(16)]

        aT, ar = self.actT, self.actTr
        S1 = [(aT[:, 12, :], ar[12]), (aT[:, 21, :], ar[21]), (aT[:, 19, :], ar[19])]
        S2 = [(aT[:, 15, :], ar[15]), (aT[:, 17, :], ar[17])]

        def score(j):
            head, st = seq[j]
            g = head // 4
            sb = sbanks[j % 4]
            self.mm(bank(sb), self.kT[:, g, st * 128:(st + 1) * 128], bufB[:, head, :], True, True,
                    [self.kTr[g][st // 4], bufBr[head]], [psr[sb]])
            self.act(ptl[j % NP][0], bank(sb), AF.Exp, [psr[sb]], [ptl[j % NP][1]], scale=SCL)
            if st % 2 == 1:
                s1, s1r = S1[(j // 2) % 3]
                self.tt("dve", s1, ptl[(j - 1) % NP][0], ptl[j % NP][0], ALU.add,
                        [ptl[(j - 1) % NP][1], ptl[j % NP][1]], [s1r])

        for j0 in range(LA):
            score(j0)
        for j in range(len(seq)):
            if j + LA < len(seq):
                score(j + LA)
            head, st = seq[j]
            g = head // 4
            ob = 4 + 2 * (head % 2)
            db = ob + 1
            pt, ptr = ptl[j % NP]
            self.mm(bank(ob), self.vatt[:, st, g * 128:(g + 1) * 128], pt, st == 0, st == 15,
                    [self.vattr[st], ptr], [psr[ob]])
            if st % 2 == 1:
                s1, s1r = S1[(j // 2) % 3]
                self.mm(bank(db), self.ones, s1, st == 1, st == 15, [self.onesr, s1r], [psr[db]])
            if st == 15:
                self.act(self.rden, bank(db), AF.Ln, [psr[db]], [self.rdenr])
                self.act(self.rden, self.rden, AF.Exp, [self.rdenr], [self.rdenr], scale=-1.0)
                self.tt("dve", bufC[:, head, :], bank(ob), self.rden, ALU.mult, [psr[ob], self.rdenr], [bufCr[head]])
        k = 0
        for grp in range(2):
            wb, wbr = self.wget("wbb", grp)
            wg, wgr = self.wget("gb", grp)
            wb3 = wb.rearrange("p (c n) -> p c n", n=512)
            wg3 = wg.rearrange("p (c n) -> p c n", n=512)
            for fl in range(4):
                fc = grp * 4 + fl
                b0, b1 = (0, 1) if k % 2 == 0 else (2, 3)
                for c in range(8):
                    self.mm(bank(b0), wb3[:, c, fl * 128:(fl + 1) * 128], bufC[:, c, :], c == 0, c == 7,
                            [wbr, bufCr[c]], [psr[b0]])
                for c in range(8):
                    self.mm(bank(b1), wg3[:, c, fl * 128:(fl + 1) * 128], bufA[:, c, :], c == 0, c == 7,
                            [wgr, bufAr[c]], [psr[b1]])
                self.act(self.sg[k % 2], bank(b1), AF.Sigmoid, [psr[b1]], [self.sgr[k % 2]])
                self.tt("dve", self.sg[k % 2], bank(b0), self.sg[k % 2], ALU.mult, [psr[b0], self.sgr[k % 2]], [self.sgr[k % 2]])
                self.tt("pool", bufB[:, fc, :], self.sg[k % 2], self.maT[:, fc, blk], ALU.add,
                        [self.sgr[k % 2], self.maTr[fc][tb]], [bufBr[fc]])
                k += 1
        srcs = [(lambda c, t: bufB[:, c, t * 128:(t + 1) * 128], lambda c, t: [bufBr[c]])]
        self.proj_res(srcs, "wo", 0)
        if self.upto == "x1":
            return
        self.norm_fin(4, C_XPRE, lambda c: bufA[:, c, :], bufAr, [0, 1])
        for grp in range(2):
            w, wr = self.wget("xwq", grp)
            w3 = w.rearrange("p (c n) -> p c n", n=512)
            for fl in range(4):
                fc = grp * 4 + fl
                b = 4 + (fc % 2)
                for c in range(8):
                    self.mm(bank(b), w3[:, c, fl * 128:(fl + 1) * 128], bufA[:, c, :], c == 0, c == 7,
                            [wr, bufAr[c]], [psr[b]])
                self.cp("act" if fc % 2 == 0 else "dve", bufC[:, fc, :], bank(b), [psr[b]], [bufCr[fc]])
        for head in range(4):
            for mt in range(2):
                sb = 6 + mt
                for dc in range(2):
                    self.mm(bank(sb), self.kmT[:, head * 2 + dc, mt * 128:(mt + 1) * 128], bufC[:, head * 2 + dc, :],
                            dc == 0, dc == 1, [self.kmTr[head * 2 + dc], bufCr[head * 2 + dc]], [psr[sb]])
                pi = (head % 2) * 2 + mt
                self.act(self.PT[pi], bank(sb), AF.Exp, [psr[sb]], [self.PTr[pi]], scale=1.0 / 16.0)
            base_b = 0 if head % 2 == 0 else 3
            for dc in range(2):
                for mt in range(2):
                    pi = (head % 2) * 2 + mt
                    self.mm(bank(base_b + dc), self.vm[:, mt, head * 256 + dc * 128:head * 256 + (dc + 1) * 128],
                            self.PT[pi], mt == 0, mt == 1, [self.vmr[mt], self.PTr[pi]], [psr[base_b + dc]])
            db = base_b + 2
            for mt in range(2):
                pi = (head % 2) * 2 + mt
                self.mm(bank(db), self.ones, self.PT[pi], mt == 0, mt == 1, [self.onesr, self.PTr[pi]], [psr[db]])
            self.act(self.rden, bank(db), AF.Ln, [psr[db]], [self.rdenr])
            self.act(self.rden, self.rden, AF.Exp, [self.rdenr], [self.rdenr], scale=-1.0)
            for dc in range(2):
                self.tt("dve", bufB[:, head * 2 + dc, :], bank(base_b + dc), self.rden, ALU.mult,
                        [psr[base_b + dc], self.rdenr], [bufBr[head * 2 + dc]])
        nxb, nxbr = self.xb2[1 - pb]
        if tb < 3 and self.upto == "all":
            nblk = self.x[s, (tb + 1) * 512:(tb + 2) * 512, :]
            for t in range(4):
                self.dma("sp", nxb[:, t, :], nblk[t * 128:(t + 1) * 128, :], self.xb_slot2[1 - pb][t], [], [nxbr[t]])
        srcs = [(lambda c, t: bufB[:, c, t * 128:(t + 1) * 128], lambda c, t: [bufBr[c]])]
        self.proj_res(srcs, "xwo", 1)
        if self.upto == "x2":
            return
        self.norm_fin(4, C_FFNPRE, lambda c: bufA[:, c, :], bufAr, [0, 1])
        k = 0
        prefetch = tb < 3 and self.upto == "all"
        for i in range(11):
            w, wr = self.wget("wi", i)
            w3 = w.rearrange("p (c n) -> p c n", n=512)
            if prefetch and i == 1:
                for t in range(4):
                    self.norm_tile(t, nxb, nxbr)
            if prefetch and i == 5:
                self.norm_fin(4, C_MIXPRE, lambda c: bufB[:, c, :], bufBr, [0, 1])
            for j in range(2):
                bg, bu = (0, 1) if k % 2 == 0 else (2, 3)
                for c in range(8):
                    self.mm(bank(bg), w3[:, c, j * 128:(j + 1) * 128], bufA[:, c, :], c == 0, c == 7, [wr, bufAr[c]], [psr[bg]])
                for c in range(8):
                    self.mm(bank(bu), w3[:, c, 256 + j * 128:256 + (j + 1) * 128], bufA[:, c, :], c == 0, c == 7,
                            [wr, bufAr[c]], [psr[bu]])
                self.act(self.sg[k % 2], bank(bg), AF.Silu, [psr[bg]], [self.sgr[k % 2]])
                self.tt("dve", self.actT[:, 2 * i + j, :], bank(bu), self.sg[k % 2], ALU.mult, [psr[bu], self.sgr[k % 2]],
                        [self.actTr[2 * i + j]])
                k += 1
        self.load_gp(2)
        for hf in range(2):
            for i in range(11):
                w3, wr = self.wo2get(hf * 11 + i)
                for t in range(4):
                    for kk in range(2):
                        self.mm(bank(2 * t + hf), self.actT[:, 2 * i + kk, t * 128:(t + 1) * 128], w3[:, kk, :],
                                i == 0 and kk == 0, i == 10 and kk == 1, [wr, self.actTr[2 * i + kk]], [psr[2 * t + hf]])
        for t in range(4):
            self.postnorm(t, t, 0)
            self.dma("sp", self.y[s, tb * 512 + t * 128:tb * 512 + (t + 1) * 128, :], self.xb[:, t, :], self.y_slot[t],
                     [self.xbr[t]], [])

    def build(self):
        self.init_consts()
        A, P = self.A, self.P
        self.ssP = A.alloc([32], F32)
        self.xb_slot = [self.newslot("xb%d" % i) for i in range(4)]
        self.xb_slot2 = [self.xb_slot, [self.newslot("xbB%d" % i) for i in range(4)]]
        self.y_slot = [self.newslot("ystore%d" % i, output=True) for i in range(4)]
        self.wgla_slot = self.newslot("wgla")
        self.woring_s = [self.newslot("wo2r%d" % i) for i in range(4)]
        self.wo2_i = 0
        self.wo2_issued = 0
        self.pn_k = 0
        self.base = A.mark()
        o = A.nbytes - 56 * 1024
        self.qoff = o
        self.maT, o = A.alloc_at(o, [8, T], BF16)
        self.kT, o = A.alloc_at(o, [2, T], BF16)
        self.vatt, o = A.alloc_at(o, [16, 256], BF16)
        self.kmT, o = A.alloc_at(o, [8, 256], BF16)
        self.vm, o = A.alloc_at(o, [2, 1024], BF16)
        assert o <= A.nbytes
        up = self.upto
        for s in range(self.nseq):
            self.cur_seq = s
            self.maTr = [regs(4) for _ in range(8)]
            self.kTr = [regs(4) for _ in range(2)]
            self.vattr = regs(16)
            self.kmTr = regs(8)
            self.vmr = regs(2)
            if s > 0:
                P.fence()
            A.reset(self.base)
            self.hT = A.alloc([8, T], BF16)
            self.hTr = [regs(4) for _ in range(8)]
            m1 = A.mark()
            self.phaseA(s)
            self.issue_conv(["gla1", "wba", "ga", "akv", "xwkv"])
            if up == "A":
                tmp = A.alloc([8, T], F32)
                self.dbg_store("hT", self.hT, [r for rr in self.hTr for r in rr], tmp)
                break
            P.fence()
            A.reset(m1)
            self.glaT = A.alloc([8, T], BF16)
            self.glaTr = [regs(16) for _ in range(8)]
            m2 = A.mark()
            self.phaseGLA(s)
            P.fence()
            A.reset(m2)
            if up == "GLA":
                tmp = A.alloc([8, T], F32)
                self.dbg_store("glaT", self.glaT, [r for rr in self.glaTr for r in rr], tmp)
                break
            self.phaseS4(s)
            self.phaseS2(s)
            assert A.off <= self.qoff, (A.off, self.qoff)
            if up == "S2":
                tmp = A.alloc_at(self.base, [8, T], F32)[0]
                P.fence()
                self.dbg_store("maT", self.maT, [r for rr in self.maTr for r in rr], tmp)
                tmp2 = A.alloc_at(self.base + 65536, [2, T], F32)[0]
                self.dbg_store("kT", self.kT, [r for rr in self.kTr for r in rr], tmp2)
                tmp3 = A.alloc_at(self.base + 65536 + 16384, [16, 256], F32)[0]
                self.dbg_store("vatt", self.vatt, self.vattr, tmp3)
                tmp4 = A.alloc_at(self.base + 65536 + 32768, [8, 256], F32)[0]
                self.dbg_store("kmT", self.kmT, self.kmTr, tmp4)
                tmp5 = A.alloc_at(self.base + 65536 + 32768 + 8192, [2, 1024], F32)[0]
                self.dbg_store("vm", self.vm, self.vmr, tmp5)
                break
            P.fence()
            A.reset(self.base)
            self.alloc_S5()
            assert A.off <= self.qoff, (A.off, self.qoff)
            for tb in range(4):
                self.phaseS5(s, tb)
                if up in ("x1", "x2"):
                    for t in range(4):
                        self.dma("sp", self.y[s, tb * 512 + t * 128:tb * 512 + (t + 1) * 128, :], self.xb[:, t, :],
                                 self.y_slot[t], [self.xbr[t]], [])
        P.emit()
        return self.nc


def make_core_inputs(inp, xs, ms, shared=None):
    if shared is None:
        shared = make_shared(inp)
    d = dict(shared)
    d["x"] = np.ascontiguousarray(xs, dtype=np.float32)
    d["mem"] = np.ascontiguousarray(ms, dtype=np.float32)
    return d


def make_shared(inp):
    gpost = np.stack([np.broadcast_to(inp[k][0][None, :], (128, D)) for k in ("ln_mix_post", "ln_x_post", "ln_ffn_post")], 0)
    gn = inp["gla_norm"][0]
    gnorm2 = np.broadcast_to(np.concatenate([gn, gn])[None, :], (128, 512))
    waug = np.zeros((17, 1024), np.float32)
    waug[:16, :512] = inp["gla_wa_f"][0]
    waug[:16, 512:] = inp["gla_wa_b"][0]
    waug[16, :512] = inp["gla_ba_f"][0]
    waug[16, 512:] = inp["gla_ba_b"][0]
    return {
        "wall": build_wall(inp),
        "kpack": build_kpack(),
        "cpack": build_cpack(inp),
        "gpost": np.ascontiguousarray(gpost, dtype=np.float32),
        "gnorm2": np.ascontiguousarray(gnorm2, dtype=np.float32),
        "waug": waug,
    }


_CACHE = {}


def kernel(**inputs):
    inp = {k: np.asarray(v) for k, v in inputs.items()}
    xs = np.concatenate([inp["x_prompt"], inp["x_sample"]], 0)
    ms = np.concatenate([inp["mem_prompt"], inp["mem_sample"]], 0)
    nb = xs.shape[0]
    assert nb == NCORES * SEQ_PER_CORE
    shared = make_shared(inp)
    if "nc" not in _CACHE:
        _CACHE["nc"] = Builder(nseq=SEQ_PER_CORE, upto="all").build()
    nc = _CACHE["nc"]
    in_maps = []
    for c in range(NCORES):
        sl = slice(c * SEQ_PER_CORE, (c + 1) * SEQ_PER_CORE)
        in_maps.append(make_core_inputs(inp, xs[sl], ms[sl], shared))
    res = run_bass_kernel_spmd(nc, in_maps, core_ids=list(range(NCORES)))
    y = np.concatenate([np.asarray(r["y"]) for r in res.results], 0).astype(np.float32)
    nprompt = inp["x_prompt"].shape[0]
    return (np.ascontiguousarray(y[:nprompt]), np.ascontiguousarray(y[nprompt:]))
```

```python
import numpy as np
import concourse.bass as bass
import concourse.mybir as mybir
from concourse.bass_utils import run_bass_kernel_spmd

F32 = mybir.dt.float32
BF16 = mybir.dt.bfloat16
AF = mybir.ActivationFunctionType
ALU = mybir.AluOpType

T = 2048
D = 1024
NMEM = 256
EPS = 1e-6
DFF = 2816
NCORES = 8
SEQ_PER_CORE = 3

ENGS = ("pe", "act", "dve", "pool", "sp")
SAME_ENGINE_FULL_SYNC = True


class Reg:
    __slots__ = ("w", "r")

    def __init__(self):
        self.w = None
        self.r = {}


def regs(n):
    return [Reg() for _ in range(n)]


class Slot:
    def __init__(self, nc, name):
        self.sem = nc.alloc_semaphore(name)
        self.n = 0


class Prog:
    def __init__(self, nc):
        self.nc = nc
        self.ops = []
        self.by_eng = {e: [] for e in ENGS}
        self.sems = {e: nc.alloc_semaphore("sem_" + e) for e in ENGS}
        self.out_slots = []
        self.fence_deps = set()
        self.fence_pending = set()
        self.dma_since = []

    def fence(self):
        deps = set(self.dma_since)
        for e in ENGS:
            for oid in reversed(self.by_eng[e]):
                if self.ops[oid][3] is None:
                    deps.add(oid)
                    break
        self.fence_deps = deps
        self.fence_pending = set(ENGS)
        self.dma_since = []

    def op(self, eng, fn, reads=(), writes=(), slot=None):
        oid = len(self.ops)
        deps = set()
        is_dma = slot is not None
        if eng in self.fence_pending:
            self.fence_pending.discard(eng)
            for d in self.fence_deps:
                if self.ops[d][3] is None and self.ops[d][0] == eng and not is_dma:
                    continue
                deps.add(d)
        if is_dma:
            self.dma_since.append(oid)

        def add(pid, kind):
            peng, _, _, pslot, _ = self.ops[pid]
            if pslot is None and not is_dma and peng == eng:
                if eng == "pe" or (kind != "raw" and not SAME_ENGINE_FULL_SYNC):
                    return
            deps.add(pid)

        for r in reads:
            if r.w is not None:
                add(r.w, "raw")
        for w in writes:
            if w.w is not None:
                add(w.w, "waw")
            for pid in w.r.values():
                add(pid, "war")
        key = ("dma", oid) if is_dma else eng
        for r in reads:
            r.r[key] = oid
        for w in writes:
            w.w = oid
            w.r = {}
        val = None
        if is_dma:
            slot.n += 1
            val = 16 * slot.n
        self.ops.append((eng, fn, deps, slot, val))
        self.by_eng[eng].append(oid)
        return oid

    def emit(self):
        nc = self.nc
        ops = self.ops
        marked = set()
        for (_, _, deps, _, _) in ops:
            for d in deps:
                if ops[d][3] is None:
                    marked.add(d)
        tok = {}
        for e in ENGS:
            cnt = 0
            for oid in self.by_eng[e]:
                eng, fn, deps, slot, val = ops[oid]
                if slot is not None:
                    tok[oid] = (slot.sem, val)
                elif oid in marked:
                    cnt += 1
                    tok[oid] = (self.sems[e], cnt)
        self.nmarked = len(marked)
        final_waits = [(s.sem, 16 * s.n) for s in self.out_slots if s.n > 0]
        handles = {"pe": "tensor", "act": "scalar", "dve": "vector", "pool": "gpsimd", "sp": "sync"}

        def run_engine(e, engine):
            waited = {}
            for oid in self.by_eng[e]:
                eng, fn, deps, slot, val = ops[oid]
                needs = {}
                for d in deps:
                    sem, v = tok[d]
                    k = sem.num
                    if waited.get(k, 0) < v and needs.get(k, (None, 0))[1] < v:
                        needs[k] = (sem, v)
                for k, (sem, v) in needs.items():
                    engine.wait_ge(sem, v)
                    waited[k] = v
                inst = fn(engine)
                if slot is not None:
                    inst.then_inc(slot.sem, 16)
                elif oid in marked:
                    inst.then_inc(self.sems[e], 1)
            if e == "sp":
                for sem, v in final_waits:
                    engine.wait_ge(sem, v)

        with nc.Block() as block:
            for e in ENGS:
                def mk(e=e):
                    def body(engine):
                        run_engine(e, engine)
                    return body
                getattr(block, handles[e])(mk())


class Arena:
    def __init__(self, nc, nbytes):
        self.t = nc.alloc_sbuf_tensor("arena", [128, nbytes // 2], BF16)
        self.nbytes = nbytes
        self.off = 0
        self.peak = 0

    def alloc(self, shape, dtype, parts=128):
        esz = 4 if dtype == F32 else 2
        n = int(np.prod(shape))
        nb = (n * esz + 31) // 32 * 32
        o = self.off
        self.off += nb
        self.peak = max(self.peak, self.off)
        assert self.off <= self.nbytes, f"arena overflow {self.off} > {self.nbytes}"
        v = self.t[0:parts, o // 2:(o + n * esz) // 2]
        if dtype == F32:
            v = v.bitcast(F32)
        if len(shape) == 2:
            v = v.rearrange("p (a b) -> p a b", b=shape[1])
        elif len(shape) == 3:
            v = v.rearrange("p (a b c) -> p a b c", b=shape[1], c=shape[2])
        return v

    def alloc_at(self, off, shape, dtype, parts=128):
        save = self.off
        self.off = off
        v = self.alloc(shape, dtype, parts)
        end = self.off
        self.off = save
        return v, end

    def mark(self):
        return self.off

    def reset(self, m):
        self.off = m


def _seg(W, groups):
    K = W.shape[0]
    KC = K // 128
    out = []
    for cols in groups:
        t = W[:, cols].reshape(KC, 128, len(cols)).transpose(1, 0, 2)
        out.append(np.ascontiguousarray(t).reshape(128, KC * len(cols)))
    return np.stack(out, 0)


def _ar(a, b):
    return np.arange(a, b)


SEG_ORDER = ["z", "gla0", "gla1", "wba", "ga", "akv", "xwkv", "aq", "wbb", "gb", "wo", "xwq", "xwo", "wi", "wo2"]
SEG_SHAPE = {
    "z": (1, 8 * 32), "akv": (1, 8 * 512), "gla0": (1, 8 * 1536), "gla1": (1, 8 * 1536),
    "ga": (2, 4096), "wba": (2, 4096), "aq": (2, 4096), "gb": (2, 4096), "wbb": (2, 4096),
    "wo": (2, 4096), "xwkv": (4, 4096), "xwq": (2, 4096), "xwo": (2, 4096),
    "wi": (11, 4096), "wo2": (22, 1024),
}


def seg_offsets():
    off = {}
    o = 0
    for s in SEG_ORDER:
        ng, L = SEG_SHAPE[s]
        off[s] = o
        o += ng * 128 * L
    return off, o


def build_wall(inp):
    w_in = inp["w_in"][0]
    segs = {}
    for p in range(2):
        h0, h1 = 2 * p, 2 * p + 1
        cols = np.concatenate([
            _ar(h0 * 128, h0 * 128 + 128), _ar(h1 * 128, h1 * 128 + 128),
            _ar(512 + h0 * 128, 512 + h0 * 128 + 128), _ar(512 + h1 * 128, 512 + h1 * 128 + 128),
            _ar(1024 + h0 * 256, 1024 + h0 * 256 + 256), _ar(1024 + h1 * 256, 1024 + h1 * 256 + 256),
            _ar(2048 + h0 * 256, 2048 + h0 * 256 + 256), _ar(2048 + h1 * 256, 2048 + h1 * 256 + 256)])
        segs["gla%d" % p] = _seg(w_in, [cols])
    segs["z"] = _seg(w_in, [_ar(3072, 3104)])
    segs["akv"] = _seg(w_in, [_ar(4128, 4640)])
    segs["aq"] = _seg(w_in, [_ar(3104, 3616), _ar(3616, 4128)])
    segs["ga"] = _seg(w_in, [_ar(4640, 5152), _ar(5152, 5664)])
    segs["gb"] = _seg(w_in, [_ar(5664, 6176), _ar(6176, 6688)])
    two = [_ar(0, 512), _ar(512, 1024)]
    segs["wba"] = _seg(inp["w_branch_gla"][0], two)
    segs["wbb"] = _seg(inp["w_branch_att"][0], two)
    segs["wo"] = _seg(inp["w_out"][0], two)
    segs["xwq"] = _seg(inp["x_wq"][0], two)
    segs["xwo"] = _seg(inp["x_wo"][0], two)
    segs["xwkv"] = _seg(inp["x_wkv"][0], [_ar(i * 512, (i + 1) * 512) for i in range(4)])
    wi = inp["ffn_wi"][0]
    segs["wi"] = _seg(wi, [np.concatenate([_ar(256 * i, 256 * i + 256), _ar(DFF + 256 * i, DFF + 256 * i + 256)])
                           for i in range(11)])
    w2 = inp["ffn_wo"][0]
    pcs = []
    for half in range(2):
        for i in range(11):
            blk = w2[256 * i:256 * i + 256, half * 512:(half + 1) * 512]
            pcs.append(np.ascontiguousarray(blk.reshape(2, 128, 512).transpose(1, 0, 2)).reshape(128, 1024))
    segs["wo2"] = np.stack(pcs, 0)
    off, tot = seg_offsets()
    wall = np.empty(tot, np.float32)
    for s in SEG_ORDER:
        ng, L = SEG_SHAPE[s]
        a = segs[s]
        assert a.shape == (ng, 128, L), (s, a.shape)
        wall[off[s]:off[s] + a.size] = a.reshape(-1)
    return wall.reshape(-1, 2048)


K_ID = 0
K_UF = 128
K_WF = 256
K_UB = 384
K_WB = 512
K_MASK = 640
K_ROT = 1152
K_COS = 1280
K_SIN = 1376
NK = 1472


def build_kpack():
    k = np.zeros((128, NK), np.float32)
    j = np.arange(128)[:, None]
    i = np.arange(128)[None, :]
    k[:, K_ID:K_ID + 128] = (j == i)
    c = -1.0 / 16.0
    k[:, K_UF:K_UF + 128] = c * (j <= i)
    k[:, K_WF:K_WF + 128] = c * (j > i)
    k[:, K_UB:K_UB + 128] = c * (j >= i)
    k[:, K_WB:K_WB + 128] = c * (j < i)
    mf = (j <= i).astype(np.float32)
    mb = (j > i).astype(np.float32)
    k[:, K_MASK:K_MASK + 512] = np.concatenate([mf, mf, mb, mb], 1)
    R = np.zeros((128, 128), np.float32)
    for m in range(128):
        if (m % 64) < 32:
            R[m + 32, m] = -1.0
        else:
            R[m - 32, m] = 1.0
    k[:, K_ROT:K_ROT + 128] = R
    inv = (10000.0 ** (-np.arange(0, 64, 2, dtype=np.float32) / np.float32(64))).astype(np.float32)
    for d in range(128):
        f = inv[d % 32]
        if d < 64:
            ang = (np.arange(32, dtype=np.float32) * f).astype(np.float32)
            k[d, K_COS:K_COS + 32] = np.cos(ang)
            k[d, K_SIN:K_SIN + 32] = np.sin(ang)
        else:
            ang = (np.arange(64, dtype=np.float32) * f).astype(np.float32)
            k[d, K_COS + 32:K_COS + 96] = np.cos(ang)
            k[d, K_SIN + 32:K_SIN + 96] = np.sin(ang)
    return k


C_MIXPRE, C_XPRE, C_FFNPRE, C_MEM, C_QN, C_KN = 0, 8, 16, 24, 32, 33
NCP = 34


def build_cpack(inp):
    c = np.zeros((128, NCP), np.float32)
    c[:, C_MIXPRE:C_MIXPRE + 8] = inp["ln_mix_pre"][0].reshape(8, 128).T
    c[:, C_XPRE:C_XPRE + 8] = inp["ln_x_pre"][0].reshape(8, 128).T
    c[:, C_FFNPRE:C_FFNPRE + 8] = inp["ln_ffn_pre"][0].reshape(8, 128).T
    c[:, C_MEM:C_MEM + 8] = inp["ln_mem"][0].reshape(8, 128).T
    c[:, C_QN] = inp["att_q_norm"][0]
    c[:, C_KN] = inp["att_k_norm"][0]
    return c


class Builder:
    def __init__(self, nseq=SEQ_PER_CORE, upto="all", dbg=None):
        self.nseq = nseq
        self.upto = upto
        self.dbg = dbg or {}
        nc = bass.Bass("TRN2", target_bir_lowering=False)
        self.nc = nc
        self.P = Prog(nc)
        self.soff, self.wtot = seg_offsets()
        self.x = nc.dram_tensor("x", [nseq, T, D], F32, kind="ExternalInput").ap()
        self.mem = nc.dram_tensor("mem", [nseq, NMEM, D], F32, kind="ExternalInput").ap()
        self.wall = nc.dram_tensor("wall", [self.wtot // 2048, 2048], F32, kind="ExternalInput").ap()
        self.kpack = nc.dram_tensor("kpack", [128, NK], F32, kind="ExternalInput").ap()
        self.cpack = nc.dram_tensor("cpack", [128, NCP], F32, kind="ExternalInput").ap()
        self.gpost = nc.dram_tensor("gpost", [3, 128, D], F32, kind="ExternalInput").ap()
        self.gnorm2 = nc.dram_tensor("gnorm2", [128, 512], F32, kind="ExternalInput").ap()
        self.waug = nc.dram_tensor("waug", [17, 1024], F32, kind="ExternalInput").ap()
        self.y = nc.dram_tensor("y", [nseq, T, D], F32, kind="ExternalOutput").ap()
        self.wbf = nc.dram_tensor("wbf", [self.wtot // 2048, 2048], BF16).ap()
        self.wbf_flat = self.wbf.rearrange("r c -> (r c)")
        self.wbf_reg = {s: Reg() for s in SEG_ORDER}
        self.dbg_out = {}
        for name, shape in self.dbg.items():
            self.dbg_out[name] = nc.dram_tensor("dbg_" + name, list(shape), F32, kind="ExternalOutput").ap()
        self.A = Arena(nc, 207 * 1024)
        self.PS = [nc.alloc_psum_tensor("ps%d" % i, [128, 1024], F32).ap() if False else
                   nc.alloc_psum_tensor("ps%d" % i, [128, 1024], F32) for i in range(4)]
        self.psr = regs(8)

    def bank(self, b, lo=0, hi=512):
        return self.PS[b // 2][:, (b % 2) * 512 + lo:(b % 2) * 512 + hi]

    def mm(self, out, lhsT, rhs, start, stop, reads, writes):
        self.P.op("pe", lambda e: e.matmul(out, lhsT, rhs, start=start, stop=stop), reads, writes)

    def tr(self, out, in_, reads, writes):
        ident = self.K[:, K_ID:K_ID + 128]
        self.P.op("pe", lambda e: e.transpose(out, in_, ident), list(reads) + [self.Kr], writes)

    def act(self, out, in_, func, reads, writes, scale=1.0, bias=0.0, accum=None):
        def fn(e):
            kw = {}
            if accum is not None:
                kw["accum_out"] = accum
            return e.activation(out, in_, func, bias=bias, scale=scale, **kw)
        self.P.op("act", fn, reads, writes)

    def tt(self, eng, out, in0, in1, op, reads, writes):
        self.P.op(eng, lambda e: e.tensor_tensor(out, in0, in1, op), reads, writes)

    def ts(self, eng, out, in0, s1, op0, reads, writes, s2=None, op1=None):
        if op1 is None:
            self.P.op(eng, lambda e: e.tensor_scalar(out, in0, s1, None, op0), reads, writes)
        else:
            self.P.op(eng, lambda e: e.tensor_scalar(out, in0, s1, s2, op0, op1), reads, writes)

    def stt(self, out, in0, scalar, in1, op0, op1, reads, writes):
        self.P.op("dve", lambda e: e.scalar_tensor_tensor(out, in0, scalar, in1, op0, op1), reads, writes)

    def cp(self, eng, out, in_, reads, writes):
        if eng == "act":
            self.P.op("act", lambda e: e.activation(out, in_, AF.Copy), reads, writes)
        else:
            self.P.op(eng, lambda e: e.tensor_copy(out, in_), reads, writes)

    def dma(self, q, out, in_, slot, reads, writes):
        self.P.op(q, lambda e: e.dma_start(out, in_), reads, writes, slot=slot)

    def newslot(self, name, output=False):
        s = Slot(self.nc, name)
        if output:
            self.P.out_slots.append(s)
        return s

    def wsrc(self, seg, g):
        ng, L = SEG_SHAPE[seg]
        o = self.soff[seg] + g * 128 * L
        return self.wbf_flat[o:o + 128 * L].rearrange("(p l) -> p l", p=128), L

    def init_consts(self):
        A, nc = self.A, self.nc
        self.K = A.alloc([NK], F32)
        self.Kr = Reg()
        self.C = A.alloc([NCP], F32)
        self.Cr = Reg()
        self.G2 = A.alloc([512], F32)
        self.G2r = Reg()
        self.gp = A.alloc([D], F32)
        self.gpr = Reg()
        self.gp_slot = self.newslot("gp")
        self.waug_f = A.alloc_at(A.nbytes - 56 * 1024, [1024], F32, parts=17)[0]
        self.waug_b = A.alloc([1024], BF16, parts=17)
        self.waugr = Reg()
        self.ones = A.alloc([128], BF16)
        self.onesr = Reg()
        self.cosB = A.alloc([512], F32)
        self.sinB = A.alloc([512], F32)
        self.csr = Reg()
        s = self.newslot("c0")
        self.dma("sp", self.K, self.kpack, s, [], [self.Kr])
        self.Kbf = A.alloc([512], BF16)
        self.cp("dve", self.Kbf, self.K[:, K_UF:K_UF + 512], [self.Kr], [self.Kr])
        self.ident_bf = A.alloc([128], BF16)
        self.cp("dve", self.ident_bf, self.K[:, K_ID:K_ID + 128], [self.Kr], [self.Kr])
        s = self.newslot("c1")
        self.dma("sp", self.C, self.cpack, s, [], [self.Cr])
        s = self.newslot("c2")
        self.dma("sp", self.G2, self.gnorm2, s, [], [self.G2r])
        s = self.newslot("c3")
        r0 = Reg()
        self.dma("sp", self.waug_f, self.waug, s, [], [r0])
        self.cp("dve", self.waug_b, self.waug_f, [r0], [self.waugr])
        self.P.op("pool", lambda e: e.memset(self.ones, 1.0), [], [self.onesr])
        self.epsv = A.alloc([8], F32)
        self.epsr = Reg()
        self.P.op("pool", lambda e: e.memset(self.epsv[:, 0:1], EPS), [], [self.epsr])
        self.P.op("pool", lambda e: e.memset(self.epsv[:, 1:2], 1.0), [], [self.epsr])
        self.conv_done = set()
        for tab, kc in ((self.cosB, K_COS), (self.sinB, K_SIN)):
            src = self.K[64:128, kc + 32:kc + 96].unsqueeze(1).broadcast_to([64, 8, 64])
            dst = tab[64:128, :].rearrange("p (r c) -> p r c", c=64)
            self.P.op("pool", (lambda e, dst=dst, src=src: e.tensor_copy(dst, src)), [self.Kr], [self.csr])
        self.nring = 4
        self.ring = [A.alloc([4096], BF16) for _ in range(self.nring)]
        self.ring_r = regs(self.nring)
        self.ring_s = [self.newslot("ring%d" % i) for i in range(self.nring)]
        self.make_sched()

    def issue_conv(self, names, after=()):
        for sname in names:
            if sname in self.conv_done:
                continue
            self.conv_done.add(sname)
            ng, L = SEG_SHAPE[sname]
            r_lo = self.soff[sname] // 2048
            r_hi = (self.soff[sname] + ng * 128 * L) // 2048
            s = self.newslot("cv_" + sname)
            src = self.wall[r_lo:r_hi, :]
            dst = self.wbf[r_lo:r_hi, :]
            self.dma("pool", dst, src, s, list(after), [self.wbf_reg[sname]])

    def make_sched(self):
        L = []
        for s in range(self.nseq):
            L.append(("z", 0))
            L += [("wba", 0), ("ga", 0), ("wba", 1), ("ga", 1)]
            L += [("akv", 0)] + [("xwkv", i) for i in range(4)]
            for tb in range(4):
                L += [("aq", 0), ("aq", 1), ("wbb", 0), ("gb", 0), ("wbb", 1), ("gb", 1), ("wo", 0), ("wo", 1),
                      ("xwq", 0), ("xwq", 1), ("xwo", 0), ("xwo", 1)] + [("wi", i) for i in range(11)]
        self.wsched = L
        self.wptr = 0
        self.wissued = 0

    def wissue_upto(self, n):
        n = min(n, len(self.wsched))
        while self.wissued < n:
            k = self.wissued
            seg, g = self.wsched[k]
            i = k % self.nring
            src, L = self.wsrc(seg, g)
            assert seg in self.conv_done, seg
            self.dma("sp", self.ring[i][:, 0:L], src, self.ring_s[i], [self.wbf_reg[seg]], [self.ring_r[i]])
            self.wissued += 1

    def wget(self, seg, g, prefetch_only=False):
        if prefetch_only:
            return None
        if self.upto != "all":
            while self.wsched[self.wptr] != (seg, g):
                self.wptr += 1
            self.wissued = max(self.wissued, self.wptr)
        k = self.wptr
        assert self.wsched[k] == (seg, g), (k, self.wsched[k], seg, g)
        self.wissue_upto(k + self.nring - 1)
        self.wptr += 1
        i = k % self.nring
        return self.ring[i], self.ring_r[i]

    def rope_top(self, tb):
        for tab, kc in ((self.cosB, K_COS), (self.sinB, K_SIN)):
            src = self.K[0:64, kc + tb * 8:kc + tb * 8 + 8].unsqueeze(2).broadcast_to([64, 8, 64])
            dst = tab[0:64, :].rearrange("p (r c) -> p r c", c=64)
            self.P.op("pool", (lambda e, dst=dst, src=src: e.tensor_copy(dst, src)), [self.Kr], [self.csr])

    def norm_tile(self, t, xb=None, xbr=None):
        sc = self.sc
        if xb is None:
            xb, xbr = self.xb, self.xbr
        ss, nsr, xn, xnr = sc["ss"], sc["nsr"], sc["xn"], sc["xnr"]
        self.act(sc["junk"], xb[:, t, :], AF.Square, [xbr[t]], [sc["junkr"], nsr[t][0]], accum=ss[:, t:t + 1])
        self.act(ss[:, 4 + t:5 + t], ss[:, t:t + 1], AF.Ln, [nsr[t][0], self.epsr], [nsr[t][1]], scale=1.0 / D,
                 bias=self.epsv[:, 0:1])
        self.act(ss[:, 8 + t:9 + t], ss[:, 4 + t:5 + t], AF.Exp, [nsr[t][1]], [nsr[t][2]], scale=-0.5)
        self.ts("dve", xn[:, t, :], xb[:, t, :], ss[:, 8 + t:9 + t], ALU.mult, [xbr[t], nsr[t][2]], [xnr[t]])

    def bank_bf(self, b, lo, hi):
        v = self.PS[b // 2][:, :].bitcast(BF16)
        o = (b % 2) * 1024
        return v[:, o + lo:o + hi]

    def norm_fin(self, nt, gcol, dst, dst_regs, banks):
        xn, xnr = self.sc["xn"], self.sc["xnr"]
        ident = self.ident_bf
        for c in range(8):
            b = banks[c % len(banks)]
            for t in range(nt):
                o_ap = self.bank_bf(b, t * 128, (t + 1) * 128)
                i_ap = xn[:, t, c * 128:(c + 1) * 128]
                self.P.op("pe", (lambda e, o_ap=o_ap, i_ap=i_ap: e.transpose(o_ap, i_ap, ident)),
                          [xnr[t], self.Kr], [self.psr[b]])
            g = self.C[:, gcol + c:gcol + c + 1]
            src = self.bank_bf(b, 0, nt * 128)
            if c % 2 == 0:
                self.P.op("act", (lambda e, o=dst(c), i=src, g=g:
                                  e.activation(o, i, AF.Copy, scale=g)), [self.psr[b], self.Cr], [dst_regs[c]])
            else:
                self.ts("dve", dst(c), src, g, ALU.mult, [self.psr[b], self.Cr], [dst_regs[c]])

    def norm_T(self, xb, xbr, nt, gcol, dst, dst_regs, scratch, banks):
        for t in range(nt):
            self.norm_tile(t)
        self.norm_fin(nt, gcol, dst, dst_regs, banks)

    def eps_ap(self):
        return self.epsv[:, 0:1]

    def pbank(self, b, lo=0, hi=512, p0=0, p1=128):
        return self.PS[b // 2][p0:p1, (b % 2) * 512 + lo:(b % 2) * 512 + hi]

    def alloc_xb(self, nt=4, junk=None, need_tmp=False):
        A = self.A
        self.xb = A.alloc([nt, D], F32)
        self.xbr = regs(nt)
        if junk is None:
            junk = (A.alloc([D], BF16), Reg())
        self.sc = {"junk": junk[0], "junkr": junk[1], "ss": self.ssP, "nsr": [regs(3) for _ in range(4)],
                   "pnr": [regs(3) for _ in range(4)],
                   "xn": A.alloc([nt, D], BF16), "xnr": regs(nt)}
        if need_tmp:
            self.sc["tmp32"] = [A.alloc([D], F32) for _ in range(2)]
            self.sc["tmp32r"] = regs(2)

    def load_xb(self, src_rows, nt=4, extra_w=(), slots=None):
        slots = slots or self.xb_slot
        for t in range(nt):
            self.dma("sp", self.xb[:, t, :], src_rows[t * 128:(t + 1) * 128, :], slots[t], [],
                     [self.xbr[t]] + list(extra_w))

    def dbg_store(self, name, src_ap, src_regs, f32tmp=None):
        if name not in self.dbg_out:
            return
        dst = self.dbg_out[name]
        s = self.newslot("dbg_" + name, output=True)
        if f32tmp is not None:
            r = Reg()
            self.cp("dve", f32tmp, src_ap, src_regs, [r])
            self.dma("sp", dst, f32tmp, s, [r], [])
        else:
            self.dma("sp", dst, src_ap, s, src_regs, [])

    def alloc_nr(self):
        A = self.A
        self.nr = {"sqb": A.alloc([512], BF16), "tmpA": A.alloc([512], F32), "knf": A.alloc([512], F32),
                   "t2": A.alloc([512], F32), "r": regs(4), "knf2": A.alloc([512], F32), "r_kn2": Reg()}

    def normrope_pipe(self, items, pbanks, bs, br):
        nr = self.nr
        sqb, tmpA, t2 = nr["sqb"], nr["tmpA"], nr["t2"]
        knfs = [nr["knf"], nr["knf2"]]
        r_sq, r_tmp, r_kn0, r_t2 = nr["r"]
        r_kns = [r_kn0, nr["r_kn2"]]
        psr = self.psr
        n = len(items)
        nb = len(pbanks)
        assert nb >= 3

        def stA(i):
            items[i][0](pbanks[i % nb])

        def stB(i):
            b = pbanks[i % nb]
            src, src_reg = self.bank(b), psr[b]
            gcol = items[i][1]
            knf, r_kn = knfs[i % 2], r_kns[i % 2]
            self.act(sqb, src, AF.Square, [src_reg], [r_sq])
            self.mm(self.bank(bs), self.ones, sqb, True, True, [self.onesr, r_sq], [psr[bs]])
            self.act(tmpA, self.bank(bs), AF.Ln, [psr[bs], self.epsr], [r_tmp], scale=1.0 / 128, bias=self.epsv[:, 0:1])
            self.act(tmpA, tmpA, AF.Exp, [r_tmp], [r_tmp], scale=-0.5)
            self.stt(knf, src, self.C[:, gcol:gcol + 1], tmpA, ALU.mult, ALU.mult, [src_reg, self.Cr, r_tmp], [r_kn])

        def stC(i):
            _, gcol, out, out_regs, pre = items[i]
            knf, r_kn = knfs[i % 2], r_kns[i % 2]
            if pre is not None:
                pre()
            self.mm(self.bank(br), self.K[:, K_ROT:K_ROT + 128], knf, True, True, [self.Kr, r_kn], [psr[br]])
            self.tt("dve", t2, self.bank(br), self.sinB, ALU.mult, [psr[br], self.csr], [r_t2])
            self.tt("pool", knf, knf, self.cosB, ALU.mult, [r_kn, self.csr], [r_kn])
            self.tt("pool", out, knf, t2, ALU.add, [r_kn, r_t2], out_regs)

        for step in range(n + 2):
            if step < n:
                stA(step)
            if 0 <= step - 1 < n:
                stB(step - 1)
            if 0 <= step - 2 < n:
                stC(step - 2)

    def postnorm(self, k, t, ssr_i):
        sc = self.sc
        ss = sc["ss"]
        r0, r1, r2 = sc["pnr"][t]
        u = self.PS[k][:, :]
        pr = [self.psr[2 * k], self.psr[2 * k + 1]]
        self.act(sc["junk"], u, AF.Square, pr, [sc["junkr"], r0], accum=ss[:, 16 + t:17 + t])
        self.act(ss[:, 20 + t:21 + t], ss[:, 16 + t:17 + t], AF.Ln, [r0, self.epsr], [r1], scale=1.0 / D,
                 bias=self.epsv[:, 0:1])
        self.act(ss[:, 24 + t:25 + t], ss[:, 20 + t:21 + t], AF.Exp, [r1], [r2], scale=-0.5)
        tmp, tmpr = sc["tmp32"][t % 2], sc["tmp32r"][t % 2]
        self.stt(tmp, u, ss[:, 24 + t:25 + t], self.gp, ALU.mult, ALU.mult, pr + [r2, self.gpr], [tmpr])
        self.tt("dve", self.xb[:, t, :], self.xb[:, t, :], tmp, ALU.add, [self.xbr[t], tmpr], [self.xbr[t]])

    def load_gp(self, idx):
        self.dma("sp", self.gp, self.gpost[idx], self.gp_slot, [], [self.gpr])

    def phaseA(self, s):
        sets = []
        for i in range(2):
            self.alloc_xb(4)
            if i == 1:
                self.sc["nsr"], self.sc["pnr"] = sets[0][2]["nsr"], sets[0][2]["pnr"]
            sets.append((self.xb, self.xbr, self.sc))

        def load(tb):
            self.xb, self.xbr, self.sc = sets[tb % 2]
            if s == 0 and tb == 0:
                start = Reg()
                self.load_xb(self.x[s, 0:512, :], extra_w=[start], slots=self.xb_slot2[tb % 2])
                self.issue_conv(["z", "gla0"], after=[start])
            else:
                self.load_xb(self.x[s, tb * 512:(tb + 1) * 512, :], slots=self.xb_slot2[tb % 2])

        load(0)
        for tb in range(4):
            if tb + 1 < 4:
                load(tb + 1)
            self.xb, self.xbr, self.sc = sets[tb % 2]
            self.norm_T(self.xb, self.xbr, 4, C_MIXPRE,
                        lambda c, tb=tb: self.hT[:, c, tb * 512:(tb + 1) * 512],
                        [self.hTr[c][tb] for c in range(8)], self.sc, [0, 1] if tb % 2 == 0 else [2, 3])

    def phaseGLA(self, s):
        A, P, psr, bank, pbank = self.A, self.P, self.psr, self.bank, self.pbank
        hT, hTr = self.hT, self.hTr
        Kc = self.K
        Kb = self.Kbf
        zT = [A.alloc([T], BF16, parts=17) for _ in range(2)]
        zTr = [regs(4), regs(4)]
        for d in range(2):
            P.op("pool", (lambda e, z=zT[d]: e.memset(z, 1.0)), [], zTr[d])
        wz, wzr = self.wget("z", 0)
        wz3 = wz[:, 0:256].rearrange("p (c n) -> p c n", n=32)
        for tb in range(4):
            blk = slice(tb * 512, (tb + 1) * 512)
            for d in range(2):
                b = d
                for c in range(8):
                    self.mm(pbank(b, 0, 512, 0, 16), wz3[:, c, d * 16:(d + 1) * 16], hT[:, c, blk], c == 0, c == 7,
                            [wzr, hTr[c][tb]], [psr[b]])
                self.cp("dve", zT[d][0:16, blk], pbank(b, 0, 512, 0, 16), [psr[b]], [zTr[d][tb]])
        wgla = A.alloc([8 * 1536], BF16)
        wglar = Reg()
        wg3 = wgla.rearrange("p (c n) -> p c n", n=1536)
        Sb = A.alloc([16, 512], BF16)
        Sbr = regs(16)
        S32 = A.alloc([512], F32)
        S32r = regs(2)
        Sfbf = A.alloc([512], BF16)
        Sfbfr = Reg()

        def dbl(shape, dt):
            return [A.alloc(shape, dt) for _ in range(2)], regs(2)
        def sgl(shape, dt):
            a, r = A.alloc(shape, dt), Reg()
            return [a, a], [r, r]
        ef, efr = sgl([512], F32)
        sp, spr = sgl([512], BF16)
        E1, E1r = dbl([512], F32)
        E2, E2r = sgl([512], F32)
        E3, E3r = sgl([256], F32)
        vst = A.alloc([16, 512], BF16)
        vstr = regs(16)
        qgf, qgfr = dbl([256], BF16)
        qgb, qgbr = dbl([256], BF16)
        kgf, kgfr = dbl([256], BF16)
        kgb, kgbr = dbl([256], BF16)
        kend, kendr = dbl([256], BF16)
        gr, grr = dbl([512], F32)
        og, ogr = dbl([512], F32)
        decb, decbr = dbl([8], F32)
        AT = A.alloc([512], BF16)
        ATr = Reg()
        ss2 = A.alloc([8], F32)
        ss2r = regs(3)
        junk = AT[:, 0:256]
        junkr = ATr
        one_ap = self.epsv[:, 1:2]
        SC = float(128.0 ** -0.5)
        for p in range(2):
            src, L = self.wsrc("gla%d" % p, 0)
            self.dma("sp", wgla, src, self.wgla_slot, [self.wbf_reg["gla%d" % p]], [wglar])
            self.issue_conv(["aq", "wbb", "gb", "wo", "xwq", "xwo"] if p == 0 else ["wi", "wo2"])
            fo = p * 256
            bo = 512 + p * 256

            def b_s1(tt):
                q = tt % 2
                tk = slice(tt * 128, (tt + 1) * 128)
                tb = tt // 4
                hr = lambda c: hTr[c][tb]
                self.mm(bank(0, 0, 256), zT[1][0:17, tk], self.waug_b[0:17, bo:bo + 256], True, True,
                        [zTr[1][tb], self.waugr], [psr[0]])
                self.act(ef[q][:, 0:256], bank(0, 0, 256), AF.Exp, [psr[0]], [efr[q]], scale=-1.0)
                self.act(sp[q][:, 0:256], ef[q][:, 0:256], AF.Ln, [efr[q], self.epsr], [spr[q]], bias=one_ap)
                for c in range(8):
                    self.mm(bank(3), hT[:, c, tk], wg3[:, c, 512:1024], c == 0, c == 7, [hr(c), wglar], [psr[3]])
                for c in range(8):
                    self.mm(bank(2, 0, 256), hT[:, c, tk], wg3[:, c, 256:512], c == 0, c == 7, [hr(c), wglar], [psr[2]])
                for h in range(2):
                    self.mm(bank(1, h * 128, (h + 1) * 128), sp[q][:, h * 128:(h + 1) * 128], Kb[:, 256:384],
                            True, True, [spr[q], self.Kr], [psr[1]])
                self.mm(bank(1, 256, 512), Kb[:, 384:512], sp[q][:, 0:256], True, True, [spr[q], self.Kr], [psr[1]])
                self.cp("act", vst[:, tt, :], bank(3), [psr[3]], [vstr[tt]])
                self.act(decb[q][:, 0:2], bank(1, 0, 256).rearrange("p (h i) -> p h i", i=128)[:, :, 0], AF.Exp,
                         [psr[1]], [decbr[q]])
                self.act(E3[q], bank(1, 256, 512), AF.Exp, [psr[1]], [E3r[q]])
                self.tt("dve", kend[q], bank(2, 0, 256), E3[q], ALU.mult, [psr[2], E3r[q]], [kendr[q]])

            def b_s2(tt):
                q = tt % 2
                for h in range(2):
                    self.mm(bank(7, h * 256, (h + 1) * 256), kend[q][:, h * 128:(h + 1) * 128],
                            vst[:, tt, h * 256:(h + 1) * 256], True, True, [kendr[q], vstr[tt]], [psr[7]])
                for h in range(2):
                    hs = slice(h * 256, (h + 1) * 256)
                    self.stt(S32[:, hs], S32[:, hs], decb[q][:, h:h + 1], bank(7, h * 256, (h + 1) * 256), ALU.mult, ALU.add,
                             [S32r[h], decbr[q], psr[7]], [S32r[h]])
                if tt > 0:
                    self.cp("pool", Sb[:, tt - 1, :], S32, S32r, [Sbr[tt - 1]])

            P.op("pool", (lambda e: e.memset(S32, 0.0)), [], S32r)
            b_s1(15)
            for tt in range(15, -1, -1):
                if tt > 0:
                    b_s1(tt - 1)
                b_s2(tt)

            def f_s1(tt, mid_hook=None, pending=None):
                q = tt % 2
                tk = slice(tt * 128, (tt + 1) * 128)
                tb = tt // 4
                hr = lambda c: hTr[c][tb]
                self.mm(bank(0, 0, 256), zT[0][0:17, tk], self.waug_b[0:17, fo:fo + 256], True, True,
                        [zTr[0][tb], self.waugr], [psr[0]])
                self.mm(bank(0, 256, 512), zT[1][0:17, tk], self.waug_b[0:17, bo:bo + 256], True, True,
                        [zTr[1][tb], self.waugr], [psr[0]])
                self.act(ef[q], bank(0), AF.Exp, [psr[0]], [efr[q]], scale=-1.0)
                self.act(sp[q], ef[q], AF.Ln, [efr[q], self.epsr], [spr[q]], bias=one_ap)
                if pending is not None:
                    pending()
                for c in range(8):
                    self.mm(bank(2, 256, 512), hT[:, c, tk], wg3[:, c, 256:512], c == 0, c == 7, [hr(c), wglar], [psr[2]])
                if mid_hook is not None:
                    mid_hook()
                for d in range(2):
                    ku = 0 if d == 0 else 256
                    for h in range(2):
                        o = d * 256 + h * 128
                        self.mm(bank(1, o, o + 128), sp[q][:, o:o + 128], Kb[:, ku:ku + 128], True, True,
                                [spr[q], self.Kr], [psr[1]])
                self.mm(bank(2, 0, 256), Kb[:, 128:256], sp[q][:, 0:256], True, True, [spr[q], self.Kr], [psr[2]])
                self.act(E1[q], bank(1), AF.Exp, [psr[1]], [E1r[q]])
                self.act(E2[q], bank(1), AF.Exp, [psr[1]], [E2r[q]], scale=-1.0)
                self.act(E3[q], bank(2, 0, 256), AF.Exp, [psr[2]], [E3r[q]])
                for j in range(4):
                    for c in range(8):
                        self.mm(bank(0, j * 128, (j + 1) * 128), wg3[:, c, j * 128:(j + 1) * 128], hT[:, c, tk],
                                c == 0, c == 7, [hr(c), wglar], [psr[0]])
                self.tt("dve", kend[q], bank(2, 256, 512), E3[q], ALU.mult, [psr[2], E3r[q]], [kendr[q]])
                self.stt(qgf[q], bank(0, 0, 256), SC, E1[q][:, 0:256], ALU.mult, ALU.mult, [psr[0], E1r[q]], [qgfr[q]])
                self.stt(qgb[q], bank(0, 0, 256), SC, E1[q][:, 256:512], ALU.mult, ALU.mult, [psr[0], E1r[q]], [qgbr[q]])
                self.tt("dve", kgf[q], bank(0, 256, 512), E2[q][:, 0:256], ALU.mult, [psr[0], E2r[q]], [kgfr[q]])
                self.tt("dve", kgb[q], bank(0, 256, 512), E2[q][:, 256:512], ALU.mult, [psr[0], E2r[q]], [kgbr[q]])

            def f_R(tt):
                tk = slice(tt * 128, (tt + 1) * 128)
                tb = tt // 4
                for c in range(8):
                    self.mm(bank(4), hT[:, c, tk], wg3[:, c, 1024:1536], c == 0, c == 7, [hTr[c][tb], wglar], [psr[4]])

            def f_gr(tt):
                q = tt % 2
                self.act(gr[q], bank(4), AF.Exp, [psr[4]], [grr[q]], scale=-1.0)
                self.act(gr[q], gr[q], AF.Ln, [grr[q], self.epsr], [grr[q]], bias=one_ap)
                self.act(gr[q], gr[q], AF.Exp, [grr[q]], [grr[q]], scale=-1.0)
                self.tt("dve", gr[q], bank(4), gr[q], ALU.mult, [psr[4], grr[q]], [grr[q]])
                self.tt("pool", gr[q], gr[q], self.G2, ALU.mult, [grr[q], self.G2r], [grr[q]])

            def f_s2a(tt):
                q = tt % 2
                kg = (kgf[q], kgb[q])
                qg = (qgf[q], qgb[q])
                kgr = (kgfr[q], kgbr[q])
                qgr = (qgfr[q], qgbr[q])
                for d in range(2):
                    for h in range(2):
                        o = (d * 2 + h) * 128
                        self.mm(bank(5, o, o + 128), kg[d][:, h * 128:(h + 1) * 128], qg[d][:, h * 128:(h + 1) * 128],
                                True, True, [kgr[d], qgr[d]], [psr[5]])
                self.tt("dve", AT, bank(5), Kc[:, K_MASK:K_MASK + 512], ALU.mult, [psr[5], self.Kr], [ATr])

            def f_s2(tt):
                q = tt % 2
                for h in range(2):
                    self.mm(bank(7, h * 256, (h + 1) * 256), kend[q][:, h * 128:(h + 1) * 128],
                            vst[:, tt, h * 256:(h + 1) * 256], True, True, [kendr[q], vstr[tt]], [psr[7]])
                for h in range(2):
                    vh = vst[:, tt, h * 256:(h + 1) * 256]
                    seq = []
                    if tt > 0:
                        seq.append((qgf[q][:, h * 128:(h + 1) * 128], Sfbf[:, h * 256:(h + 1) * 256], [qgfr[q], Sfbfr]))
                    if tt < 15:
                        seq.append((qgb[q][:, h * 128:(h + 1) * 128], Sb[:, tt, h * 256:(h + 1) * 256], [qgbr[q], Sbr[tt]]))
                    seq += [(AT[:, h * 128:(h + 1) * 128], vh, [ATr, vstr[tt]]),
                            (AT[:, (2 + h) * 128:(3 + h) * 128], vh, [ATr, vstr[tt]])]
                    for i, (l, r, rd) in enumerate(seq):
                        self.mm(bank(6, h * 256, (h + 1) * 256), l, r, i == 0, i == len(seq) - 1, rd, [psr[6]])
                for h in range(2):
                    hs = slice(h * 256, (h + 1) * 256)
                    self.stt(S32[:, hs], S32[:, hs], E1[q][:, h * 128 + 127:h * 128 + 128], bank(7, h * 256, (h + 1) * 256),
                             ALU.mult, ALU.add, [S32r[h], E1r[q], psr[7]], [S32r[h]])
                self.cp("pool", Sfbf, S32, S32r, [Sfbfr])
                for h in range(2):
                    self.act(junk[:, 0:256], bank(6, h * 256, (h + 1) * 256), AF.Square, [psr[6]], [junkr, ss2r[0]],
                             accum=ss2[:, h:h + 1])
                self.act(ss2[:, 2:4], ss2[:, 0:2], AF.Ln, [ss2r[0], self.epsr], [ss2r[1]], scale=1.0 / 256,
                         bias=self.epsv[:, 0:1])
                self.act(ss2[:, 4:6], ss2[:, 2:4], AF.Exp, [ss2r[1]], [ss2r[2]], scale=-0.5)
                for h in range(2):
                    hs = slice(h * 256, (h + 1) * 256)
                    self.stt(og[q][:, hs], bank(6, h * 256, (h + 1) * 256), ss2[:, 4 + h:5 + h], gr[q][:, hs],
                             ALU.mult, ALU.mult, [psr[6], ss2r[2], grr[q]], [ogr[q]])

            def f_s3(tt):
                q = tt % 2
                tk = slice(tt * 128, (tt + 1) * 128)
                for e4 in range(4):
                    self.tr(bank(7, e4 * 128, (e4 + 1) * 128), og[q][:, e4 * 128:(e4 + 1) * 128], [ogr[q]], [psr[7]])
                self.cp("act", self.glaT[:, p * 4:(p + 1) * 4, tk], bank(7).rearrange("p (e t) -> p e t", t=128),
                        [psr[7]], [self.glaTr[c][tt] for c in range(p * 4, p * 4 + 4)])

            P.op("pool", (lambda e: e.memset(S32, 0.0)), [], S32r)
            f_s1(0)
            f_R(0)
            f_gr(0)
            for tt in range(16):
                f_s2a(tt)
                if tt + 1 < 16:
                    f_s1(tt + 1, (lambda tt=tt: f_s3(tt - 1)) if tt > 0 else None,
                         (lambda tt=tt: f_gr(tt)) if tt > 0 else None)
                    f_s2(tt)
                    f_R(tt + 1)
                else:
                    f_gr(tt)
                    f_s2(tt)
                    f_s3(tt - 1)
            f_s3(15)

    def phaseS4(self, s):
        A, psr, bank = self.A, self.psr, self.bank
        sig = [A.alloc([512], F32) for _ in range(2)]
        sigr = regs(2)
        k = 0
        for grp in range(2):
            wb, wbr = self.wget("wba", grp)
            wg, wgr = self.wget("ga", grp)
            wb3 = wb.rearrange("p (c n) -> p c n", n=512)
            wg3 = wg.rearrange("p (c n) -> p c n", n=512)
            for tb in range(4):
                blk = slice(tb * 512, (tb + 1) * 512)
                for fl in range(4):
                    fc = grp * 4 + fl
                    b0, b1 = (0, 1) if k % 2 == 0 else (2, 3)
                    for c in range(8):
                        self.mm(bank(b0), wb3[:, c, fl * 128:(fl + 1) * 128], self.glaT[:, c, blk], c == 0, c == 7,
                                [wbr] + self.glaTr[c][tb * 4:(tb + 1) * 4], [psr[b0]])
                    for c in range(8):
                        self.mm(bank(b1), wg3[:, c, fl * 128:(fl + 1) * 128], self.hT[:, c, blk], c == 0, c == 7,
                                [wgr, self.hTr[c][tb]], [psr[b1]])
                    self.act(sig[k % 2], bank(b1), AF.Sigmoid, [psr[b1]], [sigr[k % 2]])
                    self.tt("dve", self.maT[:, fc, blk], bank(b0), sig[k % 2], ALU.mult, [psr[b0], sigr[k % 2]],
                            [self.maTr[fc][tb]])
                    k += 1

    def phaseS2(self, s):
        A, psr, bank = self.A, self.psr, self.bank
        hT, hTr = self.hT, self.hTr
        self.alloc_nr()
        wkv, wkvr = self.wget("akv", 0)
        w3 = wkv.rearrange("p (c n) -> p c n", n=512)
        items = []
        for tb in range(4):
            blk = slice(tb * 512, (tb + 1) * 512)
            for g in range(2):
                def proj(b, g=g, blk=blk, tb=tb):
                    for c in range(8):
                        self.mm(bank(b), w3[:, c, g * 128:(g + 1) * 128], hT[:, c, blk], c == 0, c == 7,
                                [wkvr, hTr[c][tb]], [psr[b]])
                pre = (lambda tb=tb: self.rope_top(tb)) if g == 0 else None
                items.append((proj, C_KN, self.kT[:, g, blk], [self.kTr[g][tb]], pre))
        self.normrope_pipe(items, [2, 3, 4], 5, 6)
        for tb in range(4):
            for t in range(4):
                tile = tb * 4 + t
                for c in range(8):
                    self.mm(bank(7, 0, 256), hT[:, c, tile * 128:(tile + 1) * 128], w3[:, c, 256:512], c == 0, c == 7,
                            [wkvr, hTr[c][tb]], [psr[7]])
                self.cp("act", self.vatt[:, tile, :], bank(7, 0, 256), [psr[7]], [self.vattr[tile]])
        self.alloc_xb(2, junk=(self.nr["t2"].bitcast(BF16), self.nr["r"][3]))
        mT = A.alloc([8, 256], BF16)
        mTr = regs(8)
        self.load_xb(self.mem[s], 2)
        self.norm_T(self.xb, self.xbr, 2, C_MEM, lambda c: mT[:, c, :], mTr, self.sc, [0, 1])
        for grp in range(2):
            w, wr = self.wget("xwkv", grp)
            w3 = w.rearrange("p (c n) -> p c n", n=512)
            for fl in range(4):
                kc = grp * 4 + fl
                b = 2 + (kc % 2)
                for c in range(8):
                    self.mm(bank(b, 0, 256), w3[:, c, fl * 128:(fl + 1) * 128], mT[:, c, :], c == 0, c == 7,
                            [wr, mTr[c]], [psr[b]])
                self.cp("act" if kc % 2 == 0 else "dve", self.kmT[:, kc, :], bank(b, 0, 256), [psr[b]], [self.kmTr[kc]])
        for grp in range(2):
            w, wr = self.wget("xwkv", 2 + grp)
            w3 = w.rearrange("p (c n) -> p c n", n=512)
            for mt in range(2):
                b = 2 + mt
                for c in range(8):
                    self.mm(bank(b), mT[:, c, mt * 128:(mt + 1) * 128], w3[:, c, :], c == 0, c == 7, [wr, mTr[c]], [psr[b]])
                self.cp("act" if mt == 0 else "dve", self.vm[:, mt, grp * 512:(grp + 1) * 512], bank(b), [psr[b]],
                        [self.vmr[mt]])

    def alloc_S5(self):
        A = self.A
        self.sg = [A.alloc([512], F32) for _ in range(2)]
        self.sgr = regs(2)
        self.alloc_xb(4, junk=(self.sg[1].bitcast(BF16), self.sgr[1]), need_tmp=True)
        self.xb2 = [(self.xb, self.xbr), (A.alloc([4, D], F32), regs(4))]
        self.bufA = A.alloc([8, 512], BF16)
        self.bufB = A.alloc([8, 512], BF16)
        self.bufAr, self.bufBr = regs(8), regs(8)
        self.actT = A.alloc([22, 512], BF16)
        r = regs(22)
        for j in (13, 15, 17, 19):
            r[j + 1] = r[j]
        self.actTr = r
        self.bufC = self.actT[:, 0:8, :]
        self.bufCr = r[0:8]
        self.PT = [self.actT[:, 8 + i, :] for i in range(4)]
        self.PTr = [r[8 + i] for i in range(4)]

        def f32v(j):
            return self.actT[:, j:j + 2, :].rearrange("p a b -> p (a b)").bitcast(F32)
        self.nr = {"sqb": self.actT[:, 12, :], "tmpA": f32v(13), "knf": f32v(15), "t2": f32v(17),
                   "r": [r[12], r[13], r[15], r[17]], "knf2": f32v(19), "r_kn2": r[19]}
        self.rden = self.nr["tmpA"]
        self.rdenr = self.nr["r"][1]
        self.woring = [A.alloc([1024], BF16) for _ in range(3)]
        self.woring_r = regs(3)

    def wo2issue_upto(self, n):
        n = min(n, (self.cur_seq + 1) * 4 * 22)
        while self.wo2_issued < n:
            k = self.wo2_issued
            i = k % 3
            src, L = self.wsrc("wo2", k % 22)
            self.dma("sp", self.woring[i], src, self.woring_s[i], [self.wbf_reg["wo2"]], [self.woring_r[i]])
            self.wo2_issued += 1

    def wo2get(self, g):
        k = self.wo2_i
        assert k % 22 == g
        self.wo2issue_upto(k + 2)
        self.wo2_i += 1
        i = k % 3
        return self.woring[i].rearrange("p (k n) -> p k n", n=512), self.woring_r[i]

    def proj_res(self, srcs, wseg, gain_idx, next_norm=True):
        psr, bank = self.psr, self.bank
        self.load_gp(gain_idx)
        w0, w0r = self.wget(wseg, 0)
        w1, w1r = self.wget(wseg, 1)
        ws = [(w0.rearrange("p (c n) -> p c n", n=512), w0r), (w1.rearrange("p (c n) -> p c n", n=512), w1r)]
        for t in range(4):
            k = self.pn_k
            self.pn_k = (k + 1) % 2
            for half in range(2):
                w3, wr = ws[half]
                n = len(srcs) * 8
                i = 0
                for (apf, rf) in srcs:
                    for c in range(8):
                        self.mm(bank(2 * k + half), apf(c, t), w3[:, c, :], i == 0, i == n - 1, [wr] + rf(c, t),
                                [psr[2 * k + half]])
                        i += 1
            self.postnorm(k, t, 0)
            if next_norm and t >= 1:
                self.norm_tile(t - 1)
        if next_norm:
            self.norm_tile(3)

    def phaseS5(self, s, tb):
        psr, bank = self.psr, self.bank
        pb = tb % 2
        bufs = [(self.bufA, self.bufAr), (self.bufB, self.bufBr)]
        bufA, bufAr = bufs[pb]
        bufB, bufBr = bufs[1 - pb]
        bufC, bufCr = self.bufC, self.bufCr
        self.xb, self.xbr = self.xb2[pb]
        blk = slice(tb * 512, (tb + 1) * 512)
        if tb == 0:
            self.load_xb(self.x[s, blk, :])
            for t in range(4):
                self.norm_tile(t)
            self.norm_fin(4, C_MIXPRE, lambda c: bufA[:, c, :], bufAr, [0, 1])
        self.rope_top(tb)
        items = []
        wq = [self.wget("aq", grp) for grp in range(2)]
        for head in range(8):
            w, wr = wq[head // 4]
            w3 = w.rearrange("p (c n) -> p c n", n=512)
            hl = head % 4

            def proj(b, w3=w3, wr=wr, hl=hl):
                for c in range(8):
                    self.mm(bank(b), w3[:, c, hl * 128:(hl + 1) * 128], bufA[:, c, :], c == 0, c == 7,
                            [wr, bufAr[c]], [psr[b]])
            items.append((proj, C_QN, bufB[:, head, :], [bufBr[head]], None))
        self.normrope_pipe(items, [2, 5, 6], 3, 4)
        SCL = float(128.0 ** -0.5)
        sbanks = [2, 3, 0, 1]
        LA = 3
        ptl = list(zip(self.PT, self.PTr)) + [(self.actT[:, 15, :], self.actTr[15]), (self.actT[:, 17, :], self.actTr[17])]
        NP = len(ptl)
        seq = [(head, st) for head in range(8) for st in range(16)]

        aT, ar = self.actT, self.actTr
        S1 = [(aT[:, 12, :], ar[12]), (aT[:, 21, :], ar[21]), (aT[:, 19, :], ar[19])]
        S2 = [(aT[:, 15, :], ar[15]), (aT[:, 17, :], ar[17])]

        def score(j):
            head, st = seq[j]
            g = head // 4
            sb = sbanks[j % 4]
            self.mm(bank(sb), self.kT[:, g, st * 128:(st + 1) * 128], bufB[:, head, :], True, True,
                    [self.kTr[g][st // 4], bufBr[head]], [psr[sb]])
            self.act(ptl[j % NP][0], bank(sb), AF.Exp, [psr[sb]], [ptl[j % NP][1]], scale=SCL)
            if st % 2 == 1:
                s1, s1r = S1[(j // 2) % 3]
                self.tt("dve", s1, ptl[(j - 1) % NP][0], ptl[j % NP][0], ALU.add,
                        [ptl[(j - 1) % NP][1], ptl[j % NP][1]], [s1r])

        for j0 in range(LA):
            score(j0)
        for j in range(len(seq)):
            if j + LA < len(seq):
                score(j + LA)
            head, st = seq[j]
            g = head // 4
            ob = 4 + 2 * (head % 2)
            db = ob + 1
            pt, ptr = ptl[j % NP]
            self.mm(bank(ob), self.vatt[:, st, g * 128:(g + 1) * 128], pt, st == 0, st == 15,
                    [self.vattr[st], ptr], [psr[ob]])
            if st % 2 == 1:
                s1, s1r = S1[(j // 2) % 3]
                self.mm(bank(db), self.ones, s1, st == 1, st == 15, [self.onesr, s1r], [psr[db]])
            if st == 15:
                self.act(self.rden, bank(db), AF.Ln, [psr[db]], [self.rdenr])
                self.act(self.rden, self.rden, AF.Exp, [self.rdenr], [self.rdenr], scale=-1.0)
                self.tt("dve", bufC[:, head, :], bank(ob), self.rden, ALU.mult, [psr[ob], self.rdenr], [bufCr[head]])
        k = 0
        for grp in range(2):
            wb, wbr = self.wget("wbb", grp)
            wg, wgr = self.wget("gb", grp)
            wb3 = wb.rearrange("p (c n) -> p c n", n=512)
            wg3 = wg.rearrange("p (c n) -> p c n", n=512)
            for fl in range(4):
                fc = grp * 4 + fl
                b0, b1 = (0, 1) if k % 2 == 0 else (2, 3)
                for c in range(8):
                    self.mm(bank(b0), wb3[:, c, fl * 128:(fl + 1) * 128], bufC[:, c, :], c == 0, c == 7,
                            [wbr, bufCr[c]], [psr[b0]])
                for c in range(8):
                    self.mm(bank(b1), wg3[:, c, fl * 128:(fl + 1) * 128], bufA[:, c, :], c == 0, c == 7,
                            [wgr, bufAr[c]], [psr[b1]])
                self.act(self.sg[k % 2], bank(b1), AF.Sigmoid, [psr[b1]], [self.sgr[k % 2]])
                self.tt("dve", self.sg[k % 2], bank(b0), self.sg[k % 2], ALU.mult, [psr[b0], self.sgr[k % 2]], [self.sgr[k % 2]])
                self.tt("pool", bufB[:, fc, :], self.sg[k % 2], self.maT[:, fc, blk], ALU.add,
                        [self.sgr[k % 2], self.maTr[fc][tb]], [bufBr[fc]])
                k += 1
        srcs = [(lambda c, t: bufB[:, c, t * 128:(t + 1) * 128], lambda c, t: [bufBr[c]])]
        self.proj_res(srcs, "wo", 0)
        if self.upto == "x1":
            return
        self.norm_fin(4, C_XPRE, lambda c: bufA[:, c, :], bufAr, [0, 1, 2, 3])
        for grp in range(2):
            w, wr = self.wget("xwq", grp)
            w3 = w.rearrange("p (c n) -> p c n", n=512)
            for fl in range(4):
                fc = grp * 4 + fl
                b = 4 + (fc % 2)
                for c in range(8):
                    self.mm(bank(b), w3[:, c, fl * 128:(fl + 1) * 128], bufA[:, c, :], c == 0, c == 7,
                            [wr, bufAr[c]], [psr[b]])
                self.cp("act" if fc % 2 == 0 else "dve", bufC[:, fc, :], bank(b), [psr[b]], [bufCr[fc]])
        for head in range(4):
            for mt in range(2):
                sb = 6 + mt
                for dc in range(2):
                    self.mm(bank(sb), self.kmT[:, head * 2 + dc, mt * 128:(mt + 1) * 128], bufC[:, head * 2 + dc, :],
                            dc == 0, dc == 1, [self.kmTr[head * 2 + dc], bufCr[head * 2 + dc]], [psr[sb]])
                pi = (head % 2) * 2 + mt
                self.act(self.PT[pi], bank(sb), AF.Exp, [psr[sb]], [self.PTr[pi]], scale=1.0 / 16.0)
            base_b = 0 if head % 2 == 0 else 3
            for dc in range(2):
                for mt in range(2):
                    pi = (head % 2) * 2 + mt
                    self.mm(bank(base_b + dc), self.vm[:, mt, head * 256 + dc * 128:head * 256 + (dc + 1) * 128],
                            self.PT[pi], mt == 0, mt == 1, [self.vmr[mt], self.PTr[pi]], [psr[base_b + dc]])
            db = base_b + 2
            for mt in range(2):
                pi = (head % 2) * 2 + mt
                self.mm(bank(db), self.ones, self.PT[pi], mt == 0, mt == 1, [self.onesr, self.PTr[pi]], [psr[db]])
            self.act(self.rden, bank(db), AF.Ln, [psr[db]], [self.rdenr])
            self.act(self.rden, self.rden, AF.Exp, [self.rdenr], [self.rdenr], scale=-1.0)
            for dc in range(2):
                self.tt("dve", bufB[:, head * 2 + dc, :], bank(base_b + dc), self.rden, ALU.mult,
                        [psr[base_b + dc], self.rdenr], [bufBr[head * 2 + dc]])
        nxb, nxbr = self.xb2[1 - pb]
        if tb < 3 and self.upto == "all":
            nblk = self.x[s, (tb + 1) * 512:(tb + 2) * 512, :]
            for t in range(4):
                self.dma("sp", nxb[:, t, :], nblk[t * 128:(t + 1) * 128, :], self.xb_slot2[1 - pb][t], [], [nxbr[t]])
        srcs = [(lambda c, t: bufB[:, c, t * 128:(t + 1) * 128], lambda c, t: [bufBr[c]])]
        self.proj_res(srcs, "xwo", 1)
        if self.upto == "x2":
            return
        self.norm_fin(4, C_FFNPRE, lambda c: bufA[:, c, :], bufAr, [0, 1, 2, 3])
        k = 0
        prefetch = tb < 3 and self.upto == "all"
        for i in range(11):
            w, wr = self.wget("wi", i)
            w3 = w.rearrange("p (c n) -> p c n", n=512)
            if prefetch and i == 1:
                for t in range(4):
                    self.norm_tile(t, nxb, nxbr)
            if prefetch and i == 5:
                self.norm_fin(4, C_MIXPRE, lambda c: bufB[:, c, :], bufBr, [0, 1])
            for j in range(2):
                bg, bu = (0, 1) if k % 2 == 0 else (2, 3)
                for c in range(8):
                    self.mm(bank(bg), w3[:, c, j * 128:(j + 1) * 128], bufA[:, c, :], c == 0, c == 7, [wr, bufAr[c]], [psr[bg]])
                for c in range(8):
                    self.mm(bank(bu), w3[:, c, 256 + j * 128:256 + (j + 1) * 128], bufA[:, c, :], c == 0, c == 7,
                            [wr, bufAr[c]], [psr[bu]])
                self.act(self.sg[k % 2], bank(bg), AF.Silu, [psr[bg]], [self.sgr[k % 2]])
                self.tt("dve", self.actT[:, 2 * i + j, :], bank(bu), self.sg[k % 2], ALU.mult, [psr[bu], self.sgr[k % 2]],
                        [self.actTr[2 * i + j]])
                k += 1
        self.load_gp(2)
        for hf in range(2):
            for i in range(11):
                w3, wr = self.wo2get(hf * 11 + i)
                for t in range(4):
                    for kk in range(2):
                        self.mm(bank(2 * t + hf), self.actT[:, 2 * i + kk, t * 128:(t + 1) * 128], w3[:, kk, :],
                                i == 0 and kk == 0, i == 10 and kk == 1, [wr, self.actTr[2 * i + kk]], [psr[2 * t + hf]])
        for t in range(4):
            self.postnorm(t, t, 0)
            self.dma("sp", self.y[s, tb * 512 + t * 128:tb * 512 + (t + 1) * 128, :], self.xb[:, t, :], self.y_slot[t],
                     [self.xbr[t]], [])

    def build(self):
        self.init_consts()
        A, P = self.A, self.P
        self.ssP = A.alloc([32], F32)
        self.xb_slot = [self.newslot("xb%d" % i) for i in range(4)]
        self.xb_slot2 = [self.xb_slot, [self.newslot("xbB%d" % i) for i in range(4)]]
        self.y_slot = [self.newslot("ystore%d" % i, output=True) for i in range(4)]
        self.wgla_slot = self.newslot("wgla")
        self.woring_s = [self.newslot("wo2r%d" % i) for i in range(4)]
        self.wo2_i = 0
        self.wo2_issued = 0
        self.pn_k = 0
        self.base = A.mark()
        o = A.nbytes - 56 * 1024
        self.qoff = o
        self.maT, o = A.alloc_at(o, [8, T], BF16)
        self.kT, o = A.alloc_at(o, [2, T], BF16)
        self.vatt, o = A.alloc_at(o, [16, 256], BF16)
        self.kmT, o = A.alloc_at(o, [8, 256], BF16)
        self.vm, o = A.alloc_at(o, [2, 1024], BF16)
        assert o <= A.nbytes
        up = self.upto
        for s in range(self.nseq):
            self.cur_seq = s
            self.maTr = [regs(4) for _ in range(8)]
            self.kTr = [regs(4) for _ in range(2)]
            self.vattr = regs(16)
            self.kmTr = regs(8)
            self.vmr = regs(2)
            if s > 0:
                P.fence()
            A.reset(self.base)
            self.hT = A.alloc([8, T], BF16)
            self.hTr = [regs(4) for _ in range(8)]
            m1 = A.mark()
            self.phaseA(s)
            self.issue_conv(["gla1", "wba", "ga", "akv", "xwkv"])
            if up == "A":
                tmp = A.alloc([8, T], F32)
                self.dbg_store("hT", self.hT, [r for rr in self.hTr for r in rr], tmp)
                break
            P.fence()
            A.reset(m1)
            self.glaT = A.alloc([8, T], BF16)
            self.glaTr = [regs(16) for _ in range(8)]
            m2 = A.mark()
            self.phaseGLA(s)
            P.fence()
            A.reset(m2)
            if up == "GLA":
                tmp = A.alloc([8, T], F32)
                self.dbg_store("glaT", self.glaT, [r for rr in self.glaTr for r in rr], tmp)
                break
            self.phaseS4(s)
            self.phaseS2(s)
            assert A.off <= self.qoff, (A.off, self.qoff)
            if up == "S2":
                tmp = A.alloc_at(self.base, [8, T], F32)[0]
                P.fence()
                self.dbg_store("maT", self.maT, [r for rr in self.maTr for r in rr], tmp)
                tmp2 = A.alloc_at(self.base + 65536, [2, T], F32)[0]
                self.dbg_store("kT", self.kT, [r for rr in self.kTr for r in rr], tmp2)
                tmp3 = A.alloc_at(self.base + 65536 + 16384, [16, 256], F32)[0]
                self.dbg_store("vatt", self.vatt, self.vattr, tmp3)
                tmp4 = A.alloc_at(self.base + 65536 + 32768, [8, 256], F32)[0]
                self.dbg_store("kmT", self.kmT, self.kmTr, tmp4)
                tmp5 = A.alloc_at(self.base + 65536 + 32768 + 8192, [2, 1024], F32)[0]
                self.dbg_store("vm", self.vm, self.vmr, tmp5)
                break
            P.fence()
            A.reset(self.base)
            self.alloc_S5()
            assert A.off <= self.qoff, (A.off, self.qoff)
            for tb in range(4):
                self.phaseS5(s, tb)
                if up in ("x1", "x2"):
                    for t in range(4):
                        self.dma("sp", self.y[s, tb * 512 + t * 128:tb * 512 + (t + 1) * 128, :], self.xb[:, t, :],
                                 self.y_slot[t], [self.xbr[t]], [])
        P.emit()
        return self.nc


def make_core_inputs(inp, xs, ms, shared=None):
    if shared is None:
        shared = make_shared(inp)
    d = dict(shared)
    d["x"] = np.ascontiguousarray(xs, dtype=np.float32)
    d["mem"] = np.ascontiguousarray(ms, dtype=np.float32)
    return d


def make_shared(inp):
    gpost = np.stack([np.broadcast_to(inp[k][0][None, :], (128, D)) for k in ("ln_mix_post", "ln_x_post", "ln_ffn_post")], 0)
    gn = inp["gla_norm"][0]
    gnorm2 = np.broadcast_to(np.concatenate([gn, gn])[None, :], (128, 512))
    waug = np.zeros((17, 1024), np.float32)
    waug[:16, :512] = inp["gla_wa_f"][0]
    waug[:16, 512:] = inp["gla_wa_b"][0]
    waug[16, :512] = inp["gla_ba_f"][0]
    waug[16, 512:] = inp["gla_ba_b"][0]
    return {
        "wall": build_wall(inp),
        "kpack": build_kpack(),
        "cpack": build_cpack(inp),
        "gpost": np.ascontiguousarray(gpost, dtype=np.float32),
        "gnorm2": np.ascontiguousarray(gnorm2, dtype=np.float32),
        "waug": waug,
    }


_CACHE = {}


def kernel(**inputs):
    inp = {k: np.asarray(v) for k, v in inputs.items()}
    xs = np.concatenate([inp["x_prompt"], inp["x_sample"]], 0)
    ms = np.concatenate([inp["mem_prompt"], inp["mem_sample"]], 0)
    nb = xs.shape[0]
    assert nb == NCORES * SEQ_PER_CORE
    shared = make_shared(inp)
    if "nc" not in _CACHE:
        _CACHE["nc"] = Builder(nseq=SEQ_PER_CORE, upto="all").build()
    nc = _CACHE["nc"]
    in_maps = []
    for c in range(NCORES):
        sl = slice(c * SEQ_PER_CORE, (c + 1) * SEQ_PER_CORE)
        in_maps.append(make_core_inputs(inp, xs[sl], ms[sl], shared))
    res = run_bass_kernel_spmd(nc, in_maps, core_ids=list(range(NCORES)))
    y = np.concatenate([np.asarray(r["y"]) for r in res.results], 0).astype(np.float32)
    nprompt = inp["x_prompt"].shape[0]
    return (np.ascontiguousarray(y[:nprompt]), np.ascontiguousarray(y[nprompt:]))
```
